# Optimizing a Trainium2 kernel written in Bass

```python
import jax
import jax.numpy as jnp
from jax import lax
import numpy as np

D_MODEL = 1024
BATCH = 2
SEQ = 8192
DEPTH = 1

CTX_LEN = 256
GRID_W = 64
D_MIX = D_MODEL
RW_WIDTH = D_MIX // 2
RW_HEAD = 64
RW_HEADS = RW_WIDTH // RW_HEAD
DECAY_LORA = 64
AAA_LORA = 64
MLA_WIDTH = D_MIX - RW_WIDTH
MLA_HEADS = 8
QK_NOPE = 64
QK_ROPE = 32
QK_DIM = QK_NOPE + QK_ROPE
V_HEAD = MLA_WIDTH // MLA_HEADS
Q_LORA = 384
KV_LORA = 256
AXIS_DIM = QK_ROPE // 2
ROPE_THETA = 10000.0
Q_BLOCK = 128
NORM_EPS = 1e-6
LNX_EPS = 64e-5
ATTN_SCALE = QK_DIM ** -0.5
N_SHIFT = 3 * RW_WIDTH + 2 * DECAY_LORA + 2 * AAA_LORA
MLA_LAT = Q_LORA + KV_LORA + QK_ROPE
D_IN = N_SHIFT + RW_WIDTH + MLA_LAT + MLA_WIDTH
SPLITS = (N_SHIFT, N_SHIFT + RW_WIDTH, N_SHIFT + RW_WIDTH + MLA_LAT)

kernel_name = 'hymba_rwkv7_mla_dit_block'


def rms_norm(x, g):
    xf = x.astype(jnp.float32)
    y = xf * lax.rsqrt(jnp.mean(xf * xf, axis=-1, keepdims=True) + NORM_EPS)
    return (y * g.astype(jnp.float32)).astype(x.dtype)


def token_shift(u, mu_prev, mu_next):
    prev = jnp.pad(u[:, :-1], ((0, 0), (1, 0), (0, 0)))
    nxt = jnp.pad(u[:, 1:], ((0, 0), (0, 1), (0, 0)))
    return u + mu_prev * (prev - u) + mu_next * (nxt - u)


def to_heads(t):
    return t.reshape(t.shape[:-1] + (RW_HEADS, RW_HEAD))


def rwkv_prep(u_shift, mu, w0, w2, a0, a2, k_k, k_a):
    us = token_shift(u_shift, mu[0], mu[1])
    r, k, v, w_in, a_in = jnp.split(
        us, [RW_WIDTH, 2 * RW_WIDTH, 3 * RW_WIDTH, 3 * RW_WIDTH + 2 * DECAY_LORA], axis=-1)
    B, L = us.shape[:2]
    w_in = w_in.reshape(B, L, 2, DECAY_LORA)
    a_in = a_in.reshape(B, L, 2, AAA_LORA)
    w_log = -jax.nn.softplus(-(w0[:, None, None, :] + jnp.einsum('blde,def->dblf', jnp.tanh(w_in), w2))) - 0.5
    decay = jnp.exp(-jnp.exp(w_log.astype(jnp.float32)))
    a = jax.nn.sigmoid(a0[:, None, None, :] + jnp.einsum('blde,def->dblf', a_in, a2))
    kkf = to_heads(k * k_k).astype(jnp.float32)
    kk = (kkf * lax.rsqrt(jnp.maximum(jnp.sum(kkf * kkf, -1, keepdims=True), 1e-24))).astype(k.dtype)
    a = to_heads(a)
    k_dir = to_heads(k)[None] * (1 + (a - 1) * to_heads(k_a))
    b_dir = kk[None] * a
    return to_heads(r), kk, to_heads(v), to_heads(decay), k_dir, b_dir


def rwkv_scan(state0, r, decay, k, v, a_neg, b, reverse, emit):
    xs = tuple(jnp.moveaxis(t.astype(jnp.float32), 1, 0) for t in (r, decay, k, v, a_neg, b))

    def step(S, inp):
        r_t, w_t, k_t, v_t, a_t, b_t = inp
        sa = jnp.einsum('bhij,bhj->bhi', S, a_t)
        S = S * w_t[:, :, None, :] + sa[..., None] * b_t[:, :, None, :] + v_t[..., None] * k_t[:, :, None, :]
        y = jnp.einsum('bhij,bhj->bhi', S, r_t) if emit else None
        return S, y

    S, ys = lax.scan(step, state0, xs, reverse=reverse)
    return S, (jnp.moveaxis(ys, 0, 1) if emit else None)


def rwkv_branch(prep, states0, r_k, lnx_g, lnx_b, gate, emit):
    r, kk, v, decay, k_dir, b_dir = prep
    S_f, y_f = rwkv_scan(states0[0], r, decay[0], k_dir[0], v, -kk, b_dir[0], False, emit)
    S_b, y_b = rwkv_scan(states0[1], r, decay[1], k_dir[1], v, -kk, b_dir[1], True, emit)
    if not emit:
        return None, (S_f, S_b)
    y = y_f + y_b
    mean = jnp.mean(y, -1, keepdims=True)
    var = jnp.mean(jnp.square(y - mean), -1, keepdims=True)
    yn = (y - mean) * lax.rsqrt(var + LNX_EPS) * to_heads(lnx_g).astype(jnp.float32) \
        + to_heads(lnx_b).astype(jnp.float32)
    bonus = jnp.sum(r * (k_dir[0] + k_dir[1]) * r_k, -1, keepdims=True) * v
    out = (yn + bonus.astype(jnp.float32)).astype(gate.dtype)
    B, L = gate.shape[:2]
    return out.reshape(B, L, RW_WIDTH) * jax.nn.silu(gate), (S_f, S_b)


def rot_half(t, cos, sin):
    t1, t2 = t[..., :AXIS_DIM // 2], t[..., AXIS_DIM // 2:]
    return jnp.concatenate([t1 * cos - t2 * sin, t1 * sin + t2 * cos], axis=-1)


def rope_axial(t, cos_r, sin_r, cos_c, sin_c):
    return jnp.concatenate([rot_half(t[..., :AXIS_DIM], cos_r, sin_r),
                            rot_half(t[..., AXIS_DIM:], cos_c, sin_c)], axis=-1)


def mla_queries(u_mla, q_g, w_uq, rot):
    B, L = u_mla.shape[:2]
    c_q = rms_norm(u_mla[..., :Q_LORA], q_g)
    q = jnp.einsum('blr,rf->blf', c_q, w_uq).reshape(B, L, MLA_HEADS, QK_DIM)
    if rot is not None:
        q = jnp.concatenate([q[..., :QK_NOPE], rope_axial(q[..., QK_NOPE:], *rot)], axis=-1)
    return q


def mla_keys_values(u_mla, kv_g, w_ukv, rot):
    B, L = u_mla.shape[:2]
    c_kv = rms_norm(u_mla[..., Q_LORA:Q_LORA + KV_LORA], kv_g)
    k_r = u_mla[..., Q_LORA + KV_LORA:][:, :, None, :]
    if rot is not None:
        k_r = rope_axial(k_r, *rot)
    kv = jnp.einsum('blr,rf->blf', c_kv, w_ukv).reshape(B, L, MLA_HEADS, QK_NOPE + V_HEAD)
    k_nope, v = kv[..., :QK_NOPE], kv[..., QK_NOPE:]
    k = jnp.concatenate([k_nope, jnp.broadcast_to(k_r, (B, L, MLA_HEADS, QK_ROPE))], axis=-1)
    return k, v


def softmax_attend(q, k, v):
    s = jnp.einsum('bqhd,bkhd->bhqk', q, k).astype(jnp.float32) * ATTN_SCALE
    p = jax.nn.softmax(s, axis=-1).astype(v.dtype)
    return jnp.einsum('bhqk,bkhd->bqhd', p, v)


def block_attention(q, k_all, v_all):
    B, L, H, dk = q.shape
    nb = L // Q_BLOCK
    qb = q.reshape(B, nb, Q_BLOCK, H, dk).swapaxes(0, 1)
    o = lax.map(lambda blk: softmax_attend(blk, k_all, v_all), qb)
    return o.swapaxes(0, 1).reshape(B, L, H * V_HEAD)


def hybrid_layer(x, ctx, mod, mod_c, rot, params, need_ctx_out):
    (norm_g, w_in, shift_mu, rw_w0, rw_w2, rw_a0, rw_a2, rw_kk, rw_ka, rw_rk,
     rw_lnx_g, rw_lnx_b, q_g, kv_g, w_uq, w_ukv, w_out) = params
    B, L = x.shape[:2]
    shift, scale, gate = jnp.split(mod, 3, axis=-1)
    shift_c, scale_c, gate_c = jnp.split(mod_c, 3, axis=-1)
    h = rms_norm(x, norm_g) * (1 + scale) + shift
    hc = rms_norm(ctx, norm_g) * (1 + scale_c) + shift_c
    u_sh, g_rw, u_mla, g_mla = jnp.split(h @ w_in, SPLITS, axis=-1)
    uc_sh, gc_rw, uc_mla, gc_mla = jnp.split(hc @ w_in, SPLITS, axis=-1)

    rw_p = (shift_mu, rw_w0, rw_w2, rw_a0, rw_a2, rw_kk, rw_ka)
    S0 = jnp.zeros((B, RW_HEADS, RW_HEAD, RW_HEAD), jnp.float32)
    rw_c, ctx_states = rwkv_branch(rwkv_prep(uc_sh, *rw_p), (S0, S0), rw_rk, rw_lnx_g, rw_lnx_b,
                                   gc_rw, need_ctx_out)
    rw_l, _ = rwkv_branch(rwkv_prep(u_sh, *rw_p), ctx_states, rw_rk, rw_lnx_g, rw_lnx_b, g_rw, True)

    kc, vc = mla_keys_values(uc_mla, kv_g, w_ukv, None)
    k, v = mla_keys_values(u_mla, kv_g, w_ukv, rot)
    q = mla_queries(u_mla, q_g, w_uq, rot)
    o = block_attention(q, jnp.concatenate([k, kc], axis=1), jnp.concatenate([v, vc], axis=1))
    mla_l = o * jax.nn.silu(g_mla)

    x = x + gate * (jnp.concatenate([rw_l, mla_l], axis=-1) @ w_out)
    if need_ctx_out:
        qc = mla_queries(uc_mla, q_g, w_uq, None)
        mla_c = softmax_attend(qc, kc, vc).reshape(B, ctx.shape[1], MLA_WIDTH) * jax.nn.silu(gc_mla)
        ctx = ctx + gate_c * (jnp.concatenate([rw_c, mla_c], axis=-1) @ w_out)
    return x, ctx


def setup_inputs(seed: int = 0) -> dict:
    key = jax.random.key(seed)
    ks = jax.random.split(key, 24)
    f32 = jnp.float32

    def nrm(k, shape, s):
        return jax.random.normal(k, shape, f32) * s

    return {
        'x': nrm(ks[0], (BATCH, SEQ, D_MODEL), 1.0),
        'c': nrm(ks[1], (BATCH, D_MODEL), 1.0),
        'ctx': nrm(ks[2], (BATCH, CTX_LEN, D_MODEL), 1.0),
        'c_ctx': nrm(ks[3], (D_MODEL,), 1.0),
        'ada_w': nrm(ks[4], (DEPTH, D_MODEL, 3 * D_MODEL), 0.5 * D_MODEL ** -0.5),
        'ada_b': nrm(ks[5], (DEPTH, 3 * D_MODEL), 0.02),
        'norm_g': 1.0 + nrm(ks[6], (DEPTH, D_MODEL), 0.05),
        'w_in': nrm(ks[7], (DEPTH, D_MODEL, D_IN), D_MODEL ** -0.5),
        'shift_mu': jax.random.uniform(ks[8], (DEPTH, 2, N_SHIFT), f32, 0.0, 0.5),
        'rw_w0': jax.random.uniform(ks[9], (DEPTH, 2, RW_WIDTH), f32, -6.5, -1.5),
        'rw_w2': nrm(ks[10], (DEPTH, 2, DECAY_LORA, RW_WIDTH), 0.1 * DECAY_LORA ** -0.5),
        'rw_a0': nrm(ks[11], (DEPTH, 2, RW_WIDTH), 0.1),
        'rw_a2': nrm(ks[12], (DEPTH, 2, AAA_LORA, RW_WIDTH), AAA_LORA ** -0.5),
        'rw_kk': 0.85 + nrm(ks[13], (DEPTH, RW_WIDTH), 0.02),
        'rw_ka': 1.0 + nrm(ks[14], (DEPTH, RW_WIDTH), 0.02),
        'rw_rk': nrm(ks[15], (DEPTH, RW_HEADS, RW_HEAD), 0.1),
        'rw_lnx_g': 1.0 + nrm(ks[16], (DEPTH, RW_WIDTH), 0.05),
        'rw_lnx_b': nrm(ks[17], (DEPTH, RW_WIDTH), 0.02),
        'mla_q_norm_g': 1.0 + nrm(ks[18], (DEPTH, Q_LORA), 0.05),
        'mla_kv_norm_g': 1.0 + nrm(ks[19], (DEPTH, KV_LORA), 0.05),
        'mla_w_uq': nrm(ks[20], (DEPTH, Q_LORA, MLA_HEADS * QK_DIM), Q_LORA ** -0.5),
        'mla_w_ukv': nrm(ks[21], (DEPTH, KV_LORA, MLA_HEADS * (QK_NOPE + V_HEAD)), KV_LORA ** -0.5),
        'w_out': nrm(ks[22], (DEPTH, D_MIX, D_MODEL), D_MIX ** -0.5),
        'final_g': 1.0 + nrm(ks[23], (D_MODEL,), 0.05),
    }


def reference(x, c, ctx, c_ctx, ada_w, ada_b, norm_g, w_in, shift_mu, rw_w0, rw_w2, rw_a0, rw_a2,
              rw_kk, rw_ka, rw_rk, rw_lnx_g, rw_lnx_b, mla_q_norm_g, mla_kv_norm_g, mla_w_uq,
              mla_w_ukv, w_out, final_g):
    L = x.shape[1]
    rows = L // GRID_W
    row = jnp.repeat(jnp.arange(rows, dtype=jnp.float32), GRID_W)
    col = jnp.tile(jnp.arange(GRID_W, dtype=jnp.float32), rows)
    inv_freq = ROPE_THETA ** (-jnp.arange(0, AXIS_DIM, 2, dtype=jnp.float32) / AXIS_DIM)
    ang_r = (row[:, None] * inv_freq)[:, None, :]
    ang_c = (col[:, None] * inv_freq)[:, None, :]
    rot = (jnp.cos(ang_r).astype(x.dtype), jnp.sin(ang_r).astype(x.dtype),
           jnp.cos(ang_c).astype(x.dtype), jnp.sin(ang_c).astype(x.dtype))

    for i in range(DEPTH):
        mod = (jax.nn.silu(c) @ ada_w[i] + ada_b[i])[:, None, :]
        mod_c = jax.nn.silu(c_ctx) @ ada_w[i] + ada_b[i]
        params = (norm_g[i], w_in[i], shift_mu[i], rw_w0[i], rw_w2[i], rw_a0[i], rw_a2[i], rw_kk[i],
                  rw_ka[i], rw_rk[i], rw_lnx_g[i], rw_lnx_b[i], mla_q_norm_g[i], mla_kv_norm_g[i],
                  mla_w_uq[i], mla_w_ukv[i], w_out[i])
        x, ctx = hybrid_layer(x, ctx, mod, mod_c, rot, params, i < DEPTH - 1)
    return rms_norm(x, final_g)
```

```python
import numpy as np
from contextlib import ExitStack
import concourse.bass as bass
import concourse.mybir as mybir
from concourse.bass_utils import run_bass_kernel_spmd

F32 = mybir.dt.float32
BF16 = mybir.dt.bfloat16
ALU = mybir.AluOpType
AF = mybir.ActivationFunctionType
AX = mybir.AxisListType
BANK = 30000
NDMA = 24
ENGS = ['pe', 'act', 'dve', 'pool', 'sp']
DEBUG = False
CFG = {'slots': None, 'p2': True, 'p3': True, 'stage': 99}

D_MODEL = 1024
D_IN = 3488
NSTATE = 52
NSLOT = 84
NKT = 68
NKEY = NKT * 128
KAPPA = -float(np.exp(-0.5))
ATTN_SCALE = 96 ** -0.5
NORM_EPS = 1e-6
LNX_EPS = 64e-5


class Trk:
    __slots__ = ('w', 'r')

    def __init__(self):
        self.w = None
        self.r = {}


class Buf:
    def __init__(self, t, ap=None):
        self.t = t
        self.k = Trk()
        self._ap = ap

    def __getitem__(self, idx):
        return (self._ap if self._ap is not None else self.t)[idx]

    @property
    def ap(self):
        return self._ap if self._ap is not None else self.t[:]


class Prog:
    def __init__(self, nc):
        self.nc = nc
        self.es = ExitStack()
        self.ops = {e: [] for e in ENGS}
        self.cnt = {e: 0 for e in ENGS}
        self.seen = {e: {} for e in ENGS}
        self.sems = {}
        self.dma_tot = [0] * NDMA
        self.dma_next = 0

    def sb(self, name, shape, dt=F32):
        return Buf(self.es.enter_context(self.nc.sbuf_tensor('sb_' + name, list(shape), dt)))

    def ps(self, name, shape, dt=F32):
        return Buf(self.es.enter_context(self.nc.psum_tensor('ps_' + name, list(shape), dt)))

    def view(self, ap):
        return Buf(None, ap)

    def _deps(self, eng, reads, writes):
        deps = {}

        def add(ev):
            if ev is None:
                return
            k, v = ev
            if deps.get(k, 0) < v:
                deps[k] = v
        for b in reads:
            add(b.k.w)
        for b in writes:
            add(b.k.w)
            for k, v in b.k.r.items():
                add((k, v))
        waits = []
        for k, v in deps.items():
            if k[0] == 'pe' and eng == 'pe':
                continue
            if self.seen[eng].get(k, 0) >= v:
                continue
            self.seen[eng][k] = v
            waits.append((k, v))
        return waits

    def _mark(self, ev, reads, writes):
        k, v = ev
        for b in writes:
            b.k.w = ev
            b.k.r = {}
        for b in reads:
            if b in writes:
                continue
            if b.k.r.get(k, 0) < v:
                b.k.r[k] = v

    def op(self, eng, fn, reads=(), writes=()):
        waits = self._deps(eng, reads, writes)
        i = self.cnt[eng]
        self.cnt[eng] += 1
        ev = ((eng, i // BANK), i % BANK + 1)
        self._mark(ev, reads, writes)
        self.ops[eng].append((waits, fn, ev[0], 1))

    def dma(self, out, in_, reads=(), writes=(), eng='sp', **kw):
        waits = self._deps(eng, reads, writes)
        s = self.dma_next % NDMA
        self.dma_next += 1
        k = ('dma', s)
        if self.dma_tot[s] > 0 and self.seen[eng].get(k, 0) < self.dma_tot[s]:
            self.seen[eng][k] = self.dma_tot[s]
            waits.append((k, self.dma_tot[s]))
        self.dma_tot[s] += 16
        ev = (k, self.dma_tot[s])
        self._mark(ev, reads, writes)
        self.ops[eng].append((waits, lambda e: e.dma_start(out=out, in_=in_, **kw), k, 16))

    def barrier(self):
        last = {}
        for e in ENGS:
            i = self.cnt[e]
            if i > 0:
                last[(e, (i - 1) // BANK)] = (i - 1) % BANK + 1
        for s in range(NDMA):
            if self.dma_tot[s] > 0:
                last[('dma', s)] = self.dma_tot[s]
        for e in ENGS:
            waits = []
            for k, v in last.items():
                if k[0] == e:
                    continue
                if self.seen[e].get(k, 0) >= v:
                    continue
                self.seen[e][k] = v
                waits.append((k, v))
            if waits:
                self.ops[e].append((waits, None, None, 0))

    def mm(self, out, lhsT, rhs, rd, wr, start=True, stop=True):
        self.op('pe', lambda e: e.matmul(out, lhsT, rhs, start=start, stop=stop), rd, wr)

    def tr(self, out, in_, ident, rd, wr):
        self.op('pe', lambda e: e.transpose(out, in_, ident), rd, wr)

    def act(self, out, in_, func, rd, wr, bias=None, scale=None):
        kw = {}
        if bias is not None:
            kw['bias'] = bias
        if scale is not None:
            kw['scale'] = scale
        self.op('act', lambda e: e.activation(out, in_, func, **kw), rd, wr)

    def tt(self, eng, out, a, b, op, rd, wr):
        self.op(eng, lambda e: e.tensor_tensor(out, a, b, op), rd, wr)

    def ts(self, eng, out, a, s1, s2, op0, op1, rd, wr):
        self.op(eng, lambda e: e.tensor_scalar(out, a, s1, s2, op0, op1), rd, wr)

    def ts1(self, eng, out, a, s, op, rd, wr):
        self.op(eng, lambda e: e.tensor_single_scalar(out, a, s, op), rd, wr)

    def stt(self, eng, out, a, s, b, op0, op1, rd, wr):
        self.op('dve', lambda e: e.scalar_tensor_tensor(out, a, s, b, op0, op1), rd, wr)

    def cp(self, eng, out, in_, rd, wr):
        if eng == 'act':
            self.op('act', lambda e: e.copy(out, in_), rd, wr)
        else:
            self.op(eng, lambda e: e.tensor_copy(out, in_), rd, wr)

    def recip(self, out, in_, rd, wr):
        self.op('dve', lambda e: e.reciprocal(out, in_), rd, wr)

    def red(self, out, in_, rd, wr, op=ALU.add):
        self.op('dve', lambda e: e.tensor_reduce(out, in_, AX.X, op), rd, wr)

    def memset(self, eng, ap, val, wr):
        self.op(eng, lambda e: e.memset(ap, val), (), wr)

    def finish(self):
        nc = self.nc
        keys = set()
        for e in ENGS:
            for waits, fn, k, inc in self.ops[e]:
                if k is not None:
                    keys.add(k)
                for wk, _ in waits:
                    keys.add(wk)
        for k in sorted(keys, key=str):
            self.sems[k] = self.es.enter_context(nc.semaphore("s_%s_%s" % k))
        fin = [(('dma', s), self.dma_tot[s]) for s in range(NDMA) if self.dma_tot[s] > 0]
        block = self.es.enter_context(nc.Block())

        def run(e, name):
            for waits, fn, k, inc in self.ops[name]:
                for wk, wv in waits:
                    e.wait_ge(self.sems[wk], wv)
                if fn is not None:
                    fn(e).then_inc(self.sems[k], inc)
            if name == 'sp':
                for wk, wv in fin:
                    e.wait_ge(self.sems[wk], wv)

        @block.tensor
        def _(e):
            run(e, 'pe')

        @block.scalar
        def _(e):
            run(e, 'act')

        @block.vector
        def _(e):
            run(e, 'dve')

        @block.gpsimd
        def _(e):
            run(e, 'pool')

        @block.sync
        def _(e):
            run(e, 'sp')
        self.es.close()


PC_NG, PC_ADAB, PC_MUP, PC_MUN, PC_KK, PC_KA, PC_RK, PC_A0, PC_QG, PC_KVG = 0, 8, 32, 46, 60, 64, 68, 72, 80, 83
PC_EPS, PC_LEPS, PC_ONE, PC_TINY = 85, 86, 87, 88
NPC = 96
C_ID, C_ONES, C_BO, C_F, C_B, C_TOT, C_BO2, C_SGN = 0, 128, 256, 384, 1280, 2176, 2177, 2179
SD_F, SD_SELW, SD_M2S, SD_PHI, SD_RHO, SD_SIG, SD_NF = 0, 1, 2, 3, 4, 5, 6
NCST = 2211

SLOTS = []
for _i in range(52):
    SLOTS.append(('S', _i))
for _j in range(16):
    SLOTS.append(('B', None))
for _j in range(16):
    SLOTS.append(('F', 52 + _j))


def _consts():
    c = np.zeros((128, NCST), np.float32)
    idx = np.arange(128)
    row, col = idx[:, None], idx[None, :]
    c[:, C_ID:C_ID + 128] = np.eye(128)
    c[:, C_ONES:C_ONES + 128] = 1.0
    c[:, C_BO:C_BO + 128] = ((row // 64) == (col // 64))
    k = KAPPA
    tf = [k * (row < col), -k * (row <= col), k * (row <= col), k * (row > col)]
    tb = [k * (row > col), -k * (row >= col), k * (row >= col), k * (row < col)]
    c[:, C_F:C_F + 896] = np.concatenate(tf + [col > row, col >= row, col < row], 1)
    c[:, C_B:C_B + 896] = np.concatenate(tb + [col < row, col <= row, col > row], 1)
    c[:, C_TOT] = k
    c[:, C_BO2] = (idx < 64)
    c[:, C_BO2 + 1] = (idx >= 64)
    d = np.arange(32)
    c[:, C_SGN:C_SGN + 32] = np.where(d % 16 < 8, -1.0, 1.0)[None, :]
    return c


ROT_PERM = np.array([dd + 8 if dd % 16 < 8 else dd - 8 for dd in range(32)])


def _rope_tables(pos):
    pos = np.asarray(pos)
    inv = (np.float32(10000.0) ** (-np.arange(0, 16, 2, dtype=np.float32) / np.float32(16))).astype(np.float32)
    rowf = (pos // 64).astype(np.float32)
    colf = (pos % 64).astype(np.float32)
    ar = (rowf[:, None] * inv[None, :]).astype(np.float32)
    ac = (colf[:, None] * inv[None, :]).astype(np.float32)
    cos = np.concatenate([np.cos(ar), np.cos(ar), np.cos(ac), np.cos(ac)], 1).astype(np.float32)
    sin = np.concatenate([np.sin(ar), np.sin(ar), np.sin(ac), np.sin(ac)], 1).astype(np.float32)
    return cos.T.copy(), sin.T.copy()


def _tile_halo(seq, c):
    n = seq.shape[0]
    t = np.zeros((130, seq.shape[1]), np.float32)
    t[:128] = seq[c * 128:(c + 1) * 128]
    hv = [0.0, 0.0]
    if c > 0:
        t[128] = seq[c * 128 - 1]
        hv[0] = 1.0
    if (c + 1) * 128 < n:
        t[129] = seq[(c + 1) * 128]
        hv[1] = 1.0
    return t, hv


def host_prep(inp):
    g = {k: np.asarray(v, np.float32) for k, v in inp.items()}
    sh = {}
    w_in = g['w_in'][0]
    sh['w_in'] = w_in
    sh['wkr_rot'] = np.ascontiguousarray(w_in[:, 2944:2976][:, ROT_PERM])
    w_uq = g['mla_w_uq'][0]
    sh['w_uq'] = w_uq
    sh['w_uq_rot'] = np.ascontiguousarray(
        np.stack([w_uq[:, h * 96 + 64:h * 96 + 96][:, ROT_PERM] for h in range(8)], 1))
    w_ukv = g['mla_w_ukv'][0].reshape(256, 8, 128)
    sh['w_uk'] = np.ascontiguousarray(w_ukv[:, :, :64])
    sh['w_uv'] = np.ascontiguousarray(w_ukv[:, :, 64:])
    sh['w_out'] = g['w_out'][0]
    sh['ada_w'] = g['ada_w'][0]
    sh['w2cat'] = np.ascontiguousarray(g['rw_w2'][0].reshape(128, 512))
    sh['a2cat'] = np.ascontiguousarray(g['rw_a2'][0].reshape(128, 512))
    sh['w0r'] = np.ascontiguousarray(g['rw_w0'][0].reshape(1, 1024))
    pc = np.zeros((128, NPC), np.float32)
    pc[:, PC_NG:PC_NG + 8] = g['norm_g'][0].reshape(8, 128).T
    pc[:, PC_ADAB:PC_ADAB + 24] = g['ada_b'][0].reshape(24, 128).T
    pc[:, PC_MUP:PC_MUP + 14] = g['shift_mu'][0, 0].reshape(14, 128).T
    pc[:, PC_MUN:PC_MUN + 14] = g['shift_mu'][0, 1].reshape(14, 128).T
    pc[:, PC_KK:PC_KK + 4] = g['rw_kk'][0].reshape(4, 128).T
    pc[:, PC_KA:PC_KA + 4] = g['rw_ka'][0].reshape(4, 128).T
    pc[:, PC_RK:PC_RK + 4] = g['rw_rk'][0].reshape(4, 128).T
    pc[:, PC_A0:PC_A0 + 4] = g['rw_a0'][0, 0].reshape(4, 128).T
    pc[:, PC_A0 + 4:PC_A0 + 8] = g['rw_a0'][0, 1].reshape(4, 128).T
    pc[:, PC_QG:PC_QG + 3] = g['mla_q_norm_g'][0].reshape(3, 128).T
    pc[:, PC_KVG:PC_KVG + 2] = g['mla_kv_norm_g'][0].reshape(2, 128).T
    pc[:, PC_EPS] = NORM_EPS
    pc[:, PC_LEPS] = LNX_EPS
    pc[:, PC_ONE] = 1.0
    pc[:, PC_TINY] = 1e-24
    sh['pcol'] = pc
    br = np.zeros((128, 2048), np.float32)
    br[:, 0:512] = g['rw_lnx_g'][0][None, :]
    br[:, 512:1024] = g['rw_lnx_b'][0][None, :]
    br[:, 1024:2048] = g['final_g'][None, :]
    sh['brow'] = br
    sh['cst'] = _consts()
    maps = []
    for b in range(2):
        x, ctx = g['x'][b], g['ctx'][b]
        cvec = np.stack([g['c'][b].reshape(128, 8), g['c_ctx'].reshape(128, 8)], -1)
        for p in range(4):
            nf = 2 + 16 * p
            fsrc = [(ctx, 0, True), (ctx, 1, True)] + [(x, c, False) for c in range(16 * p)]
            bsrc = [(ctx, 1, True), (ctx, 0, True)] + [(x, c, False) for c in range(63, 16 * p + 15, -1)]
            own = [(x, 16 * p + j, False) for j in range(16)]
            allsrc = fsrc + bsrc + own[::-1] + own
            assert len(allsrc) == NSLOT
            tiles, hvs = [], []
            for seq, c, _ in allsrc:
                t, hv = _tile_halo(seq, c)
                tiles.append(t)
                hvs.append(hv)
            xs = np.stack(tiles, 0)
            hvs = np.array(hvs, np.float32)
            sd = np.zeros((128, 7, NSLOT), np.float32)
            f = np.array([1.0] * nf + [0.0] * (52 - nf) + [0.0] * 16 + [1.0] * 16, np.float32)
            sd[:, SD_F, :] = f[None]
            sd[:, SD_NF, :] = 1.0 - f[None]
            sd[:64, SD_SELW, :] = f[None]
            sd[64:, SD_SELW, :] = 1.0 - f[None]
            sd[:, SD_M2S, :] = -2.0 * sd[:, SD_SELW, :]
            sd[:, SD_PHI, :] = np.array([1.0 if ic else 0.0 for _, _, ic in allsrc], np.float32)[None]
            rho = np.ones(NSLOT, np.float32)
            rho[nf] = 0.0
            sd[:, SD_RHO, :] = rho[None]
            sgm = np.zeros(NSLOT, np.float32)
            sgm[nf - 1] = 1.0
            sd[:, SD_SIG, :] = sgm[None]
            kvsrc = fsrc + bsrc + own
            rk = np.zeros((NKT, 2, 32, 128), np.float32)
            rk[:, 0] = 1.0
            kvv = np.ones(NKT, np.float32)
            kvv[nf] = 0.0
            kvv[nf + 1] = 0.0
            for kt, (seq, c, ic) in enumerate(kvsrc):
                if not ic:
                    cs, sn = _rope_tables(np.arange(c * 128, (c + 1) * 128))
                    rk[kt, 0], rk[kt, 1] = cs, sn
            cq, sq = _rope_tables(np.arange(2048 * p, 2048 * (p + 1)))
            m = dict(sh)
            m['xs'] = xs
            m['hv'] = np.ascontiguousarray(np.broadcast_to(hvs[None], (128, NSLOT, 2)))
            m['sd'] = sd
            m['kvv'] = np.ascontiguousarray(np.broadcast_to(kvv[None], (128, NKT)))
            m['ropek'] = rk
            m['ropeq'] = np.stack([cq, sq], 0)
            m['cvec'] = cvec
            maps.append(m)
    return maps


def build():
    nc = bass.Bass("TRN2", target_bir_lowering=False)
    P = Prog(nc)

    def din(name, shape):
        return nc.dram_tensor(name, list(shape), F32, kind="ExternalInput").ap()

    xs = din('xs', [NSLOT, 130, 1024])
    hv_d = din('hv', [128, NSLOT, 2])
    sd_d = din('sd', [128, 7, NSLOT])
    kvv_d = din('kvv', [128, NKT])
    ropek = din('ropek', [NKT, 2, 32, 128])
    ropeq = din('ropeq', [2, 32, 2048])
    cvec_d = din('cvec', [128, 8, 2])
    w_in_d = din('w_in', [1024, D_IN])
    wkr_rot_d = din('wkr_rot', [1024, 32])
    w_uq_d = din('w_uq', [384, 768])
    w_uq_rot_d = din('w_uq_rot', [384, 8, 32])
    w_uk_d = din('w_uk', [256, 8, 64])
    w_uv_d = din('w_uv', [256, 8, 64])
    w_out_d = din('w_out', [1024, 1024])
    ada_w_d = din('ada_w', [1024, 3072])
    w2cat_d = din('w2cat', [128, 512])
    a2cat_d = din('a2cat', [128, 512])
    w0r_d = din('w0r', [1, 1024])
    pcol_d = din('pcol', [128, NPC])
    brow_d = din('brow', [128, 2048])
    cst_d = din('cst', [128, NCST])
    out_d = nc.dram_tensor('out', [2048, 1024], F32, kind="ExternalOutput").ap()
    dbg = {}
    if DEBUG:
        for nm, shp in DEBUG_SPECS.items():
            dbg[nm] = nc.dram_tensor('dbg_' + nm, list(shp), F32, kind="ExternalOutput").ap()

    def dscr(name, shape, dt):
        t = nc.dram_tensor(name, list(shape), dt)
        return Buf(t, t.ap())

    CKVs = dscr('s_ckv', [128, 2, NKEY], BF16)
    KRs = dscr('s_kr', [32, NKEY], BF16)
    CQs = dscr('s_cq', [128, 3, 2048], BF16)
    GMs = dscr('s_gm', [64, 8, 2048], BF16)
    MIXRs = dscr('s_mixr', [128, 4, 2048], BF16)
    MIXMs = dscr('s_mixm', [64, 8, 2048], BF16)
    YBs = dscr('s_yb', [16, 128, 512], F32)
    SBs = dscr('s_sb', [16, 128, 8], F32)

    CST = P.sb('cst', [128, NCST])
    PCOL = P.sb('pcol', [128, NPC])
    LGB = P.sb('lgb', [128, 1024])
    HV = P.sb('hv', [128, NSLOT, 2])
    SD = P.sb('sd', [128, 7, NSLOT])
    KVV = P.sb('kvv', [128, NKT])
    CSEL = P.sb('csel', [128, 512])
    GSL = P.sb('gsl', [128, 4, 8])
    W0S = P.sb('w0s', [1, 512])
    NA0S = P.sb('na0s', [128, 4])
    W2C = P.sb('w2c', [128, 512])
    A2C = P.sb('a2c', [128, 512])
    W0R = P.sb('w0r', [1, 1024])
    MODT = P.sb('modt', [128, 24, 2])
    GS = P.sb('gs', [128, 2, 8])
    SH = P.sb('shc', [128, 2, 8])
    M0 = P.sb('m0', [128, 14])
    NA0 = P.sb('na0', [128, 8])
    ARENA = P.sb('arena', [128, 28432], BF16)
    WB = P.view(ARENA.t[:, 0:27904].rearrange("p (k c) -> p k c", k=8))
    WKR = P.sb('wkr', [128, 8, 2, 96], BF16)

    for (b_, d_) in [(CST, cst_d), (PCOL, pcol_d), (HV, hv_d), (SD, sd_d), (KVV, kvv_d), (W2C, w2cat_d), (A2C, a2cat_d),
                     (W0R, w0r_d)]:
        P.dma(b_.ap, d_, writes=[b_])
    P.dma(LGB[:, :], brow_d[:, 0:1024], writes=[LGB])
    ident = CST[:, C_ID:C_ID + 128]
    ones = CST[:, C_ONES:C_ONES + 128]

    def pc(c0, n=1):
        return PCOL[:, c0:c0 + n]

    def ps_pair(name):
        q = P.ps(name, [128, 512])
        v0, v1 = P.view(q.t[:, 0:256]), P.view(q.t[:, 256:512])
        v0.k = q.k
        v1.k = q.k
        return v0, v1, q
    pu = [P.ps('pu0', [128, 512]), P.ps('pu1', [128, 512])]
    pz = P.ps('pz', [128, 512])
    pt = pz
    pA = P.ps('pA', [128, 512])
    pB, pX, pBXf = ps_pair('qbx')
    pS, pM, _ = ps_pair('qsm')
    pY = P.ps('pY', [128, 512])
    pH, pG, _ = ps_pair('qhg')

    xt = [P.sb('xt0', [128, 1024]), P.sb('xt1', [128, 1024])]
    xh = [P.sb('xh0', [2, 1024])]
    junk = P.sb('junk', [128, 1024])
    tD = P.view(junk.t[:, 0:512])
    tE = P.view(junk.t[:, 512:1024])
    tD.k = junk.k
    tE.k = junk.k
    st4 = P.sb('st4', [128, 8])
    hT = [P.sb('hT0', [128, 8, 130], BF16), P.sb('hT1', [128, 8, 130], BF16)]
    shv = P.sb('shv', [128, 2, 8])
    ue = [P.sb('ue0', [128, 130]), P.sb('ue1', [128, 130])]
    RKV = P.sb('rkv', [128, 12, 128])
    WA = P.sb('wa', [128, 2, 128])
    tA = P.sb('tA', [128, 512])
    tB = P.sb('tB', [128, 512])
    tC = P.sb('tC', [128, 512])
    th0 = P.sb('th0', [128, 128])
    th1 = P.sb('th1', [128, 128])
    sg = P.sb('sg', [128, 512])
    EALL = P.sb('eall', [128, 4, 4, 128])
    STG = P.view(EALL.t[:, :, :, :].rearrange('p m a t -> p (m a t)')[:, 0:1792])
    STG.k = EALL.k
    GCL = [P.sb('gc0', [128, 4]), P.sb('gc1', [128, 4])]
    MSKL = [P.sb('msk0', [128, 384]), P.sb('msk1', [128, 384])]
    ARtL = [P.sb('art%d' % q_, [128, 4, 256]) for q_ in range(2)]
    BKtL = [P.sb('bkt%d' % q_, [128, 4, 256]) for q_ in range(2)]
    ARt, BKt = ARtL[0], BKtL[0]
    BKh = P.sb('bkh', [128, 1024])
    BhT = P.view(BKh.t[:, 0:512])
    KhT = P.view(BKh.t[:, 512:1024])
    BhT.k = BKh.k
    KhT.k = BKh.k
    AtokL = [P.sb('atok%d' % q_, [128, 512]) for q_ in range(2)]
    BhtokL = [P.sb('bhtok%d' % q_, [128, 512]) for q_ in range(2)]
    KhtokL = [P.sb('khtok%d' % q_, [128, 512]) for q_ in range(2)]
    VtokL = [P.sb('vtok%d' % q_, [128, 512]) for q_ in range(2)]
    AT = [P.sb('at0', [128, 512])] * 2
    AN = [P.sb('an0', [128, 128])] * 2
    XPB = [P.sb('xp%d' % q_, [128, 320]) for q_ in range(5)]
    PTB = [P.sb('ptq%d' % q_, [128, 128]) for q_ in range(2)]
    MTs = P.sb('mts', [128, 4, 64])
    Ns = P.sb('ns', [128, 4, 64])
    GTs = [P.sb('gts0', [128, 128]), P.sb('gts1', [128, 128])]
    Hc = [P.sb('hc0', [128, 4, 64]), P.sb('hc1', [128, 4, 64])]
    HFa = P.sb('hfa', [128, 4, 64])
    ysb = P.sb('ysb', [128, 512])
    ybl = P.sb('ybl', [128, 512])
    sbl = P.sb('sbl', [128, 8])
    ssumL = [P.sb('ssum0', [128, 8]), P.sb('ssum1', [128, 8])]
    mla1 = P.sb('mla1', [128, 4, 128])
    mla2 = P.sb('mla2', [128, 4, 128])
    kvb = P.sb('kvb', [128, 2, 128], BF16)
    krb = P.sb('krb', [96, 128], BF16)
    rpk = P.sb('rpk', [96, 2, 128])
    cqb = P.sb('cqb', [128, 3, 128], BF16)
    gmb = P.sb('gmb', [64, 8, 128], BF16)
    mixb = P.sb('mixb', [128, 4, 128], BF16)

    def dump(name, buf, ap):
        if DEBUG and name in dbg:
            P.dma(dbg[name], ap, reads=[buf])

    w_in_v = w_in_d.rearrange("(k p) c -> p k c", p=128)
    for kc in range(8):
        for hf in range(2):
            P.dma(STG[:, 0:1744], w_in_v[:, kc, hf * 1744:(hf + 1) * 1744], writes=[STG])
            P.cp(['act', 'dve'][hf], WB[:, kc, hf * 1744:(hf + 1) * 1744], STG[:, 0:1744], [STG], [WB])
    P.memset('pool', WKR[:, :, :, :], 0.0, [WKR])
    for kc in range(8):
        P.cp('dve', WKR[:, kc, 0, 64:96], WB[:, kc, 2944:2976], [WB], [WKR])
    wkr_v = wkr_rot_d.rearrange("(k p) c -> p k c", p=128)
    P.dma(junk[:, 0:256].rearrange("p (k c) -> p k c", k=8), wkr_v, writes=[junk])
    for kc in range(8):
        P.tt('dve', WKR[:, kc, 1, 64:96], junk[:, kc * 32:(kc + 1) * 32], CST[:, C_SGN:C_SGN + 32], ALU.mult,
             [junk, CST], [WKR])
    CV = P.sb('cv', [128, 8, 2])
    CV2 = P.sb('cv2', [128, 8, 2])
    P.dma(CV[:, :, :], cvec_d, writes=[CV])
    P.act(CV2[:, :, :], CV[:, :, :], AF.Exp, [CV], [CV2], scale=-1.0)
    P.ts1('dve', CV2[:, :, :], CV2[:, :, :], 1.0, ALU.add, [CV2], [CV2])
    P.recip(CV2[:, :, :], CV2[:, :, :], [CV2], [CV2])
    P.tt('dve', CV[:, :, :], CV[:, :, :], CV2[:, :, :], ALU.mult, [CV, CV2], [CV])
    ada_v = ada_w_d.rearrange("(p k) n -> p k n", k=8)
    MACC = P.sb('macc', [128, 48])
    P.memset('dve', MACC[:, :], 0.0, [MACC])
    for k in range(8):
        for hf in range(2):
            P.dma(STG[:, 0:1536], ada_v[:, k, hf * 1536:(hf + 1) * 1536], writes=[STG])
            for n in range(12):
                nn = hf * 12 + n
                P.mm(pz[:, 2 * nn:2 * nn + 2], STG[:, n * 128:(n + 1) * 128], CV[:, k, :], [STG, CV], [pz])
            P.tt('dve', MACC[:, hf * 24:(hf + 1) * 24], MACC[:, hf * 24:(hf + 1) * 24], pz[:, hf * 24:(hf + 1) * 24],
                 ALU.add, [MACC, pz], [MACC])
    for j in range(2):
        P.tt('dve', MODT[:, :, j], MACC[:, j:48:2], PCOL[:, PC_ADAB:PC_ADAB + 24], ALU.add, [MACC, PCOL], [MODT])
    for j in range(2):
        P.stt('dve', GS[:, j, :], MODT[:, 8:16, j], 1.0, PCOL[:, PC_NG:PC_NG + 8], ALU.add, ALU.mult,
              [MODT, PCOL], [GS])
        P.cp('dve', SH[:, j, :], MODT[:, 0:8, j], [MODT], [SH])
    P.cp('dve', GSL[:, 0, :], GS[:, 0, :], [GS], [GSL])
    P.tt('dve', GSL[:, 1, :], GS[:, 1, :], GS[:, 0, :], ALU.subtract, [GS], [GSL])
    P.cp('dve', GSL[:, 2, :], SH[:, 0, :], [SH], [GSL])
    P.tt('dve', GSL[:, 3, :], SH[:, 1, :], SH[:, 0, :], ALU.subtract, [SH], [GSL])
    P.tt('dve', M0[:, :], PCOL[:, PC_MUP:PC_MUP + 14], PCOL[:, PC_MUN:PC_MUN + 14], ALU.add, [PCOL], [M0])
    P.ts('dve', M0[:, :], M0[:, :], -1.0, 1.0, ALU.mult, ALU.add, [M0], [M0])
    P.ts1('dve', NA0[:, :], PCOL[:, PC_A0:PC_A0 + 8], -1.0, ALU.mult, [PCOL], [NA0])
    for b_ in XPB:
        P.memset('pool', b_[:, :], 0.0, [b_])
    P.memset('pool', Hc[0][:, :, :], 0.0, [Hc[0]])
    P.memset('pool', HFa[:, :, :], 0.0, [HFa])

    def load_slot(i):
        P.dma(xt[i % 2][:, :], xs[i, 0:128, :], writes=[xt[i % 2]])

    def rsqrt_ln(dst_buf, dst, src_buf, src, scale, eps_ap):
        P.act(dst, src, AF.Ln, [src_buf, PCOL], [dst_buf], bias=eps_ap, scale=scale)
        P.act(dst, dst, AF.Exp, [dst_buf], [dst_buf], scale=-0.5)

    state = {'h': 0}

    PCNT = {'n': 0}
    ADV = 1

    def proj_for(hT_, cnt):
        def proj(c0, width, ncols=128, lhs=None):
            pb_ = pu[cnt['n'] % 2]
            cnt['n'] += 1
            for kc in range(8):
                l = WB[:, kc, c0:c0 + width] if lhs is None else lhs(kc)
                P.mm(pb_[0:(width if lhs is None else 96), 0:ncols], l, hT_[:, kc, 0:ncols],
                     [WB if lhs is None else WKR, hT_], [pb_], start=(kc == 0), stop=(kc == 7))
            return pb_
        return proj

    def prep(i):
        kind, kv = SLOTS[i]
        S = i % 2
        ARt, BKt, Atok, Bhtok, Khtok, Vtok = ARtL[S], BKtL[S], AtokL[S], BhtokL[S], KhtokL[S], VtokL[S]
        GC, MSK, ssum = GCL[S], MSKL[S], ssumL[S]
        x_, xh_, hT_ = xt[i % 2], xh[0], hT[i % 2]
        P.dma(xh_[:, :], xs[i, 128:130, :], writes=[xh_])
        fcol = SD[:, SD_F, i:i + 1]
        nfcol = SD[:, SD_NF, i:i + 1]
        phi = SD[:, SD_PHI, i:i + 1]
        P.stt('dve', GS[:, 0, :], GSL[:, 1, :], phi, GSL[:, 0, :], ALU.mult, ALU.add, [GSL, SD], [GS])
        P.stt('dve', SH[:, 0, :], GSL[:, 3, :], phi, GSL[:, 2, :], ALU.mult, ALU.add, [GSL, SD], [SH])
        P.ts1('dve', CSEL[:, :], CST[:, C_F:C_F + 512], fcol, ALU.mult, [CST, SD], [CSEL])
        P.stt('dve', CSEL[:, :], CST[:, C_B:C_B + 512], nfcol, CSEL[:, :], ALU.mult, ALU.add, [CST, SD, CSEL], [CSEL])
        P.ts1('dve', MSK[:, :], CST[:, C_F + 512:C_F + 896], fcol, ALU.mult, [CST, SD], [MSK])
        P.stt('dve', MSK[:, :], CST[:, C_B + 512:C_B + 896], nfcol, MSK[:, :], ALU.mult, ALU.add, [CST, SD, MSK], [MSK])
        P.ts1('dve', W0S[0:1, :], W0R[0:1, 0:512], SD[0:1, SD_F, i:i + 1], ALU.mult, [W0R, SD], [W0S])
        P.stt('dve', W0S[0:1, :], W0R[0:1, 512:1024], SD[0:1, SD_NF, i:i + 1], W0S[0:1, :], ALU.mult, ALU.add,
              [W0R, SD, W0S], [W0S])
        P.ts1('dve', NA0S[:, :], NA0[:, 0:4], fcol, ALU.mult, [NA0, SD], [NA0S])
        P.stt('dve', NA0S[:, :], NA0[:, 4:8], nfcol, NA0S[:, :], ALU.mult, ALU.add, [NA0, SD, NA0S], [NA0S])
        j = 0
        P.act(junk[:, :], x_[:, :], AF.Square, [x_], [junk])
        P.red(st4[:, 0:1], junk[:, :], [junk], [st4])
        rsqrt_ln(st4, st4[:, 0:1], st4, st4[:, 0:1], 1.0 / 1024, pc(PC_EPS))
        P.ts1('dve', x_[:, :], x_[:, :], st4[:, 0:1], ALU.mult, [x_, st4], [x_])
        for half in range(2):
            yield
            for q in range(4):
                kc = half * 4 + q
                P.tr(pt[:, q * 128:(q + 1) * 128], x_[:, kc * 128:(kc + 1) * 128], ident, [x_, CST], [pt])
            for q in range(4):
                kc = half * 4 + q
                if q % 2 == 0:
                    P.act(hT_[:, kc, 0:128], pt[:, q * 128:(q + 1) * 128], AF.Identity, [pt, GS, SH], [hT_],
                          bias=SH[:, j, kc:kc + 1], scale=GS[:, j, kc:kc + 1])
                else:
                    P.ts('dve', hT_[:, kc, 0:128], pt[:, q * 128:(q + 1) * 128], GS[:, j, kc:kc + 1],
                         SH[:, j, kc:kc + 1], ALU.mult, ALU.add, [pt, GS, SH], [hT_])
        yield
        P.act(junk[0:2, :], xh_[:, :], AF.Square, [xh_], [junk])
        P.red(st4[0:2, 1:2], junk[0:2, :], [junk], [st4])
        rsqrt_ln(st4, st4[0:2, 1:2], st4, st4[0:2, 1:2], 1.0 / 1024, PCOL[0:2, PC_EPS:PC_EPS + 1])
        P.ts1('dve', xh_[:, :], xh_[:, :], st4[0:2, 1:2], ALU.mult, [xh_, st4], [xh_])
        yield
        for kc in range(8):
            P.tr(pt[:, 2 * kc:2 * kc + 2], xh_[0:2, kc * 128:(kc + 1) * 128], CST[0:2, C_ID:C_ID + 2], [xh_, CST], [pt])
        for c in range(2):
            P.ts1('dve', shv[:, c, :], SH[:, j, :], HV[:, i, c:c + 1], ALU.mult, [SH, HV], [shv])
            P.tt('dve', junk[:, 8 * c:8 * c + 8], pt[:, c:16:2], GS[:, j, :], ALU.mult, [pt, GS], [junk])
            P.tt('dve', hT_[:, :, 128 + c], junk[:, 8 * c:8 * c + 8], shv[:, c, :], ALU.add, [junk, shv], [hT_])

        yield
        proj = proj_for(hT_, PCNT)

        for jc in range(14):
            pb_ = proj(jc * 128, 128, 130)
            ue_ = ue[jc % 2]
            P.cp('act', ue_[:, :], pb_[:, 0:130], [pb_], [ue_])
            yield
            if jc < 12:
                dstb, dst = RKV, RKV[:, jc, :]
            else:
                dstb, dst = WA, WA[:, jc - 12, :]
            eng = 'pool' if jc % 2 == 0 else 'dve'
            mp_, mn_ = pc(PC_MUP + jc), pc(PC_MUN + jc)
            P.ts1(eng, dst, ue_[:, 0:128], M0[:, jc:jc + 1], ALU.mult, [ue_, M0], [dstb])
            for (dsl, ssl, mu_) in [((1, 128), (0, 127), mp_), ((0, 1), (128, 129), mp_),
                                    ((0, 127), (1, 128), mn_), ((127, 128), (129, 130), mn_)]:
                dd = dst[:, dsl[0]:dsl[1]]
                if eng == 'dve':
                    P.stt('dve', dd, ue_[:, ssl[0]:ssl[1]], mu_, dd, ALU.mult, ALU.add, [ue_, PCOL, dstb], [dstb])
                else:
                    tp = th1[:, 0:dsl[1] - dsl[0]]
                    P.ts1('pool', tp, ue_[:, ssl[0]:ssl[1]], mu_, ALU.mult, [ue_, PCOL], [th1])
                    P.tt('pool', dd, dd, tp, ALU.add, [dstb, th1], [dstb])
            yield
        yield
        r3 = RKV[:, 0:4, :]
        k3 = RKV[:, 4:8, :]
        v3 = RKV[:, 8:12, :]

        def f3(b_):
            return b_[:, :].rearrange("p (m t) -> p m t", m=4)

        for m in range(4):
            P.ts1('pool', tA[:, m * 128:(m + 1) * 128], RKV[:, 4 + m, :], pc(PC_KK + m), ALU.mult, [RKV, PCOL], [tA])
        P.act(tB[:, :], tA[:, :], AF.Square, [tA], [tB])
        P.mm(pz[:, :], CST[:, C_BO:C_BO + 128], tB[:, :], [CST, tB], [pz])
        yield
        P.ts1('dve', tB[:, :], pz[:, :], 1e-24, ALU.max, [pz], [tB])
        P.act(tB[:, :], tB[:, :], AF.Ln, [tB], [tB])
        P.act(tB[:, :], tB[:, :], AF.Exp, [tB], [tB], scale=-0.5)
        P.tt('dve', tA[:, :], tA[:, :], tB[:, :], ALU.mult, [tA, tB], [tA])
        yield
        P.act(th0[:, :], WA[:, 0, :], AF.Exp, [WA], [th0], scale=2.0)
        P.ts1('dve', th0[:, :], th0[:, :], 1.0, ALU.add, [th0], [th0])
        P.recip(th0[:, :], th0[:, :], [th0], [th0])
        P.ts('dve', th0[:, :], th0[:, :], SD[:, SD_M2S, i:i + 1], SD[:, SD_SELW, i:i + 1], ALU.mult, ALU.add,
             [th0, SD], [th0])
        P.mm(pz[:, :], th0[:, :], W2C[:, :], [th0, W2C], [pz], start=True, stop=False)
        P.mm(pz[:, :], CST[0:1, C_ONES:C_ONES + 128], W0S[0:1, :], [CST, W0S], [pz], start=False, stop=True)
        yield
        P.act(sg[:, :], pz[:, :], AF.Exp, [pz], [sg], scale=-1.0)
        P.ts1('dve', sg[:, :], sg[:, :], 1.0, ALU.add, [sg], [sg])
        P.recip(sg[:, :], sg[:, :], [sg], [sg])
        yield
        tri = CSEL[:, 0:512]
        for m in range(4):
            P.mm(pz[:, :], sg[:, m * 128:(m + 1) * 128], tri, [sg, CSEL], [pz])
            P.act(EALL[:, m, :, :], pz[:, :].rearrange("p (a t) -> p a t", a=4), AF.Exp, [pz], [EALL])
            yield
        P.tt('dve', GC[:, :], EALL[:, :, 2, 0], EALL[:, :, 3, 0], ALU.mult, [EALL], [GC])
        yield
        P.ts1('dve', th1[:, :], WA[:, 1, :], SD[:, SD_SELW, i:i + 1], ALU.mult, [WA, SD], [th1])
        for m in range(4):
            P.mm(pz[:, m * 128:(m + 1) * 128], A2C[:, m * 128:(m + 1) * 128], th1[:, :], [A2C, th1], [pz])
        for m in range(4):
            P.act(tB[:, m * 128:(m + 1) * 128], pz[:, m * 128:(m + 1) * 128], AF.Exp, [pz, NA0S], [tB],
                  bias=NA0S[:, m:m + 1], scale=-1.0)
        yield
        P.ts1('dve', tB[:, :], tB[:, :], 1.0, ALU.add, [tB], [tB])
        P.recip(tB[:, :], tB[:, :], [tB], [tB])
        yield
        for m in range(4):
            P.ts('pool', tC[:, m * 128:(m + 1) * 128], tB[:, m * 128:(m + 1) * 128], -1.0, pc(PC_KA + m),
                 ALU.add, ALU.mult, [tB, PCOL], [tC])
        P.stt('dve', tC[:, :], tC[:, :], 1.0, RKV[:, 4:8, :].rearrange("p m t -> p (m t)"), ALU.add, ALU.mult,
              [tC, RKV], [tC])
        P.tt('pool', tD[:, :], tA[:, :], tB[:, :], ALU.mult, [tA, tB], [tD])
        if kind != 'S':
            for m in range(4):
                P.stt('pool', tE[:, m * 128:(m + 1) * 128], RKV[:, m, :], pc(PC_RK + m),
                      tC[:, m * 128:(m + 1) * 128], ALU.mult, ALU.mult, [RKV, PCOL, tC], [tE])
            for m in range(4):
                P.mm(pz[:, 2 * m:2 * m + 2], tE[:, m * 128:(m + 1) * 128], CST[:, C_BO2:C_BO2 + 2],
                     [tE, CST], [pz])
            P.cp('act', ssum[:, :], pz[:, 0:8], [pz], [ssum])
        yield
        E = [EALL[:, :, a, :] for a in range(4)]
        A3 = ARt[:, :, :].rearrange("p m (a t) -> p m a t", a=2)
        B3 = BKt[:, :, :].rearrange("p m (a t) -> p m a t", a=2)
        P.stt('dve', A3[:, :, 0, :], f3(tA), -1.0, E[0], ALU.mult, ALU.mult, [tA, EALL], [ARt])
        P.tt('pool', A3[:, :, 1, :], r3, E[2], ALU.mult, [RKV, EALL], [ARt])
        yield
        P.tt('dve', B3[:, :, 0, :], f3(tD), E[1], ALU.mult, [tD, EALL], [BKt])
        P.tt('pool', B3[:, :, 1, :], f3(tC), E[1], ALU.mult, [tC, EALL], [BKt])
        yield
        P.tt('dve', f3(BhT), f3(tD), E[3], ALU.mult, [tD, EALL], [BhT])
        P.tt('pool', f3(KhT), f3(tC), E[3], ALU.mult, [tC, EALL], [KhT])
        for (srcb, srcf, dstb) in [(ARt, lambda m: ARt[:, m, 0:128], Atok), (BhT, lambda m: BhT[:, m * 128:(m + 1) * 128], Bhtok),
                                   (KhT, lambda m: KhT[:, m * 128:(m + 1) * 128], Khtok), (RKV, lambda m: RKV[:, 8 + m, :], Vtok)]:
            for m in range(4):
                P.tr(pt[:, m * 128:(m + 1) * 128], srcf(m), ident, [srcb, CST], [pt])
            P.cp('act', dstb[:, :], pt[:, :], [pt], [dstb])
            yield

        yield
        if kv is not None:
            pb2 = proj(2688, 128, 128)
            P.cp('act', mla1[:, 0, :], pb2[:, 0:128], [pb2], [mla1])
            pb2 = proj(2816, 128, 128)
            P.cp('act', mla1[:, 1, :], pb2[:, 0:128], [pb2], [mla1])
            P.act(mla2[:, 0:2, :], mla1[:, 0:2, :], AF.Square, [mla1], [mla2])
            P.mm(pz[:, 0:128], ones, mla2[:, 0, :], [CST, mla2], [pz], start=True, stop=False)
            P.mm(pz[:, 0:128], ones, mla2[:, 1, :], [CST, mla2], [pz], start=False, stop=True)
            rsqrt_ln(mla2, mla2[:, 2, :], pz, pz[:, 0:128], 1.0 / 256, pc(PC_EPS))
            for c in range(2):
                P.stt('dve', kvb[:, c, :], mla1[:, c, :], pc(PC_KVG + c), mla2[:, 2, :], ALU.mult, ALU.mult,
                      [mla1, PCOL, mla2], [kvb])
            P.dma(CKVs[:, :, kv * 128:(kv + 1) * 128], kvb[:, :, :], reads=[kvb], writes=[CKVs])
            yield
            P.dma(rpk[64:96, :, :], ropek[kv].rearrange("a d t -> d a t"), writes=[rpk])
            pk1 = proj(0, 96, 128, lhs=lambda kc: WKR[:, kc, 0, :])
            yield
            pk2 = proj(0, 96, 128, lhs=lambda kc: WKR[:, kc, 1, :])
            P.tt('dve', mla1[64:96, 2, :], pk1[64:96, 0:128], rpk[64:96, 0, :], ALU.mult, [pk1, rpk], [mla1])
            P.tt('dve', mla1[64:96, 3, :], pk2[64:96, 0:128], rpk[64:96, 1, :], ALU.mult, [pk2, rpk], [mla1])
            P.tt('dve', krb[64:96, :], mla1[64:96, 2, :], mla1[64:96, 3, :], ALU.add, [mla1], [krb])
            P.dma(KRs[:, kv * 128:(kv + 1) * 128], krb[64:96, :], reads=[krb], writes=[KRs])

        yield

    def heads(i, gen):
        kind, kv = SLOTS[i]
        S = i % 2
        ARt, BKt, Atok, Bhtok, Khtok, Vtok = ARtL[S], BKtL[S], AtokL[S], BhtokL[S], KhtokL[S], VtokL[S]
        GC, MSK, ssum = GCL[S], MSKL[S], ssumL[S]
        hT_ = hT[i % 2]
        proj = proj_for(hT_, PCNT)

        def f3(b_):
            return b_[:, :].rearrange("p (m t) -> p m t", m=4)

        def advance(n):
            if gen is None:
                return
            for _ in range(n):
                try:
                    next(gen)
                except StopIteration:
                    return
        Hin = Hc[state['h'] % 2]
        Hout = Hc[(state['h'] + 1) % 2]
        mskt = MSK[:, 0:256]
        mskn = MSK[:, 256:384]
        if kind == 'S':
            P.ts1('dve', Hin[:, :, :], Hin[:, :, :], SD[:, SD_RHO, i:i + 1], ALU.mult, [Hin, SD], [Hin])
        for h in range(8):
            m, pb = h // 2, 64 * (h % 2)
            cb = 64 * h
            at_, an_ = AT[h % 2], AN[h % 2]
            Bt_h = BKt[pb:pb + 64, m, 0:128]
            Kt_h = BKt[pb:pb + 64, m, 128:256]
            AR_h = ARt[pb:pb + 64, m, 0:256]
            own = kind != 'S'
            P.mm(pA[:, 0:256], Bt_h, AR_h, [BKt, ARt], [pA])
            P.mm(pA[:, 256:512], Kt_h, AR_h, [BKt, ARt], [pA])
            P.mm(pB[:, 0:128], ARt[pb:pb + 64, m, 0:128], Bt_h, [ARt, BKt], [pB])
            xi = 0 if h % 2 == 0 else 3
            XP = XPB[xi]
            pt0 = PTB[0]
            P.tt('dve', pt0[:, :], pA[:, 0:128], mskt[:, 0:128], ALU.mult, [pA, MSK], [pt0])
            if own:
                P.tt('dve', at_[:, 128:256], pA[:, 128:256], mskt[:, 128:256], ALU.mult, [pA, MSK], [at_])
                P.tt('dve', at_[:, 256:512], pA[:, 256:512], mskt, ALU.mult, [pA, MSK], [at_])
            else:
                P.tt('dve', at_[:, 256:384], pA[:, 256:384], mskt[:, 0:128], ALU.mult, [pA, MSK], [at_])
            P.tt('dve', XP[:, 192:320], pB[:, 0:128], mskn, ALU.mult, [pB, MSK], [XP])
            P.cp('act', XP[:, 64:128], Atok[:, cb:cb + 64], [Atok], [XP])
            P.mm(pB[:, 128:192], at_[:, 256:384], Vtok[:, cb:cb + 64], [at_, Vtok], [pB])
            P.cp('act', XP[:, 128:192], pB[:, 128:192], [pB], [XP])
            PT_ = pt0
            advance(1)
            for lvl in range(7):
                last = lvl == 6
                XPn = XPB[(xi + 1) % 5]
                P.mm(pX[:, 0:(128 if last else 256)], PT_[:, :], XP[:, 64:(192 if last else 320)], [PT_, XP], [pX])
                if not last:
                    P.mm(pS[:, 0:128], XP[:, 192:320], PT_[:, :], [XP, PT_], [pS])
                P.tt('dve', XPn[:, 64:192], XP[:, 64:192], pX[:, 0:128], ALU.add, [XP, pX], [XPn])
                if not last:
                    P.cp('act', XPn[:, 192:320], pX[:, 128:256], [pX], [XPn])
                    ptn = PTB[(lvl + 1) % 2]
                    P.cp('act', ptn[:, :], pS[:, 0:128], [pS], [ptn])
                    PT_ = ptn
                xi += 1
                XP = XPn
                advance(1)
            X = XP
            lx = X[:, 64:128] if pb == 0 else X[:, 0:128]
            if i == CFG.get('dump_slot', 0) and h == 0:
                dump('xf', X, X[:, :])
                dump('at', at_, at_[:, :])
                dump('atok', Atok, Atok[:, :])
                dump('vtok', Vtok, Vtok[:, :])
                dump('eall', EALL, EALL[:, :, :, :].rearrange("p m a t -> p (m a t)"))
                dump('sg', sg, sg[:, :])
                dump('rkv', RKV, RKV[:, :, :].rearrange("p m t -> p (m t)"))
            P.mm(pM[0:(64 if pb == 0 else 128), 0:64], lx, Bhtok[:, cb:cb + 64], [X, Bhtok], [pM])
            P.mm(pM[:, 64:128], Bhtok[:, m * 128:(m + 1) * 128], X[:, 128:192], [Bhtok, X], [pM], start=True, stop=False)
            P.mm(pM[:, 64:128], Khtok[:, m * 128:(m + 1) * 128], Vtok[:, cb:cb + 64], [Khtok, Vtok], [pM],
                 start=False, stop=True)
            P.stt('dve', MTs[pb:pb + 64, m, :], CST[pb:pb + 64, C_ID + pb:C_ID + pb + 64], GC[pb:pb + 64, m:m + 1],
                  pM[pb:pb + 64, 0:64], ALU.mult, ALU.add, [CST, GC, pM], [MTs])
            P.cp('act', Ns[pb:pb + 64, m, :], pM[pb:pb + 64, 64:128], [pM], [Ns])
            if kind != 'S':
                gt_ = GTs[h % 2]
                P.mm(pG[0:(64 if pb == 0 else 128), 0:128], lx, at_[:, 128:256], [X, at_], [pG])
                P.tt('dve', gt_[pb:pb + 64, :], pG[pb:pb + 64, 0:128], ARt[pb:pb + 64, m, 128:256], ALU.add,
                     [pG, ARt], [gt_])
                yo = pY[:, cb:cb + 64]
                P.mm(yo, at_[:, 128:256], X[:, 128:192], [at_, X], [pY], start=True, stop=False)
                P.mm(yo, at_[:, 384:512], Vtok[:, cb:cb + 64], [at_, Vtok], [pY], start=False, stop=False)
                P.mm(yo, gt_[pb:pb + 64, :], Hin[pb:pb + 64, m, :], [gt_, Hin], [pY], start=False, stop=True)
            P.mm(pH[pb:pb + 64, m * 64:(m + 1) * 64], MTs[pb:pb + 64, m, :], Hin[pb:pb + 64, m, :], [MTs, Hin], [pH])
            advance(ADV)
        P.tt('dve', Hout[:, :, :], pH[:, :].rearrange("p (m i) -> p m i", m=4), Ns[:, :, :], ALU.add, [pH, Ns], [Hout])
        if i == CFG.get('dump_slot', 0):
            dump('hout', Hout, Hout[:, :, :].rearrange('p m i -> p (m i)'))
        state['h'] += 1
        advance(10 ** 6)
        if kind == 'S':
            P.stt('dve', HFa[:, :, :], Hout[:, :, :], SD[:, SD_SIG, i:i + 1], HFa[:, :, :], ALU.mult, ALU.add,
                  [Hout, SD, HFa], [HFa])

        if kind == 'B':
            jj = 15 - (i - 52)
            P.cp('act', ysb[:, :], pY[:, :], [pY], [ysb])
            P.dma(YBs[jj], ysb[:, :], reads=[ysb], writes=[YBs])
            P.dma(SBs[jj], ssum[:, :], reads=[ssum], writes=[SBs])
        if kind == 'F':
            jj = i - 68
            tok = slice(jj * 128, (jj + 1) * 128)
            P.dma(ybl[:, :], YBs[jj], reads=[YBs], writes=[ybl])
            P.dma(sbl[:, :], SBs[jj], reads=[SBs], writes=[sbl])
            P.tt('dve', ysb[:, :], pY[:, :], ybl[:, :], ALU.add, [pY, ybl], [ysb])
            P.tt('dve', ssum[:, :], ssum[:, :], sbl[:, :], ALU.add, [ssum, sbl], [ssum])
            y3 = ysb[:, :].rearrange("p (h i) -> p h i", h=8)
            P.red(st4[:, :], y3, [ysb], [st4])
            P.ts1('dve', st4[:, :], st4[:, :], 1.0 / 64, ALU.mult, [st4], [st4])
            P.act(tE[:, :], ysb[:, :], AF.Square, [ysb], [tE])
            P.red(sbl[:, :], tE[:, :].rearrange("p (h i) -> p h i", h=8), [tE], [sbl])
            P.tt('dve', ybl[:, 0:8], st4[:, :], st4[:, :], ALU.mult, [st4], [ybl])
            P.stt('dve', sbl[:, :], sbl[:, :], 1.0 / 64, ybl[:, 0:8], ALU.mult, ALU.subtract, [sbl, ybl], [sbl])
            rsqrt_ln(sbl, sbl[:, :], sbl, sbl[:, :], 1.0, pc(PC_LEPS))
            for h in range(8):
                P.ts('dve' if h % 2 else 'pool', ysb[:, h * 64:(h + 1) * 64], ysb[:, h * 64:(h + 1) * 64],
                     st4[:, h:h + 1], sbl[:, h:h + 1], ALU.subtract, ALU.mult, [ysb, st4, sbl], [ysb])
            P.tt('dve', ysb[:, :], ysb[:, :], LGB[:, 0:512], ALU.mult, [ysb, LGB], [ysb])
            P.tt('pool', ysb[:, :], ysb[:, :], LGB[:, 512:1024], ALU.add, [ysb, LGB], [ysb])
            for h in range(8):
                P.stt('dve' if h % 2 else 'pool', ysb[:, h * 64:(h + 1) * 64], Vtok[:, h * 64:(h + 1) * 64],
                      ssum[:, h:h + 1], ysb[:, h * 64:(h + 1) * 64], ALU.mult, ALU.add, [Vtok, ssum, ysb], [ysb])
            for m in range(4):
                pb2 = proj(1792 + m * 128, 128, 128)
                P.cp('act', tE[:, m * 128:(m + 1) * 128], pb2[:, 0:128], [pb2], [tE])
            P.act(tD[:, :], tE[:, :], AF.Exp, [tE], [tD], scale=-1.0)
            P.ts1('dve', tD[:, :], tD[:, :], 1.0, ALU.add, [tD], [tD])
            P.recip(tD[:, :], tD[:, :], [tD], [tD])
            P.tt('pool', tE[:, :], tE[:, :], tD[:, :], ALU.mult, [tE, tD], [tE])
            for m in range(4):
                P.tr(pt[:, m * 128:(m + 1) * 128], ysb[:, m * 128:(m + 1) * 128], ident, [ysb, CST], [pt])
            P.tt('dve', mixb[:, :, :], pt[:, :].rearrange("p (m t) -> p m t", m=4), f3(tE), ALU.mult, [pt, tE], [mixb])
            P.dma(MIXRs[:, :, tok], mixb[:, :, :], reads=[mixb], writes=[MIXRs])
            for c in range(3):
                pb2 = proj(2304 + c * 128, 128, 128)
                P.cp('act', mla1[:, c, :], pb2[:, 0:128], [pb2], [mla1])
            P.act(mla2[:, 0:3, :], mla1[:, 0:3, :], AF.Square, [mla1], [mla2])
            for c in range(3):
                P.mm(pz[:, 0:128], ones, mla2[:, c, :], [CST, mla2], [pz], start=(c == 0), stop=(c == 2))
            rsqrt_ln(mla2, mla2[:, 3, :], pz, pz[:, 0:128], 1.0 / 384, pc(PC_EPS))
            for c in range(3):
                P.stt('dve', cqb[:, c, :], mla1[:, c, :], pc(PC_QG + c), mla2[:, 3, :], ALU.mult, ALU.mult,
                      [mla1, PCOL, mla2], [cqb])
            P.dma(CQs[:, :, tok], cqb[:, :, :], reads=[cqb], writes=[CQs])
            for h in range(8):
                pb2 = proj(2976 + h * 64, 64, 128)
                P.cp('act', tD[0:64, 0:128], pb2[0:64, 0:128], [pb2], [tD])
                P.act(tD[0:64, 128:256], tD[0:64, 0:128], AF.Exp, [tD], [tD], scale=-1.0)
                P.ts1('dve', tD[0:64, 128:256], tD[0:64, 128:256], 1.0, ALU.add, [tD], [tD])
                P.recip(tD[0:64, 128:256], tD[0:64, 128:256], [tD], [tD])
                P.tt('dve', gmb[:, h, :], tD[0:64, 0:128], tD[0:64, 128:256], ALU.mult, [tD], [gmb])
            P.dma(GMs[:, :, tok], gmb[:, :, :], reads=[gmb], writes=[GMs])

    slot_list = list(range(NSLOT)) if CFG['slots'] is None else list(CFG['slots'])
    if slot_list:
        load_slot(slot_list[0])
        if len(slot_list) > 1:
            load_slot(slot_list[1])
        for _ in prep(slot_list[0]):
            pass
    for si, i in enumerate(slot_list):
        if si + 2 < len(slot_list):
            load_slot(slot_list[si + 2])
        if i == 68:
            P.cp('dve', Hc[state['h'] % 2][:, :, :], HFa[:, :, :], [HFa], [Hc[state['h'] % 2]])
        heads(i, prep(slot_list[si + 1]) if si + 1 < len(slot_list) else None)

    AR = ARENA.t
    o = 0

    def carve(n, shape_str=None, **kw):
        nonlocal o
        ap = AR[:, o:o + n]
        o += n
        if shape_str:
            ap = ap.rearrange(shape_str, **kw)
        return P.view(ap)

    ps2 = [pA, pY]

    def phase2():
        nonlocal o
        P.barrier()
        o = 0

        KT = [carve(NKEY)] * 2
        V4 = carve(NKT * 4 * 65, "p (k h c) -> p k h c", k=NKT, h=4)
        QT = [carve(2048)] * 2
        def bview(buf, nbf, pat=None, rows=128, **kw):
            t = buf.t
            flat = t[0:rows]
            nd = len(t.shape)
            if nd == 3:
                flat = t[0:rows, :, :].rearrange("p a b -> p (a b)")
            elif nd == 4:
                flat = t[0:rows, :, :, :].rearrange("p a b c -> p (a b c)")
            ap = flat[:, 0:nbf // 2].bitcast(BF16)
            if pat:
                ap = ap.rearrange(pat, **kw)
            return P.view(ap)
        WUQ = bview(RKV, 2304, "p (k c) -> p k c", k=3)
        WUQR = bview(ybl, 288, "p (k c) -> p k c", k=3)
        WROT = bview(ysb, 768, "p (k h c) -> p k h c", k=3, h=8)
        WUK = bview(AtokL[0], 1024, "p (k c) -> p k c", k=2)
        WUV = bview(BhtokL[0], 1024, "p (k c) -> p k c", k=2)
        ckb = [bview(KhtokL[0], 1024, "p (k c) -> p k c", k=2), bview(VtokL[0], 1024, "p (k c) -> p k c", k=2)]
        PTb_ = [bview(AtokL[1], 512), bview(BhtokL[1], 512), bview(KhtokL[1], 512)]
        rq = P.view(ARt.t[:, :, :].rearrange('p m c -> p (m c)').rearrange('p (a t) -> p a t', a=2))
        onrm = P.view(tA.t[0:64, :])
        rec = P.view(tB.t[0:65, :])
        bcs = P.view(tC.t[0:64, :])
        mxb = bview(XPB[0], 512, rows=64)
        gml = bview(XPB[1], 512, rows=64)
        cql = bview(junk, 1536, "p (k c) -> p k c", k=3)
        w_uq_v = w_uq_d.rearrange("(k p) c -> p k c", p=128)
        w_uqr_v = w_uq_rot_d.rearrange("(k p) h c -> p k h c", p=128)
        for kc in range(3):
            P.dma(STG[:, 0:768], w_uq_v[:, kc, :], writes=[STG])
            P.cp('dve', WUQ[:, kc, :], STG[:, 0:768], [STG], [WUQ])
        P.memset('pool', WUQR[:, :, :], 0.0, [WUQR])
        P.dma(STG[:, 0:768].rearrange("p (k h c) -> p k h c", k=3, h=8), w_uqr_v, writes=[STG])
        for kc in range(3):
            for h in range(8):
                c0 = kc * 256 + h * 32
                P.tt('dve', WROT[:, kc, h, :], STG[:, c0:c0 + 32], CST[:, C_SGN:C_SGN + 32], ALU.mult, [STG, CST], [WROT])
        for kc_ in range(2):
            P.dma(STG[:, 0:512], w_uk_d.rearrange("(k p) h c -> p k (h c)", p=128)[:, kc_, :], writes=[STG])
            P.cp('dve', WUK[:, kc_, :], STG[:, 0:512], [STG], [WUK])
        P.dma(tA[:, :].rearrange("p (k c) -> p k c", k=2)[:, :, 0:256], w_uv_d.rearrange("(k p) h c -> p k (h c)", p=128)[:, :, 0:256], writes=[tA])
        P.dma(tB[:, :].rearrange("p (k c) -> p k c", k=2)[:, :, 0:256], w_uv_d.rearrange("(k p) h c -> p k (h c)", p=128)[:, :, 256:512], writes=[tB])
        P.cp('dve', WUV[:, :, 0:256], tA[:, :].rearrange("p (k c) -> p k c", k=2), [tA], [WUV])
        P.cp('dve', WUV[:, :, 256:512], tB[:, :].rearrange("p (k c) -> p k c", k=2), [tB], [WUV])

        po = [pz, pBXf]
        KB = [(kb * 512, 512) for kb in range(NKEY // 512)]
        for hg in range(2):
            for hh_ in range(4):
                P.cp('dve', V4[:, :, hh_, 64], KVV[:, :], [KVV], [V4])
            for bi, (k0, kw) in enumerate(KB):
                cb_ = ckb[bi % 2]
                P.dma(cb_[:, :, 0:kw], CKVs[:, :, k0:k0 + kw], reads=[CKVs], writes=[cb_])
                for s in range(kw // 128):
                    kt = k0 // 128 + s
                    pv = pu[kt % 2]
                    for c in range(2):
                        P.mm(pv[:, 0:256], cb_[:, c, s * 128:(s + 1) * 128], WUV[:, c, hg * 256:(hg + 1) * 256], [cb_, WUV], [pv],
                             start=(c == 0), stop=(c == 1))
                    P.ts1('dve', V4[:, kt, :, 0:64], pv[:, 0:256].rearrange("p (h c) -> p h c", h=4), KVV[:, kt:kt + 1], ALU.mult, [pv, KVV], [V4])
            for hh in range(4):
                h = hg * 4 + hh
                kt_, qt_ = KT[h % 2], QT[h % 2]
                P.dma(kt_[64:96, :], KRs[:, :], reads=[KRs], writes=[kt_])
                for bi, (k0, kw) in enumerate(KB):
                    cb_ = ckb[bi % 2]
                    P.dma(cb_[:, :, 0:kw], CKVs[:, :, k0:k0 + kw], reads=[CKVs], writes=[cb_])
                    for c in range(2):
                        P.mm(ps2[bi % 2][0:64, 0:kw], WUK[:, c, h * 64:(h + 1) * 64], cb_[:, c, 0:kw],
                             [cb_, WUK], [ps2[bi % 2]], start=(c == 0), stop=(c == 1))
                    P.cp('act' if bi % 2 else 'dve', kt_[0:64, k0:k0 + kw], ps2[bi % 2][0:64, 0:kw], [ps2[bi % 2]], [kt_])
                for kc in range(3):
                    P.cp('pool', WUQR[:, kc, 64:96], WROT[:, kc, h, :], [WROT], [WUQR])
                for qb in range(4):
                    qs = slice(qb * 512, (qb + 1) * 512)
                    P.dma(cql[:, :, :], CQs[:, :, qs], reads=[CQs], writes=[cql])
                    P.dma(rq[64:96, :, :], ropeq[:, :, qs].rearrange("a d t -> d a t"), writes=[rq])
                    for c in range(3):
                        P.mm(po[0][0:96, :], WUQ[:, c, h * 96:(h + 1) * 96], cql[:, c, :], [WUQ, cql], [po[0]],
                             start=(c == 0), stop=(c == 2))
                    for c in range(3):
                        P.mm(po[1][0:96, :], WUQR[:, c, :], cql[:, c, :], [WUQR, cql], [po[1]], start=(c == 0), stop=(c == 2))
                    P.cp('act', qt_[0:64, qs], po[0][0:64, :], [po[0]], [qt_])
                    P.tt('dve', rq[64:96, 0, :], po[0][64:96, :], rq[64:96, 0, :], ALU.mult, [po[0], rq], [rq])
                    P.tt('dve', rq[64:96, 1, :], po[1][64:96, :], rq[64:96, 1, :], ALU.mult, [po[1], rq], [rq])
                    P.tt('dve', qt_[64:96, qs], rq[64:96, 0, :], rq[64:96, 1, :], ALU.add, [rq], [qt_])
                for qb in range(4):
                    qs = slice(qb * 512, (qb + 1) * 512)
                    pacc = po[qb % 2]
                    P.dma(gml[:, :], GMs[:, h, qs], reads=[GMs], writes=[gml])
                    for kt in range(NKT):
                        sc = ps2[kt % 2]
                        P.mm(sc[:, :], kt_[0:96, kt * 128:(kt + 1) * 128], qt_[0:96, qs], [kt_, qt_], [sc])
                        pT = PTb_[kt % 3]
                        P.act(pT[:, :], sc[:, :], AF.Exp, [sc], [pT], scale=ATTN_SCALE)
                        P.mm(pacc[0:65, :], V4[:, kt, hh, :], pT[:, :], [V4, pT], [pacc], start=(kt == 0), stop=(kt == NKT - 1))
                    P.recip(rec[64:65, :], pacc[64:65, :], [pacc], [rec])
                    P.mm(pH[0:64, 0:256], CST[64:65, C_ONES:C_ONES + 64], rec[64:65, 0:256], [CST, rec], [pH])
                    P.mm(pG[0:64, 0:256], CST[64:65, C_ONES:C_ONES + 64], rec[64:65, 256:512], [CST, rec], [pG])
                    P.cp('act', bcs[:, 0:256], pH[0:64, 0:256], [pH], [bcs])
                    P.cp('act', bcs[:, 256:512], pG[0:64, 0:256], [pG], [bcs])
                    P.tt('dve', onrm[:, :], pacc[0:64, :], bcs[:, :], ALU.mult, [pacc, bcs], [onrm])
                    P.tt('dve', mxb[:, :], onrm[:, :], gml[:, :], ALU.mult, [onrm, gml], [mxb])
                    P.dma(MIXMs[:, h, qs], mxb[:, :], reads=[mxb], writes=[MIXMs])

    def phase3():
        nonlocal o
        P.barrier()
        o = 0
        WOR = carve(4 * 1024, "p (k c) -> p k c", k=4)
        WOM = carve(8 * 1024, "p (k c) -> p k c", k=8)
        mr = [carve(512, "p (k c) -> p k c", k=4), carve(512, "p (k c) -> p k c", k=4)]
        mm_ = [carve(1024, "p (k c) -> p k c", k=8), carve(1024, "p (k c) -> p k c", k=8)]
        w_out_r = w_out_d[0:512, :].rearrange("(k p) c -> p k c", p=128)
        w_out_m = w_out_d[512:1024, :].rearrange("(k p) c -> p k c", p=64)
        for k in range(4):
            P.dma(STG[:, 0:1024], w_out_r[:, k, :], writes=[STG])
            P.cp('dve', WOR[:, k, :], STG[:, 0:1024], [STG], [WOR])
        for k in range(8):
            P.dma(STG[0:64, 0:1024], w_out_m[:, k, :], writes=[STG])
            P.cp('dve', WOM[0:64, k, :], STG[0:64, 0:1024], [STG], [WOM])
        GB = P.view(BKh.t[:, :])
        FG = P.view(BKt.t[:, :, :].rearrange('p m c -> p (m c)'))
        for half in range(2):
            for q in range(4):
                kc = half * 4 + q
                P.ts1('dve', th0[:, :], ident, MODT[:, 16 + kc, 0:1], ALU.mult, [CST, MODT], [th0])
                P.mm(pt[:, q * 128:(q + 1) * 128], ones, th0[:, :], [CST, th0], [pt])
            P.cp('act', GB[:, half * 512:(half + 1) * 512], pt[:, :], [pt], [GB])
        P.dma(FG[:, :], brow_d[:, 1024:2048], writes=[FG])
        yo_ = [P.view(RKV.t[:, 0:8, :].rearrange('p m t -> p (m t)')), P.view(EALL.t[:, 0:2, :, :].rearrange('p m a t -> p (m a t)'))]
        for jj in range(16):
            tok = slice(jj * 128, (jj + 1) * 128)
            x_ = xt[jj % 2]
            P.dma(x_[:, :], xs[68 + jj, 0:128, :], writes=[x_])
            P.dma(mr[jj % 2][:, :, :], MIXRs[:, :, tok], reads=[MIXRs], writes=[mr[jj % 2]])
            P.dma(mm_[jj % 2][0:64, :, :], MIXMs[:, :, tok], reads=[MIXMs], writes=[mm_[jj % 2]])
            y_ = yo_[jj % 2]
            for half in range(2):
                cs = slice(half * 512, (half + 1) * 512)
                pp = ps2[half]
                for k in range(4):
                    P.mm(pp[:, :], mr[jj % 2][:, k, :], WOR[:, k, cs], [mr[jj % 2], WOR], [pp], start=(k == 0), stop=False)
                for k in range(8):
                    P.mm(pp[:, :], mm_[jj % 2][0:64, k, :], WOM[0:64, k, cs], [mm_[jj % 2], WOM], [pp], start=False, stop=(k == 7))
                P.tt('dve', y_[:, cs], pp[:, :], GB[:, cs], ALU.mult, [pp, GB], [y_])
            P.tt('pool', y_[:, :], y_[:, :], x_[:, :], ALU.add, [y_, x_], [y_])
            P.act(junk[:, :], y_[:, :], AF.Square, [y_], [junk])
            P.red(st4[:, 0:1], junk[:, :], [junk], [st4])
            rsqrt_ln(st4, st4[:, 0:1], st4, st4[:, 0:1], 1.0 / 1024, pc(PC_EPS))
            P.stt('dve', y_[:, :], y_[:, :], st4[:, 0:1], FG[:, :], ALU.mult, ALU.mult, [y_, st4, FG], [y_])
            P.dma(out_d[tok, :], y_[:, :], reads=[y_])
    if CFG['p2']:
        phase2()
    if CFG['p3']:
        phase3()
    P.finish()
    return nc


DEBUG_SPECS = {'xf': (128, 192), 'at': (128, 512), 'atok': (128, 512), 'vtok': (128, 512), 'eall': (128, 2048), 'sg': (128, 512), 'rkv': (128, 1536), 'hout': (128, 256)}
_CACHE = {}


def kernel(**inputs):
    maps = host_prep(inputs)
    if 'nc' not in _CACHE:
        _CACHE['nc'] = build()
    nc = _CACHE['nc']
    res = run_bass_kernel_spmd(nc, maps, core_ids=list(range(8)))
    out = np.zeros((2, 8192, 1024), np.float32)
    for c in range(8):
        b, p = c // 4, c % 4
        out[b, 2048 * p:2048 * (p + 1)] = np.asarray(res.results[c]['out'], np.float32)
    if DEBUG:
        _CACHE['res'] = res
    return out
```

```python
import numpy as np
from contextlib import ExitStack
import concourse.bass as bass
import concourse.mybir as mybir
from concourse.bass_utils import run_bass_kernel_spmd

F32 = mybir.dt.float32
BF16 = mybir.dt.bfloat16
ALU = mybir.AluOpType
AF = mybir.ActivationFunctionType
AX = mybir.AxisListType
BANK = 30000
NDMA = 24
ENGS = ['pe', 'act', 'dve', 'pool', 'sp']
DEBUG = False
CFG = {'slots': None, 'p2': True, 'p3': True, 'stage': 99}

D_MODEL = 1024
D_IN = 3488
NSTATE = 52
NSLOT = 84
NKT = 68
NKEY = NKT * 128
KAPPA = -float(np.exp(-0.5))
ATTN_SCALE = 96 ** -0.5
NORM_EPS = 1e-6
LNX_EPS = 64e-5


class Trk:
    __slots__ = ('w', 'r')

    def __init__(self):
        self.w = None
        self.r = {}


class Buf:
    def __init__(self, t, ap=None):
        self.t = t
        self.k = Trk()
        self._ap = ap

    def __getitem__(self, idx):
        return (self._ap if self._ap is not None else self.t)[idx]

    @property
    def ap(self):
        return self._ap if self._ap is not None else self.t[:]


class Prog:
    def __init__(self, nc):
        self.nc = nc
        self.es = ExitStack()
        self.ops = {e: [] for e in ENGS}
        self.cnt = {e: 0 for e in ENGS}
        self.seen = {e: {} for e in ENGS}
        self.sems = {}
        self.dma_tot = [0] * NDMA
        self.dma_next = 0

    def sb(self, name, shape, dt=F32):
        return Buf(self.es.enter_context(self.nc.sbuf_tensor('sb_' + name, list(shape), dt)))

    def ps(self, name, shape, dt=F32):
        return Buf(self.es.enter_context(self.nc.psum_tensor('ps_' + name, list(shape), dt)))

    def view(self, ap):
        return Buf(None, ap)

    def _deps(self, eng, reads, writes):
        deps = {}

        def add(ev):
            if ev is None:
                return
            k, v = ev
            if deps.get(k, 0) < v:
                deps[k] = v
        for b in reads:
            add(b.k.w)
        for b in writes:
            add(b.k.w)
            for k, v in b.k.r.items():
                add((k, v))
        waits = []
        for k, v in deps.items():
            if k[0] == 'pe' and eng == 'pe':
                continue
            if self.seen[eng].get(k, 0) >= v:
                continue
            self.seen[eng][k] = v
            waits.append((k, v))
        return waits

    def _mark(self, ev, reads, writes):
        k, v = ev
        for b in writes:
            b.k.w = ev
            b.k.r = {}
        for b in reads:
            if b in writes:
                continue
            if b.k.r.get(k, 0) < v:
                b.k.r[k] = v

    def op(self, eng, fn, reads=(), writes=()):
        waits = self._deps(eng, reads, writes)
        i = self.cnt[eng]
        self.cnt[eng] += 1
        ev = ((eng, i // BANK), i % BANK + 1)
        self._mark(ev, reads, writes)
        self.ops[eng].append((waits, fn, ev[0], 1))

    def dma(self, out, in_, reads=(), writes=(), eng='sp', **kw):
        waits = self._deps(eng, reads, writes)
        s = self.dma_next % NDMA
        self.dma_next += 1
        k = ('dma', s)
        if self.dma_tot[s] > 0 and self.seen[eng].get(k, 0) < self.dma_tot[s]:
            self.seen[eng][k] = self.dma_tot[s]
            waits.append((k, self.dma_tot[s]))
        self.dma_tot[s] += 16
        ev = (k, self.dma_tot[s])
        self._mark(ev, reads, writes)
        self.ops[eng].append((waits, lambda e: e.dma_start(out=out, in_=in_, **kw), k, 16))

    def barrier(self):
        last = {}
        for e in ENGS:
            i = self.cnt[e]
            if i > 0:
                last[(e, (i - 1) // BANK)] = (i - 1) % BANK + 1
        for s in range(NDMA):
            if self.dma_tot[s] > 0:
                last[('dma', s)] = self.dma_tot[s]
        for e in ENGS:
            waits = []
            for k, v in last.items():
                if k[0] == e:
                    continue
                if self.seen[e].get(k, 0) >= v:
                    continue
                self.seen[e][k] = v
                waits.append((k, v))
            if waits:
                self.ops[e].append((waits, None, None, 0))

    def mm(self, out, lhsT, rhs, rd, wr, start=True, stop=True):
        self.op('pe', lambda e: e.matmul(out, lhsT, rhs, start=start, stop=stop), rd, wr)

    def tr(self, out, in_, ident, rd, wr):
        self.op('pe', lambda e: e.transpose(out, in_, ident), rd, wr)

    def act(self, out, in_, func, rd, wr, bias=None, scale=None):
        kw = {}
        if bias is not None:
            kw['bias'] = bias
        if scale is not None:
            kw['scale'] = scale
        self.op('act', lambda e: e.activation(out, in_, func, **kw), rd, wr)

    def tt(self, eng, out, a, b, op, rd, wr):
        self.op(eng, lambda e: e.tensor_tensor(out, a, b, op), rd, wr)

    def ts(self, eng, out, a, s1, s2, op0, op1, rd, wr):
        self.op(eng, lambda e: e.tensor_scalar(out, a, s1, s2, op0, op1), rd, wr)

    def ts1(self, eng, out, a, s, op, rd, wr):
        self.op(eng, lambda e: e.tensor_single_scalar(out, a, s, op), rd, wr)

    def stt(self, eng, out, a, s, b, op0, op1, rd, wr):
        self.op('dve', lambda e: e.scalar_tensor_tensor(out, a, s, b, op0, op1), rd, wr)

    def cp(self, eng, out, in_, rd, wr):
        if eng == 'act':
            self.op('act', lambda e: e.copy(out, in_), rd, wr)
        else:
            self.op(eng, lambda e: e.tensor_copy(out, in_), rd, wr)

    def recip(self, out, in_, rd, wr):
        self.op('dve', lambda e: e.reciprocal(out, in_), rd, wr)

    def red(self, out, in_, rd, wr, op=ALU.add):
        self.op('dve', lambda e: e.tensor_reduce(out, in_, AX.X, op), rd, wr)

    def memset(self, eng, ap, val, wr):
        self.op(eng, lambda e: e.memset(ap, val), (), wr)

    def finish(self):
        nc = self.nc
        keys = set()
        for e in ENGS:
            for waits, fn, k, inc in self.ops[e]:
                if k is not None:
                    keys.add(k)
                for wk, _ in waits:
                    keys.add(wk)
        for k in sorted(keys, key=str):
            self.sems[k] = self.es.enter_context(nc.semaphore("s_%s_%s" % k))
        fin = [(('dma', s), self.dma_tot[s]) for s in range(NDMA) if self.dma_tot[s] > 0]
        block = self.es.enter_context(nc.Block())

        def run(e, name):
            for waits, fn, k, inc in self.ops[name]:
                for wk, wv in waits:
                    e.wait_ge(self.sems[wk], wv)
                if fn is not None:
                    fn(e).then_inc(self.sems[k], inc)
            if name == 'sp':
                for wk, wv in fin:
                    e.wait_ge(self.sems[wk], wv)

        @block.tensor
        def _(e):
            run(e, 'pe')

        @block.scalar
        def _(e):
            run(e, 'act')

        @block.vector
        def _(e):
            run(e, 'dve')

        @block.gpsimd
        def _(e):
            run(e, 'pool')

        @block.sync
        def _(e):
            run(e, 'sp')
        self.es.close()


PC_NG, PC_ADAB, PC_MUP, PC_MUN, PC_KK, PC_KA, PC_RK, PC_A0, PC_QG, PC_KVG = 0, 8, 32, 46, 60, 64, 68, 72, 80, 83
PC_EPS, PC_LEPS, PC_ONE, PC_TINY = 85, 86, 87, 88
NPC = 96
C_ID, C_ONES, C_BO, C_F, C_B, C_TOT, C_BO2, C_SGN = 0, 128, 256, 384, 1280, 2176, 2177, 2179
SD_F, SD_SELW, SD_M2S, SD_PHI, SD_RHO, SD_SIG, SD_NF = 0, 1, 2, 3, 4, 5, 6
NCST = 2211

SLOTS = []
for _i in range(52):
    SLOTS.append(('S', _i))
for _j in range(16):
    SLOTS.append(('B', None))
for _j in range(16):
    SLOTS.append(('F', 52 + _j))


def _consts():
    c = np.zeros((128, NCST), np.float32)
    idx = np.arange(128)
    row, col = idx[:, None], idx[None, :]
    c[:, C_ID:C_ID + 128] = np.eye(128)
    c[:, C_ONES:C_ONES + 128] = 1.0
    c[:, C_BO:C_BO + 128] = ((row // 64) == (col // 64))
    k = KAPPA
    tf = [k * (row < col), -k * (row <= col), k * (row <= col), k * (row > col)]
    tb = [k * (row > col), -k * (row >= col), k * (row >= col), k * (row < col)]
    c[:, C_F:C_F + 896] = np.concatenate(tf + [col > row, col >= row, col < row], 1)
    c[:, C_B:C_B + 896] = np.concatenate(tb + [col < row, col <= row, col > row], 1)
    c[:, C_TOT] = k
    c[:, C_BO2] = (idx < 64)
    c[:, C_BO2 + 1] = (idx >= 64)
    d = np.arange(32)
    c[:, C_SGN:C_SGN + 32] = np.where(d % 16 < 8, -1.0, 1.0)[None, :]
    return c


ROT_PERM = np.array([dd + 8 if dd % 16 < 8 else dd - 8 for dd in range(32)])


def _rope_tables(pos):
    pos = np.asarray(pos)
    inv = (np.float32(10000.0) ** (-np.arange(0, 16, 2, dtype=np.float32) / np.float32(16))).astype(np.float32)
    rowf = (pos // 64).astype(np.float32)
    colf = (pos % 64).astype(np.float32)
    ar = (rowf[:, None] * inv[None, :]).astype(np.float32)
    ac = (colf[:, None] * inv[None, :]).astype(np.float32)
    cos = np.concatenate([np.cos(ar), np.cos(ar), np.cos(ac), np.cos(ac)], 1).astype(np.float32)
    sin = np.concatenate([np.sin(ar), np.sin(ar), np.sin(ac), np.sin(ac)], 1).astype(np.float32)
    return cos.T.copy(), sin.T.copy()


def _tile_halo(seq, c):
    n = seq.shape[0]
    t = np.zeros((130, seq.shape[1]), np.float32)
    t[:128] = seq[c * 128:(c + 1) * 128]
    hv = [0.0, 0.0]
    if c > 0:
        t[128] = seq[c * 128 - 1]
        hv[0] = 1.0
    if (c + 1) * 128 < n:
        t[129] = seq[(c + 1) * 128]
        hv[1] = 1.0
    return t, hv


def host_prep(inp):
    g = {k: np.asarray(v, np.float32) for k, v in inp.items()}
    sh = {}
    w_in = g['w_in'][0]
    sh['w_in'] = w_in
    sh['wkr_rot'] = np.ascontiguousarray(w_in[:, 2944:2976][:, ROT_PERM])
    w_uq = g['mla_w_uq'][0]
    sh['w_uq'] = w_uq
    sh['w_uq_rot'] = np.ascontiguousarray(
        np.stack([w_uq[:, h * 96 + 64:h * 96 + 96][:, ROT_PERM] for h in range(8)], 1))
    w_ukv = g['mla_w_ukv'][0].reshape(256, 8, 128)
    sh['w_uk'] = np.ascontiguousarray(w_ukv[:, :, :64])
    sh['w_uv'] = np.ascontiguousarray(w_ukv[:, :, 64:])
    sh['w_out'] = g['w_out'][0]
    sh['ada_w'] = g['ada_w'][0]
    sh['w2cat'] = np.ascontiguousarray(g['rw_w2'][0].reshape(128, 512))
    sh['a2cat'] = np.ascontiguousarray(g['rw_a2'][0].reshape(128, 512))
    sh['w0r'] = np.ascontiguousarray(g['rw_w0'][0].reshape(1, 1024))
    pc = np.zeros((128, NPC), np.float32)
    pc[:, PC_NG:PC_NG + 8] = g['norm_g'][0].reshape(8, 128).T
    pc[:, PC_ADAB:PC_ADAB + 24] = g['ada_b'][0].reshape(24, 128).T
    pc[:, PC_MUP:PC_MUP + 14] = g['shift_mu'][0, 0].reshape(14, 128).T
    pc[:, PC_MUN:PC_MUN + 14] = g['shift_mu'][0, 1].reshape(14, 128).T
    pc[:, PC_KK:PC_KK + 4] = g['rw_kk'][0].reshape(4, 128).T
    pc[:, PC_KA:PC_KA + 4] = g['rw_ka'][0].reshape(4, 128).T
    pc[:, PC_RK:PC_RK + 4] = g['rw_rk'][0].reshape(4, 128).T
    pc[:, PC_A0:PC_A0 + 4] = g['rw_a0'][0, 0].reshape(4, 128).T
    pc[:, PC_A0 + 4:PC_A0 + 8] = g['rw_a0'][0, 1].reshape(4, 128).T
    pc[:, PC_QG:PC_QG + 3] = g['mla_q_norm_g'][0].reshape(3, 128).T
    pc[:, PC_KVG:PC_KVG + 2] = g['mla_kv_norm_g'][0].reshape(2, 128).T
    pc[:, PC_EPS] = NORM_EPS
    pc[:, PC_LEPS] = LNX_EPS
    pc[:, PC_ONE] = 1.0
    pc[:, PC_TINY] = 1e-24
    sh['pcol'] = pc
    br = np.zeros((128, 2048), np.float32)
    br[:, 0:512] = g['rw_lnx_g'][0][None, :]
    br[:, 512:1024] = g['rw_lnx_b'][0][None, :]
    br[:, 1024:2048] = g['final_g'][None, :]
    sh['brow'] = br
    sh['cst'] = _consts()
    maps = []
    for b in range(2):
        x, ctx = g['x'][b], g['ctx'][b]
        cvec = np.stack([g['c'][b].reshape(128, 8), g['c_ctx'].reshape(128, 8)], -1)
        for p in range(4):
            nf = 2 + 16 * p
            fsrc = [(ctx, 0, True), (ctx, 1, True)] + [(x, c, False) for c in range(16 * p)]
            bsrc = [(ctx, 1, True), (ctx, 0, True)] + [(x, c, False) for c in range(63, 16 * p + 15, -1)]
            own = [(x, 16 * p + j, False) for j in range(16)]
            allsrc = fsrc + bsrc + own[::-1] + own
            assert len(allsrc) == NSLOT
            tiles, hvs = [], []
            for seq, c, _ in allsrc:
                t, hv = _tile_halo(seq, c)
                tiles.append(t)
                hvs.append(hv)
            xs = np.stack(tiles, 0)
            hvs = np.array(hvs, np.float32)
            sd = np.zeros((128, 7, NSLOT), np.float32)
            f = np.array([1.0] * nf + [0.0] * (52 - nf) + [0.0] * 16 + [1.0] * 16, np.float32)
            sd[:, SD_F, :] = f[None]
            sd[:, SD_NF, :] = 1.0 - f[None]
            sd[:64, SD_SELW, :] = f[None]
            sd[64:, SD_SELW, :] = 1.0 - f[None]
            sd[:, SD_M2S, :] = -2.0 * sd[:, SD_SELW, :]
            sd[:, SD_PHI, :] = np.array([1.0 if ic else 0.0 for _, _, ic in allsrc], np.float32)[None]
            rho = np.ones(NSLOT, np.float32)
            rho[nf] = 0.0
            sd[:, SD_RHO, :] = rho[None]
            sgm = np.zeros(NSLOT, np.float32)
            sgm[nf - 1] = 1.0
            sd[:, SD_SIG, :] = sgm[None]
            kvsrc = fsrc + bsrc + own
            rk = np.zeros((NKT, 2, 32, 128), np.float32)
            rk[:, 0] = 1.0
            kvv = np.ones(NKT, np.float32)
            kvv[nf] = 0.0
            kvv[nf + 1] = 0.0
            for kt, (seq, c, ic) in enumerate(kvsrc):
                if not ic:
                    cs, sn = _rope_tables(np.arange(c * 128, (c + 1) * 128))
                    rk[kt, 0], rk[kt, 1] = cs, sn
            cq, sq = _rope_tables(np.arange(2048 * p, 2048 * (p + 1)))
            m = dict(sh)
            m['xs'] = xs
            m['hv'] = np.ascontiguousarray(np.broadcast_to(hvs[None], (128, NSLOT, 2)))
            m['sd'] = sd
            m['kvv'] = np.ascontiguousarray(np.broadcast_to(kvv[None], (128, NKT)))
            m['ropek'] = rk
            m['ropeq'] = np.stack([cq, sq], 0)
            m['cvec'] = cvec
            maps.append(m)
    return maps


def build():
    nc = bass.Bass("TRN2", target_bir_lowering=False)
    P = Prog(nc)

    def din(name, shape):
        return nc.dram_tensor(name, list(shape), F32, kind="ExternalInput").ap()

    xs = din('xs', [NSLOT, 130, 1024])
    hv_d = din('hv', [128, NSLOT, 2])
    sd_d = din('sd', [128, 7, NSLOT])
    kvv_d = din('kvv', [128, NKT])
    ropek = din('ropek', [NKT, 2, 32, 128])
    ropeq = din('ropeq', [2, 32, 2048])
    cvec_d = din('cvec', [128, 8, 2])
    w_in_d = din('w_in', [1024, D_IN])
    wkr_rot_d = din('wkr_rot', [1024, 32])
    w_uq_d = din('w_uq', [384, 768])
    w_uq_rot_d = din('w_uq_rot', [384, 8, 32])
    w_uk_d = din('w_uk', [256, 8, 64])
    w_uv_d = din('w_uv', [256, 8, 64])
    w_out_d = din('w_out', [1024, 1024])
    ada_w_d = din('ada_w', [1024, 3072])
    w2cat_d = din('w2cat', [128, 512])
    a2cat_d = din('a2cat', [128, 512])
    w0r_d = din('w0r', [1, 1024])
    pcol_d = din('pcol', [128, NPC])
    brow_d = din('brow', [128, 2048])
    cst_d = din('cst', [128, NCST])
    out_d = nc.dram_tensor('out', [2048, 1024], F32, kind="ExternalOutput").ap()
    dbg = {}
    if DEBUG:
        for nm, shp in DEBUG_SPECS.items():
            dbg[nm] = nc.dram_tensor('dbg_' + nm, list(shp), F32, kind="ExternalOutput").ap()

    def dscr(name, shape, dt):
        t = nc.dram_tensor(name, list(shape), dt)
        return Buf(t, t.ap())

    CKVs = dscr('s_ckv', [128, 2, NKEY], BF16)
    KRs = dscr('s_kr', [32, NKEY], BF16)
    CQs = dscr('s_cq', [128, 3, 2048], BF16)
    GMs = dscr('s_gm', [64, 8, 2048], BF16)
    MIXRs = dscr('s_mixr', [128, 4, 2048], BF16)
    MIXMs = dscr('s_mixm', [64, 8, 2048], BF16)
    YBs = dscr('s_yb', [16, 128, 512], F32)
    SBs = dscr('s_sb', [16, 128, 8], F32)

    CST = P.sb('cst', [128, NCST])
    PCOL = P.sb('pcol', [128, NPC])
    LGB = P.sb('lgb', [128, 1024])
    HV = P.sb('hv', [128, NSLOT, 2])
    SD = P.sb('sd', [128, 7, NSLOT])
    KVV = P.sb('kvv', [128, NKT])
    CSEL = P.sb('csel', [128, 512])
    GSL = P.sb('gsl', [128, 4, 8])
    W0S = P.sb('w0s', [1, 512])
    NA0S = P.sb('na0s', [128, 4])
    W2C = P.sb('w2c', [128, 512])
    A2C = P.sb('a2c', [128, 512])
    W0R = P.sb('w0r', [1, 1024])
    MODT = P.sb('modt', [128, 24, 2])
    GS = P.sb('gs', [128, 2, 8])
    SH = P.sb('shc', [128, 2, 8])
    M0 = P.sb('m0', [128, 14])
    NA0 = P.sb('na0', [128, 8])
    ARENA = P.sb('arena', [128, 28432], BF16)
    WB = P.view(ARENA.t[:, 0:27904].rearrange("p (k c) -> p k c", k=8))
    WKR = P.sb('wkr', [128, 8, 2, 96], BF16)

    for (b_, d_) in [(CST, cst_d), (PCOL, pcol_d), (HV, hv_d), (SD, sd_d), (KVV, kvv_d), (W2C, w2cat_d), (A2C, a2cat_d),
                     (W0R, w0r_d)]:
        P.dma(b_.ap, d_, writes=[b_])
    P.dma(LGB[:, :], brow_d[:, 0:1024], writes=[LGB])
    ident = CST[:, C_ID:C_ID + 128]
    ones = CST[:, C_ONES:C_ONES + 128]

    def pc(c0, n=1):
        return PCOL[:, c0:c0 + n]

    def ps_pair(name):
        q = P.ps(name, [128, 512])
        v0, v1 = P.view(q.t[:, 0:256]), P.view(q.t[:, 256:512])
        v0.k = q.k
        v1.k = q.k
        return v0, v1, q
    pu = [P.ps('pu0', [128, 512]), P.ps('pu1', [128, 512])]
    pz = P.ps('pz', [128, 512])
    pt = pz
    pA = P.ps('pA', [128, 512])
    pB, pX, pBXf = ps_pair('qbx')
    pS, pM, _ = ps_pair('qsm')
    pY = P.ps('pY', [128, 512])
    pH, pG, _ = ps_pair('qhg')

    xt = [P.sb('xt0', [128, 1024]), P.sb('xt1', [128, 1024])]
    xh = [P.sb('xh0', [2, 1024])]
    junk = P.sb('junk', [128, 1024])
    tD = P.view(junk.t[:, 0:512])
    tE = P.view(junk.t[:, 512:1024])
    tD.k = junk.k
    tE.k = junk.k
    st4 = P.sb('st4', [128, 8])
    hT = [P.sb('hT0', [128, 8, 130], BF16), P.sb('hT1', [128, 8, 130], BF16)]
    shv = P.sb('shv', [128, 2, 8])
    ue = [P.sb('ue0', [128, 130]), P.sb('ue1', [128, 130])]
    RKV = P.sb('rkv', [128, 12, 128])
    WA = P.sb('wa', [128, 2, 128])
    tA = P.sb('tA', [128, 512])
    tB = P.sb('tB', [128, 512])
    tC = P.sb('tC', [128, 512])
    th0 = P.sb('th0', [128, 128])
    th1 = P.sb('th1', [128, 128])
    sg = P.sb('sg', [128, 512])
    EALL = P.sb('eall', [128, 4, 4, 128])
    STG = P.view(EALL.t[:, :, :, :].rearrange('p m a t -> p (m a t)')[:, 0:1792])
    STG.k = EALL.k
    GCL = [P.sb('gc0', [128, 4]), P.sb('gc1', [128, 4])]
    MSKL = [P.sb('msk0', [128, 384]), P.sb('msk1', [128, 384])]
    ARtL = [P.sb('art%d' % q_, [128, 4, 256]) for q_ in range(2)]
    BKtL = [P.sb('bkt%d' % q_, [128, 4, 256]) for q_ in range(2)]
    ARt, BKt = ARtL[0], BKtL[0]
    BKh = P.sb('bkh', [128, 1024])
    BhT = P.view(BKh.t[:, 0:512])
    KhT = P.view(BKh.t[:, 512:1024])
    BhT.k = BKh.k
    KhT.k = BKh.k
    AtokL = [P.sb('atok%d' % q_, [128, 512]) for q_ in range(2)]
    BhtokL = [P.sb('bhtok%d' % q_, [128, 512]) for q_ in range(2)]
    KhtokL = [P.sb('khtok%d' % q_, [128, 512]) for q_ in range(2)]
    VtokL = [P.sb('vtok%d' % q_, [128, 512]) for q_ in range(2)]
    AT = [P.sb('at0', [128, 512])] * 2
    AN = [P.sb('an0', [128, 128])] * 2
    XPB = [P.sb('xp%d' % q_, [128, 320]) for q_ in range(5)]
    PTB = [P.sb('ptq%d' % q_, [128, 128]) for q_ in range(2)]
    MTs = P.sb('mts', [128, 4, 64])
    Ns = P.sb('ns', [128, 4, 64])
    GTs = [P.sb('gts0', [128, 128]), P.sb('gts1', [128, 128])]
    Hc = [P.sb('hc0', [128, 4, 64]), P.sb('hc1', [128, 4, 64])]
    HFa = P.sb('hfa', [128, 4, 64])
    ysb = P.sb('ysb', [128, 512])
    ybl = P.sb('ybl', [128, 512])
    sbl = P.sb('sbl', [128, 8])
    ssumL = [P.sb('ssum0', [128, 8]), P.sb('ssum1', [128, 8])]
    mla1 = P.sb('mla1', [128, 4, 128])
    mla2 = P.sb('mla2', [128, 4, 128])
    kvb = P.sb('kvb', [128, 2, 128], BF16)
    krb = P.sb('krb', [96, 128], BF16)
    rpk = P.sb('rpk', [96, 2, 128])
    cqb = P.sb('cqb', [128, 3, 128], BF16)
    gmb = P.sb('gmb', [64, 8, 128], BF16)
    mixb = P.sb('mixb', [128, 4, 128], BF16)

    def dump(name, buf, ap):
        if DEBUG and name in dbg:
            P.dma(dbg[name], ap, reads=[buf])

    w_in_v = w_in_d.rearrange("(k p) c -> p k c", p=128)
    for kc in range(8):
        for hf in range(2):
            P.dma(STG[:, 0:1744], w_in_v[:, kc, hf * 1744:(hf + 1) * 1744], writes=[STG])
            P.cp(['act', 'dve'][hf], WB[:, kc, hf * 1744:(hf + 1) * 1744], STG[:, 0:1744], [STG], [WB])
    P.memset('pool', WKR[:, :, :, :], 0.0, [WKR])
    for kc in range(8):
        P.cp('dve', WKR[:, kc, 0, 64:96], WB[:, kc, 2944:2976], [WB], [WKR])
    wkr_v = wkr_rot_d.rearrange("(k p) c -> p k c", p=128)
    P.dma(junk[:, 0:256].rearrange("p (k c) -> p k c", k=8), wkr_v, writes=[junk])
    for kc in range(8):
        P.tt('dve', WKR[:, kc, 1, 64:96], junk[:, kc * 32:(kc + 1) * 32], CST[:, C_SGN:C_SGN + 32], ALU.mult,
             [junk, CST], [WKR])
    CV = P.sb('cv', [128, 8, 2])
    CV2 = P.sb('cv2', [128, 8, 2])
    P.dma(CV[:, :, :], cvec_d, writes=[CV])
    P.act(CV2[:, :, :], CV[:, :, :], AF.Exp, [CV], [CV2], scale=-1.0)
    P.ts1('dve', CV2[:, :, :], CV2[:, :, :], 1.0, ALU.add, [CV2], [CV2])
    P.recip(CV2[:, :, :], CV2[:, :, :], [CV2], [CV2])
    P.tt('dve', CV[:, :, :], CV[:, :, :], CV2[:, :, :], ALU.mult, [CV, CV2], [CV])
    ada_v = ada_w_d.rearrange("(p k) n -> p k n", k=8)
    MACC = P.sb('macc', [128, 48])
    P.memset('dve', MACC[:, :], 0.0, [MACC])
    for k in range(8):
        for hf in range(2):
            P.dma(STG[:, 0:1536], ada_v[:, k, hf * 1536:(hf + 1) * 1536], writes=[STG])
            for n in range(12):
                nn = hf * 12 + n
                P.mm(pz[:, 2 * nn:2 * nn + 2], STG[:, n * 128:(n + 1) * 128], CV[:, k, :], [STG, CV], [pz])
            P.tt('dve', MACC[:, hf * 24:(hf + 1) * 24], MACC[:, hf * 24:(hf + 1) * 24], pz[:, hf * 24:(hf + 1) * 24],
                 ALU.add, [MACC, pz], [MACC])
    for j in range(2):
        P.tt('dve', MODT[:, :, j], MACC[:, j:48:2], PCOL[:, PC_ADAB:PC_ADAB + 24], ALU.add, [MACC, PCOL], [MODT])
    for j in range(2):
        P.stt('dve', GS[:, j, :], MODT[:, 8:16, j], 1.0, PCOL[:, PC_NG:PC_NG + 8], ALU.add, ALU.mult,
              [MODT, PCOL], [GS])
        P.cp('dve', SH[:, j, :], MODT[:, 0:8, j], [MODT], [SH])
    P.cp('dve', GSL[:, 0, :], GS[:, 0, :], [GS], [GSL])
    P.tt('dve', GSL[:, 1, :], GS[:, 1, :], GS[:, 0, :], ALU.subtract, [GS], [GSL])
    P.cp('dve', GSL[:, 2, :], SH[:, 0, :], [SH], [GSL])
    P.tt('dve', GSL[:, 3, :], SH[:, 1, :], SH[:, 0, :], ALU.subtract, [SH], [GSL])
    P.tt('dve', M0[:, :], PCOL[:, PC_MUP:PC_MUP + 14], PCOL[:, PC_MUN:PC_MUN + 14], ALU.add, [PCOL], [M0])
    P.ts('dve', M0[:, :], M0[:, :], -1.0, 1.0, ALU.mult, ALU.add, [M0], [M0])
    P.ts1('dve', NA0[:, :], PCOL[:, PC_A0:PC_A0 + 8], -1.0, ALU.mult, [PCOL], [NA0])
    for b_ in XPB:
        P.memset('pool', b_[:, :], 0.0, [b_])
    P.memset('pool', Hc[0][:, :, :], 0.0, [Hc[0]])
    P.memset('pool', HFa[:, :, :], 0.0, [HFa])

    def load_slot(i):
        P.dma(xt[i % 2][:, :], xs[i, 0:128, :], writes=[xt[i % 2]])

    def rsqrt_ln(dst_buf, dst, src_buf, src, scale, eps_ap):
        P.act(dst, src, AF.Ln, [src_buf, PCOL], [dst_buf], bias=eps_ap, scale=scale)
        P.act(dst, dst, AF.Exp, [dst_buf], [dst_buf], scale=-0.5)

    state = {'h': 0}

    PCNT = {'n': 0}
    ADV = 1

    def proj_for(hT_, cnt):
        def proj(c0, width, ncols=128, lhs=None):
            pb_ = pu[cnt['n'] % 2]
            cnt['n'] += 1
            for kc in range(8):
                l = WB[:, kc, c0:c0 + width] if lhs is None else lhs(kc)
                P.mm(pb_[0:(width if lhs is None else 96), 0:ncols], l, hT_[:, kc, 0:ncols],
                     [WB if lhs is None else WKR, hT_], [pb_], start=(kc == 0), stop=(kc == 7))
            return pb_
        return proj

    def prep(i):
        kind, kv = SLOTS[i]
        S = i % 2
        ARt, BKt, Atok, Bhtok, Khtok, Vtok = ARtL[S], BKtL[S], AtokL[S], BhtokL[S], KhtokL[S], VtokL[S]
        GC, MSK, ssum = GCL[S], MSKL[S], ssumL[S]
        x_, xh_, hT_ = xt[i % 2], xh[0], hT[i % 2]
        P.dma(xh_[:, :], xs[i, 128:130, :], writes=[xh_])
        fcol = SD[:, SD_F, i:i + 1]
        nfcol = SD[:, SD_NF, i:i + 1]
        phi = SD[:, SD_PHI, i:i + 1]
        P.stt('dve', GS[:, 0, :], GSL[:, 1, :], phi, GSL[:, 0, :], ALU.mult, ALU.add, [GSL, SD], [GS])
        P.stt('dve', SH[:, 0, :], GSL[:, 3, :], phi, GSL[:, 2, :], ALU.mult, ALU.add, [GSL, SD], [SH])
        P.ts1('dve', CSEL[:, :], CST[:, C_F:C_F + 512], fcol, ALU.mult, [CST, SD], [CSEL])
        P.stt('dve', CSEL[:, :], CST[:, C_B:C_B + 512], nfcol, CSEL[:, :], ALU.mult, ALU.add, [CST, SD, CSEL], [CSEL])
        P.ts1('dve', MSK[:, :], CST[:, C_F + 512:C_F + 896], fcol, ALU.mult, [CST, SD], [MSK])
        P.stt('dve', MSK[:, :], CST[:, C_B + 512:C_B + 896], nfcol, MSK[:, :], ALU.mult, ALU.add, [CST, SD, MSK], [MSK])
        P.ts1('dve', W0S[0:1, :], W0R[0:1, 0:512], SD[0:1, SD_F, i:i + 1], ALU.mult, [W0R, SD], [W0S])
        P.stt('dve', W0S[0:1, :], W0R[0:1, 512:1024], SD[0:1, SD_NF, i:i + 1], W0S[0:1, :], ALU.mult, ALU.add,
              [W0R, SD, W0S], [W0S])
        P.ts1('dve', NA0S[:, :], NA0[:, 0:4], fcol, ALU.mult, [NA0, SD], [NA0S])
        P.stt('dve', NA0S[:, :], NA0[:, 4:8], nfcol, NA0S[:, :], ALU.mult, ALU.add, [NA0, SD, NA0S], [NA0S])
        j = 0
        P.act(junk[:, :], x_[:, :], AF.Square, [x_], [junk])
        P.red(st4[:, 0:1], junk[:, :], [junk], [st4])
        rsqrt_ln(st4, st4[:, 0:1], st4, st4[:, 0:1], 1.0 / 1024, pc(PC_EPS))
        P.ts1('dve', x_[:, :], x_[:, :], st4[:, 0:1], ALU.mult, [x_, st4], [x_])
        for half in range(2):
            yield
            for q in range(4):
                kc = half * 4 + q
                P.tr(pt[:, q * 128:(q + 1) * 128], x_[:, kc * 128:(kc + 1) * 128], ident, [x_, CST], [pt])
            for q in range(4):
                kc = half * 4 + q
                if q % 2 == 0:
                    P.act(hT_[:, kc, 0:128], pt[:, q * 128:(q + 1) * 128], AF.Identity, [pt, GS, SH], [hT_],
                          bias=SH[:, j, kc:kc + 1], scale=GS[:, j, kc:kc + 1])
                else:
                    P.ts('dve', hT_[:, kc, 0:128], pt[:, q * 128:(q + 1) * 128], GS[:, j, kc:kc + 1],
                         SH[:, j, kc:kc + 1], ALU.mult, ALU.add, [pt, GS, SH], [hT_])
        yield
        P.act(junk[0:2, :], xh_[:, :], AF.Square, [xh_], [junk])
        P.red(st4[0:2, 1:2], junk[0:2, :], [junk], [st4])
        rsqrt_ln(st4, st4[0:2, 1:2], st4, st4[0:2, 1:2], 1.0 / 1024, PCOL[0:2, PC_EPS:PC_EPS + 1])
        P.ts1('dve', xh_[:, :], xh_[:, :], st4[0:2, 1:2], ALU.mult, [xh_, st4], [xh_])
        yield
        for kc in range(8):
            P.tr(pt[:, 2 * kc:2 * kc + 2], xh_[0:2, kc * 128:(kc + 1) * 128], CST[0:2, C_ID:C_ID + 2], [xh_, CST], [pt])
        for c in range(2):
            P.ts1('dve', shv[:, c, :], SH[:, j, :], HV[:, i, c:c + 1], ALU.mult, [SH, HV], [shv])
            P.tt('dve', junk[:, 8 * c:8 * c + 8], pt[:, c:16:2], GS[:, j, :], ALU.mult, [pt, GS], [junk])
            P.tt('dve', hT_[:, :, 128 + c], junk[:, 8 * c:8 * c + 8], shv[:, c, :], ALU.add, [junk, shv], [hT_])

        yield
        proj = proj_for(hT_, PCNT)

        for jc in range(14):
            if kind == 'S' and jc < 4:
                continue
            pb_ = proj(jc * 128, 128, 130)
            ue_ = ue[jc % 2]
            P.cp('act', ue_[:, :], pb_[:, 0:130], [pb_], [ue_])
            yield
            if jc < 12:
                dstb, dst = RKV, RKV[:, jc, :]
            else:
                dstb, dst = WA, WA[:, jc - 12, :]
            eng = 'pool' if jc % 2 == 0 else 'dve'
            mp_, mn_ = pc(PC_MUP + jc), pc(PC_MUN + jc)
            P.ts1(eng, dst, ue_[:, 0:128], M0[:, jc:jc + 1], ALU.mult, [ue_, M0], [dstb])
            for (dsl, ssl, mu_) in [((1, 128), (0, 127), mp_), ((0, 1), (128, 129), mp_),
                                    ((0, 127), (1, 128), mn_), ((127, 128), (129, 130), mn_)]:
                dd = dst[:, dsl[0]:dsl[1]]
                if eng == 'dve':
                    P.stt('dve', dd, ue_[:, ssl[0]:ssl[1]], mu_, dd, ALU.mult, ALU.add, [ue_, PCOL, dstb], [dstb])
                else:
                    tp = th1[:, 0:dsl[1] - dsl[0]]
                    P.ts1('pool', tp, ue_[:, ssl[0]:ssl[1]], mu_, ALU.mult, [ue_, PCOL], [th1])
                    P.tt('pool', dd, dd, tp, ALU.add, [dstb, th1], [dstb])
            yield
        yield
        r3 = RKV[:, 0:4, :]
        k3 = RKV[:, 4:8, :]
        v3 = RKV[:, 8:12, :]

        def f3(b_):
            return b_[:, :].rearrange("p (m t) -> p m t", m=4)

        for m in range(4):
            P.ts1('pool', tA[:, m * 128:(m + 1) * 128], RKV[:, 4 + m, :], pc(PC_KK + m), ALU.mult, [RKV, PCOL], [tA])
        P.act(tB[:, :], tA[:, :], AF.Square, [tA], [tB])
        P.mm(pz[:, :], CST[:, C_BO:C_BO + 128], tB[:, :], [CST, tB], [pz])
        yield
        P.ts1('dve', tB[:, :], pz[:, :], 1e-24, ALU.max, [pz], [tB])
        P.act(tB[:, :], tB[:, :], AF.Ln, [tB], [tB])
        P.act(tB[:, :], tB[:, :], AF.Exp, [tB], [tB], scale=-0.5)
        P.tt('dve', tA[:, :], tA[:, :], tB[:, :], ALU.mult, [tA, tB], [tA])
        yield
        P.act(th0[:, :], WA[:, 0, :], AF.Exp, [WA], [th0], scale=2.0)
        P.ts1('dve', th0[:, :], th0[:, :], 1.0, ALU.add, [th0], [th0])
        P.recip(th0[:, :], th0[:, :], [th0], [th0])
        P.ts('dve', th0[:, :], th0[:, :], SD[:, SD_M2S, i:i + 1], SD[:, SD_SELW, i:i + 1], ALU.mult, ALU.add,
             [th0, SD], [th0])
        P.mm(pz[:, :], th0[:, :], W2C[:, :], [th0, W2C], [pz], start=True, stop=False)
        P.mm(pz[:, :], CST[0:1, C_ONES:C_ONES + 128], W0S[0:1, :], [CST, W0S], [pz], start=False, stop=True)
        yield
        P.act(sg[:, :], pz[:, :], AF.Exp, [pz], [sg], scale=-1.0)
        P.ts1('dve', sg[:, :], sg[:, :], 1.0, ALU.add, [sg], [sg])
        P.recip(sg[:, :], sg[:, :], [sg], [sg])
        yield
        tri = CSEL[:, 0:512]
        for m in range(4):
            P.mm(pz[:, :], sg[:, m * 128:(m + 1) * 128], tri, [sg, CSEL], [pz])
            P.act(EALL[:, m, :, :], pz[:, :].rearrange("p (a t) -> p a t", a=4), AF.Exp, [pz], [EALL])
            yield
        P.tt('dve', GC[:, :], EALL[:, :, 2, 0], EALL[:, :, 3, 0], ALU.mult, [EALL], [GC])
        yield
        P.ts1('dve', th1[:, :], WA[:, 1, :], SD[:, SD_SELW, i:i + 1], ALU.mult, [WA, SD], [th1])
        for m in range(4):
            P.mm(pz[:, m * 128:(m + 1) * 128], A2C[:, m * 128:(m + 1) * 128], th1[:, :], [A2C, th1], [pz])
        for m in range(4):
            P.act(tB[:, m * 128:(m + 1) * 128], pz[:, m * 128:(m + 1) * 128], AF.Exp, [pz, NA0S], [tB],
                  bias=NA0S[:, m:m + 1], scale=-1.0)
        yield
        P.ts1('dve', tB[:, :], tB[:, :], 1.0, ALU.add, [tB], [tB])
        P.recip(tB[:, :], tB[:, :], [tB], [tB])
        yield
        for m in range(4):
            P.ts('pool', tC[:, m * 128:(m + 1) * 128], tB[:, m * 128:(m + 1) * 128], -1.0, pc(PC_KA + m),
                 ALU.add, ALU.mult, [tB, PCOL], [tC])
        P.stt('dve', tC[:, :], tC[:, :], 1.0, RKV[:, 4:8, :].rearrange("p m t -> p (m t)"), ALU.add, ALU.mult,
              [tC, RKV], [tC])
        P.tt('pool', tD[:, :], tA[:, :], tB[:, :], ALU.mult, [tA, tB], [tD])
        if kind != 'S':
            for m in range(4):
                P.stt('pool', tE[:, m * 128:(m + 1) * 128], RKV[:, m, :], pc(PC_RK + m),
                      tC[:, m * 128:(m + 1) * 128], ALU.mult, ALU.mult, [RKV, PCOL, tC], [tE])
            for m in range(4):
                P.mm(pz[:, 2 * m:2 * m + 2], tE[:, m * 128:(m + 1) * 128], CST[:, C_BO2:C_BO2 + 2],
                     [tE, CST], [pz])
            P.cp('act', ssum[:, :], pz[:, 0:8], [pz], [ssum])
        yield
        E = [EALL[:, :, a, :] for a in range(4)]
        A3 = ARt[:, :, :].rearrange("p m (a t) -> p m a t", a=2)
        B3 = BKt[:, :, :].rearrange("p m (a t) -> p m a t", a=2)
        P.stt('dve', A3[:, :, 0, :], f3(tA), -1.0, E[0], ALU.mult, ALU.mult, [tA, EALL], [ARt])
        if kind != 'S':
            P.tt('pool', A3[:, :, 1, :], r3, E[2], ALU.mult, [RKV, EALL], [ARt])
        yield
        P.tt('dve', B3[:, :, 0, :], f3(tD), E[1], ALU.mult, [tD, EALL], [BKt])
        P.tt('pool', B3[:, :, 1, :], f3(tC), E[1], ALU.mult, [tC, EALL], [BKt])
        yield
        P.tt('dve', f3(BhT), f3(tD), E[3], ALU.mult, [tD, EALL], [BhT])
        P.tt('pool', f3(KhT), f3(tC), E[3], ALU.mult, [tC, EALL], [KhT])
        for (srcb, srcf, dstb) in [(ARt, lambda m: ARt[:, m, 0:128], Atok), (BhT, lambda m: BhT[:, m * 128:(m + 1) * 128], Bhtok),
                                   (KhT, lambda m: KhT[:, m * 128:(m + 1) * 128], Khtok), (RKV, lambda m: RKV[:, 8 + m, :], Vtok)]:
            for m in range(4):
                P.tr(pt[:, m * 128:(m + 1) * 128], srcf(m), ident, [srcb, CST], [pt])
            P.cp('act', dstb[:, :], pt[:, :], [pt], [dstb])
            yield

        yield
        if kv is not None:
            pb2 = proj(2688, 128, 128)
            P.cp('act', mla1[:, 0, :], pb2[:, 0:128], [pb2], [mla1])
            pb2 = proj(2816, 128, 128)
            P.cp('act', mla1[:, 1, :], pb2[:, 0:128], [pb2], [mla1])
            P.act(mla2[:, 0:2, :], mla1[:, 0:2, :], AF.Square, [mla1], [mla2])
            P.mm(pz[:, 0:128], ones, mla2[:, 0, :], [CST, mla2], [pz], start=True, stop=False)
            P.mm(pz[:, 0:128], ones, mla2[:, 1, :], [CST, mla2], [pz], start=False, stop=True)
            rsqrt_ln(mla2, mla2[:, 2, :], pz, pz[:, 0:128], 1.0 / 256, pc(PC_EPS))
            for c in range(2):
                P.stt('dve', kvb[:, c, :], mla1[:, c, :], pc(PC_KVG + c), mla2[:, 2, :], ALU.mult, ALU.mult,
                      [mla1, PCOL, mla2], [kvb])
            P.dma(CKVs[:, :, kv * 128:(kv + 1) * 128], kvb[:, :, :], reads=[kvb], writes=[CKVs])
            yield
            P.dma(rpk[64:96, :, :], ropek[kv].rearrange("a d t -> d a t"), writes=[rpk])
            pk1 = proj(0, 96, 128, lhs=lambda kc: WKR[:, kc, 0, :])
            yield
            pk2 = proj(0, 96, 128, lhs=lambda kc: WKR[:, kc, 1, :])
            P.tt('dve', mla1[64:96, 2, :], pk1[64:96, 0:128], rpk[64:96, 0, :], ALU.mult, [pk1, rpk], [mla1])
            P.tt('dve', mla1[64:96, 3, :], pk2[64:96, 0:128], rpk[64:96, 1, :], ALU.mult, [pk2, rpk], [mla1])
            P.tt('dve', krb[64:96, :], mla1[64:96, 2, :], mla1[64:96, 3, :], ALU.add, [mla1], [krb])
            P.dma(KRs[:, kv * 128:(kv + 1) * 128], krb[64:96, :], reads=[krb], writes=[KRs])

        yield

    def heads(i, gen):
        kind, kv = SLOTS[i]
        S = i % 2
        ARt, BKt, Atok, Bhtok, Khtok, Vtok = ARtL[S], BKtL[S], AtokL[S], BhtokL[S], KhtokL[S], VtokL[S]
        GC, MSK, ssum = GCL[S], MSKL[S], ssumL[S]
        hT_ = hT[i % 2]
        proj = proj_for(hT_, PCNT)

        def f3(b_):
            return b_[:, :].rearrange("p (m t) -> p m t", m=4)

        def advance(n):
            if gen is None:
                return
            for _ in range(n):
                try:
                    next(gen)
                except StopIteration:
                    return
        Hin = Hc[state['h'] % 2]
        Hout = Hc[(state['h'] + 1) % 2]
        mskt = MSK[:, 0:256]
        mskn = MSK[:, 256:384]
        if kind == 'S':
            P.ts1('dve', Hin[:, :, :], Hin[:, :, :], SD[:, SD_RHO, i:i + 1], ALU.mult, [Hin, SD], [Hin])
        for h in range(8):
            m, pb = h // 2, 64 * (h % 2)
            cb = 64 * h
            at_, an_ = AT[h % 2], AN[h % 2]
            Bt_h = BKt[pb:pb + 64, m, 0:128]
            Kt_h = BKt[pb:pb + 64, m, 128:256]
            AR_h = ARt[pb:pb + 64, m, 0:256]
            own = kind != 'S'
            if own:
                P.mm(pA[:, 0:256], Bt_h, AR_h, [BKt, ARt], [pA])
                P.mm(pA[:, 256:512], Kt_h, AR_h, [BKt, ARt], [pA])
            else:
                P.mm(pA[:, 0:128], Bt_h, ARt[pb:pb + 64, m, 0:128], [BKt, ARt], [pA])
                P.mm(pA[:, 256:384], Kt_h, ARt[pb:pb + 64, m, 0:128], [BKt, ARt], [pA])
            P.mm(pB[:, 0:128], ARt[pb:pb + 64, m, 0:128], Bt_h, [ARt, BKt], [pB])
            xi = 0 if h % 2 == 0 else 3
            XP = XPB[xi]
            pt0 = PTB[0]
            P.tt('dve', pt0[:, :], pA[:, 0:128], mskt[:, 0:128], ALU.mult, [pA, MSK], [pt0])
            if own:
                P.tt('dve', at_[:, 128:256], pA[:, 128:256], mskt[:, 128:256], ALU.mult, [pA, MSK], [at_])
                P.tt('dve', at_[:, 256:512], pA[:, 256:512], mskt, ALU.mult, [pA, MSK], [at_])
            else:
                P.tt('dve', at_[:, 256:384], pA[:, 256:384], mskt[:, 0:128], ALU.mult, [pA, MSK], [at_])
            P.tt('dve', XP[:, 192:320], pB[:, 0:128], mskn, ALU.mult, [pB, MSK], [XP])
            P.cp('act', XP[:, 64:128], Atok[:, cb:cb + 64], [Atok], [XP])
            P.mm(pB[:, 128:192], at_[:, 256:384], Vtok[:, cb:cb + 64], [at_, Vtok], [pB])
            P.cp('act', XP[:, 128:192], pB[:, 128:192], [pB], [XP])
            PT_ = pt0
            advance(1)
            for lvl in range(7):
                last = lvl == 6
                XPn = XPB[(xi + 1) % 5]
                P.mm(pX[:, 0:(128 if last else 256)], PT_[:, :], XP[:, 64:(192 if last else 320)], [PT_, XP], [pX])
                if not last:
                    P.mm(pS[:, 0:128], XP[:, 192:320], PT_[:, :], [XP, PT_], [pS])
                P.tt('dve', XPn[:, 64:192], XP[:, 64:192], pX[:, 0:128], ALU.add, [XP, pX], [XPn])
                if not last:
                    P.cp('act', XPn[:, 192:320], pX[:, 128:256], [pX], [XPn])
                    ptn = PTB[(lvl + 1) % 2]
                    P.cp('act', ptn[:, :], pS[:, 0:128], [pS], [ptn])
                    PT_ = ptn
                xi += 1
                XP = XPn
                advance(1)
            X = XP
            lx = X[:, 64:128] if pb == 0 else X[:, 0:128]
            if i == CFG.get('dump_slot', 0) and h == 0:
                dump('xf', X, X[:, :])
                dump('at', at_, at_[:, :])
                dump('atok', Atok, Atok[:, :])
                dump('vtok', Vtok, Vtok[:, :])
                dump('eall', EALL, EALL[:, :, :, :].rearrange("p m a t -> p (m a t)"))
                dump('sg', sg, sg[:, :])
                dump('rkv', RKV, RKV[:, :, :].rearrange("p m t -> p (m t)"))
            P.mm(pM[0:(64 if pb == 0 else 128), 0:64], lx, Bhtok[:, cb:cb + 64], [X, Bhtok], [pM])
            P.mm(pM[:, 64:128], Bhtok[:, m * 128:(m + 1) * 128], X[:, 128:192], [Bhtok, X], [pM], start=True, stop=False)
            P.mm(pM[:, 64:128], Khtok[:, m * 128:(m + 1) * 128], Vtok[:, cb:cb + 64], [Khtok, Vtok], [pM],
                 start=False, stop=True)
            P.stt('dve', MTs[pb:pb + 64, m, :], CST[pb:pb + 64, C_ID + pb:C_ID + pb + 64], GC[pb:pb + 64, m:m + 1],
                  pM[pb:pb + 64, 0:64], ALU.mult, ALU.add, [CST, GC, pM], [MTs])
            P.cp('act', Ns[pb:pb + 64, m, :], pM[pb:pb + 64, 64:128], [pM], [Ns])
            if kind != 'S':
                gt_ = GTs[h % 2]
                P.mm(pG[0:(64 if pb == 0 else 128), 0:128], lx, at_[:, 128:256], [X, at_], [pG])
                P.tt('dve', gt_[pb:pb + 64, :], pG[pb:pb + 64, 0:128], ARt[pb:pb + 64, m, 128:256], ALU.add,
                     [pG, ARt], [gt_])
                yo = pY[:, cb:cb + 64]
                P.mm(yo, at_[:, 128:256], X[:, 128:192], [at_, X], [pY], start=True, stop=False)
                P.mm(yo, at_[:, 384:512], Vtok[:, cb:cb + 64], [at_, Vtok], [pY], start=False, stop=False)
                P.mm(yo, gt_[pb:pb + 64, :], Hin[pb:pb + 64, m, :], [gt_, Hin], [pY], start=False, stop=True)
            P.mm(pH[pb:pb + 64, m * 64:(m + 1) * 64], MTs[pb:pb + 64, m, :], Hin[pb:pb + 64, m, :], [MTs, Hin], [pH])
            advance(ADV)
        P.tt('dve', Hout[:, :, :], pH[:, :].rearrange("p (m i) -> p m i", m=4), Ns[:, :, :], ALU.add, [pH, Ns], [Hout])
        if i == CFG.get('dump_slot', 0):
            dump('hout', Hout, Hout[:, :, :].rearrange('p m i -> p (m i)'))
        state['h'] += 1
        advance(10 ** 6)
        if kind == 'S':
            P.stt('dve', HFa[:, :, :], Hout[:, :, :], SD[:, SD_SIG, i:i + 1], HFa[:, :, :], ALU.mult, ALU.add,
                  [Hout, SD, HFa], [HFa])

        if kind == 'B':
            jj = 15 - (i - 52)
            P.cp('act', ysb[:, :], pY[:, :], [pY], [ysb])
            P.dma(YBs[jj], ysb[:, :], reads=[ysb], writes=[YBs])
            P.dma(SBs[jj], ssum[:, :], reads=[ssum], writes=[SBs])
        if kind == 'F':
            jj = i - 68
            tok = slice(jj * 128, (jj + 1) * 128)
            P.dma(ybl[:, :], YBs[jj], reads=[YBs], writes=[ybl])
            P.dma(sbl[:, :], SBs[jj], reads=[SBs], writes=[sbl])
            P.tt('dve', ysb[:, :], pY[:, :], ybl[:, :], ALU.add, [pY, ybl], [ysb])
            P.tt('dve', ssum[:, :], ssum[:, :], sbl[:, :], ALU.add, [ssum, sbl], [ssum])
            y3 = ysb[:, :].rearrange("p (h i) -> p h i", h=8)
            P.red(st4[:, :], y3, [ysb], [st4])
            P.ts1('dve', st4[:, :], st4[:, :], 1.0 / 64, ALU.mult, [st4], [st4])
            P.act(tE[:, :], ysb[:, :], AF.Square, [ysb], [tE])
            P.red(sbl[:, :], tE[:, :].rearrange("p (h i) -> p h i", h=8), [tE], [sbl])
            P.tt('dve', ybl[:, 0:8], st4[:, :], st4[:, :], ALU.mult, [st4], [ybl])
            P.stt('dve', sbl[:, :], sbl[:, :], 1.0 / 64, ybl[:, 0:8], ALU.mult, ALU.subtract, [sbl, ybl], [sbl])
            rsqrt_ln(sbl, sbl[:, :], sbl, sbl[:, :], 1.0, pc(PC_LEPS))
            for h in range(8):
                P.ts('dve' if h % 2 else 'pool', ysb[:, h * 64:(h + 1) * 64], ysb[:, h * 64:(h + 1) * 64],
                     st4[:, h:h + 1], sbl[:, h:h + 1], ALU.subtract, ALU.mult, [ysb, st4, sbl], [ysb])
            P.tt('dve', ysb[:, :], ysb[:, :], LGB[:, 0:512], ALU.mult, [ysb, LGB], [ysb])
            P.tt('pool', ysb[:, :], ysb[:, :], LGB[:, 512:1024], ALU.add, [ysb, LGB], [ysb])
            for h in range(8):
                P.stt('dve' if h % 2 else 'pool', ysb[:, h * 64:(h + 1) * 64], Vtok[:, h * 64:(h + 1) * 64],
                      ssum[:, h:h + 1], ysb[:, h * 64:(h + 1) * 64], ALU.mult, ALU.add, [Vtok, ssum, ysb], [ysb])
            for m in range(4):
                pb2 = proj(1792 + m * 128, 128, 128)
                P.cp('act', tE[:, m * 128:(m + 1) * 128], pb2[:, 0:128], [pb2], [tE])
            P.act(tD[:, :], tE[:, :], AF.Exp, [tE], [tD], scale=-1.0)
            P.ts1('dve', tD[:, :], tD[:, :], 1.0, ALU.add, [tD], [tD])
            P.recip(tD[:, :], tD[:, :], [tD], [tD])
            P.tt('pool', tE[:, :], tE[:, :], tD[:, :], ALU.mult, [tE, tD], [tE])
            for m in range(4):
                P.tr(pt[:, m * 128:(m + 1) * 128], ysb[:, m * 128:(m + 1) * 128], ident, [ysb, CST], [pt])
            P.tt('dve', mixb[:, :, :], pt[:, :].rearrange("p (m t) -> p m t", m=4), f3(tE), ALU.mult, [pt, tE], [mixb])
            P.dma(MIXRs[:, :, tok], mixb[:, :, :], reads=[mixb], writes=[MIXRs])
            for c in range(3):
                pb2 = proj(2304 + c * 128, 128, 128)
                P.cp('act', mla1[:, c, :], pb2[:, 0:128], [pb2], [mla1])
            P.act(mla2[:, 0:3, :], mla1[:, 0:3, :], AF.Square, [mla1], [mla2])
            for c in range(3):
                P.mm(pz[:, 0:128], ones, mla2[:, c, :], [CST, mla2], [pz], start=(c == 0), stop=(c == 2))
            rsqrt_ln(mla2, mla2[:, 3, :], pz, pz[:, 0:128], 1.0 / 384, pc(PC_EPS))
            for c in range(3):
                P.stt('dve', cqb[:, c, :], mla1[:, c, :], pc(PC_QG + c), mla2[:, 3, :], ALU.mult, ALU.mult,
                      [mla1, PCOL, mla2], [cqb])
            P.dma(CQs[:, :, tok], cqb[:, :, :], reads=[cqb], writes=[CQs])
            for h in range(8):
                pb2 = proj(2976 + h * 64, 64, 128)
                P.cp('act', tD[0:64, 0:128], pb2[0:64, 0:128], [pb2], [tD])
                P.act(tD[0:64, 128:256], tD[0:64, 0:128], AF.Exp, [tD], [tD], scale=-1.0)
                P.ts1('dve', tD[0:64, 128:256], tD[0:64, 128:256], 1.0, ALU.add, [tD], [tD])
                P.recip(tD[0:64, 128:256], tD[0:64, 128:256], [tD], [tD])
                P.tt('dve', gmb[:, h, :], tD[0:64, 0:128], tD[0:64, 128:256], ALU.mult, [tD], [gmb])
            P.dma(GMs[:, :, tok], gmb[:, :, :], reads=[gmb], writes=[GMs])

    slot_list = list(range(NSLOT)) if CFG['slots'] is None else list(CFG['slots'])
    if slot_list:
        load_slot(slot_list[0])
        if len(slot_list) > 1:
            load_slot(slot_list[1])
        for _ in prep(slot_list[0]):
            pass
    for si, i in enumerate(slot_list):
        if si + 2 < len(slot_list):
            load_slot(slot_list[si + 2])
        if i == 68:
            P.cp('dve', Hc[state['h'] % 2][:, :, :], HFa[:, :, :], [HFa], [Hc[state['h'] % 2]])
        heads(i, prep(slot_list[si + 1]) if si + 1 < len(slot_list) else None)

    AR = ARENA.t
    o = 0

    def carve(n, shape_str=None, **kw):
        nonlocal o
        ap = AR[:, o:o + n]
        o += n
        if shape_str:
            ap = ap.rearrange(shape_str, **kw)
        return P.view(ap)

    ps2 = [pA, pY]

    def phase2():
        nonlocal o
        P.barrier()
        o = 0

        KT = [carve(NKEY)] * 2
        V4 = carve(NKT * 4 * 65, "p (k h c) -> p k h c", k=NKT, h=4)
        QT = [carve(2048)] * 2
        def bview(buf, nbf, pat=None, rows=128, **kw):
            t = buf.t
            flat = t[0:rows]
            nd = len(t.shape)
            if nd == 3:
                flat = t[0:rows, :, :].rearrange("p a b -> p (a b)")
            elif nd == 4:
                flat = t[0:rows, :, :, :].rearrange("p a b c -> p (a b c)")
            ap = flat[:, 0:nbf // 2].bitcast(BF16)
            if pat:
                ap = ap.rearrange(pat, **kw)
            return P.view(ap)
        WUQ = bview(RKV, 2304, "p (k c) -> p k c", k=3)
        WUQR = bview(ybl, 288, "p (k c) -> p k c", k=3)
        WROT = bview(ysb, 768, "p (k h c) -> p k h c", k=3, h=8)
        WUK = bview(AtokL[0], 1024, "p (k c) -> p k c", k=2)
        WUV = bview(BhtokL[0], 1024, "p (k c) -> p k c", k=2)
        ckb = [bview(KhtokL[0], 1024, "p (k c) -> p k c", k=2), bview(VtokL[0], 1024, "p (k c) -> p k c", k=2)]
        PTb_ = [bview(AtokL[1], 512), bview(BhtokL[1], 512), bview(KhtokL[1], 512)]
        rq = P.view(ARt.t[:, :, :].rearrange('p m c -> p (m c)').rearrange('p (a t) -> p a t', a=2))
        onrm = P.view(tA.t[0:64, :])
        rec = P.view(tB.t[0:65, :])
        bcs = P.view(tC.t[0:64, :])
        mxb = bview(XPB[0], 512, rows=64)
        gml = bview(XPB[1], 512, rows=64)
        cql = bview(junk, 1536, "p (k c) -> p k c", k=3)
        w_uq_v = w_uq_d.rearrange("(k p) c -> p k c", p=128)
        w_uqr_v = w_uq_rot_d.rearrange("(k p) h c -> p k h c", p=128)
        for kc in range(3):
            P.dma(STG[:, 0:768], w_uq_v[:, kc, :], writes=[STG])
            P.cp('dve', WUQ[:, kc, :], STG[:, 0:768], [STG], [WUQ])
        P.memset('pool', WUQR[:, :, :], 0.0, [WUQR])
        P.dma(STG[:, 0:768].rearrange("p (k h c) -> p k h c", k=3, h=8), w_uqr_v, writes=[STG])
        for kc in range(3):
            for h in range(8):
                c0 = kc * 256 + h * 32
                P.tt('dve', WROT[:, kc, h, :], STG[:, c0:c0 + 32], CST[:, C_SGN:C_SGN + 32], ALU.mult, [STG, CST], [WROT])
        for kc_ in range(2):
            P.dma(STG[:, 0:512], w_uk_d.rearrange("(k p) h c -> p k (h c)", p=128)[:, kc_, :], writes=[STG])
            P.cp('dve', WUK[:, kc_, :], STG[:, 0:512], [STG], [WUK])
        P.dma(tA[:, :].rearrange("p (k c) -> p k c", k=2)[:, :, 0:256], w_uv_d.rearrange("(k p) h c -> p k (h c)", p=128)[:, :, 0:256], writes=[tA])
        P.dma(tB[:, :].rearrange("p (k c) -> p k c", k=2)[:, :, 0:256], w_uv_d.rearrange("(k p) h c -> p k (h c)", p=128)[:, :, 256:512], writes=[tB])
        P.cp('dve', WUV[:, :, 0:256], tA[:, :].rearrange("p (k c) -> p k c", k=2), [tA], [WUV])
        P.cp('dve', WUV[:, :, 256:512], tB[:, :].rearrange("p (k c) -> p k c", k=2), [tB], [WUV])

        po = [pz, pBXf]
        KB = [(kb * 512, 512) for kb in range(NKEY // 512)]
        for hg in range(2):
            for hh_ in range(4):
                P.cp('dve', V4[:, :, hh_, 64], KVV[:, :], [KVV], [V4])
            for bi, (k0, kw) in enumerate(KB):
                cb_ = ckb[bi % 2]
                P.dma(cb_[:, :, 0:kw], CKVs[:, :, k0:k0 + kw], reads=[CKVs], writes=[cb_])
                for s in range(kw // 128):
                    kt = k0 // 128 + s
                    pv = pu[kt % 2]
                    for c in range(2):
                        P.mm(pv[:, 0:256], cb_[:, c, s * 128:(s + 1) * 128], WUV[:, c, hg * 256:(hg + 1) * 256], [cb_, WUV], [pv],
                             start=(c == 0), stop=(c == 1))
                    P.ts1('dve', V4[:, kt, :, 0:64], pv[:, 0:256].rearrange("p (h c) -> p h c", h=4), KVV[:, kt:kt + 1], ALU.mult, [pv, KVV], [V4])
            for hh in range(4):
                h = hg * 4 + hh
                kt_, qt_ = KT[h % 2], QT[h % 2]
                P.dma(kt_[64:96, :], KRs[:, :], reads=[KRs], writes=[kt_])
                for bi, (k0, kw) in enumerate(KB):
                    cb_ = ckb[bi % 2]
                    P.dma(cb_[:, :, 0:kw], CKVs[:, :, k0:k0 + kw], reads=[CKVs], writes=[cb_])
                    for c in range(2):
                        P.mm(ps2[bi % 2][0:64, 0:kw], WUK[:, c, h * 64:(h + 1) * 64], cb_[:, c, 0:kw],
                             [cb_, WUK], [ps2[bi % 2]], start=(c == 0), stop=(c == 1))
                    P.cp('act' if bi % 2 else 'dve', kt_[0:64, k0:k0 + kw], ps2[bi % 2][0:64, 0:kw], [ps2[bi % 2]], [kt_])
                for kc in range(3):
                    P.cp('pool', WUQR[:, kc, 64:96], WROT[:, kc, h, :], [WROT], [WUQR])
                for qb in range(4):
                    qs = slice(qb * 512, (qb + 1) * 512)
                    P.dma(cql[:, :, :], CQs[:, :, qs], reads=[CQs], writes=[cql])
                    P.dma(rq[64:96, :, :], ropeq[:, :, qs].rearrange("a d t -> d a t"), writes=[rq])
                    for c in range(3):
                        P.mm(po[0][0:96, :], WUQ[:, c, h * 96:(h + 1) * 96], cql[:, c, :], [WUQ, cql], [po[0]],
                             start=(c == 0), stop=(c == 2))
                    for c in range(3):
                        P.mm(po[1][0:96, :], WUQR[:, c, :], cql[:, c, :], [WUQR, cql], [po[1]], start=(c == 0), stop=(c == 2))
                    P.cp('act', qt_[0:64, qs], po[0][0:64, :], [po[0]], [qt_])
                    P.tt('dve', rq[64:96, 0, :], po[0][64:96, :], rq[64:96, 0, :], ALU.mult, [po[0], rq], [rq])
                    P.tt('dve', rq[64:96, 1, :], po[1][64:96, :], rq[64:96, 1, :], ALU.mult, [po[1], rq], [rq])
                    P.tt('dve', qt_[64:96, qs], rq[64:96, 0, :], rq[64:96, 1, :], ALU.add, [rq], [qt_])
                for qb in range(4):
                    qs = slice(qb * 512, (qb + 1) * 512)
                    pacc = po[qb % 2]
                    P.dma(gml[:, :], GMs[:, h, qs], reads=[GMs], writes=[gml])
                    for kt in range(NKT):
                        sc = ps2[kt % 2]
                        P.mm(sc[:, :], kt_[0:96, kt * 128:(kt + 1) * 128], qt_[0:96, qs], [kt_, qt_], [sc])
                        pT = PTb_[kt % 3]
                        P.act(pT[:, :], sc[:, :], AF.Exp, [sc], [pT], scale=ATTN_SCALE)
                        P.mm(pacc[0:65, :], V4[:, kt, hh, :], pT[:, :], [V4, pT], [pacc], start=(kt == 0), stop=(kt == NKT - 1))
                    P.recip(rec[64:65, :], pacc[64:65, :], [pacc], [rec])
                    P.mm(pH[0:64, 0:256], CST[64:65, C_ONES:C_ONES + 64], rec[64:65, 0:256], [CST, rec], [pH])
                    P.mm(pG[0:64, 0:256], CST[64:65, C_ONES:C_ONES + 64], rec[64:65, 256:512], [CST, rec], [pG])
                    P.cp('act', bcs[:, 0:256], pH[0:64, 0:256], [pH], [bcs])
                    P.cp('act', bcs[:, 256:512], pG[0:64, 0:256], [pG], [bcs])
                    P.tt('dve', onrm[:, :], pacc[0:64, :], bcs[:, :], ALU.mult, [pacc, bcs], [onrm])
                    P.tt('dve', mxb[:, :], onrm[:, :], gml[:, :], ALU.mult, [onrm, gml], [mxb])
                    P.dma(MIXMs[:, h, qs], mxb[:, :], reads=[mxb], writes=[MIXMs])

    def phase3():
        nonlocal o
        P.barrier()
        o = 0
        WOR = carve(4 * 1024, "p (k c) -> p k c", k=4)
        WOM = carve(8 * 1024, "p (k c) -> p k c", k=8)
        mr = [carve(512, "p (k c) -> p k c", k=4), carve(512, "p (k c) -> p k c", k=4)]
        mm_ = [carve(1024, "p (k c) -> p k c", k=8), carve(1024, "p (k c) -> p k c", k=8)]
        w_out_r = w_out_d[0:512, :].rearrange("(k p) c -> p k c", p=128)
        w_out_m = w_out_d[512:1024, :].rearrange("(k p) c -> p k c", p=64)
        for k in range(4):
            P.dma(STG[:, 0:1024], w_out_r[:, k, :], writes=[STG])
            P.cp('dve', WOR[:, k, :], STG[:, 0:1024], [STG], [WOR])
        for k in range(8):
            P.dma(STG[0:64, 0:1024], w_out_m[:, k, :], writes=[STG])
            P.cp('dve', WOM[0:64, k, :], STG[0:64, 0:1024], [STG], [WOM])
        GB = P.view(BKh.t[:, :])
        FG = P.view(BKt.t[:, :, :].rearrange('p m c -> p (m c)'))
        for half in range(2):
            for q in range(4):
                kc = half * 4 + q
                P.ts1('dve', th0[:, :], ident, MODT[:, 16 + kc, 0:1], ALU.mult, [CST, MODT], [th0])
                P.mm(pt[:, q * 128:(q + 1) * 128], ones, th0[:, :], [CST, th0], [pt])
            P.cp('act', GB[:, half * 512:(half + 1) * 512], pt[:, :], [pt], [GB])
        P.dma(FG[:, :], brow_d[:, 1024:2048], writes=[FG])
        yo_ = [P.view(RKV.t[:, 0:8, :].rearrange('p m t -> p (m t)')), P.view(EALL.t[:, 0:2, :, :].rearrange('p m a t -> p (m a t)'))]
        for jj in range(16):
            tok = slice(jj * 128, (jj + 1) * 128)
            x_ = xt[jj % 2]
            P.dma(x_[:, :], xs[68 + jj, 0:128, :], writes=[x_])
            P.dma(mr[jj % 2][:, :, :], MIXRs[:, :, tok], reads=[MIXRs], writes=[mr[jj % 2]])
            P.dma(mm_[jj % 2][0:64, :, :], MIXMs[:, :, tok], reads=[MIXMs], writes=[mm_[jj % 2]])
            y_ = yo_[jj % 2]
            for half in range(2):
                cs = slice(half * 512, (half + 1) * 512)
                pp = ps2[half]
                for k in range(4):
                    P.mm(pp[:, :], mr[jj % 2][:, k, :], WOR[:, k, cs], [mr[jj % 2], WOR], [pp], start=(k == 0), stop=False)
                for k in range(8):
                    P.mm(pp[:, :], mm_[jj % 2][0:64, k, :], WOM[0:64, k, cs], [mm_[jj % 2], WOM], [pp], start=False, stop=(k == 7))
                P.tt('dve', y_[:, cs], pp[:, :], GB[:, cs], ALU.mult, [pp, GB], [y_])
            P.tt('pool', y_[:, :], y_[:, :], x_[:, :], ALU.add, [y_, x_], [y_])
            P.act(junk[:, :], y_[:, :], AF.Square, [y_], [junk])
            P.red(st4[:, 0:1], junk[:, :], [junk], [st4])
            rsqrt_ln(st4, st4[:, 0:1], st4, st4[:, 0:1], 1.0 / 1024, pc(PC_EPS))
            P.stt('dve', y_[:, :], y_[:, :], st4[:, 0:1], FG[:, :], ALU.mult, ALU.mult, [y_, st4, FG], [y_])
            P.dma(out_d[tok, :], y_[:, :], reads=[y_])
    if CFG['p2']:
        phase2()
    if CFG['p3']:
        phase3()
    P.finish()
    return nc


DEBUG_SPECS = {'xf': (128, 192), 'at': (128, 512), 'atok': (128, 512), 'vtok': (128, 512), 'eall': (128, 2048), 'sg': (128, 512), 'rkv': (128, 1536), 'hout': (128, 256)}
_CACHE = {}


def kernel(**inputs):
    maps = host_prep(inputs)
    if 'nc' not in _CACHE:
        _CACHE['nc'] = build()
    nc = _CACHE['nc']
    res = run_bass_kernel_spmd(nc, maps, core_ids=list(range(8)))
    out = np.zeros((2, 8192, 1024), np.float32)
    for c in range(8):
        b, p = c // 4, c % 4
        out[b, 2048 * p:2048 * (p + 1)] = np.asarray(res.results[c]['out'], np.float32)
    if DEBUG:
        _CACHE['res'] = res
    return out
```

```python
import numpy as np
from contextlib import ExitStack
import concourse.bass as bass
import concourse.mybir as mybir
from concourse.bass_utils import run_bass_kernel_spmd

F32 = mybir.dt.float32
BF16 = mybir.dt.bfloat16
ALU = mybir.AluOpType
AF = mybir.ActivationFunctionType
AX = mybir.AxisListType
BANK = 30000
NDMA = 24
ENGS = ['pe', 'act', 'dve', 'pool', 'sp']
DEBUG = False
CFG = {'slots': None, 'p2': True, 'p3': True, 'stage': 99}

D_MODEL = 1024
D_IN = 3488
NSTATE = 52
NSLOT = 84
NKT = 68
NKEY = NKT * 128
KAPPA = -float(np.exp(-0.5))
ATTN_SCALE = 96 ** -0.5
NORM_EPS = 1e-6
LNX_EPS = 64e-5


class Trk:
    __slots__ = ('w', 'r')

    def __init__(self):
        self.w = None
        self.r = {}


class Buf:
    def __init__(self, t, ap=None):
        self.t = t
        self.k = Trk()
        self._ap = ap

    def __getitem__(self, idx):
        return (self._ap if self._ap is not None else self.t)[idx]

    @property
    def ap(self):
        return self._ap if self._ap is not None else self.t[:]


class Prog:
    def __init__(self, nc):
        self.nc = nc
        self.es = ExitStack()
        self.ops = {e: [] for e in ENGS}
        self.cnt = {e: 0 for e in ENGS}
        self.seen = {e: {} for e in ENGS}
        self.sems = {}
        self.dma_tot = [0] * NDMA
        self.dma_next = 0

    def sb(self, name, shape, dt=F32):
        return Buf(self.es.enter_context(self.nc.sbuf_tensor('sb_' + name, list(shape), dt)))

    def ps(self, name, shape, dt=F32):
        return Buf(self.es.enter_context(self.nc.psum_tensor('ps_' + name, list(shape), dt)))

    def view(self, ap):
        return Buf(None, ap)

    def _deps(self, eng, reads, writes):
        deps = {}

        def add(ev):
            if ev is None:
                return
            k, v = ev
            if deps.get(k, 0) < v:
                deps[k] = v
        for b in reads:
            add(b.k.w)
        for b in writes:
            add(b.k.w)
            for k, v in b.k.r.items():
                add((k, v))
        waits = []
        for k, v in deps.items():
            if k[0] == 'pe' and eng == 'pe':
                continue
            if self.seen[eng].get(k, 0) >= v:
                continue
            self.seen[eng][k] = v
            waits.append((k, v))
        return waits

    def _mark(self, ev, reads, writes):
        k, v = ev
        for b in writes:
            b.k.w = ev
            b.k.r = {}
        for b in reads:
            if b in writes:
                continue
            if b.k.r.get(k, 0) < v:
                b.k.r[k] = v

    def op(self, eng, fn, reads=(), writes=()):
        waits = self._deps(eng, reads, writes)
        i = self.cnt[eng]
        self.cnt[eng] += 1
        ev = ((eng, i // BANK), i % BANK + 1)
        self._mark(ev, reads, writes)
        self.ops[eng].append((waits, fn, ev[0], 1))

    def dma(self, out, in_, reads=(), writes=(), eng='sp', **kw):
        waits = self._deps(eng, reads, writes)
        s = self.dma_next % NDMA
        self.dma_next += 1
        k = ('dma', s)
        if self.dma_tot[s] > 0 and self.seen[eng].get(k, 0) < self.dma_tot[s]:
            self.seen[eng][k] = self.dma_tot[s]
            waits.append((k, self.dma_tot[s]))
        self.dma_tot[s] += 16
        ev = (k, self.dma_tot[s])
        self._mark(ev, reads, writes)
        self.ops[eng].append((waits, lambda e: e.dma_start(out=out, in_=in_, **kw), k, 16))

    def barrier(self):
        last = {}
        for e in ENGS:
            i = self.cnt[e]
            if i > 0:
                last[(e, (i - 1) // BANK)] = (i - 1) % BANK + 1
        for s in range(NDMA):
            if self.dma_tot[s] > 0:
                last[('dma', s)] = self.dma_tot[s]
        for e in ENGS:
            waits = []
            for k, v in last.items():
                if k[0] == e:
                    continue
                if self.seen[e].get(k, 0) >= v:
                    continue
                self.seen[e][k] = v
                waits.append((k, v))
            if waits:
                self.ops[e].append((waits, None, None, 0))

    def mm(self, out, lhsT, rhs, rd, wr, start=True, stop=True):
        self.op('pe', lambda e: e.matmul(out, lhsT, rhs, start=start, stop=stop), rd, wr)

    def tr(self, out, in_, ident, rd, wr):
        self.op('pe', lambda e: e.transpose(out, in_, ident), rd, wr)

    def act(self, out, in_, func, rd, wr, bias=None, scale=None):
        kw = {}
        if bias is not None:
            kw['bias'] = bias
        if scale is not None:
            kw['scale'] = scale
        self.op('act', lambda e: e.activation(out, in_, func, **kw), rd, wr)

    def tt(self, eng, out, a, b, op, rd, wr):
        self.op(eng, lambda e: e.tensor_tensor(out, a, b, op), rd, wr)

    def ts(self, eng, out, a, s1, s2, op0, op1, rd, wr):
        self.op(eng, lambda e: e.tensor_scalar(out, a, s1, s2, op0, op1), rd, wr)

    def ts1(self, eng, out, a, s, op, rd, wr):
        self.op(eng, lambda e: e.tensor_single_scalar(out, a, s, op), rd, wr)

    def stt(self, eng, out, a, s, b, op0, op1, rd, wr):
        self.op('dve', lambda e: e.scalar_tensor_tensor(out, a, s, b, op0, op1), rd, wr)

    def cp(self, eng, out, in_, rd, wr):
        if eng == 'act':
            self.op('act', lambda e: e.copy(out, in_), rd, wr)
        else:
            self.op(eng, lambda e: e.tensor_copy(out, in_), rd, wr)

    def recip(self, out, in_, rd, wr):
        self.op('dve', lambda e: e.reciprocal(out, in_), rd, wr)

    def red(self, out, in_, rd, wr, op=ALU.add):
        self.op('dve', lambda e: e.tensor_reduce(out, in_, AX.X, op), rd, wr)

    def memset(self, eng, ap, val, wr):
        self.op(eng, lambda e: e.memset(ap, val), (), wr)

    def finish(self):
        nc = self.nc
        keys = set()
        for e in ENGS:
            for waits, fn, k, inc in self.ops[e]:
                if k is not None:
                    keys.add(k)
                for wk, _ in waits:
                    keys.add(wk)
        for k in sorted(keys, key=str):
            self.sems[k] = self.es.enter_context(nc.semaphore("s_%s_%s" % k))
        fin = [(('dma', s), self.dma_tot[s]) for s in range(NDMA) if self.dma_tot[s] > 0]
        block = self.es.enter_context(nc.Block())

        def run(e, name):
            for waits, fn, k, inc in self.ops[name]:
                for wk, wv in waits:
                    e.wait_ge(self.sems[wk], wv)
                if fn is not None:
                    fn(e).then_inc(self.sems[k], inc)
            if name == 'sp':
                for wk, wv in fin:
                    e.wait_ge(self.sems[wk], wv)

        @block.tensor
        def _(e):
            run(e, 'pe')

        @block.scalar
        def _(e):
            run(e, 'act')

        @block.vector
        def _(e):
            run(e, 'dve')

        @block.gpsimd
        def _(e):
            run(e, 'pool')

        @block.sync
        def _(e):
            run(e, 'sp')
        self.es.close()


PC_NG, PC_ADAB, PC_MUP, PC_MUN, PC_KK, PC_KA, PC_RK, PC_A0, PC_QG, PC_KVG = 0, 8, 32, 46, 60, 64, 68, 72, 80, 83
PC_EPS, PC_LEPS, PC_ONE, PC_TINY = 85, 86, 87, 88
NPC = 96
C_ID, C_ONES, C_BO, C_F, C_B, C_TOT, C_BO2, C_SGN = 0, 128, 256, 384, 1280, 2176, 2177, 2179
SD_F, SD_SELW, SD_M2S, SD_PHI, SD_RHO, SD_SIG, SD_NF = 0, 1, 2, 3, 4, 5, 6
NCST = 2211

SLOTS = []
for _i in range(52):
    SLOTS.append(('S', _i))
for _j in range(16):
    SLOTS.append(('B', None))
for _j in range(16):
    SLOTS.append(('F', 52 + _j))


def _consts():
    c = np.zeros((128, NCST), np.float32)
    idx = np.arange(128)
    row, col = idx[:, None], idx[None, :]
    c[:, C_ID:C_ID + 128] = np.eye(128)
    c[:, C_ONES:C_ONES + 128] = 1.0
    c[:, C_BO:C_BO + 128] = ((row // 64) == (col // 64))
    k = KAPPA
    tf = [k * (row < col), -k * (row <= col), k * (row <= col), k * (row > col)]
    tb = [k * (row > col), -k * (row >= col), k * (row >= col), k * (row < col)]
    c[:, C_F:C_F + 896] = np.concatenate(tf + [col > row, col >= row, col < row], 1)
    c[:, C_B:C_B + 896] = np.concatenate(tb + [col < row, col <= row, col > row], 1)
    c[:, C_TOT] = k
    c[:, C_BO2] = (idx < 64)
    c[:, C_BO2 + 1] = (idx >= 64)
    d = np.arange(32)
    c[:, C_SGN:C_SGN + 32] = np.where(d % 16 < 8, -1.0, 1.0)[None, :]
    return c


ROT_PERM = np.array([dd + 8 if dd % 16 < 8 else dd - 8 for dd in range(32)])


def _rope_tables(pos):
    pos = np.asarray(pos)
    inv = (np.float32(10000.0) ** (-np.arange(0, 16, 2, dtype=np.float32) / np.float32(16))).astype(np.float32)
    rowf = (pos // 64).astype(np.float32)
    colf = (pos % 64).astype(np.float32)
    ar = (rowf[:, None] * inv[None, :]).astype(np.float32)
    ac = (colf[:, None] * inv[None, :]).astype(np.float32)
    cos = np.concatenate([np.cos(ar), np.cos(ar), np.cos(ac), np.cos(ac)], 1).astype(np.float32)
    sin = np.concatenate([np.sin(ar), np.sin(ar), np.sin(ac), np.sin(ac)], 1).astype(np.float32)
    return cos.T.copy(), sin.T.copy()


def _tile_halo(seq, c):
    n = seq.shape[0]
    t = np.zeros((130, seq.shape[1]), np.float32)
    t[:128] = seq[c * 128:(c + 1) * 128]
    hv = [0.0, 0.0]
    if c > 0:
        t[128] = seq[c * 128 - 1]
        hv[0] = 1.0
    if (c + 1) * 128 < n:
        t[129] = seq[(c + 1) * 128]
        hv[1] = 1.0
    return t, hv


def host_prep(inp):
    g = {k: np.asarray(v, np.float32) for k, v in inp.items()}
    sh = {}
    w_in = g['w_in'][0]
    sh['w_in'] = w_in
    sh['wkr_rot'] = np.ascontiguousarray(w_in[:, 2944:2976][:, ROT_PERM])
    w_uq = g['mla_w_uq'][0]
    sh['w_uq'] = w_uq
    sh['w_uq_rot'] = np.ascontiguousarray(
        np.stack([w_uq[:, h * 96 + 64:h * 96 + 96][:, ROT_PERM] for h in range(8)], 1))
    w_ukv = g['mla_w_ukv'][0].reshape(256, 8, 128)
    sh['w_uk'] = np.ascontiguousarray(w_ukv[:, :, :64])
    sh['w_uv'] = np.ascontiguousarray(w_ukv[:, :, 64:])
    sh['w_out'] = g['w_out'][0]
    sh['ada_w'] = g['ada_w'][0]
    sh['w2cat'] = np.ascontiguousarray(g['rw_w2'][0].reshape(128, 512))
    sh['a2cat'] = np.ascontiguousarray(g['rw_a2'][0].reshape(128, 512))
    sh['w0r'] = np.ascontiguousarray(g['rw_w0'][0].reshape(1, 1024))
    pc = np.zeros((128, NPC), np.float32)
    pc[:, PC_NG:PC_NG + 8] = g['norm_g'][0].reshape(8, 128).T
    pc[:, PC_ADAB:PC_ADAB + 24] = g['ada_b'][0].reshape(24, 128).T
    pc[:, PC_MUP:PC_MUP + 14] = g['shift_mu'][0, 0].reshape(14, 128).T
    pc[:, PC_MUN:PC_MUN + 14] = g['shift_mu'][0, 1].reshape(14, 128).T
    pc[:, PC_KK:PC_KK + 4] = g['rw_kk'][0].reshape(4, 128).T
    pc[:, PC_KA:PC_KA + 4] = g['rw_ka'][0].reshape(4, 128).T
    pc[:, PC_RK:PC_RK + 4] = g['rw_rk'][0].reshape(4, 128).T
    pc[:, PC_A0:PC_A0 + 4] = g['rw_a0'][0, 0].reshape(4, 128).T
    pc[:, PC_A0 + 4:PC_A0 + 8] = g['rw_a0'][0, 1].reshape(4, 128).T
    pc[:, PC_QG:PC_QG + 3] = g['mla_q_norm_g'][0].reshape(3, 128).T
    pc[:, PC_KVG:PC_KVG + 2] = g['mla_kv_norm_g'][0].reshape(2, 128).T
    pc[:, PC_EPS] = NORM_EPS
    pc[:, PC_LEPS] = LNX_EPS
    pc[:, PC_ONE] = 1.0
    pc[:, PC_TINY] = 1e-24
    sh['pcol'] = pc
    br = np.zeros((128, 2048), np.float32)
    br[:, 0:512] = g['rw_lnx_g'][0][None, :]
    br[:, 512:1024] = g['rw_lnx_b'][0][None, :]
    br[:, 1024:2048] = g['final_g'][None, :]
    sh['brow'] = br
    sh['cst'] = _consts()
    maps = []
    for b in range(2):
        x, ctx = g['x'][b], g['ctx'][b]
        cvec = np.stack([g['c'][b].reshape(128, 8), g['c_ctx'].reshape(128, 8)], -1)
        for p in range(4):
            nf = 2 + 16 * p
            fsrc = [(ctx, 0, True), (ctx, 1, True)] + [(x, c, False) for c in range(16 * p)]
            bsrc = [(ctx, 1, True), (ctx, 0, True)] + [(x, c, False) for c in range(63, 16 * p + 15, -1)]
            own = [(x, 16 * p + j, False) for j in range(16)]
            allsrc = fsrc + bsrc + own[::-1] + own
            assert len(allsrc) == NSLOT
            tiles, hvs = [], []
            for seq, c, _ in allsrc:
                t, hv = _tile_halo(seq, c)
                tiles.append(t)
                hvs.append(hv)
            xs = np.stack(tiles, 0)
            hvs = np.array(hvs, np.float32)
            sd = np.zeros((128, 7, NSLOT), np.float32)
            f = np.array([1.0] * nf + [0.0] * (52 - nf) + [0.0] * 16 + [1.0] * 16, np.float32)
            sd[:, SD_F, :] = f[None]
            sd[:, SD_NF, :] = 1.0 - f[None]
            sd[:64, SD_SELW, :] = f[None]
            sd[64:, SD_SELW, :] = 1.0 - f[None]
            sd[:, SD_M2S, :] = -2.0 * sd[:, SD_SELW, :]
            sd[:, SD_PHI, :] = np.array([1.0 if ic else 0.0 for _, _, ic in allsrc], np.float32)[None]
            rho = np.ones(NSLOT, np.float32)
            rho[nf] = 0.0
            sd[:, SD_RHO, :] = rho[None]
            sgm = np.zeros(NSLOT, np.float32)
            sgm[nf - 1] = 1.0
            sd[:, SD_SIG, :] = sgm[None]
            kvsrc = fsrc + bsrc + own
            rk = np.zeros((NKT, 2, 32, 128), np.float32)
            rk[:, 0] = 1.0
            kvv = np.ones(NKT, np.float32)
            kvv[nf] = 0.0
            kvv[nf + 1] = 0.0
            for kt, (seq, c, ic) in enumerate(kvsrc):
                if not ic:
                    cs, sn = _rope_tables(np.arange(c * 128, (c + 1) * 128))
                    rk[kt, 0], rk[kt, 1] = cs, sn
            cq, sq = _rope_tables(np.arange(2048 * p, 2048 * (p + 1)))
            m = dict(sh)
            m['xs'] = xs
            m['hv'] = np.ascontiguousarray(np.broadcast_to(hvs[None], (128, NSLOT, 2)))
            m['sd'] = sd
            m['kvv'] = np.ascontiguousarray(np.broadcast_to(kvv[None], (128, NKT)))
            m['ropek'] = rk
            m['ropeq'] = np.stack([cq, sq], 0)
            m['cvec'] = cvec
            maps.append(m)
    return maps


def build():
    nc = bass.Bass("TRN2", target_bir_lowering=False)
    P = Prog(nc)

    def din(name, shape):
        return nc.dram_tensor(name, list(shape), F32, kind="ExternalInput").ap()

    xs = din('xs', [NSLOT, 130, 1024])
    hv_d = din('hv', [128, NSLOT, 2])
    sd_d = din('sd', [128, 7, NSLOT])
    kvv_d = din('kvv', [128, NKT])
    ropek = din('ropek', [NKT, 2, 32, 128])
    ropeq = din('ropeq', [2, 32, 2048])
    cvec_d = din('cvec', [128, 8, 2])
    w_in_d = din('w_in', [1024, D_IN])
    wkr_rot_d = din('wkr_rot', [1024, 32])
    w_uq_d = din('w_uq', [384, 768])
    w_uq_rot_d = din('w_uq_rot', [384, 8, 32])
    w_uk_d = din('w_uk', [256, 8, 64])
    w_uv_d = din('w_uv', [256, 8, 64])
    w_out_d = din('w_out', [1024, 1024])
    ada_w_d = din('ada_w', [1024, 3072])
    w2cat_d = din('w2cat', [128, 512])
    a2cat_d = din('a2cat', [128, 512])
    w0r_d = din('w0r', [1, 1024])
    pcol_d = din('pcol', [128, NPC])
    brow_d = din('brow', [128, 2048])
    cst_d = din('cst', [128, NCST])
    out_d = nc.dram_tensor('out', [2048, 1024], F32, kind="ExternalOutput").ap()
    dbg = {}
    if DEBUG:
        for nm, shp in DEBUG_SPECS.items():
            dbg[nm] = nc.dram_tensor('dbg_' + nm, list(shp), F32, kind="ExternalOutput").ap()

    def dscr(name, shape, dt):
        t = nc.dram_tensor(name, list(shape), dt)
        return Buf(t, t.ap())

    CKVs = dscr('s_ckv', [128, 2, NKEY], BF16)
    KRs = dscr('s_kr', [32, NKEY], BF16)
    CQs = dscr('s_cq', [128, 3, 2048], BF16)
    GMs = dscr('s_gm', [64, 8, 2048], BF16)
    MIXRs = dscr('s_mixr', [128, 4, 2048], BF16)
    MIXMs = dscr('s_mixm', [64, 8, 2048], BF16)
    YBs = dscr('s_yb', [16, 128, 512], F32)
    SBs = dscr('s_sb', [16, 128, 8], F32)

    CST = P.sb('cst', [128, NCST])
    PCOL = P.sb('pcol', [128, NPC])
    LGB = P.sb('lgb', [128, 1024])
    HV = P.sb('hv', [128, NSLOT, 2])
    SD = P.sb('sd', [128, 7, NSLOT])
    KVV = P.sb('kvv', [128, NKT])
    CSEL = P.sb('csel', [128, 512])
    GSL = P.sb('gsl', [128, 4, 8])
    W0S = P.sb('w0s', [1, 512])
    NA0S = P.sb('na0s', [128, 4])
    W2C = P.sb('w2c', [128, 512])
    A2C = P.sb('a2c', [128, 512])
    W0R = P.sb('w0r', [1, 1024])
    MODT = P.sb('modt', [128, 24, 2])
    GS = P.sb('gs', [128, 2, 8])
    SH = P.sb('shc', [128, 2, 8])
    M0 = P.sb('m0', [128, 14])
    NA0 = P.sb('na0', [128, 8])
    ARENA = P.sb('arena', [128, 28432], BF16)
    WB = P.view(ARENA.t[:, 0:27904].rearrange("p (k c) -> p k c", k=8))
    WKR = P.sb('wkr', [128, 8, 2, 96], BF16)

    for (b_, d_) in [(CST, cst_d), (PCOL, pcol_d), (HV, hv_d), (SD, sd_d), (KVV, kvv_d), (W2C, w2cat_d), (A2C, a2cat_d),
                     (W0R, w0r_d)]:
        P.dma(b_.ap, d_, writes=[b_])
    P.dma(LGB[:, :], brow_d[:, 0:1024], writes=[LGB])
    ident = CST[:, C_ID:C_ID + 128]
    ones = CST[:, C_ONES:C_ONES + 128]

    def pc(c0, n=1):
        return PCOL[:, c0:c0 + n]

    def ps_pair(name):
        q = P.ps(name, [128, 512])
        v0, v1 = P.view(q.t[:, 0:256]), P.view(q.t[:, 256:512])
        v0.k = q.k
        v1.k = q.k
        return v0, v1, q
    pu = [P.ps('pu0', [128, 512]), P.ps('pu1', [128, 512])]
    pz = P.ps('pz', [128, 512])
    pt = pz
    pA = P.ps('pA', [128, 512])
    pB, pX, pBXf = ps_pair('qbx')
    pS, pM, _ = ps_pair('qsm')
    pY = P.ps('pY', [128, 512])
    pH, pG, _ = ps_pair('qhg')

    xt = [P.sb('xt0', [128, 1024]), P.sb('xt1', [128, 1024])]
    xh = [P.sb('xh0', [2, 1024])]
    junk = P.sb('junk', [128, 1024])
    tD = P.view(junk.t[:, 0:512])
    tE = P.view(junk.t[:, 512:1024])
    tD.k = junk.k
    tE.k = junk.k
    st4 = P.sb('st4', [128, 8])
    hT = [P.sb('hT0', [128, 8, 130], BF16), P.sb('hT1', [128, 8, 130], BF16)]
    shv = P.sb('shv', [128, 2, 8])
    ue = [P.sb('ue0', [128, 130]), P.sb('ue1', [128, 130])]
    RKV = P.sb('rkv', [128, 12, 128])
    WA = P.sb('wa', [128, 2, 128])
    tA = P.sb('tA', [128, 512])
    tB = P.sb('tB', [128, 512])
    tC = P.sb('tC', [128, 512])
    th0 = P.sb('th0', [128, 128])
    th1 = P.sb('th1', [128, 128])
    sg = P.sb('sg', [128, 512])
    EALL = P.sb('eall', [128, 4, 4, 128])
    STG = P.view(EALL.t[:, :, :, :].rearrange('p m a t -> p (m a t)')[:, 0:1792])
    STG.k = EALL.k
    GCL = [P.sb('gc0', [128, 4]), P.sb('gc1', [128, 4])]
    MSKL = [P.sb('msk0', [128, 384]), P.sb('msk1', [128, 384])]
    ARtL = [P.sb('art%d' % q_, [128, 4, 256]) for q_ in range(2)]
    BKtL = [P.sb('bkt%d' % q_, [128, 4, 256]) for q_ in range(2)]
    ARt, BKt = ARtL[0], BKtL[0]
    BKh = P.sb('bkh', [128, 1024])
    BhT = P.view(BKh.t[:, 0:512])
    KhT = P.view(BKh.t[:, 512:1024])
    BhT.k = BKh.k
    KhT.k = BKh.k
    AtokL = [P.sb('atok%d' % q_, [128, 512]) for q_ in range(2)]
    BhtokL = [P.sb('bhtok%d' % q_, [128, 512]) for q_ in range(2)]
    KhtokL = [P.sb('khtok%d' % q_, [128, 512]) for q_ in range(2)]
    VtokL = [P.sb('vtok%d' % q_, [128, 512]) for q_ in range(2)]
    AT = [P.sb('at0', [128, 512])] * 2
    AN = [P.sb('an0', [128, 128])] * 2
    XPB = [P.sb('xp%d' % q_, [128, 320]) for q_ in range(5)]
    PTB = [P.sb('ptq%d' % q_, [128, 128]) for q_ in range(2)]
    MTs = P.sb('mts', [128, 4, 64])
    Ns = P.sb('ns', [128, 4, 64])
    GTs = [P.sb('gts0', [128, 128]), P.sb('gts1', [128, 128])]
    Hc = [P.sb('hc0', [128, 4, 64]), P.sb('hc1', [128, 4, 64])]
    HFa = P.sb('hfa', [128, 4, 64])
    ysb = P.sb('ysb', [128, 512])
    ybl = P.sb('ybl', [128, 512])
    sbl = P.sb('sbl', [128, 8])
    ssumL = [P.sb('ssum0', [128, 8]), P.sb('ssum1', [128, 8])]
    mla1 = P.sb('mla1', [128, 4, 128])
    mla2 = P.sb('mla2', [128, 4, 128])
    kvb = P.sb('kvb', [128, 2, 128], BF16)
    krb = P.sb('krb', [96, 128], BF16)
    rpk = P.sb('rpk', [96, 2, 128])
    cqb = P.sb('cqb', [128, 3, 128], BF16)
    gmb = P.sb('gmb', [64, 8, 128], BF16)
    mixb = P.sb('mixb', [128, 4, 128], BF16)

    def dump(name, buf, ap):
        if DEBUG and name in dbg:
            P.dma(dbg[name], ap, reads=[buf])

    w_in_v = w_in_d.rearrange("(k p) c -> p k c", p=128)
    for kc in range(8):
        for hf in range(2):
            P.dma(STG[:, 0:1744], w_in_v[:, kc, hf * 1744:(hf + 1) * 1744], writes=[STG])
            P.cp(['act', 'dve'][hf], WB[:, kc, hf * 1744:(hf + 1) * 1744], STG[:, 0:1744], [STG], [WB])
    P.memset('pool', WKR[:, :, :, :], 0.0, [WKR])
    for kc in range(8):
        P.cp('dve', WKR[:, kc, 0, 64:96], WB[:, kc, 2944:2976], [WB], [WKR])
    wkr_v = wkr_rot_d.rearrange("(k p) c -> p k c", p=128)
    P.dma(junk[:, 0:256].rearrange("p (k c) -> p k c", k=8), wkr_v, writes=[junk])
    for kc in range(8):
        P.tt('dve', WKR[:, kc, 1, 64:96], junk[:, kc * 32:(kc + 1) * 32], CST[:, C_SGN:C_SGN + 32], ALU.mult,
             [junk, CST], [WKR])
    CV = P.sb('cv', [128, 8, 2])
    CV2 = P.sb('cv2', [128, 8, 2])
    P.dma(CV[:, :, :], cvec_d, writes=[CV])
    P.act(CV2[:, :, :], CV[:, :, :], AF.Exp, [CV], [CV2], scale=-1.0)
    P.ts1('dve', CV2[:, :, :], CV2[:, :, :], 1.0, ALU.add, [CV2], [CV2])
    P.recip(CV2[:, :, :], CV2[:, :, :], [CV2], [CV2])
    P.tt('dve', CV[:, :, :], CV[:, :, :], CV2[:, :, :], ALU.mult, [CV, CV2], [CV])
    ada_v = ada_w_d.rearrange("(p k) n -> p k n", k=8)
    MACC = P.sb('macc', [128, 48])
    P.memset('dve', MACC[:, :], 0.0, [MACC])
    for k in range(8):
        for hf in range(2):
            P.dma(STG[:, 0:1536], ada_v[:, k, hf * 1536:(hf + 1) * 1536], writes=[STG])
            for n in range(12):
                nn = hf * 12 + n
                P.mm(pz[:, 2 * nn:2 * nn + 2], STG[:, n * 128:(n + 1) * 128], CV[:, k, :], [STG, CV], [pz])
            P.tt('dve', MACC[:, hf * 24:(hf + 1) * 24], MACC[:, hf * 24:(hf + 1) * 24], pz[:, hf * 24:(hf + 1) * 24],
                 ALU.add, [MACC, pz], [MACC])
    for j in range(2):
        P.tt('dve', MODT[:, :, j], MACC[:, j:48:2], PCOL[:, PC_ADAB:PC_ADAB + 24], ALU.add, [MACC, PCOL], [MODT])
    for j in range(2):
        P.stt('dve', GS[:, j, :], MODT[:, 8:16, j], 1.0, PCOL[:, PC_NG:PC_NG + 8], ALU.add, ALU.mult,
              [MODT, PCOL], [GS])
        P.cp('dve', SH[:, j, :], MODT[:, 0:8, j], [MODT], [SH])
    P.cp('dve', GSL[:, 0, :], GS[:, 0, :], [GS], [GSL])
    P.tt('dve', GSL[:, 1, :], GS[:, 1, :], GS[:, 0, :], ALU.subtract, [GS], [GSL])
    P.cp('dve', GSL[:, 2, :], SH[:, 0, :], [SH], [GSL])
    P.tt('dve', GSL[:, 3, :], SH[:, 1, :], SH[:, 0, :], ALU.subtract, [SH], [GSL])
    P.tt('dve', M0[:, :], PCOL[:, PC_MUP:PC_MUP + 14], PCOL[:, PC_MUN:PC_MUN + 14], ALU.add, [PCOL], [M0])
    P.ts('dve', M0[:, :], M0[:, :], -1.0, 1.0, ALU.mult, ALU.add, [M0], [M0])
    P.ts1('dve', NA0[:, :], PCOL[:, PC_A0:PC_A0 + 8], -1.0, ALU.mult, [PCOL], [NA0])
    for b_ in XPB:
        P.memset('pool', b_[:, :], 0.0, [b_])
    P.memset('pool', Hc[0][:, :, :], 0.0, [Hc[0]])
    P.memset('pool', HFa[:, :, :], 0.0, [HFa])

    def load_slot(i):
        P.dma(xt[i % 2][:, :], xs[i, 0:128, :], writes=[xt[i % 2]])

    def rsqrt_ln(dst_buf, dst, src_buf, src, scale, eps_ap):
        P.act(dst, src, AF.Ln, [src_buf, PCOL], [dst_buf], bias=eps_ap, scale=scale)
        P.act(dst, dst, AF.Exp, [dst_buf], [dst_buf], scale=-0.5)

    state = {'h': 0}

    PCNT = {'n': 0}
    ADV = 1

    def proj_for(hT_, cnt):
        def proj(c0, width, ncols=128, lhs=None):
            pb_ = pu[cnt['n'] % 2]
            cnt['n'] += 1
            for kc in range(8):
                l = WB[:, kc, c0:c0 + width] if lhs is None else lhs(kc)
                P.mm(pb_[0:(width if lhs is None else 96), 0:ncols], l, hT_[:, kc, 0:ncols],
                     [WB if lhs is None else WKR, hT_], [pb_], start=(kc == 0), stop=(kc == 7))
            return pb_
        return proj

    def prep(i):
        kind, kv = SLOTS[i]
        S = i % 2
        ARt, BKt, Atok, Bhtok, Khtok, Vtok = ARtL[S], BKtL[S], AtokL[S], BhtokL[S], KhtokL[S], VtokL[S]
        GC, MSK, ssum = GCL[S], MSKL[S], ssumL[S]
        x_, xh_, hT_ = xt[i % 2], xh[0], hT[i % 2]
        P.dma(xh_[:, :], xs[i, 128:130, :], writes=[xh_])
        fcol = SD[:, SD_F, i:i + 1]
        nfcol = SD[:, SD_NF, i:i + 1]
        phi = SD[:, SD_PHI, i:i + 1]
        P.stt('dve', GS[:, 0, :], GSL[:, 1, :], phi, GSL[:, 0, :], ALU.mult, ALU.add, [GSL, SD], [GS])
        P.stt('dve', SH[:, 0, :], GSL[:, 3, :], phi, GSL[:, 2, :], ALU.mult, ALU.add, [GSL, SD], [SH])
        P.ts1('dve', CSEL[:, :], CST[:, C_F:C_F + 512], fcol, ALU.mult, [CST, SD], [CSEL])
        P.stt('dve', CSEL[:, :], CST[:, C_B:C_B + 512], nfcol, CSEL[:, :], ALU.mult, ALU.add, [CST, SD, CSEL], [CSEL])
        P.ts1('dve', MSK[:, :], CST[:, C_F + 512:C_F + 896], fcol, ALU.mult, [CST, SD], [MSK])
        P.stt('dve', MSK[:, :], CST[:, C_B + 512:C_B + 896], nfcol, MSK[:, :], ALU.mult, ALU.add, [CST, SD, MSK], [MSK])
        P.ts1('dve', W0S[0:1, :], W0R[0:1, 0:512], SD[0:1, SD_F, i:i + 1], ALU.mult, [W0R, SD], [W0S])
        P.stt('dve', W0S[0:1, :], W0R[0:1, 512:1024], SD[0:1, SD_NF, i:i + 1], W0S[0:1, :], ALU.mult, ALU.add,
              [W0R, SD, W0S], [W0S])
        P.ts1('dve', NA0S[:, :], NA0[:, 0:4], fcol, ALU.mult, [NA0, SD], [NA0S])
        P.stt('dve', NA0S[:, :], NA0[:, 4:8], nfcol, NA0S[:, :], ALU.mult, ALU.add, [NA0, SD, NA0S], [NA0S])
        j = 0
        P.act(junk[:, :], x_[:, :], AF.Square, [x_], [junk])
        P.red(st4[:, 0:1], junk[:, :], [junk], [st4])
        rsqrt_ln(st4, st4[:, 0:1], st4, st4[:, 0:1], 1.0 / 1024, pc(PC_EPS))
        P.ts1('dve', x_[:, :], x_[:, :], st4[:, 0:1], ALU.mult, [x_, st4], [x_])
        for half in range(2):
            yield
            for q in range(4):
                kc = half * 4 + q
                P.tr(pt[:, q * 128:(q + 1) * 128], x_[:, kc * 128:(kc + 1) * 128], ident, [x_, CST], [pt])
            for q in range(4):
                kc = half * 4 + q
                if q % 2 == 0:
                    P.act(hT_[:, kc, 0:128], pt[:, q * 128:(q + 1) * 128], AF.Identity, [pt, GS, SH], [hT_],
                          bias=SH[:, j, kc:kc + 1], scale=GS[:, j, kc:kc + 1])
                else:
                    P.ts('dve', hT_[:, kc, 0:128], pt[:, q * 128:(q + 1) * 128], GS[:, j, kc:kc + 1],
                         SH[:, j, kc:kc + 1], ALU.mult, ALU.add, [pt, GS, SH], [hT_])
        yield
        P.act(junk[0:2, :], xh_[:, :], AF.Square, [xh_], [junk])
        P.red(st4[0:2, 1:2], junk[0:2, :], [junk], [st4])
        rsqrt_ln(st4, st4[0:2, 1:2], st4, st4[0:2, 1:2], 1.0 / 1024, PCOL[0:2, PC_EPS:PC_EPS + 1])
        P.ts1('dve', xh_[:, :], xh_[:, :], st4[0:2, 1:2], ALU.mult, [xh_, st4], [xh_])
        yield
        for kc in range(8):
            P.tr(pt[:, 2 * kc:2 * kc + 2], xh_[0:2, kc * 128:(kc + 1) * 128], CST[0:2, C_ID:C_ID + 2], [xh_, CST], [pt])
        for c in range(2):
            P.ts1('dve', shv[:, c, :], SH[:, j, :], HV[:, i, c:c + 1], ALU.mult, [SH, HV], [shv])
            P.tt('dve', junk[:, 8 * c:8 * c + 8], pt[:, c:16:2], GS[:, j, :], ALU.mult, [pt, GS], [junk])
            P.tt('dve', hT_[:, :, 128 + c], junk[:, 8 * c:8 * c + 8], shv[:, c, :], ALU.add, [junk, shv], [hT_])

        yield
        proj = proj_for(hT_, PCNT)

        for jc in range(14):
            if kind == 'S' and jc < 4:
                continue
            pb_ = proj(jc * 128, 128, 130)
            ue_ = ue[jc % 2]
            P.cp('act', ue_[:, :], pb_[:, 0:130], [pb_], [ue_])
            yield
            if jc < 12:
                dstb, dst = RKV, RKV[:, jc, :]
            else:
                dstb, dst = WA, WA[:, jc - 12, :]
            eng = 'pool' if jc % 2 == 0 else 'dve'
            mp_, mn_ = pc(PC_MUP + jc), pc(PC_MUN + jc)
            P.ts1(eng, dst, ue_[:, 0:128], M0[:, jc:jc + 1], ALU.mult, [ue_, M0], [dstb])
            for (dsl, ssl, mu_) in [((1, 128), (0, 127), mp_), ((0, 1), (128, 129), mp_),
                                    ((0, 127), (1, 128), mn_), ((127, 128), (129, 130), mn_)]:
                dd = dst[:, dsl[0]:dsl[1]]
                if eng == 'dve':
                    P.stt('dve', dd, ue_[:, ssl[0]:ssl[1]], mu_, dd, ALU.mult, ALU.add, [ue_, PCOL, dstb], [dstb])
                else:
                    tp = th1[:, 0:dsl[1] - dsl[0]]
                    P.ts1('pool', tp, ue_[:, ssl[0]:ssl[1]], mu_, ALU.mult, [ue_, PCOL], [th1])
                    P.tt('pool', dd, dd, tp, ALU.add, [dstb, th1], [dstb])
            yield
        yield
        r3 = RKV[:, 0:4, :]
        k3 = RKV[:, 4:8, :]
        v3 = RKV[:, 8:12, :]

        def f3(b_):
            return b_[:, :].rearrange("p (m t) -> p m t", m=4)

        for m in range(4):
            P.ts1('pool', tA[:, m * 128:(m + 1) * 128], RKV[:, 4 + m, :], pc(PC_KK + m), ALU.mult, [RKV, PCOL], [tA])
        P.act(tB[:, :], tA[:, :], AF.Square, [tA], [tB])
        P.mm(pz[:, :], CST[:, C_BO:C_BO + 128], tB[:, :], [CST, tB], [pz])
        yield
        P.ts1('dve', tB[:, :], pz[:, :], 1e-24, ALU.max, [pz], [tB])
        P.act(tB[:, :], tB[:, :], AF.Ln, [tB], [tB])
        P.act(tB[:, :], tB[:, :], AF.Exp, [tB], [tB], scale=-0.5)
        P.tt('dve', tA[:, :], tA[:, :], tB[:, :], ALU.mult, [tA, tB], [tA])
        yield
        P.act(th0[:, :], WA[:, 0, :], AF.Exp, [WA], [th0], scale=2.0)
        P.ts1('dve', th0[:, :], th0[:, :], 1.0, ALU.add, [th0], [th0])
        P.recip(th0[:, :], th0[:, :], [th0], [th0])
        P.ts('dve', th0[:, :], th0[:, :], SD[:, SD_M2S, i:i + 1], SD[:, SD_SELW, i:i + 1], ALU.mult, ALU.add,
             [th0, SD], [th0])
        P.mm(pz[:, :], th0[:, :], W2C[:, :], [th0, W2C], [pz], start=True, stop=False)
        P.mm(pz[:, :], CST[0:1, C_ONES:C_ONES + 128], W0S[0:1, :], [CST, W0S], [pz], start=False, stop=True)
        yield
        P.act(sg[:, :], pz[:, :], AF.Exp, [pz], [sg], scale=-1.0)
        P.ts1('dve', sg[:, :], sg[:, :], 1.0, ALU.add, [sg], [sg])
        P.recip(sg[:, :], sg[:, :], [sg], [sg])
        yield
        tri = CSEL[:, 0:512]
        for m in range(4):
            P.mm(pz[:, :], sg[:, m * 128:(m + 1) * 128], tri, [sg, CSEL], [pz])
            P.act(EALL[:, m, :, :], pz[:, :].rearrange("p (a t) -> p a t", a=4), AF.Exp, [pz], [EALL])
            yield
        P.tt('dve', GC[:, :], EALL[:, :, 2, 0], EALL[:, :, 3, 0], ALU.mult, [EALL], [GC])
        yield
        P.ts1('dve', th1[:, :], WA[:, 1, :], SD[:, SD_SELW, i:i + 1], ALU.mult, [WA, SD], [th1])
        for m in range(4):
            P.mm(pz[:, m * 128:(m + 1) * 128], A2C[:, m * 128:(m + 1) * 128], th1[:, :], [A2C, th1], [pz])
        for m in range(4):
            P.act(tB[:, m * 128:(m + 1) * 128], pz[:, m * 128:(m + 1) * 128], AF.Exp, [pz, NA0S], [tB],
                  bias=NA0S[:, m:m + 1], scale=-1.0)
        yield
        P.ts1('dve', tB[:, :], tB[:, :], 1.0, ALU.add, [tB], [tB])
        P.recip(tB[:, :], tB[:, :], [tB], [tB])
        yield
        for m in range(4):
            P.ts('pool', tC[:, m * 128:(m + 1) * 128], tB[:, m * 128:(m + 1) * 128], -1.0, pc(PC_KA + m),
                 ALU.add, ALU.mult, [tB, PCOL], [tC])
        P.stt('dve', tC[:, :], tC[:, :], 1.0, RKV[:, 4:8, :].rearrange("p m t -> p (m t)"), ALU.add, ALU.mult,
              [tC, RKV], [tC])
        P.tt('pool', tD[:, :], tA[:, :], tB[:, :], ALU.mult, [tA, tB], [tD])
        if kind != 'S':
            for m in range(4):
                P.stt('pool', tE[:, m * 128:(m + 1) * 128], RKV[:, m, :], pc(PC_RK + m),
                      tC[:, m * 128:(m + 1) * 128], ALU.mult, ALU.mult, [RKV, PCOL, tC], [tE])
            for m in range(4):
                P.mm(pz[:, 2 * m:2 * m + 2], tE[:, m * 128:(m + 1) * 128], CST[:, C_BO2:C_BO2 + 2],
                     [tE, CST], [pz])
            P.cp('act', ssum[:, :], pz[:, 0:8], [pz], [ssum])
        yield
        E = [EALL[:, :, a, :] for a in range(4)]
        A3 = ARt[:, :, :].rearrange("p m (a t) -> p m a t", a=2)
        B3 = BKt[:, :, :].rearrange("p m (a t) -> p m a t", a=2)
        P.stt('dve', A3[:, :, 0, :], f3(tA), -1.0, E[0], ALU.mult, ALU.mult, [tA, EALL], [ARt])
        if kind != 'S':
            P.tt('pool', A3[:, :, 1, :], r3, E[2], ALU.mult, [RKV, EALL], [ARt])
        yield
        P.tt('dve', B3[:, :, 0, :], f3(tD), E[1], ALU.mult, [tD, EALL], [BKt])
        P.tt('pool', B3[:, :, 1, :], f3(tC), E[1], ALU.mult, [tC, EALL], [BKt])
        yield
        P.tt('dve', f3(BhT), f3(tD), E[3], ALU.mult, [tD, EALL], [BhT])
        P.tt('pool', f3(KhT), f3(tC), E[3], ALU.mult, [tC, EALL], [KhT])
        for (srcb, srcf, dstb) in [(ARt, lambda m: ARt[:, m, 0:128], Atok), (BhT, lambda m: BhT[:, m * 128:(m + 1) * 128], Bhtok),
                                   (KhT, lambda m: KhT[:, m * 128:(m + 1) * 128], Khtok), (RKV, lambda m: RKV[:, 8 + m, :], Vtok)]:
            for m in range(4):
                P.tr(pt[:, m * 128:(m + 1) * 128], srcf(m), ident, [srcb, CST], [pt])
            P.cp('act', dstb[:, :], pt[:, :], [pt], [dstb])
            yield

        yield
        if kv is not None:
            pb2 = proj(2688, 128, 128)
            P.cp('act', mla1[:, 0, :], pb2[:, 0:128], [pb2], [mla1])
            pb2 = proj(2816, 128, 128)
            P.cp('act', mla1[:, 1, :], pb2[:, 0:128], [pb2], [mla1])
            P.act(mla2[:, 0:2, :], mla1[:, 0:2, :], AF.Square, [mla1], [mla2])
            P.mm(pz[:, 0:128], ones, mla2[:, 0, :], [CST, mla2], [pz], start=True, stop=False)
            P.mm(pz[:, 0:128], ones, mla2[:, 1, :], [CST, mla2], [pz], start=False, stop=True)
            rsqrt_ln(mla2, mla2[:, 2, :], pz, pz[:, 0:128], 1.0 / 256, pc(PC_EPS))
            for c in range(2):
                P.stt('dve', kvb[:, c, :], mla1[:, c, :], pc(PC_KVG + c), mla2[:, 2, :], ALU.mult, ALU.mult,
                      [mla1, PCOL, mla2], [kvb])
            P.dma(CKVs[:, :, kv * 128:(kv + 1) * 128], kvb[:, :, :], reads=[kvb], writes=[CKVs])
            yield
            P.dma(rpk[64:96, :, :], ropek[kv].rearrange("a d t -> d a t"), writes=[rpk])
            pk1 = proj(0, 96, 128, lhs=lambda kc: WKR[:, kc, 0, :])
            yield
            pk2 = proj(0, 96, 128, lhs=lambda kc: WKR[:, kc, 1, :])
            P.tt('dve', mla1[64:96, 2, :], pk1[64:96, 0:128], rpk[64:96, 0, :], ALU.mult, [pk1, rpk], [mla1])
            P.tt('dve', mla1[64:96, 3, :], pk2[64:96, 0:128], rpk[64:96, 1, :], ALU.mult, [pk2, rpk], [mla1])
            P.tt('dve', krb[64:96, :], mla1[64:96, 2, :], mla1[64:96, 3, :], ALU.add, [mla1], [krb])
            P.dma(KRs[:, kv * 128:(kv + 1) * 128], krb[64:96, :], reads=[krb], writes=[KRs])

        yield

    def heads(i, gen):
        kind, kv = SLOTS[i]
        S = i % 2
        ARt, BKt, Atok, Bhtok, Khtok, Vtok = ARtL[S], BKtL[S], AtokL[S], BhtokL[S], KhtokL[S], VtokL[S]
        GC, MSK, ssum = GCL[S], MSKL[S], ssumL[S]
        hT_ = hT[i % 2]
        proj = proj_for(hT_, PCNT)

        def f3(b_):
            return b_[:, :].rearrange("p (m t) -> p m t", m=4)

        def advance(n):
            if gen is None:
                return
            for _ in range(n):
                try:
                    next(gen)
                except StopIteration:
                    return
        Hin = Hc[state['h'] % 2]
        Hout = Hc[(state['h'] + 1) % 2]
        mskt = MSK[:, 0:256]
        mskn = MSK[:, 256:384]
        if kind == 'S':
            P.ts1('dve', Hin[:, :, :], Hin[:, :, :], SD[:, SD_RHO, i:i + 1], ALU.mult, [Hin, SD], [Hin])
        for h in range(8):
            m, pb = h // 2, 64 * (h % 2)
            cb = 64 * h
            at_, an_ = AT[h % 2], AN[h % 2]
            Bt_h = BKt[pb:pb + 64, m, 0:128]
            Kt_h = BKt[pb:pb + 64, m, 128:256]
            AR_h = ARt[pb:pb + 64, m, 0:256]
            own = kind != 'S'
            if own:
                P.mm(pA[:, 0:256], Bt_h, AR_h, [BKt, ARt], [pA])
                P.mm(pA[:, 256:512], Kt_h, AR_h, [BKt, ARt], [pA])
            else:
                P.mm(pA[:, 0:128], Bt_h, ARt[pb:pb + 64, m, 0:128], [BKt, ARt], [pA])
                P.mm(pA[:, 256:384], Kt_h, ARt[pb:pb + 64, m, 0:128], [BKt, ARt], [pA])
            P.mm(pB[:, 0:128], ARt[pb:pb + 64, m, 0:128], Bt_h, [ARt, BKt], [pB])
            xi = 0 if h % 2 == 0 else 3
            XP = XPB[xi]
            pt0 = PTB[0]
            P.tt('dve', pt0[:, :], pA[:, 0:128], mskt[:, 0:128], ALU.mult, [pA, MSK], [pt0])
            if own:
                P.tt('dve', at_[:, 128:256], pA[:, 128:256], mskt[:, 128:256], ALU.mult, [pA, MSK], [at_])
                P.tt('dve', at_[:, 256:512], pA[:, 256:512], mskt, ALU.mult, [pA, MSK], [at_])
            else:
                P.tt('dve', at_[:, 256:384], pA[:, 256:384], mskt[:, 0:128], ALU.mult, [pA, MSK], [at_])
            P.tt('dve', XP[:, 192:320], pB[:, 0:128], mskn, ALU.mult, [pB, MSK], [XP])
            P.cp('act', XP[:, 64:128], Atok[:, cb:cb + 64], [Atok], [XP])
            P.mm(pB[:, 128:192], at_[:, 256:384], Vtok[:, cb:cb + 64], [at_, Vtok], [pB])
            P.cp('act', XP[:, 128:192], pB[:, 128:192], [pB], [XP])
            PT_ = pt0
            advance(1)
            for lvl in range(7):
                last = lvl == 6
                XPn = XPB[(xi + 1) % 5]
                P.mm(pX[:, 0:(128 if last else 256)], PT_[:, :], XP[:, 64:(192 if last else 320)], [PT_, XP], [pX])
                if not last:
                    P.mm(pS[:, 0:128], XP[:, 192:320], PT_[:, :], [XP, PT_], [pS])
                P.tt('dve', XPn[:, 64:192], XP[:, 64:192], pX[:, 0:128], ALU.add, [XP, pX], [XPn])
                if not last:
                    P.cp('act', XPn[:, 192:320], pX[:, 128:256], [pX], [XPn])
                    ptn = PTB[(lvl + 1) % 2]
                    P.cp('act', ptn[:, :], pS[:, 0:128], [pS], [ptn])
                    PT_ = ptn
                xi += 1
                XP = XPn
                advance(1)
            X = XP
            lx = X[:, 64:128] if pb == 0 else X[:, 0:128]
            if i == CFG.get('dump_slot', 0) and h == 0:
                dump('xf', X, X[:, :])
                dump('at', at_, at_[:, :])
                dump('atok', Atok, Atok[:, :])
                dump('vtok', Vtok, Vtok[:, :])
                dump('eall', EALL, EALL[:, :, :, :].rearrange("p m a t -> p (m a t)"))
                dump('sg', sg, sg[:, :])
                dump('rkv', RKV, RKV[:, :, :].rearrange("p m t -> p (m t)"))
            P.mm(pM[0:(64 if pb == 0 else 128), 0:64], lx, Bhtok[:, cb:cb + 64], [X, Bhtok], [pM])
            P.mm(pM[:, 64:128], Bhtok[:, m * 128:(m + 1) * 128], X[:, 128:192], [Bhtok, X], [pM], start=True, stop=False)
            P.mm(pM[:, 64:128], Khtok[:, m * 128:(m + 1) * 128], Vtok[:, cb:cb + 64], [Khtok, Vtok], [pM],
                 start=False, stop=True)
            P.stt('dve', MTs[pb:pb + 64, m, :], CST[pb:pb + 64, C_ID + pb:C_ID + pb + 64], GC[pb:pb + 64, m:m + 1],
                  pM[pb:pb + 64, 0:64], ALU.mult, ALU.add, [CST, GC, pM], [MTs])
            P.cp('act', Ns[pb:pb + 64, m, :], pM[pb:pb + 64, 64:128], [pM], [Ns])
            if kind != 'S':
                gt_ = GTs[h % 2]
                P.mm(pG[0:(64 if pb == 0 else 128), 0:128], lx, at_[:, 128:256], [X, at_], [pG])
                P.tt('dve', gt_[pb:pb + 64, :], pG[pb:pb + 64, 0:128], ARt[pb:pb + 64, m, 128:256], ALU.add,
                     [pG, ARt], [gt_])
                yo = pY[:, cb:cb + 64]
                P.mm(yo, at_[:, 128:256], X[:, 128:192], [at_, X], [pY], start=True, stop=False)
                P.mm(yo, at_[:, 384:512], Vtok[:, cb:cb + 64], [at_, Vtok], [pY], start=False, stop=False)
                P.mm(yo, gt_[pb:pb + 64, :], Hin[pb:pb + 64, m, :], [gt_, Hin], [pY], start=False, stop=True)
            P.mm(pH[pb:pb + 64, m * 64:(m + 1) * 64], MTs[pb:pb + 64, m, :], Hin[pb:pb + 64, m, :], [MTs, Hin], [pH])
            advance(ADV)
        P.tt('dve', Hout[:, :, :], pH[:, :].rearrange("p (m i) -> p m i", m=4), Ns[:, :, :], ALU.add, [pH, Ns], [Hout])
        if i == CFG.get('dump_slot', 0):
            dump('hout', Hout, Hout[:, :, :].rearrange('p m i -> p (m i)'))
        state['h'] += 1
        advance(10 ** 6)
        if kind == 'S':
            P.stt('dve', HFa[:, :, :], Hout[:, :, :], SD[:, SD_SIG, i:i + 1], HFa[:, :, :], ALU.mult, ALU.add,
                  [Hout, SD, HFa], [HFa])

        if kind == 'B':
            jj = 15 - (i - 52)
            P.cp('act', ysb[:, :], pY[:, :], [pY], [ysb])
            P.dma(YBs[jj], ysb[:, :], reads=[ysb], writes=[YBs])
            P.dma(SBs[jj], ssum[:, :], reads=[ssum], writes=[SBs])
        if kind == 'F':
            jj = i - 68
            tok = slice(jj * 128, (jj + 1) * 128)
            P.dma(ybl[:, :], YBs[jj], reads=[YBs], writes=[ybl])
            P.dma(sbl[:, :], SBs[jj], reads=[SBs], writes=[sbl])
            P.tt('dve', ysb[:, :], pY[:, :], ybl[:, :], ALU.add, [pY, ybl], [ysb])
            P.tt('dve', ssum[:, :], ssum[:, :], sbl[:, :], ALU.add, [ssum, sbl], [ssum])
            y3 = ysb[:, :].rearrange("p (h i) -> p h i", h=8)
            P.red(st4[:, :], y3, [ysb], [st4])
            P.ts1('dve', st4[:, :], st4[:, :], 1.0 / 64, ALU.mult, [st4], [st4])
            P.act(tE[:, :], ysb[:, :], AF.Square, [ysb], [tE])
            P.red(sbl[:, :], tE[:, :].rearrange("p (h i) -> p h i", h=8), [tE], [sbl])
            P.tt('dve', ybl[:, 0:8], st4[:, :], st4[:, :], ALU.mult, [st4], [ybl])
            P.stt('dve', sbl[:, :], sbl[:, :], 1.0 / 64, ybl[:, 0:8], ALU.mult, ALU.subtract, [sbl, ybl], [sbl])
            rsqrt_ln(sbl, sbl[:, :], sbl, sbl[:, :], 1.0, pc(PC_LEPS))
            for h in range(8):
                P.ts('dve' if h % 2 else 'pool', ysb[:, h * 64:(h + 1) * 64], ysb[:, h * 64:(h + 1) * 64],
                     st4[:, h:h + 1], sbl[:, h:h + 1], ALU.subtract, ALU.mult, [ysb, st4, sbl], [ysb])
            P.tt('dve', ysb[:, :], ysb[:, :], LGB[:, 0:512], ALU.mult, [ysb, LGB], [ysb])
            P.tt('pool', ysb[:, :], ysb[:, :], LGB[:, 512:1024], ALU.add, [ysb, LGB], [ysb])
            for h in range(8):
                P.stt('dve' if h % 2 else 'pool', ysb[:, h * 64:(h + 1) * 64], Vtok[:, h * 64:(h + 1) * 64],
                      ssum[:, h:h + 1], ysb[:, h * 64:(h + 1) * 64], ALU.mult, ALU.add, [Vtok, ssum, ysb], [ysb])
            for m in range(4):
                pb2 = proj(1792 + m * 128, 128, 128)
                P.cp('act', tE[:, m * 128:(m + 1) * 128], pb2[:, 0:128], [pb2], [tE])
            P.act(tD[:, :], tE[:, :], AF.Exp, [tE], [tD], scale=-1.0)
            P.ts1('dve', tD[:, :], tD[:, :], 1.0, ALU.add, [tD], [tD])
            P.recip(tD[:, :], tD[:, :], [tD], [tD])
            P.tt('pool', tE[:, :], tE[:, :], tD[:, :], ALU.mult, [tE, tD], [tE])
            for m in range(4):
                P.tr(pt[:, m * 128:(m + 1) * 128], ysb[:, m * 128:(m + 1) * 128], ident, [ysb, CST], [pt])
            P.tt('dve', mixb[:, :, :], pt[:, :].rearrange("p (m t) -> p m t", m=4), f3(tE), ALU.mult, [pt, tE], [mixb])
            P.dma(MIXRs[:, :, tok], mixb[:, :, :], reads=[mixb], writes=[MIXRs])
            for c in range(3):
                pb2 = proj(2304 + c * 128, 128, 128)
                P.cp('act', mla1[:, c, :], pb2[:, 0:128], [pb2], [mla1])
            P.act(mla2[:, 0:3, :], mla1[:, 0:3, :], AF.Square, [mla1], [mla2])
            for c in range(3):
                P.mm(pz[:, 0:128], ones, mla2[:, c, :], [CST, mla2], [pz], start=(c == 0), stop=(c == 2))
            rsqrt_ln(mla2, mla2[:, 3, :], pz, pz[:, 0:128], 1.0 / 384, pc(PC_EPS))
            for c in range(3):
                P.stt('dve', cqb[:, c, :], mla1[:, c, :], pc(PC_QG + c), mla2[:, 3, :], ALU.mult, ALU.mult,
                      [mla1, PCOL, mla2], [cqb])
            P.dma(CQs[:, :, tok], cqb[:, :, :], reads=[cqb], writes=[CQs])
            for h in range(8):
                pb2 = proj(2976 + h * 64, 64, 128)
                P.cp('act', tD[0:64, 0:128], pb2[0:64, 0:128], [pb2], [tD])
                P.act(tD[0:64, 128:256], tD[0:64, 0:128], AF.Exp, [tD], [tD], scale=-1.0)
                P.ts1('dve', tD[0:64, 128:256], tD[0:64, 128:256], 1.0, ALU.add, [tD], [tD])
                P.recip(tD[0:64, 128:256], tD[0:64, 128:256], [tD], [tD])
                P.tt('dve', gmb[:, h, :], tD[0:64, 0:128], tD[0:64, 128:256], ALU.mult, [tD], [gmb])
            P.dma(GMs[:, :, tok], gmb[:, :, :], reads=[gmb], writes=[GMs])

    slot_list = list(range(NSLOT)) if CFG['slots'] is None else list(CFG['slots'])
    if slot_list:
        load_slot(slot_list[0])
        if len(slot_list) > 1:
            load_slot(slot_list[1])
        for _ in prep(slot_list[0]):
            pass
    for si, i in enumerate(slot_list):
        if si + 2 < len(slot_list):
            load_slot(slot_list[si + 2])
        if i == 68:
            P.cp('dve', Hc[state['h'] % 2][:, :, :], HFa[:, :, :], [HFa], [Hc[state['h'] % 2]])
        heads(i, prep(slot_list[si + 1]) if si + 1 < len(slot_list) else None)

    AR = ARENA.t
    o = 0

    def carve(n, shape_str=None, **kw):
        nonlocal o
        ap = AR[:, o:o + n]
        o += n
        if shape_str:
            ap = ap.rearrange(shape_str, **kw)
        return P.view(ap)

    ps2 = [pA, pY]

    def phase2():
        nonlocal o
        P.barrier()
        o = 0

        KT = [carve(NKEY)] * 2
        V4 = carve(NKT * 4 * 65, "p (k h c) -> p k h c", k=NKT, h=4)
        QT = [carve(2048)] * 2
        def bview(buf, nbf, pat=None, rows=128, **kw):
            t = buf.t
            flat = t[0:rows]
            nd = len(t.shape)
            if nd == 3:
                flat = t[0:rows, :, :].rearrange("p a b -> p (a b)")
            elif nd == 4:
                flat = t[0:rows, :, :, :].rearrange("p a b c -> p (a b c)")
            ap = flat[:, 0:nbf // 2].bitcast(BF16)
            if pat:
                ap = ap.rearrange(pat, **kw)
            return P.view(ap)
        WUQ = bview(RKV, 2304, "p (k c) -> p k c", k=3)
        WUQR = bview(ybl, 288, "p (k c) -> p k c", k=3)
        WROT = bview(ysb, 768, "p (k h c) -> p k h c", k=3, h=8)
        WUK = bview(AtokL[0], 1024, "p (k c) -> p k c", k=2)
        WUV = bview(BhtokL[0], 1024, "p (k c) -> p k c", k=2)
        ckb = [bview(KhtokL[0], 1024, "p (k c) -> p k c", k=2), bview(VtokL[0], 1024, "p (k c) -> p k c", k=2)]
        PTb_ = [bview(AtokL[1], 512), bview(BhtokL[1], 512), bview(KhtokL[1], 512)]
        rq = P.view(ARt.t[:, :, :].rearrange('p m c -> p (m c)').rearrange('p (a t) -> p a t', a=2))
        onrm = P.view(tA.t[0:64, :])
        rec = P.view(tB.t[0:65, :])
        bcs = P.view(tC.t[0:64, :])
        mxb = bview(XPB[0], 512, rows=64)
        gml = bview(XPB[1], 512, rows=64)
        cql = bview(junk, 1536, "p (k c) -> p k c", k=3)
        w_uq_v = w_uq_d.rearrange("(k p) c -> p k c", p=128)
        w_uqr_v = w_uq_rot_d.rearrange("(k p) h c -> p k h c", p=128)
        for kc in range(3):
            P.dma(STG[:, 0:768], w_uq_v[:, kc, :], writes=[STG])
            P.cp('dve', WUQ[:, kc, :], STG[:, 0:768], [STG], [WUQ])
        P.memset('pool', WUQR[:, :, :], 0.0, [WUQR])
        P.dma(STG[:, 0:768].rearrange("p (k h c) -> p k h c", k=3, h=8), w_uqr_v, writes=[STG])
        for kc in range(3):
            for h in range(8):
                c0 = kc * 256 + h * 32
                P.tt('dve', WROT[:, kc, h, :], STG[:, c0:c0 + 32], CST[:, C_SGN:C_SGN + 32], ALU.mult, [STG, CST], [WROT])
        for kc_ in range(2):
            P.dma(STG[:, 0:512], w_uk_d.rearrange("(k p) h c -> p k (h c)", p=128)[:, kc_, :], writes=[STG])
            P.cp('dve', WUK[:, kc_, :], STG[:, 0:512], [STG], [WUK])
        P.dma(tA[:, :].rearrange("p (k c) -> p k c", k=2)[:, :, 0:256], w_uv_d.rearrange("(k p) h c -> p k (h c)", p=128)[:, :, 0:256], writes=[tA])
        P.dma(tB[:, :].rearrange("p (k c) -> p k c", k=2)[:, :, 0:256], w_uv_d.rearrange("(k p) h c -> p k (h c)", p=128)[:, :, 256:512], writes=[tB])
        P.cp('dve', WUV[:, :, 0:256], tA[:, :].rearrange("p (k c) -> p k c", k=2), [tA], [WUV])
        P.cp('dve', WUV[:, :, 256:512], tB[:, :].rearrange("p (k c) -> p k c", k=2), [tB], [WUV])

        po = [pz, pBXf]
        KB = [(kb * 512, 512) for kb in range(NKEY // 512)]
        for hg in range(2):
            for hh_ in range(4):
                P.cp('dve', V4[:, :, hh_, 64], KVV[:, :], [KVV], [V4])
            for bi, (k0, kw) in enumerate(KB):
                cb_ = ckb[bi % 2]
                P.dma(cb_[:, :, 0:kw], CKVs[:, :, k0:k0 + kw], reads=[CKVs], writes=[cb_])
                for s in range(kw // 128):
                    kt = k0 // 128 + s
                    pv = pu[kt % 2]
                    for c in range(2):
                        P.mm(pv[:, 0:256], cb_[:, c, s * 128:(s + 1) * 128], WUV[:, c, hg * 256:(hg + 1) * 256], [cb_, WUV], [pv],
                             start=(c == 0), stop=(c == 1))
                    P.ts1('dve', V4[:, kt, :, 0:64], pv[:, 0:256].rearrange("p (h c) -> p h c", h=4), KVV[:, kt:kt + 1], ALU.mult, [pv, KVV], [V4])
            for hh in range(4):
                h = hg * 4 + hh
                kt_, qt_ = KT[h % 2], QT[h % 2]
                P.dma(kt_[64:96, :], KRs[:, :], reads=[KRs], writes=[kt_])
                for bi, (k0, kw) in enumerate(KB):
                    cb_ = ckb[bi % 2]
                    P.dma(cb_[:, :, 0:kw], CKVs[:, :, k0:k0 + kw], reads=[CKVs], writes=[cb_])
                    for c in range(2):
                        P.mm(ps2[bi % 2][0:64, 0:kw], WUK[:, c, h * 64:(h + 1) * 64], cb_[:, c, 0:kw],
                             [cb_, WUK], [ps2[bi % 2]], start=(c == 0), stop=(c == 1))
                    P.cp('act' if bi % 2 else 'dve', kt_[0:64, k0:k0 + kw], ps2[bi % 2][0:64, 0:kw], [ps2[bi % 2]], [kt_])
                for kc in range(3):
                    P.cp('pool', WUQR[:, kc, 64:96], WROT[:, kc, h, :], [WROT], [WUQR])
                for qb in range(4):
                    qs = slice(qb * 512, (qb + 1) * 512)
                    P.dma(cql[:, :, :], CQs[:, :, qs], reads=[CQs], writes=[cql])
                    P.dma(rq[64:96, :, :], ropeq[:, :, qs].rearrange("a d t -> d a t"), writes=[rq])
                    for c in range(3):
                        P.mm(po[0][0:96, :], WUQ[:, c, h * 96:(h + 1) * 96], cql[:, c, :], [WUQ, cql], [po[0]],
                             start=(c == 0), stop=(c == 2))
                    for c in range(3):
                        P.mm(po[1][0:96, :], WUQR[:, c, :], cql[:, c, :], [WUQR, cql], [po[1]], start=(c == 0), stop=(c == 2))
                    P.cp('act', qt_[0:64, qs], po[0][0:64, :], [po[0]], [qt_])
                    P.tt('dve', rq[64:96, 0, :], po[0][64:96, :], rq[64:96, 0, :], ALU.mult, [po[0], rq], [rq])
                    P.tt('dve', rq[64:96, 1, :], po[1][64:96, :], rq[64:96, 1, :], ALU.mult, [po[1], rq], [rq])
                    P.tt('dve', qt_[64:96, qs], rq[64:96, 0, :], rq[64:96, 1, :], ALU.add, [rq], [qt_])
                for qb in range(4):
                    qs = slice(qb * 512, (qb + 1) * 512)
                    pacc = po[qb % 2]
                    P.dma(gml[:, :], GMs[:, h, qs], reads=[GMs], writes=[gml])
                    def qk(kt):
                        sc = ps2[kt % 2]
                        P.mm(sc[:, :], kt_[0:96, kt * 128:(kt + 1) * 128], qt_[0:96, qs], [kt_, qt_], [sc])
                    qk(0)
                    for kt in range(NKT):
                        if kt + 1 < NKT:
                            qk(kt + 1)
                        sc = ps2[kt % 2]
                        pT = PTb_[kt % 3]
                        P.act(pT[:, :], sc[:, :], AF.Exp, [sc], [pT], scale=ATTN_SCALE)
                        P.mm(pacc[0:65, :], V4[:, kt, hh, :], pT[:, :], [V4, pT], [pacc], start=(kt == 0), stop=(kt == NKT - 1))
                    P.recip(rec[64:65, :], pacc[64:65, :], [pacc], [rec])
                    P.mm(pH[0:64, 0:256], CST[64:65, C_ONES:C_ONES + 64], rec[64:65, 0:256], [CST, rec], [pH])
                    P.mm(pG[0:64, 0:256], CST[64:65, C_ONES:C_ONES + 64], rec[64:65, 256:512], [CST, rec], [pG])
                    P.cp('act', bcs[:, 0:256], pH[0:64, 0:256], [pH], [bcs])
                    P.cp('act', bcs[:, 256:512], pG[0:64, 0:256], [pG], [bcs])
                    P.tt('dve', onrm[:, :], pacc[0:64, :], bcs[:, :], ALU.mult, [pacc, bcs], [onrm])
                    P.tt('dve', mxb[:, :], onrm[:, :], gml[:, :], ALU.mult, [onrm, gml], [mxb])
                    P.dma(MIXMs[:, h, qs], mxb[:, :], reads=[mxb], writes=[MIXMs])

    def phase3():
        nonlocal o
        P.barrier()
        o = 0
        WOR = carve(4 * 1024, "p (k c) -> p k c", k=4)
        WOM = carve(8 * 1024, "p (k c) -> p k c", k=8)
        mr = [carve(512, "p (k c) -> p k c", k=4), carve(512, "p (k c) -> p k c", k=4)]
        mm_ = [carve(1024, "p (k c) -> p k c", k=8), carve(1024, "p (k c) -> p k c", k=8)]
        w_out_r = w_out_d[0:512, :].rearrange("(k p) c -> p k c", p=128)
        w_out_m = w_out_d[512:1024, :].rearrange("(k p) c -> p k c", p=64)
        for k in range(4):
            P.dma(STG[:, 0:1024], w_out_r[:, k, :], writes=[STG])
            P.cp('dve', WOR[:, k, :], STG[:, 0:1024], [STG], [WOR])
        for k in range(8):
            P.dma(STG[0:64, 0:1024], w_out_m[:, k, :], writes=[STG])
            P.cp('dve', WOM[0:64, k, :], STG[0:64, 0:1024], [STG], [WOM])
        GB = P.view(BKh.t[:, :])
        FG = P.view(BKt.t[:, :, :].rearrange('p m c -> p (m c)'))
        for half in range(2):
            for q in range(4):
                kc = half * 4 + q
                P.ts1('dve', th0[:, :], ident, MODT[:, 16 + kc, 0:1], ALU.mult, [CST, MODT], [th0])
                P.mm(pt[:, q * 128:(q + 1) * 128], ones, th0[:, :], [CST, th0], [pt])
            P.cp('act', GB[:, half * 512:(half + 1) * 512], pt[:, :], [pt], [GB])
        P.dma(FG[:, :], brow_d[:, 1024:2048], writes=[FG])
        yo_ = [P.view(RKV.t[:, 0:8, :].rearrange('p m t -> p (m t)')), P.view(EALL.t[:, 0:2, :, :].rearrange('p m a t -> p (m a t)'))]
        for jj in range(16):
            tok = slice(jj * 128, (jj + 1) * 128)
            x_ = xt[jj % 2]
            P.dma(x_[:, :], xs[68 + jj, 0:128, :], writes=[x_])
            P.dma(mr[jj % 2][:, :, :], MIXRs[:, :, tok], reads=[MIXRs], writes=[mr[jj % 2]])
            P.dma(mm_[jj % 2][0:64, :, :], MIXMs[:, :, tok], reads=[MIXMs], writes=[mm_[jj % 2]])
            y_ = yo_[jj % 2]
            for half in range(2):
                cs = slice(half * 512, (half + 1) * 512)
                pp = ps2[half]
                for k in range(4):
                    P.mm(pp[:, :], mr[jj % 2][:, k, :], WOR[:, k, cs], [mr[jj % 2], WOR], [pp], start=(k == 0), stop=False)
                for k in range(8):
                    P.mm(pp[:, :], mm_[jj % 2][0:64, k, :], WOM[0:64, k, cs], [mm_[jj % 2], WOM], [pp], start=False, stop=(k == 7))
                P.tt('dve', y_[:, cs], pp[:, :], GB[:, cs], ALU.mult, [pp, GB], [y_])
            P.tt('pool', y_[:, :], y_[:, :], x_[:, :], ALU.add, [y_, x_], [y_])
            P.act(junk[:, :], y_[:, :], AF.Square, [y_], [junk])
            P.red(st4[:, 0:1], junk[:, :], [junk], [st4])
            rsqrt_ln(st4, st4[:, 0:1], st4, st4[:, 0:1], 1.0 / 1024, pc(PC_EPS))
            P.stt('dve', y_[:, :], y_[:, :], st4[:, 0:1], FG[:, :], ALU.mult, ALU.mult, [y_, st4, FG], [y_])
            P.dma(out_d[tok, :], y_[:, :], reads=[y_])
    if CFG['p2']:
        phase2()
    if CFG['p3']:
        phase3()
    P.finish()
    return nc


DEBUG_SPECS = {'xf': (128, 192), 'at': (128, 512), 'atok': (128, 512), 'vtok': (128, 512), 'eall': (128, 2048), 'sg': (128, 512), 'rkv': (128, 1536), 'hout': (128, 256)}
_CACHE = {}


def kernel(**inputs):
    maps = host_prep(inputs)
    if 'nc' not in _CACHE:
        _CACHE['nc'] = build()
    nc = _CACHE['nc']
    res = run_bass_kernel_spmd(nc, maps, core_ids=list(range(8)))
    out = np.zeros((2, 8192, 1024), np.float32)
    for c in range(8):
        b, p = c // 4, c % 4
        out[b, 2048 * p:2048 * (p + 1)] = np.asarray(res.results[c]['out'], np.float32)
    if DEBUG:
        _CACHE['res'] = res
    return out
```

```python
import numpy as np
from contextlib import ExitStack
import concourse.bass as bass
import concourse.mybir as mybir
from concourse.bass_utils import run_bass_kernel_spmd

F32 = mybir.dt.float32
BF16 = mybir.dt.bfloat16
ALU = mybir.AluOpType
AF = mybir.ActivationFunctionType
AX = mybir.AxisListType
BANK = 30000
NDMA = 24
ENGS = ['pe', 'act', 'dve', 'pool', 'sp']
DEBUG = False
CFG = {'slots': None, 'p2': True, 'p3': True, 'stage': 99}

D_MODEL = 1024
D_IN = 3488
NSTATE = 52
NSLOT = 84
NKT = 68
NKEY = NKT * 128
KAPPA = -float(np.exp(-0.5))
ATTN_SCALE = 96 ** -0.5
NORM_EPS = 1e-6
LNX_EPS = 64e-5


class Trk:
    __slots__ = ('w', 'r')

    def __init__(self):
        self.w = None
        self.r = {}


class Buf:
    def __init__(self, t, ap=None):
        self.t = t
        self.k = Trk()
        self._ap = ap

    def __getitem__(self, idx):
        return (self._ap if self._ap is not None else self.t)[idx]

    @property
    def ap(self):
        return self._ap if self._ap is not None else self.t[:]


class Prog:
    def __init__(self, nc):
        self.nc = nc
        self.es = ExitStack()
        self.ops = {e: [] for e in ENGS}
        self.cnt = {e: 0 for e in ENGS}
        self.seen = {e: {} for e in ENGS}
        self.sems = {}
        self.dma_tot = [0] * NDMA
        self.dma_next = 0

    def sb(self, name, shape, dt=F32):
        return Buf(self.es.enter_context(self.nc.sbuf_tensor('sb_' + name, list(shape), dt)))

    def ps(self, name, shape, dt=F32):
        return Buf(self.es.enter_context(self.nc.psum_tensor('ps_' + name, list(shape), dt)))

    def view(self, ap):
        return Buf(None, ap)

    def _deps(self, eng, reads, writes):
        deps = {}

        def add(ev):
            if ev is None:
                return
            k, v = ev
            if deps.get(k, 0) < v:
                deps[k] = v
        for b in reads:
            add(b.k.w)
        for b in writes:
            add(b.k.w)
            for k, v in b.k.r.items():
                add((k, v))
        waits = []
        for k, v in deps.items():
            if k[0] == 'pe' and eng == 'pe':
                continue
            if self.seen[eng].get(k, 0) >= v:
                continue
            self.seen[eng][k] = v
            waits.append((k, v))
        return waits

    def _mark(self, ev, reads, writes):
        k, v = ev
        for b in writes:
            b.k.w = ev
            b.k.r = {}
        for b in reads:
            if b in writes:
                continue
            if b.k.r.get(k, 0) < v:
                b.k.r[k] = v

    def op(self, eng, fn, reads=(), writes=()):
        waits = self._deps(eng, reads, writes)
        i = self.cnt[eng]
        self.cnt[eng] += 1
        ev = ((eng, i // BANK), i % BANK + 1)
        self._mark(ev, reads, writes)
        self.ops[eng].append((waits, fn, ev[0], 1))

    def dma(self, out, in_, reads=(), writes=(), eng='sp', **kw):
        waits = self._deps(eng, reads, writes)
        s = self.dma_next % NDMA
        self.dma_next += 1
        k = ('dma', s)
        if self.dma_tot[s] > 0 and self.seen[eng].get(k, 0) < self.dma_tot[s]:
            self.seen[eng][k] = self.dma_tot[s]
            waits.append((k, self.dma_tot[s]))
        self.dma_tot[s] += 16
        ev = (k, self.dma_tot[s])
        self._mark(ev, reads, writes)
        self.ops[eng].append((waits, lambda e: e.dma_start(out=out, in_=in_, **kw), k, 16))

    def barrier(self):
        last = {}
        for e in ENGS:
            i = self.cnt[e]
            if i > 0:
                last[(e, (i - 1) // BANK)] = (i - 1) % BANK + 1
        for s in range(NDMA):
            if self.dma_tot[s] > 0:
                last[('dma', s)] = self.dma_tot[s]
        for e in ENGS:
            waits = []
            for k, v in last.items():
                if k[0] == e:
                    continue
                if self.seen[e].get(k, 0) >= v:
                    continue
                self.seen[e][k] = v
                waits.append((k, v))
            if waits:
                self.ops[e].append((waits, None, None, 0))

    def mm(self, out, lhsT, rhs, rd, wr, start=True, stop=True):
        self.op('pe', lambda e: e.matmul(out, lhsT, rhs, start=start, stop=stop), rd, wr)

    def tr(self, out, in_, ident, rd, wr):
        self.op('pe', lambda e: e.transpose(out, in_, ident), rd, wr)

    def act(self, out, in_, func, rd, wr, bias=None, scale=None):
        kw = {}
        if bias is not None:
            kw['bias'] = bias
        if scale is not None:
            kw['scale'] = scale
        self.op('act', lambda e: e.activation(out, in_, func, **kw), rd, wr)

    def tt(self, eng, out, a, b, op, rd, wr):
        self.op(eng, lambda e: e.tensor_tensor(out, a, b, op), rd, wr)

    def ts(self, eng, out, a, s1, s2, op0, op1, rd, wr):
        self.op(eng, lambda e: e.tensor_scalar(out, a, s1, s2, op0, op1), rd, wr)

    def ts1(self, eng, out, a, s, op, rd, wr):
        self.op(eng, lambda e: e.tensor_single_scalar(out, a, s, op), rd, wr)

    def stt(self, eng, out, a, s, b, op0, op1, rd, wr):
        self.op('dve', lambda e: e.scalar_tensor_tensor(out, a, s, b, op0, op1), rd, wr)

    def cp(self, eng, out, in_, rd, wr):
        if eng == 'act':
            self.op('act', lambda e: e.copy(out, in_), rd, wr)
        else:
            self.op(eng, lambda e: e.tensor_copy(out, in_), rd, wr)

    def recip(self, out, in_, rd, wr):
        self.op('dve', lambda e: e.reciprocal(out, in_), rd, wr)

    def red(self, out, in_, rd, wr, op=ALU.add):
        self.op('dve', lambda e: e.tensor_reduce(out, in_, AX.X, op), rd, wr)

    def memset(self, eng, ap, val, wr):
        self.op(eng, lambda e: e.memset(ap, val), (), wr)

    def finish(self):
        nc = self.nc
        keys = set()
        for e in ENGS:
            for waits, fn, k, inc in self.ops[e]:
                if k is not None:
                    keys.add(k)
                for wk, _ in waits:
                    keys.add(wk)
        for k in sorted(keys, key=str):
            self.sems[k] = self.es.enter_context(nc.semaphore("s_%s_%s" % k))
        fin = [(('dma', s), self.dma_tot[s]) for s in range(NDMA) if self.dma_tot[s] > 0]
        block = self.es.enter_context(nc.Block())

        def run(e, name):
            for waits, fn, k, inc in self.ops[name]:
                for wk, wv in waits:
                    e.wait_ge(self.sems[wk], wv)
                if fn is not None:
                    fn(e).then_inc(self.sems[k], inc)
            if name == 'sp':
                for wk, wv in fin:
                    e.wait_ge(self.sems[wk], wv)

        @block.tensor
        def _(e):
            run(e, 'pe')

        @block.scalar
        def _(e):
            run(e, 'act')

        @block.vector
        def _(e):
            run(e, 'dve')

        @block.gpsimd
        def _(e):
            run(e, 'pool')

        @block.sync
        def _(e):
            run(e, 'sp')
        self.es.close()


PC_NG, PC_ADAB, PC_MUP, PC_MUN, PC_KK, PC_KA, PC_RK, PC_A0, PC_QG, PC_KVG = 0, 8, 32, 46, 60, 64, 68, 72, 80, 83
PC_EPS, PC_LEPS, PC_ONE, PC_TINY = 85, 86, 87, 88
NPC = 96
C_ID, C_ONES, C_BO, C_F, C_B, C_TOT, C_BO2, C_SGN = 0, 128, 256, 384, 1280, 2176, 2177, 2179
SD_F, SD_SELW, SD_M2S, SD_PHI, SD_RHO, SD_SIG, SD_NF = 0, 1, 2, 3, 4, 5, 6
NCST = 2211

SLOTS = []
for _i in range(52):
    SLOTS.append(('S', _i))
for _j in range(16):
    SLOTS.append(('B', None))
for _j in range(16):
    SLOTS.append(('F', 52 + _j))


def _consts():
    c = np.zeros((128, NCST), np.float32)
    idx = np.arange(128)
    row, col = idx[:, None], idx[None, :]
    c[:, C_ID:C_ID + 128] = np.eye(128)
    c[:, C_ONES:C_ONES + 128] = 1.0
    c[:, C_BO:C_BO + 128] = ((row // 64) == (col // 64))
    k = KAPPA
    tf = [k * (row < col), -k * (row <= col), k * (row <= col), k * (row > col)]
    tb = [k * (row > col), -k * (row >= col), k * (row >= col), k * (row < col)]
    c[:, C_F:C_F + 896] = np.concatenate(tf + [col > row, col >= row, col < row], 1)
    c[:, C_B:C_B + 896] = np.concatenate(tb + [col < row, col <= row, col > row], 1)
    c[:, C_TOT] = k
    c[:, C_BO2] = (idx < 64)
    c[:, C_BO2 + 1] = (idx >= 64)
    d = np.arange(32)
    c[:, C_SGN:C_SGN + 32] = np.where(d % 16 < 8, -1.0, 1.0)[None, :]
    return c


ROT_PERM = np.array([dd + 8 if dd % 16 < 8 else dd - 8 for dd in range(32)])


def _rope_tables(pos):
    pos = np.asarray(pos)
    inv = (np.float32(10000.0) ** (-np.arange(0, 16, 2, dtype=np.float32) / np.float32(16))).astype(np.float32)
    rowf = (pos // 64).astype(np.float32)
    colf = (pos % 64).astype(np.float32)
    ar = (rowf[:, None] * inv[None, :]).astype(np.float32)
    ac = (colf[:, None] * inv[None, :]).astype(np.float32)
    cos = np.concatenate([np.cos(ar), np.cos(ar), np.cos(ac), np.cos(ac)], 1).astype(np.float32)
    sin = np.concatenate([np.sin(ar), np.sin(ar), np.sin(ac), np.sin(ac)], 1).astype(np.float32)
    return cos.T.copy(), sin.T.copy()


def _tile_halo(seq, c):
    n = seq.shape[0]
    t = np.zeros((130, seq.shape[1]), np.float32)
    t[:128] = seq[c * 128:(c + 1) * 128]
    hv = [0.0, 0.0]
    if c > 0:
        t[128] = seq[c * 128 - 1]
        hv[0] = 1.0
    if (c + 1) * 128 < n:
        t[129] = seq[(c + 1) * 128]
        hv[1] = 1.0
    return t, hv


def host_prep(inp):
    g = {k: np.asarray(v, np.float32) for k, v in inp.items()}
    sh = {}
    w_in = g['w_in'][0]
    sh['w_in'] = w_in
    sh['wkr_rot'] = np.ascontiguousarray(w_in[:, 2944:2976][:, ROT_PERM])
    w_uq = g['mla_w_uq'][0]
    sh['w_uq'] = w_uq
    sh['w_uq_rot'] = np.ascontiguousarray(
        np.stack([w_uq[:, h * 96 + 64:h * 96 + 96][:, ROT_PERM] for h in range(8)], 1))
    w_ukv = g['mla_w_ukv'][0].reshape(256, 8, 128)
    sh['w_uk'] = np.ascontiguousarray(w_ukv[:, :, :64])
    sh['w_uv'] = np.ascontiguousarray(w_ukv[:, :, 64:])
    sh['w_out'] = g['w_out'][0]
    sh['ada_w'] = g['ada_w'][0]
    sh['w2cat'] = np.ascontiguousarray(g['rw_w2'][0].reshape(128, 512))
    sh['a2cat'] = np.ascontiguousarray(g['rw_a2'][0].reshape(128, 512))
    sh['w0r'] = np.ascontiguousarray(g['rw_w0'][0].reshape(1, 1024))
    pc = np.zeros((128, NPC), np.float32)
    pc[:, PC_NG:PC_NG + 8] = g['norm_g'][0].reshape(8, 128).T
    pc[:, PC_ADAB:PC_ADAB + 24] = g['ada_b'][0].reshape(24, 128).T
    pc[:, PC_MUP:PC_MUP + 14] = g['shift_mu'][0, 0].reshape(14, 128).T
    pc[:, PC_MUN:PC_MUN + 14] = g['shift_mu'][0, 1].reshape(14, 128).T
    pc[:, PC_KK:PC_KK + 4] = g['rw_kk'][0].reshape(4, 128).T
    pc[:, PC_KA:PC_KA + 4] = g['rw_ka'][0].reshape(4, 128).T
    pc[:, PC_RK:PC_RK + 4] = g['rw_rk'][0].reshape(4, 128).T
    pc[:, PC_A0:PC_A0 + 4] = g['rw_a0'][0, 0].reshape(4, 128).T
    pc[:, PC_A0 + 4:PC_A0 + 8] = g['rw_a0'][0, 1].reshape(4, 128).T
    pc[:, PC_QG:PC_QG + 3] = g['mla_q_norm_g'][0].reshape(3, 128).T
    pc[:, PC_KVG:PC_KVG + 2] = g['mla_kv_norm_g'][0].reshape(2, 128).T
    pc[:, PC_EPS] = NORM_EPS
    pc[:, PC_LEPS] = LNX_EPS
    pc[:, PC_ONE] = 1.0
    pc[:, PC_TINY] = 1e-24
    sh['pcol'] = pc
    br = np.zeros((128, 2048), np.float32)
    br[:, 0:512] = g['rw_lnx_g'][0][None, :]
    br[:, 512:1024] = g['rw_lnx_b'][0][None, :]
    br[:, 1024:2048] = g['final_g'][None, :]
    sh['brow'] = br
    sh['cst'] = _consts()
    maps = []
    for b in range(2):
        x, ctx = g['x'][b], g['ctx'][b]
        cvec = np.stack([g['c'][b].reshape(128, 8), g['c_ctx'].reshape(128, 8)], -1)
        for p in range(4):
            nf = 2 + 16 * p
            fsrc = [(ctx, 0, True), (ctx, 1, True)] + [(x, c, False) for c in range(16 * p)]
            bsrc = [(ctx, 1, True), (ctx, 0, True)] + [(x, c, False) for c in range(63, 16 * p + 15, -1)]
            own = [(x, 16 * p + j, False) for j in range(16)]
            allsrc = fsrc + bsrc + own[::-1] + own
            assert len(allsrc) == NSLOT
            tiles, hvs = [], []
            for seq, c, _ in allsrc:
                t, hv = _tile_halo(seq, c)
                tiles.append(t)
                hvs.append(hv)
            xs = np.stack(tiles, 0)
            hvs = np.array(hvs, np.float32)
            sd = np.zeros((128, 7, NSLOT), np.float32)
            f = np.array([1.0] * nf + [0.0] * (52 - nf) + [0.0] * 16 + [1.0] * 16, np.float32)
            sd[:, SD_F, :] = f[None]
            sd[:, SD_NF, :] = 1.0 - f[None]
            sd[:64, SD_SELW, :] = f[None]
            sd[64:, SD_SELW, :] = 1.0 - f[None]
            sd[:, SD_M2S, :] = -2.0 * sd[:, SD_SELW, :]
            sd[:, SD_PHI, :] = np.array([1.0 if ic else 0.0 for _, _, ic in allsrc], np.float32)[None]
            rho = np.ones(NSLOT, np.float32)
            rho[nf] = 0.0
            sd[:, SD_RHO, :] = rho[None]
            sgm = np.zeros(NSLOT, np.float32)
            sgm[nf - 1] = 1.0
            sd[:, SD_SIG, :] = sgm[None]
            kvsrc = fsrc + bsrc + own
            rk = np.zeros((NKT, 2, 32, 128), np.float32)
            rk[:, 0] = 1.0
            kvv = np.ones(NKT, np.float32)
            kvv[nf] = 0.0
            kvv[nf + 1] = 0.0
            for kt, (seq, c, ic) in enumerate(kvsrc):
                if not ic:
                    cs, sn = _rope_tables(np.arange(c * 128, (c + 1) * 128))
                    rk[kt, 0], rk[kt, 1] = cs, sn
            cq, sq = _rope_tables(np.arange(2048 * p, 2048 * (p + 1)))
            m = dict(sh)
            m['xs'] = xs
            m['hv'] = np.ascontiguousarray(np.broadcast_to(hvs[None], (128, NSLOT, 2)))
            m['sd'] = sd
            m['kvv'] = np.ascontiguousarray(np.broadcast_to(kvv[None], (128, NKT)))
            m['ropek'] = rk
            m['ropeq'] = np.stack([cq, sq], 0)
            m['cvec'] = cvec
            maps.append(m)
    return maps


def build():
    nc = bass.Bass("TRN2", target_bir_lowering=False)
    P = Prog(nc)

    def din(name, shape):
        return nc.dram_tensor(name, list(shape), F32, kind="ExternalInput").ap()

    xs = din('xs', [NSLOT, 130, 1024])
    hv_d = din('hv', [128, NSLOT, 2])
    sd_d = din('sd', [128, 7, NSLOT])
    kvv_d = din('kvv', [128, NKT])
    ropek = din('ropek', [NKT, 2, 32, 128])
    ropeq = din('ropeq', [2, 32, 2048])
    cvec_d = din('cvec', [128, 8, 2])
    w_in_d = din('w_in', [1024, D_IN])
    wkr_rot_d = din('wkr_rot', [1024, 32])
    w_uq_d = din('w_uq', [384, 768])
    w_uq_rot_d = din('w_uq_rot', [384, 8, 32])
    w_uk_d = din('w_uk', [256, 8, 64])
    w_uv_d = din('w_uv', [256, 8, 64])
    w_out_d = din('w_out', [1024, 1024])
    ada_w_d = din('ada_w', [1024, 3072])
    w2cat_d = din('w2cat', [128, 512])
    a2cat_d = din('a2cat', [128, 512])
    w0r_d = din('w0r', [1, 1024])
    pcol_d = din('pcol', [128, NPC])
    brow_d = din('brow', [128, 2048])
    cst_d = din('cst', [128, NCST])
    out_d = nc.dram_tensor('out', [2048, 1024], F32, kind="ExternalOutput").ap()
    dbg = {}
    if DEBUG:
        for nm, shp in DEBUG_SPECS.items():
            dbg[nm] = nc.dram_tensor('dbg_' + nm, list(shp), F32, kind="ExternalOutput").ap()

    def dscr(name, shape, dt):
        t = nc.dram_tensor(name, list(shape), dt)
        return Buf(t, t.ap())

    CKVs = dscr('s_ckv', [128, 2, NKEY], BF16)
    KRs = dscr('s_kr', [32, NKEY], BF16)
    CQs = dscr('s_cq', [128, 3, 2048], BF16)
    GMs = dscr('s_gm', [64, 8, 2048], BF16)
    MIXRs = dscr('s_mixr', [128, 4, 2048], BF16)
    MIXMs = dscr('s_mixm', [64, 8, 2048], BF16)
    YBs = dscr('s_yb', [16, 128, 512], F32)
    SBs = dscr('s_sb', [16, 128, 8], F32)

    CST = P.sb('cst', [128, NCST])
    PCOL = P.sb('pcol', [128, NPC])
    LGB = P.sb('lgb', [128, 1024])
    HV = P.sb('hv', [128, NSLOT, 2])
    SD = P.sb('sd', [128, 7, NSLOT])
    KVV = P.sb('kvv', [128, NKT])
    CSEL = P.sb('csel', [128, 512])
    GSL = P.sb('gsl', [128, 4, 8])
    W0S = P.sb('w0s', [1, 512])
    NA0S = P.sb('na0s', [128, 4])
    W2C = P.sb('w2c', [128, 512])
    A2C = P.sb('a2c', [128, 512])
    W0R = P.sb('w0r', [1, 1024])
    MODT = P.sb('modt', [128, 24, 2])
    GS = P.sb('gs', [128, 2, 8])
    SH = P.sb('shc', [128, 2, 8])
    M0 = P.sb('m0', [128, 14])
    NA0 = P.sb('na0', [128, 8])
    ARENA = P.sb('arena', [128, 28432], BF16)
    WB = P.view(ARENA.t[:, 0:27904].rearrange("p (k c) -> p k c", k=8))
    WKR = P.sb('wkr', [128, 8, 2, 96], BF16)

    for (b_, d_) in [(CST, cst_d), (PCOL, pcol_d), (HV, hv_d), (SD, sd_d), (KVV, kvv_d), (W2C, w2cat_d), (A2C, a2cat_d),
                     (W0R, w0r_d)]:
        P.dma(b_.ap, d_, writes=[b_])
    P.dma(LGB[:, :], brow_d[:, 0:1024], writes=[LGB])
    ident = CST[:, C_ID:C_ID + 128]
    ones = CST[:, C_ONES:C_ONES + 128]

    def pc(c0, n=1):
        return PCOL[:, c0:c0 + n]

    def ps_pair(name):
        q = P.ps(name, [128, 512])
        v0, v1 = P.view(q.t[:, 0:256]), P.view(q.t[:, 256:512])
        v0.k = q.k
        v1.k = q.k
        return v0, v1, q
    pu = [P.ps('pu0', [128, 512]), P.ps('pu1', [128, 512])]
    pz = P.ps('pz', [128, 512])
    pt = pz
    pA = P.ps('pA', [128, 512])
    pB, pX, pBXf = ps_pair('qbx')
    pS, pM, _ = ps_pair('qsm')
    pY = P.ps('pY', [128, 512])
    pH, pG, _ = ps_pair('qhg')

    xt = [P.sb('xt0', [128, 1024]), P.sb('xt1', [128, 1024])]
    xh = [P.sb('xh0', [2, 1024])]
    junk = P.sb('junk', [128, 1024])
    tD = P.view(junk.t[:, 0:512])
    tE = P.view(junk.t[:, 512:1024])
    tD.k = junk.k
    tE.k = junk.k
    st4 = P.sb('st4', [128, 8])
    hT = [P.sb('hT0', [128, 8, 130], BF16), P.sb('hT1', [128, 8, 130], BF16)]
    shv = P.sb('shv', [128, 2, 8])
    ue = [P.sb('ue0', [128, 130]), P.sb('ue1', [128, 130])]
    RKV = P.sb('rkv', [128, 12, 128])
    WA = P.sb('wa', [128, 2, 128])
    tA = P.sb('tA', [128, 512])
    tB = P.sb('tB', [128, 512])
    tC = P.sb('tC', [128, 512])
    th0 = P.sb('th0', [128, 128])
    th1 = P.sb('th1', [128, 128])
    sg = P.sb('sg', [128, 512])
    EALL = P.sb('eall', [128, 4, 4, 128])
    STG = P.view(EALL.t[:, :, :, :].rearrange('p m a t -> p (m a t)')[:, 0:1792])
    STG.k = EALL.k
    GCL = [P.sb('gc0', [128, 4]), P.sb('gc1', [128, 4])]
    MSKL = [P.sb('msk0', [128, 384]), P.sb('msk1', [128, 384])]
    ARtL = [P.sb('art%d' % q_, [128, 4, 256]) for q_ in range(2)]
    BKtL = [P.sb('bkt%d' % q_, [128, 4, 256]) for q_ in range(2)]
    ARt, BKt = ARtL[0], BKtL[0]
    BKh = P.sb('bkh', [128, 1024])
    BhT = P.view(BKh.t[:, 0:512])
    KhT = P.view(BKh.t[:, 512:1024])
    BhT.k = BKh.k
    KhT.k = BKh.k
    AtokL = [P.sb('atok%d' % q_, [128, 512]) for q_ in range(2)]
    BhtokL = [P.sb('bhtok%d' % q_, [128, 512]) for q_ in range(2)]
    KhtokL = [P.sb('khtok%d' % q_, [128, 512]) for q_ in range(2)]
    VtokL = [P.sb('vtok%d' % q_, [128, 512]) for q_ in range(2)]
    AT = [P.sb('at0', [128, 512]), P.sb('at1', [128, 512])]
    XPBS = [[P.sb('xp%d_%d' % (p_, q_), [128, 320]) for q_ in range(3)] for p_ in range(2)]
    PTBS = [[P.sb('ptq%d_%d' % (p_, q_), [128, 128]) for q_ in range(2)] for p_ in range(2)]
    XPB = XPBS[0] + XPBS[1]
    MTs = P.sb('mts', [128, 4, 64])
    Ns = P.sb('ns', [128, 4, 64])
    GTs = [P.sb('gts0', [128, 128]), P.sb('gts1', [128, 128])]
    Hc = [P.sb('hc0', [128, 4, 64]), P.sb('hc1', [128, 4, 64])]
    HFa = P.sb('hfa', [128, 4, 64])
    ysb = P.sb('ysb', [128, 512])
    ybl = P.view(sg.t[:, :])
    ybl.k = sg.k
    sbl = P.sb('sbl', [128, 8])
    ssumL = [P.sb('ssum0', [128, 8]), P.sb('ssum1', [128, 8])]
    mla1 = P.sb('mla1', [128, 4, 128])
    mla2 = P.view(tC.t[:, :].rearrange('p (a t) -> p a t', a=4))
    mla2.k = tC.k
    kvb = P.sb('kvb', [128, 2, 128], BF16)
    krb = P.sb('krb', [96, 128], BF16)
    rpk = P.sb('rpk', [96, 2, 128])
    cqb = P.sb('cqb', [128, 3, 128], BF16)
    gmb = P.sb('gmb', [64, 8, 128], BF16)
    mixb = P.sb('mixb', [128, 4, 128], BF16)

    def dump(name, buf, ap):
        if DEBUG and name in dbg:
            P.dma(dbg[name], ap, reads=[buf])

    w_in_v = w_in_d.rearrange("(k p) c -> p k c", p=128)
    for kc in range(8):
        for hf in range(2):
            P.dma(STG[:, 0:1744], w_in_v[:, kc, hf * 1744:(hf + 1) * 1744], writes=[STG])
            P.cp(['act', 'dve'][hf], WB[:, kc, hf * 1744:(hf + 1) * 1744], STG[:, 0:1744], [STG], [WB])
    P.memset('pool', WKR[:, :, :, :], 0.0, [WKR])
    for kc in range(8):
        P.cp('dve', WKR[:, kc, 0, 64:96], WB[:, kc, 2944:2976], [WB], [WKR])
    wkr_v = wkr_rot_d.rearrange("(k p) c -> p k c", p=128)
    P.dma(junk[:, 0:256].rearrange("p (k c) -> p k c", k=8), wkr_v, writes=[junk])
    for kc in range(8):
        P.tt('dve', WKR[:, kc, 1, 64:96], junk[:, kc * 32:(kc + 1) * 32], CST[:, C_SGN:C_SGN + 32], ALU.mult,
             [junk, CST], [WKR])
    CV = P.sb('cv', [128, 8, 2])
    CV2 = P.sb('cv2', [128, 8, 2])
    P.dma(CV[:, :, :], cvec_d, writes=[CV])
    P.act(CV2[:, :, :], CV[:, :, :], AF.Exp, [CV], [CV2], scale=-1.0)
    P.ts1('dve', CV2[:, :, :], CV2[:, :, :], 1.0, ALU.add, [CV2], [CV2])
    P.recip(CV2[:, :, :], CV2[:, :, :], [CV2], [CV2])
    P.tt('dve', CV[:, :, :], CV[:, :, :], CV2[:, :, :], ALU.mult, [CV, CV2], [CV])
    ada_v = ada_w_d.rearrange("(p k) n -> p k n", k=8)
    MACC = P.sb('macc', [128, 48])
    P.memset('dve', MACC[:, :], 0.0, [MACC])
    for k in range(8):
        for hf in range(2):
            P.dma(STG[:, 0:1536], ada_v[:, k, hf * 1536:(hf + 1) * 1536], writes=[STG])
            for n in range(12):
                nn = hf * 12 + n
                P.mm(pz[:, 2 * nn:2 * nn + 2], STG[:, n * 128:(n + 1) * 128], CV[:, k, :], [STG, CV], [pz])
            P.tt('dve', MACC[:, hf * 24:(hf + 1) * 24], MACC[:, hf * 24:(hf + 1) * 24], pz[:, hf * 24:(hf + 1) * 24],
                 ALU.add, [MACC, pz], [MACC])
    for j in range(2):
        P.tt('dve', MODT[:, :, j], MACC[:, j:48:2], PCOL[:, PC_ADAB:PC_ADAB + 24], ALU.add, [MACC, PCOL], [MODT])
    for j in range(2):
        P.stt('dve', GS[:, j, :], MODT[:, 8:16, j], 1.0, PCOL[:, PC_NG:PC_NG + 8], ALU.add, ALU.mult,
              [MODT, PCOL], [GS])
        P.cp('dve', SH[:, j, :], MODT[:, 0:8, j], [MODT], [SH])
    P.cp('dve', GSL[:, 0, :], GS[:, 0, :], [GS], [GSL])
    P.tt('dve', GSL[:, 1, :], GS[:, 1, :], GS[:, 0, :], ALU.subtract, [GS], [GSL])
    P.cp('dve', GSL[:, 2, :], SH[:, 0, :], [SH], [GSL])
    P.tt('dve', GSL[:, 3, :], SH[:, 1, :], SH[:, 0, :], ALU.subtract, [SH], [GSL])
    P.tt('dve', M0[:, :], PCOL[:, PC_MUP:PC_MUP + 14], PCOL[:, PC_MUN:PC_MUN + 14], ALU.add, [PCOL], [M0])
    P.ts('dve', M0[:, :], M0[:, :], -1.0, 1.0, ALU.mult, ALU.add, [M0], [M0])
    P.ts1('dve', NA0[:, :], PCOL[:, PC_A0:PC_A0 + 8], -1.0, ALU.mult, [PCOL], [NA0])
    for b_ in XPB:
        P.memset('pool', b_[:, :], 0.0, [b_])
    P.memset('pool', Hc[0][:, :, :], 0.0, [Hc[0]])
    P.memset('pool', HFa[:, :, :], 0.0, [HFa])

    def load_slot(i):
        P.dma(xt[i % 2][:, :], xs[i, 0:128, :], writes=[xt[i % 2]])

    def rsqrt_ln(dst_buf, dst, src_buf, src, scale, eps_ap):
        P.act(dst, src, AF.Ln, [src_buf, PCOL], [dst_buf], bias=eps_ap, scale=scale)
        P.act(dst, dst, AF.Exp, [dst_buf], [dst_buf], scale=-0.5)

    state = {'h': 0}

    PCNT = {'n': 0}
    ADV = 1

    def proj_for(hT_, cnt):
        def proj(c0, width, ncols=128, lhs=None):
            pb_ = pu[cnt['n'] % 2]
            cnt['n'] += 1
            for kc in range(8):
                l = WB[:, kc, c0:c0 + width] if lhs is None else lhs(kc)
                P.mm(pb_[0:(width if lhs is None else 96), 0:ncols], l, hT_[:, kc, 0:ncols],
                     [WB if lhs is None else WKR, hT_], [pb_], start=(kc == 0), stop=(kc == 7))
            return pb_
        return proj

    def prep(i):
        kind, kv = SLOTS[i]
        S = i % 2
        ARt, BKt, Atok, Bhtok, Khtok, Vtok = ARtL[S], BKtL[S], AtokL[S], BhtokL[S], KhtokL[S], VtokL[S]
        GC, MSK, ssum = GCL[S], MSKL[S], ssumL[S]
        x_, xh_, hT_ = xt[i % 2], xh[0], hT[i % 2]
        P.dma(xh_[:, :], xs[i, 128:130, :], writes=[xh_])
        fcol = SD[:, SD_F, i:i + 1]
        nfcol = SD[:, SD_NF, i:i + 1]
        phi = SD[:, SD_PHI, i:i + 1]
        P.stt('dve', GS[:, 0, :], GSL[:, 1, :], phi, GSL[:, 0, :], ALU.mult, ALU.add, [GSL, SD], [GS])
        P.stt('dve', SH[:, 0, :], GSL[:, 3, :], phi, GSL[:, 2, :], ALU.mult, ALU.add, [GSL, SD], [SH])
        P.ts1('dve', CSEL[:, :], CST[:, C_F:C_F + 512], fcol, ALU.mult, [CST, SD], [CSEL])
        P.stt('dve', CSEL[:, :], CST[:, C_B:C_B + 512], nfcol, CSEL[:, :], ALU.mult, ALU.add, [CST, SD, CSEL], [CSEL])
        P.ts1('dve', MSK[:, :], CST[:, C_F + 512:C_F + 896], fcol, ALU.mult, [CST, SD], [MSK])
        P.stt('dve', MSK[:, :], CST[:, C_B + 512:C_B + 896], nfcol, MSK[:, :], ALU.mult, ALU.add, [CST, SD, MSK], [MSK])
        P.ts1('dve', W0S[0:1, :], W0R[0:1, 0:512], SD[0:1, SD_F, i:i + 1], ALU.mult, [W0R, SD], [W0S])
        P.stt('dve', W0S[0:1, :], W0R[0:1, 512:1024], SD[0:1, SD_NF, i:i + 1], W0S[0:1, :], ALU.mult, ALU.add,
              [W0R, SD, W0S], [W0S])
        P.ts1('dve', NA0S[:, :], NA0[:, 0:4], fcol, ALU.mult, [NA0, SD], [NA0S])
        P.stt('dve', NA0S[:, :], NA0[:, 4:8], nfcol, NA0S[:, :], ALU.mult, ALU.add, [NA0, SD, NA0S], [NA0S])
        j = 0
        P.act(junk[:, :], x_[:, :], AF.Square, [x_], [junk])
        P.red(st4[:, 0:1], junk[:, :], [junk], [st4])
        rsqrt_ln(st4, st4[:, 0:1], st4, st4[:, 0:1], 1.0 / 1024, pc(PC_EPS))
        P.ts1('dve', x_[:, :], x_[:, :], st4[:, 0:1], ALU.mult, [x_, st4], [x_])
        for half in range(2):
            yield
            for q in range(4):
                kc = half * 4 + q
                P.tr(pt[:, q * 128:(q + 1) * 128], x_[:, kc * 128:(kc + 1) * 128], ident, [x_, CST], [pt])
            for q in range(4):
                kc = half * 4 + q
                if q % 2 == 0:
                    P.act(hT_[:, kc, 0:128], pt[:, q * 128:(q + 1) * 128], AF.Identity, [pt, GS, SH], [hT_],
                          bias=SH[:, j, kc:kc + 1], scale=GS[:, j, kc:kc + 1])
                else:
                    P.ts('dve', hT_[:, kc, 0:128], pt[:, q * 128:(q + 1) * 128], GS[:, j, kc:kc + 1],
                         SH[:, j, kc:kc + 1], ALU.mult, ALU.add, [pt, GS, SH], [hT_])
        yield
        P.act(junk[0:2, :], xh_[:, :], AF.Square, [xh_], [junk])
        P.red(st4[0:2, 1:2], junk[0:2, :], [junk], [st4])
        rsqrt_ln(st4, st4[0:2, 1:2], st4, st4[0:2, 1:2], 1.0 / 1024, PCOL[0:2, PC_EPS:PC_EPS + 1])
        P.ts1('dve', xh_[:, :], xh_[:, :], st4[0:2, 1:2], ALU.mult, [xh_, st4], [xh_])
        yield
        for kc in range(8):
            P.tr(pt[:, 2 * kc:2 * kc + 2], xh_[0:2, kc * 128:(kc + 1) * 128], CST[0:2, C_ID:C_ID + 2], [xh_, CST], [pt])
        for c in range(2):
            P.ts1('dve', shv[:, c, :], SH[:, j, :], HV[:, i, c:c + 1], ALU.mult, [SH, HV], [shv])
            P.tt('dve', junk[:, 8 * c:8 * c + 8], pt[:, c:16:2], GS[:, j, :], ALU.mult, [pt, GS], [junk])
            P.tt('dve', hT_[:, :, 128 + c], junk[:, 8 * c:8 * c + 8], shv[:, c, :], ALU.add, [junk, shv], [hT_])

        yield
        proj = proj_for(hT_, PCNT)

        for jc in range(14):
            if kind == 'S' and jc < 4:
                continue
            pb_ = proj(jc * 128, 128, 130)
            ue_ = ue[jc % 2]
            P.cp('act', ue_[:, :], pb_[:, 0:130], [pb_], [ue_])
            yield
            if jc < 12:
                dstb, dst = RKV, RKV[:, jc, :]
            else:
                dstb, dst = WA, WA[:, jc - 12, :]
            eng = 'pool' if jc % 2 == 0 else 'dve'
            mp_, mn_ = pc(PC_MUP + jc), pc(PC_MUN + jc)
            P.ts1(eng, dst, ue_[:, 0:128], M0[:, jc:jc + 1], ALU.mult, [ue_, M0], [dstb])
            for (dsl, ssl, mu_) in [((1, 128), (0, 127), mp_), ((0, 1), (128, 129), mp_),
                                    ((0, 127), (1, 128), mn_), ((127, 128), (129, 130), mn_)]:
                dd = dst[:, dsl[0]:dsl[1]]
                if eng == 'dve':
                    P.stt('dve', dd, ue_[:, ssl[0]:ssl[1]], mu_, dd, ALU.mult, ALU.add, [ue_, PCOL, dstb], [dstb])
                else:
                    tp = th1[:, 0:dsl[1] - dsl[0]]
                    P.ts1('pool', tp, ue_[:, ssl[0]:ssl[1]], mu_, ALU.mult, [ue_, PCOL], [th1])
                    P.tt('pool', dd, dd, tp, ALU.add, [dstb, th1], [dstb])
            yield
        yield
        r3 = RKV[:, 0:4, :]
        k3 = RKV[:, 4:8, :]
        v3 = RKV[:, 8:12, :]

        def f3(b_):
            return b_[:, :].rearrange("p (m t) -> p m t", m=4)

        for m in range(4):
            P.ts1('pool', tA[:, m * 128:(m + 1) * 128], RKV[:, 4 + m, :], pc(PC_KK + m), ALU.mult, [RKV, PCOL], [tA])
        P.act(tB[:, :], tA[:, :], AF.Square, [tA], [tB])
        P.mm(pz[:, :], CST[:, C_BO:C_BO + 128], tB[:, :], [CST, tB], [pz])
        yield
        P.ts1('dve', tB[:, :], pz[:, :], 1e-24, ALU.max, [pz], [tB])
        P.act(tB[:, :], tB[:, :], AF.Ln, [tB], [tB])
        P.act(tB[:, :], tB[:, :], AF.Exp, [tB], [tB], scale=-0.5)
        P.tt('dve', tA[:, :], tA[:, :], tB[:, :], ALU.mult, [tA, tB], [tA])
        yield
        P.act(th0[:, :], WA[:, 0, :], AF.Exp, [WA], [th0], scale=2.0)
        P.ts1('dve', th0[:, :], th0[:, :], 1.0, ALU.add, [th0], [th0])
        P.recip(th0[:, :], th0[:, :], [th0], [th0])
        P.ts('dve', th0[:, :], th0[:, :], SD[:, SD_M2S, i:i + 1], SD[:, SD_SELW, i:i + 1], ALU.mult, ALU.add,
             [th0, SD], [th0])
        P.mm(pz[:, :], th0[:, :], W2C[:, :], [th0, W2C], [pz], start=True, stop=False)
        P.mm(pz[:, :], CST[0:1, C_ONES:C_ONES + 128], W0S[0:1, :], [CST, W0S], [pz], start=False, stop=True)
        yield
        P.act(sg[:, :], pz[:, :], AF.Exp, [pz], [sg], scale=-1.0)
        P.ts1('dve', sg[:, :], sg[:, :], 1.0, ALU.add, [sg], [sg])
        P.recip(sg[:, :], sg[:, :], [sg], [sg])
        yield
        tri = CSEL[:, 0:512]
        for m in range(4):
            P.mm(pz[:, :], sg[:, m * 128:(m + 1) * 128], tri, [sg, CSEL], [pz])
            P.act(EALL[:, m, :, :], pz[:, :].rearrange("p (a t) -> p a t", a=4), AF.Exp, [pz], [EALL])
            yield
        P.tt('dve', GC[:, :], EALL[:, :, 2, 0], EALL[:, :, 3, 0], ALU.mult, [EALL], [GC])
        yield
        P.ts1('dve', th1[:, :], WA[:, 1, :], SD[:, SD_SELW, i:i + 1], ALU.mult, [WA, SD], [th1])
        for m in range(4):
            P.mm(pz[:, m * 128:(m + 1) * 128], A2C[:, m * 128:(m + 1) * 128], th1[:, :], [A2C, th1], [pz])
        for m in range(4):
            P.act(tB[:, m * 128:(m + 1) * 128], pz[:, m * 128:(m + 1) * 128], AF.Exp, [pz, NA0S], [tB],
                  bias=NA0S[:, m:m + 1], scale=-1.0)
        yield
        P.ts1('dve', tB[:, :], tB[:, :], 1.0, ALU.add, [tB], [tB])
        P.recip(tB[:, :], tB[:, :], [tB], [tB])
        yield
        for m in range(4):
            P.ts('pool', tC[:, m * 128:(m + 1) * 128], tB[:, m * 128:(m + 1) * 128], -1.0, pc(PC_KA + m),
                 ALU.add, ALU.mult, [tB, PCOL], [tC])
        P.stt('dve', tC[:, :], tC[:, :], 1.0, RKV[:, 4:8, :].rearrange("p m t -> p (m t)"), ALU.add, ALU.mult,
              [tC, RKV], [tC])
        P.tt('pool', tD[:, :], tA[:, :], tB[:, :], ALU.mult, [tA, tB], [tD])
        if kind != 'S':
            for m in range(4):
                P.stt('pool', tE[:, m * 128:(m + 1) * 128], RKV[:, m, :], pc(PC_RK + m),
                      tC[:, m * 128:(m + 1) * 128], ALU.mult, ALU.mult, [RKV, PCOL, tC], [tE])
            for m in range(4):
                P.mm(pz[:, 2 * m:2 * m + 2], tE[:, m * 128:(m + 1) * 128], CST[:, C_BO2:C_BO2 + 2],
                     [tE, CST], [pz])
            P.cp('act', ssum[:, :], pz[:, 0:8], [pz], [ssum])
        yield
        E = [EALL[:, :, a, :] for a in range(4)]
        A3 = ARt[:, :, :].rearrange("p m (a t) -> p m a t", a=2)
        B3 = BKt[:, :, :].rearrange("p m (a t) -> p m a t", a=2)
        P.stt('dve', A3[:, :, 0, :], f3(tA), -1.0, E[0], ALU.mult, ALU.mult, [tA, EALL], [ARt])
        if kind != 'S':
            P.tt('pool', A3[:, :, 1, :], r3, E[2], ALU.mult, [RKV, EALL], [ARt])
        yield
        P.tt('dve', B3[:, :, 0, :], f3(tD), E[1], ALU.mult, [tD, EALL], [BKt])
        P.tt('pool', B3[:, :, 1, :], f3(tC), E[1], ALU.mult, [tC, EALL], [BKt])
        yield
        P.tt('dve', f3(BhT), f3(tD), E[3], ALU.mult, [tD, EALL], [BhT])
        P.tt('pool', f3(KhT), f3(tC), E[3], ALU.mult, [tC, EALL], [KhT])
        for (srcb, srcf, dstb) in [(ARt, lambda m: ARt[:, m, 0:128], Atok), (BhT, lambda m: BhT[:, m * 128:(m + 1) * 128], Bhtok),
                                   (KhT, lambda m: KhT[:, m * 128:(m + 1) * 128], Khtok), (RKV, lambda m: RKV[:, 8 + m, :], Vtok)]:
            for m in range(4):
                P.tr(pt[:, m * 128:(m + 1) * 128], srcf(m), ident, [srcb, CST], [pt])
            P.cp('act', dstb[:, :], pt[:, :], [pt], [dstb])
            yield

        yield
        if kv is not None:
            pb2 = proj(2688, 128, 128)
            P.cp('act', mla1[:, 0, :], pb2[:, 0:128], [pb2], [mla1])
            pb2 = proj(2816, 128, 128)
            P.cp('act', mla1[:, 1, :], pb2[:, 0:128], [pb2], [mla1])
            P.act(mla2[:, 0:2, :], mla1[:, 0:2, :], AF.Square, [mla1], [mla2])
            P.mm(pz[:, 0:128], ones, mla2[:, 0, :], [CST, mla2], [pz], start=True, stop=False)
            P.mm(pz[:, 0:128], ones, mla2[:, 1, :], [CST, mla2], [pz], start=False, stop=True)
            rsqrt_ln(mla2, mla2[:, 2, :], pz, pz[:, 0:128], 1.0 / 256, pc(PC_EPS))
            for c in range(2):
                P.stt('dve', kvb[:, c, :], mla1[:, c, :], pc(PC_KVG + c), mla2[:, 2, :], ALU.mult, ALU.mult,
                      [mla1, PCOL, mla2], [kvb])
            P.dma(CKVs[:, :, kv * 128:(kv + 1) * 128], kvb[:, :, :], reads=[kvb], writes=[CKVs])
            yield
            P.dma(rpk[64:96, :, :], ropek[kv].rearrange("a d t -> d a t"), writes=[rpk])
            pk1 = proj(0, 96, 128, lhs=lambda kc: WKR[:, kc, 0, :])
            yield
            pk2 = proj(0, 96, 128, lhs=lambda kc: WKR[:, kc, 1, :])
            P.tt('dve', mla1[64:96, 2, :], pk1[64:96, 0:128], rpk[64:96, 0, :], ALU.mult, [pk1, rpk], [mla1])
            P.tt('dve', mla1[64:96, 3, :], pk2[64:96, 0:128], rpk[64:96, 1, :], ALU.mult, [pk2, rpk], [mla1])
            P.tt('dve', krb[64:96, :], mla1[64:96, 2, :], mla1[64:96, 3, :], ALU.add, [mla1], [krb])
            P.dma(KRs[:, kv * 128:(kv + 1) * 128], krb[64:96, :], reads=[krb], writes=[KRs])

        yield

    def heads(i, gen):
        kind, kv = SLOTS[i]
        S = i % 2
        ARt, BKt, Atok, Bhtok, Khtok, Vtok = ARtL[S], BKtL[S], AtokL[S], BhtokL[S], KhtokL[S], VtokL[S]
        GC, MSK, ssum = GCL[S], MSKL[S], ssumL[S]
        hT_ = hT[i % 2]
        proj = proj_for(hT_, PCNT)

        def f3(b_):
            return b_[:, :].rearrange("p (m t) -> p m t", m=4)

        def advance(n):
            if gen is None:
                return
            for _ in range(n):
                try:
                    next(gen)
                except StopIteration:
                    return
        Hin = Hc[state['h'] % 2]
        Hout = Hc[(state['h'] + 1) % 2]
        mskt = MSK[:, 0:256]
        mskn = MSK[:, 256:384]
        if kind == 'S':
            P.ts1('dve', Hin[:, :, :], Hin[:, :, :], SD[:, SD_RHO, i:i + 1], ALU.mult, [Hin, SD], [Hin])
        own = kind != 'S'

        def head_pre(h):
            m, pb = h // 2, 64 * (h % 2)
            par = h % 2
            cb = 64 * h
            at_, xpb, ptb = AT[par], XPBS[par], PTBS[par]
            Bt_h = BKt[pb:pb + 64, m, 0:128]
            Kt_h = BKt[pb:pb + 64, m, 128:256]
            AR_h = ARt[pb:pb + 64, m, 0:256]
            At_h = ARt[pb:pb + 64, m, 0:128]
            XP = xpb[0]
            pt0 = ptb[0]
            if own:
                P.mm(pA[:, 0:256], Bt_h, AR_h, [BKt, ARt], [pA])
                P.mm(pA[:, 256:512], Kt_h, AR_h, [BKt, ARt], [pA])
                P.mm(pB[:, 0:128], At_h, Bt_h, [ARt, BKt], [pB])
                an_src, an_buf = pB[:, 0:128], pB
            else:
                P.mm(pA[:, 0:128], Bt_h, At_h, [BKt, ARt], [pA])
                P.mm(pA[:, 256:384], Kt_h, At_h, [BKt, ARt], [pA])
                P.mm(pA[:, 128:256], At_h, Bt_h, [ARt, BKt], [pA])
                an_src, an_buf = pA[:, 128:256], pA
            P.tt('dve', pt0[:, :], pA[:, 0:128], mskt[:, 0:128], ALU.mult, [pA, MSK], [pt0])
            if own:
                P.tt('dve', at_[:, 128:256], pA[:, 128:256], mskt[:, 128:256], ALU.mult, [pA, MSK], [at_])
                P.tt('dve', at_[:, 256:512], pA[:, 256:512], mskt, ALU.mult, [pA, MSK], [at_])
            else:
                P.tt('dve', at_[:, 256:384], pA[:, 256:384], mskt[:, 0:128], ALU.mult, [pA, MSK], [at_])
            P.tt('dve', XP[:, 192:320], an_src, mskn, ALU.mult, [an_buf, MSK], [XP])
            P.cp('act', XP[:, 64:128], Atok[:, cb:cb + 64], [Atok], [XP])
            yield
            if own:
                wdst, wbuf = pB[:, 128:192], pB
            else:
                wdst, wbuf = pA[:, 384:448], pA
            P.mm(wdst, at_[:, 256:384], Vtok[:, cb:cb + 64], [at_, Vtok], [wbuf])
            P.cp('act', XP[:, 128:192], wdst, [wbuf], [XP])
            yield

        def head_main(h, nxt):
            m, pb = h // 2, 64 * (h % 2)
            par = h % 2
            cb = 64 * h
            at_, xpb, ptb = AT[par], XPBS[par], PTBS[par]
            xi = 0
            XP = xpb[0]
            PT_ = ptb[0]
            for lvl in range(7):
                last = lvl == 6
                XPn = xpb[(xi + 1) % 3]
                P.mm(pX[:, 0:(128 if last else 256)], PT_[:, :], XP[:, 64:(192 if last else 320)], [PT_, XP], [pX])
                if not last:
                    P.mm(pS[:, 0:128], XP[:, 192:320], PT_[:, :], [XP, PT_], [pS])
                if nxt is not None and lvl in (1, 3):
                    next(nxt, None)
                P.tt('dve', XPn[:, 64:192], XP[:, 64:192], pX[:, 0:128], ALU.add, [XP, pX], [XPn])
                if not last:
                    P.cp('act', XPn[:, 192:320], pX[:, 128:256], [pX], [XPn])
                    ptn = ptb[(lvl + 1) % 2]
                    P.cp('act', ptn[:, :], pS[:, 0:128], [pS], [ptn])
                    PT_ = ptn
                xi += 1
                XP = XPn
                advance(1)
            X = XP
            lx = X[:, 64:128] if pb == 0 else X[:, 0:128]
            P.mm(pM[0:(64 if pb == 0 else 128), 0:64], lx, Bhtok[:, cb:cb + 64], [X, Bhtok], [pM])
            P.mm(pM[:, 64:128], Bhtok[:, m * 128:(m + 1) * 128], X[:, 128:192], [Bhtok, X], [pM], start=True, stop=False)
            P.mm(pM[:, 64:128], Khtok[:, m * 128:(m + 1) * 128], Vtok[:, cb:cb + 64], [Khtok, Vtok], [pM],
                 start=False, stop=True)
            P.stt('dve', MTs[pb:pb + 64, m, :], CST[pb:pb + 64, C_ID + pb:C_ID + pb + 64], GC[pb:pb + 64, m:m + 1],
                  pM[pb:pb + 64, 0:64], ALU.mult, ALU.add, [CST, GC, pM], [MTs])
            P.cp('act', Ns[pb:pb + 64, m, :], pM[pb:pb + 64, 64:128], [pM], [Ns])
            if own:
                gt_ = GTs[h % 2]
                P.mm(pG[0:(64 if pb == 0 else 128), 0:128], lx, at_[:, 128:256], [X, at_], [pG])
                P.tt('dve', gt_[pb:pb + 64, :], pG[pb:pb + 64, 0:128], ARt[pb:pb + 64, m, 128:256], ALU.add,
                     [pG, ARt], [gt_])
                yo = pY[:, cb:cb + 64]
                P.mm(yo, at_[:, 128:256], X[:, 128:192], [at_, X], [pY], start=True, stop=False)
                P.mm(yo, at_[:, 384:512], Vtok[:, cb:cb + 64], [at_, Vtok], [pY], start=False, stop=False)
                P.mm(yo, gt_[pb:pb + 64, :], Hin[pb:pb + 64, m, :], [gt_, Hin], [pY], start=False, stop=True)
            P.mm(pH[pb:pb + 64, m * 64:(m + 1) * 64], MTs[pb:pb + 64, m, :], Hin[pb:pb + 64, m, :], [MTs, Hin], [pH])
            advance(ADV)

        for _ in head_pre(0):
            pass
        for h in range(8):
            nxt = head_pre(h + 1) if h + 1 < 8 else None
            head_main(h, nxt)
            if nxt is not None:
                for _ in nxt:
                    pass
        P.tt('dve', Hout[:, :, :], pH[:, :].rearrange("p (m i) -> p m i", m=4), Ns[:, :, :], ALU.add, [pH, Ns], [Hout])
        if i == CFG.get('dump_slot', 0):
            dump('hout', Hout, Hout[:, :, :].rearrange('p m i -> p (m i)'))
        state['h'] += 1
        advance(10 ** 6)
        if kind == 'S':
            P.stt('dve', HFa[:, :, :], Hout[:, :, :], SD[:, SD_SIG, i:i + 1], HFa[:, :, :], ALU.mult, ALU.add,
                  [Hout, SD, HFa], [HFa])

        if kind == 'B':
            jj = 15 - (i - 52)
            P.cp('act', ysb[:, :], pY[:, :], [pY], [ysb])
            P.dma(YBs[jj], ysb[:, :], reads=[ysb], writes=[YBs])
            P.dma(SBs[jj], ssum[:, :], reads=[ssum], writes=[SBs])
        if kind == 'F':
            jj = i - 68
            tok = slice(jj * 128, (jj + 1) * 128)
            P.dma(ybl[:, :], YBs[jj], reads=[YBs], writes=[ybl])
            P.dma(sbl[:, :], SBs[jj], reads=[SBs], writes=[sbl])
            P.tt('dve', ysb[:, :], pY[:, :], ybl[:, :], ALU.add, [pY, ybl], [ysb])
            P.tt('dve', ssum[:, :], ssum[:, :], sbl[:, :], ALU.add, [ssum, sbl], [ssum])
            y3 = ysb[:, :].rearrange("p (h i) -> p h i", h=8)
            P.red(st4[:, :], y3, [ysb], [st4])
            P.ts1('dve', st4[:, :], st4[:, :], 1.0 / 64, ALU.mult, [st4], [st4])
            P.act(tE[:, :], ysb[:, :], AF.Square, [ysb], [tE])
            P.red(sbl[:, :], tE[:, :].rearrange("p (h i) -> p h i", h=8), [tE], [sbl])
            P.tt('dve', ybl[:, 0:8], st4[:, :], st4[:, :], ALU.mult, [st4], [ybl])
            P.stt('dve', sbl[:, :], sbl[:, :], 1.0 / 64, ybl[:, 0:8], ALU.mult, ALU.subtract, [sbl, ybl], [sbl])
            rsqrt_ln(sbl, sbl[:, :], sbl, sbl[:, :], 1.0, pc(PC_LEPS))
            for h in range(8):
                P.ts('dve' if h % 2 else 'pool', ysb[:, h * 64:(h + 1) * 64], ysb[:, h * 64:(h + 1) * 64],
                     st4[:, h:h + 1], sbl[:, h:h + 1], ALU.subtract, ALU.mult, [ysb, st4, sbl], [ysb])
            P.tt('dve', ysb[:, :], ysb[:, :], LGB[:, 0:512], ALU.mult, [ysb, LGB], [ysb])
            P.tt('pool', ysb[:, :], ysb[:, :], LGB[:, 512:1024], ALU.add, [ysb, LGB], [ysb])
            for h in range(8):
                P.stt('dve' if h % 2 else 'pool', ysb[:, h * 64:(h + 1) * 64], Vtok[:, h * 64:(h + 1) * 64],
                      ssum[:, h:h + 1], ysb[:, h * 64:(h + 1) * 64], ALU.mult, ALU.add, [Vtok, ssum, ysb], [ysb])
            for m in range(4):
                pb2 = proj(1792 + m * 128, 128, 128)
                P.cp('act', tE[:, m * 128:(m + 1) * 128], pb2[:, 0:128], [pb2], [tE])
            P.act(tD[:, :], tE[:, :], AF.Exp, [tE], [tD], scale=-1.0)
            P.ts1('dve', tD[:, :], tD[:, :], 1.0, ALU.add, [tD], [tD])
            P.recip(tD[:, :], tD[:, :], [tD], [tD])
            P.tt('pool', tE[:, :], tE[:, :], tD[:, :], ALU.mult, [tE, tD], [tE])
            for m in range(4):
                P.tr(pt[:, m * 128:(m + 1) * 128], ysb[:, m * 128:(m + 1) * 128], ident, [ysb, CST], [pt])
            P.tt('dve', mixb[:, :, :], pt[:, :].rearrange("p (m t) -> p m t", m=4), f3(tE), ALU.mult, [pt, tE], [mixb])
            P.dma(MIXRs[:, :, tok], mixb[:, :, :], reads=[mixb], writes=[MIXRs])
            for c in range(3):
                pb2 = proj(2304 + c * 128, 128, 128)
                P.cp('act', mla1[:, c, :], pb2[:, 0:128], [pb2], [mla1])
            P.act(mla2[:, 0:3, :], mla1[:, 0:3, :], AF.Square, [mla1], [mla2])
            for c in range(3):
                P.mm(pz[:, 0:128], ones, mla2[:, c, :], [CST, mla2], [pz], start=(c == 0), stop=(c == 2))
            rsqrt_ln(mla2, mla2[:, 3, :], pz, pz[:, 0:128], 1.0 / 384, pc(PC_EPS))
            for c in range(3):
                P.stt('dve', cqb[:, c, :], mla1[:, c, :], pc(PC_QG + c), mla2[:, 3, :], ALU.mult, ALU.mult,
                      [mla1, PCOL, mla2], [cqb])
            P.dma(CQs[:, :, tok], cqb[:, :, :], reads=[cqb], writes=[CQs])
            for h in range(8):
                pb2 = proj(2976 + h * 64, 64, 128)
                P.cp('act', tD[0:64, 0:128], pb2[0:64, 0:128], [pb2], [tD])
                P.act(tD[0:64, 128:256], tD[0:64, 0:128], AF.Exp, [tD], [tD], scale=-1.0)
                P.ts1('dve', tD[0:64, 128:256], tD[0:64, 128:256], 1.0, ALU.add, [tD], [tD])
                P.recip(tD[0:64, 128:256], tD[0:64, 128:256], [tD], [tD])
                P.tt('dve', gmb[:, h, :], tD[0:64, 0:128], tD[0:64, 128:256], ALU.mult, [tD], [gmb])
            P.dma(GMs[:, :, tok], gmb[:, :, :], reads=[gmb], writes=[GMs])

    slot_list = list(range(NSLOT)) if CFG['slots'] is None else list(CFG['slots'])
    if slot_list:
        load_slot(slot_list[0])
        if len(slot_list) > 1:
            load_slot(slot_list[1])
        for _ in prep(slot_list[0]):
            pass
    for si, i in enumerate(slot_list):
        if si + 2 < len(slot_list):
            load_slot(slot_list[si + 2])
        if i == 68:
            P.cp('dve', Hc[state['h'] % 2][:, :, :], HFa[:, :, :], [HFa], [Hc[state['h'] % 2]])
        heads(i, prep(slot_list[si + 1]) if si + 1 < len(slot_list) else None)

    AR = ARENA.t
    o = 0

    def carve(n, shape_str=None, **kw):
        nonlocal o
        ap = AR[:, o:o + n]
        o += n
        if shape_str:
            ap = ap.rearrange(shape_str, **kw)
        return P.view(ap)

    ps2 = [pA, pY]

    def phase2():
        nonlocal o
        P.barrier()
        o = 0

        KT = [carve(NKEY)] * 2
        V4 = carve(NKT * 4 * 65, "p (k h c) -> p k h c", k=NKT, h=4)
        QT = [carve(2048)] * 2
        def bview(buf, nbf, pat=None, rows=128, **kw):
            t = buf.t
            flat = t[0:rows]
            nd = len(t.shape)
            if nd == 3:
                flat = t[0:rows, :, :].rearrange("p a b -> p (a b)")
            elif nd == 4:
                flat = t[0:rows, :, :, :].rearrange("p a b c -> p (a b c)")
            ap = flat[:, 0:nbf // 2].bitcast(BF16)
            if pat:
                ap = ap.rearrange(pat, **kw)
            return P.view(ap)
        WUQ = bview(RKV, 2304, "p (k c) -> p k c", k=3)
        WUQR = bview(sg, 288, "p (k c) -> p k c", k=3)
        WROT = bview(ysb, 768, "p (k h c) -> p k h c", k=3, h=8)
        WUK = bview(AtokL[0], 1024, "p (k c) -> p k c", k=2)
        WUV = bview(BhtokL[0], 1024, "p (k c) -> p k c", k=2)
        ckb = [bview(KhtokL[0], 1024, "p (k c) -> p k c", k=2), bview(VtokL[0], 1024, "p (k c) -> p k c", k=2)]
        PTb_ = [bview(AtokL[1], 512), bview(BhtokL[1], 512), bview(KhtokL[1], 512)]
        rq = P.view(ARt.t[:, :, :].rearrange('p m c -> p (m c)').rearrange('p (a t) -> p a t', a=2))
        onrm = P.view(tA.t[0:64, :])
        rec = P.view(tB.t[0:65, :])
        bcs = P.view(tC.t[0:64, :])
        mxb = bview(XPB[0], 512, rows=64)
        gml = bview(XPB[1], 512, rows=64)
        cql = bview(junk, 1536, "p (k c) -> p k c", k=3)
        w_uq_v = w_uq_d.rearrange("(k p) c -> p k c", p=128)
        w_uqr_v = w_uq_rot_d.rearrange("(k p) h c -> p k h c", p=128)
        for kc in range(3):
            P.dma(STG[:, 0:768], w_uq_v[:, kc, :], writes=[STG])
            P.cp('dve', WUQ[:, kc, :], STG[:, 0:768], [STG], [WUQ])
        P.memset('pool', WUQR[:, :, :], 0.0, [WUQR])
        P.dma(STG[:, 0:768].rearrange("p (k h c) -> p k h c", k=3, h=8), w_uqr_v, writes=[STG])
        for kc in range(3):
            for h in range(8):
                c0 = kc * 256 + h * 32
                P.tt('dve', WROT[:, kc, h, :], STG[:, c0:c0 + 32], CST[:, C_SGN:C_SGN + 32], ALU.mult, [STG, CST], [WROT])
        for kc_ in range(2):
            P.dma(STG[:, 0:512], w_uk_d.rearrange("(k p) h c -> p k (h c)", p=128)[:, kc_, :], writes=[STG])
            P.cp('dve', WUK[:, kc_, :], STG[:, 0:512], [STG], [WUK])
        P.dma(tA[:, :].rearrange("p (k c) -> p k c", k=2)[:, :, 0:256], w_uv_d.rearrange("(k p) h c -> p k (h c)", p=128)[:, :, 0:256], writes=[tA])
        P.dma(tB[:, :].rearrange("p (k c) -> p k c", k=2)[:, :, 0:256], w_uv_d.rearrange("(k p) h c -> p k (h c)", p=128)[:, :, 256:512], writes=[tB])
        P.cp('dve', WUV[:, :, 0:256], tA[:, :].rearrange("p (k c) -> p k c", k=2), [tA], [WUV])
        P.cp('dve', WUV[:, :, 256:512], tB[:, :].rearrange("p (k c) -> p k c", k=2), [tB], [WUV])

        po = [pz, pBXf]
        KB = [(kb * 512, 512) for kb in range(NKEY // 512)]
        for hg in range(2):
            for hh_ in range(4):
                P.cp('dve', V4[:, :, hh_, 64], KVV[:, :], [KVV], [V4])
            for bi, (k0, kw) in enumerate(KB):
                cb_ = ckb[bi % 2]
                P.dma(cb_[:, :, 0:kw], CKVs[:, :, k0:k0 + kw], reads=[CKVs], writes=[cb_])
                for s in range(kw // 128):
                    kt = k0 // 128 + s
                    pv = pu[kt % 2]
                    for c in range(2):
                        P.mm(pv[:, 0:256], cb_[:, c, s * 128:(s + 1) * 128], WUV[:, c, hg * 256:(hg + 1) * 256], [cb_, WUV], [pv],
                             start=(c == 0), stop=(c == 1))
                    P.ts1('dve', V4[:, kt, :, 0:64], pv[:, 0:256].rearrange("p (h c) -> p h c", h=4), KVV[:, kt:kt + 1], ALU.mult, [pv, KVV], [V4])
            for hh in range(4):
                h = hg * 4 + hh
                kt_, qt_ = KT[h % 2], QT[h % 2]
                P.dma(kt_[64:96, :], KRs[:, :], reads=[KRs], writes=[kt_])
                for bi, (k0, kw) in enumerate(KB):
                    cb_ = ckb[bi % 2]
                    P.dma(cb_[:, :, 0:kw], CKVs[:, :, k0:k0 + kw], reads=[CKVs], writes=[cb_])
                    for c in range(2):
                        P.mm(ps2[bi % 2][0:64, 0:kw], WUK[:, c, h * 64:(h + 1) * 64], cb_[:, c, 0:kw],
                             [cb_, WUK], [ps2[bi % 2]], start=(c == 0), stop=(c == 1))
                    P.cp('act' if bi % 2 else 'dve', kt_[0:64, k0:k0 + kw], ps2[bi % 2][0:64, 0:kw], [ps2[bi % 2]], [kt_])
                for kc in range(3):
                    P.cp('pool', WUQR[:, kc, 64:96], WROT[:, kc, h, :], [WROT], [WUQR])
                for qb in range(4):
                    qs = slice(qb * 512, (qb + 1) * 512)
                    P.dma(cql[:, :, :], CQs[:, :, qs], reads=[CQs], writes=[cql])
                    P.dma(rq[64:96, :, :], ropeq[:, :, qs].rearrange("a d t -> d a t"), writes=[rq])
                    for c in range(3):
                        P.mm(po[0][0:96, :], WUQ[:, c, h * 96:(h + 1) * 96], cql[:, c, :], [WUQ, cql], [po[0]],
                             start=(c == 0), stop=(c == 2))
                    for c in range(3):
                        P.mm(po[1][0:96, :], WUQR[:, c, :], cql[:, c, :], [WUQR, cql], [po[1]], start=(c == 0), stop=(c == 2))
                    P.cp('act', qt_[0:64, qs], po[0][0:64, :], [po[0]], [qt_])
                    P.tt('dve', rq[64:96, 0, :], po[0][64:96, :], rq[64:96, 0, :], ALU.mult, [po[0], rq], [rq])
                    P.tt('dve', rq[64:96, 1, :], po[1][64:96, :], rq[64:96, 1, :], ALU.mult, [po[1], rq], [rq])
                    P.tt('dve', qt_[64:96, qs], rq[64:96, 0, :], rq[64:96, 1, :], ALU.add, [rq], [qt_])
                for qb in range(4):
                    qs = slice(qb * 512, (qb + 1) * 512)
                    pacc = po[qb % 2]
                    P.dma(gml[:, :], GMs[:, h, qs], reads=[GMs], writes=[gml])
                    def qk(kt):
                        sc = ps2[kt % 2]
                        P.mm(sc[:, :], kt_[0:96, kt * 128:(kt + 1) * 128], qt_[0:96, qs], [kt_, qt_], [sc])
                    qk(0)
                    for kt in range(NKT):
                        if kt + 1 < NKT:
                            qk(kt + 1)
                        sc = ps2[kt % 2]
                        pT = PTb_[kt % 3]
                        P.act(pT[:, :], sc[:, :], AF.Exp, [sc], [pT], scale=ATTN_SCALE)
                        P.mm(pacc[0:65, :], V4[:, kt, hh, :], pT[:, :], [V4, pT], [pacc], start=(kt == 0), stop=(kt == NKT - 1))
                    P.recip(rec[64:65, :], pacc[64:65, :], [pacc], [rec])
                    P.mm(pH[0:64, 0:256], CST[64:65, C_ONES:C_ONES + 64], rec[64:65, 0:256], [CST, rec], [pH])
                    P.mm(pG[0:64, 0:256], CST[64:65, C_ONES:C_ONES + 64], rec[64:65, 256:512], [CST, rec], [pG])
                    P.cp('act', bcs[:, 0:256], pH[0:64, 0:256], [pH], [bcs])
                    P.cp('act', bcs[:, 256:512], pG[0:64, 0:256], [pG], [bcs])
                    P.tt('dve', onrm[:, :], pacc[0:64, :], bcs[:, :], ALU.mult, [pacc, bcs], [onrm])
                    P.tt('dve', mxb[:, :], onrm[:, :], gml[:, :], ALU.mult, [onrm, gml], [mxb])
                    P.dma(MIXMs[:, h, qs], mxb[:, :], reads=[mxb], writes=[MIXMs])

    def phase3():
        nonlocal o
        P.barrier()
        o = 0
        WOR = carve(4 * 1024, "p (k c) -> p k c", k=4)
        WOM = carve(8 * 1024, "p (k c) -> p k c", k=8)
        mr = [carve(512, "p (k c) -> p k c", k=4), carve(512, "p (k c) -> p k c", k=4)]
        mm_ = [carve(1024, "p (k c) -> p k c", k=8), carve(1024, "p (k c) -> p k c", k=8)]
        w_out_r = w_out_d[0:512, :].rearrange("(k p) c -> p k c", p=128)
        w_out_m = w_out_d[512:1024, :].rearrange("(k p) c -> p k c", p=64)
        for k in range(4):
            P.dma(STG[:, 0:1024], w_out_r[:, k, :], writes=[STG])
            P.cp('dve', WOR[:, k, :], STG[:, 0:1024], [STG], [WOR])
        for k in range(8):
            P.dma(STG[0:64, 0:1024], w_out_m[:, k, :], writes=[STG])
            P.cp('dve', WOM[0:64, k, :], STG[0:64, 0:1024], [STG], [WOM])
        GB = P.view(BKh.t[:, :])
        FG = P.view(BKt.t[:, :, :].rearrange('p m c -> p (m c)'))
        for half in range(2):
            for q in range(4):
                kc = half * 4 + q
                P.ts1('dve', th0[:, :], ident, MODT[:, 16 + kc, 0:1], ALU.mult, [CST, MODT], [th0])
                P.mm(pt[:, q * 128:(q + 1) * 128], ones, th0[:, :], [CST, th0], [pt])
            P.cp('act', GB[:, half * 512:(half + 1) * 512], pt[:, :], [pt], [GB])
        P.dma(FG[:, :], brow_d[:, 1024:2048], writes=[FG])
        yo_ = [P.view(RKV.t[:, 0:8, :].rearrange('p m t -> p (m t)')), P.view(EALL.t[:, 0:2, :, :].rearrange('p m a t -> p (m a t)'))]
        for jj in range(16):
            tok = slice(jj * 128, (jj + 1) * 128)
            x_ = xt[jj % 2]
            P.dma(x_[:, :], xs[68 + jj, 0:128, :], writes=[x_])
            P.dma(mr[jj % 2][:, :, :], MIXRs[:, :, tok], reads=[MIXRs], writes=[mr[jj % 2]])
            P.dma(mm_[jj % 2][0:64, :, :], MIXMs[:, :, tok], reads=[MIXMs], writes=[mm_[jj % 2]])
            y_ = yo_[jj % 2]
            for half in range(2):
                cs = slice(half * 512, (half + 1) * 512)
                pp = ps2[half]
                for k in range(4):
                    P.mm(pp[:, :], mr[jj % 2][:, k, :], WOR[:, k, cs], [mr[jj % 2], WOR], [pp], start=(k == 0), stop=False)
                for k in range(8):
                    P.mm(pp[:, :], mm_[jj % 2][0:64, k, :], WOM[0:64, k, cs], [mm_[jj % 2], WOM], [pp], start=False, stop=(k == 7))
                P.tt('dve', y_[:, cs], pp[:, :], GB[:, cs], ALU.mult, [pp, GB], [y_])
            P.tt('pool', y_[:, :], y_[:, :], x_[:, :], ALU.add, [y_, x_], [y_])
            P.act(junk[:, :], y_[:, :], AF.Square, [y_], [junk])
            P.red(st4[:, 0:1], junk[:, :], [junk], [st4])
            rsqrt_ln(st4, st4[:, 0:1], st4, st4[:, 0:1], 1.0 / 1024, pc(PC_EPS))
            P.stt('dve', y_[:, :], y_[:, :], st4[:, 0:1], FG[:, :], ALU.mult, ALU.mult, [y_, st4, FG], [y_])
            P.dma(out_d[tok, :], y_[:, :], reads=[y_])
    if CFG['p2']:
        phase2()
    if CFG['p3']:
        phase3()
    P.finish()
    return nc


DEBUG_SPECS = {'xf': (128, 192), 'at': (128, 512), 'atok': (128, 512), 'vtok': (128, 512), 'eall': (128, 2048), 'sg': (128, 512), 'rkv': (128, 1536), 'hout': (128, 256)}
_CACHE = {}


def kernel(**inputs):
    maps = host_prep(inputs)
    if 'nc' not in _CACHE:
        _CACHE['nc'] = build()
    nc = _CACHE['nc']
    res = run_bass_kernel_spmd(nc, maps, core_ids=list(range(8)))
    out = np.zeros((2, 8192, 1024), np.float32)
    for c in range(8):
        b, p = c // 4, c % 4
        out[b, 2048 * p:2048 * (p + 1)] = np.asarray(res.results[c]['out'], np.float32)
    if DEBUG:
        _CACHE['res'] = res
    return out
```

```python
import numpy as np
from contextlib import ExitStack
import concourse.bass as bass
import concourse.mybir as mybir
from concourse.bass_utils import run_bass_kernel_spmd

F32 = mybir.dt.float32
BF16 = mybir.dt.bfloat16
ALU = mybir.AluOpType
AF = mybir.ActivationFunctionType
AX = mybir.AxisListType
BANK = 30000
NDMA = 24
ENGS = ['pe', 'act', 'dve', 'pool', 'sp']
DEBUG = False
CFG = {'slots': None, 'p2': True, 'p3': True, 'stage': 99}

D_MODEL = 1024
D_IN = 3488
NSTATE = 52
NSLOT = 84
NKT = 68
NKEY = NKT * 128
KAPPA = -float(np.exp(-0.5))
ATTN_SCALE = 96 ** -0.5
NORM_EPS = 1e-6
LNX_EPS = 64e-5


class Trk:
    __slots__ = ('w', 'r')

    def __init__(self):
        self.w = None
        self.r = {}


class Buf:
    def __init__(self, t, ap=None):
        self.t = t
        self.k = Trk()
        self._ap = ap

    def __getitem__(self, idx):
        return (self._ap if self._ap is not None else self.t)[idx]

    @property
    def ap(self):
        return self._ap if self._ap is not None else self.t[:]


class Prog:
    def __init__(self, nc):
        self.nc = nc
        self.es = ExitStack()
        self.ops = {e: [] for e in ENGS}
        self.cnt = {e: 0 for e in ENGS}
        self.seen = {e: {} for e in ENGS}
        self.sems = {}
        self.dma_tot = [0] * NDMA
        self.dma_next = 0

    def sb(self, name, shape, dt=F32):
        return Buf(self.es.enter_context(self.nc.sbuf_tensor('sb_' + name, list(shape), dt)))

    def ps(self, name, shape, dt=F32):
        return Buf(self.es.enter_context(self.nc.psum_tensor('ps_' + name, list(shape), dt)))

    def view(self, ap):
        return Buf(None, ap)

    def _deps(self, eng, reads, writes):
        deps = {}

        def add(ev):
            if ev is None:
                return
            k, v = ev
            if deps.get(k, 0) < v:
                deps[k] = v
        for b in reads:
            add(b.k.w)
        for b in writes:
            add(b.k.w)
            for k, v in b.k.r.items():
                add((k, v))
        waits = []
        for k, v in deps.items():
            if k[0] == 'pe' and eng == 'pe':
                continue
            if self.seen[eng].get(k, 0) >= v:
                continue
            self.seen[eng][k] = v
            waits.append((k, v))
        return waits

    def _mark(self, ev, reads, writes):
        k, v = ev
        for b in writes:
            b.k.w = ev
            b.k.r = {}
        for b in reads:
            if b in writes:
                continue
            if b.k.r.get(k, 0) < v:
                b.k.r[k] = v

    def op(self, eng, fn, reads=(), writes=()):
        waits = self._deps(eng, reads, writes)
        i = self.cnt[eng]
        self.cnt[eng] += 1
        ev = ((eng, i // BANK), i % BANK + 1)
        self._mark(ev, reads, writes)
        self.ops[eng].append((waits, fn, ev[0], 1))

    def dma(self, out, in_, reads=(), writes=(), eng='sp', **kw):
        waits = self._deps(eng, reads, writes)
        s = self.dma_next % NDMA
        self.dma_next += 1
        k = ('dma', s)
        if self.dma_tot[s] > 0 and self.seen[eng].get(k, 0) < self.dma_tot[s]:
            self.seen[eng][k] = self.dma_tot[s]
            waits.append((k, self.dma_tot[s]))
        self.dma_tot[s] += 16
        ev = (k, self.dma_tot[s])
        self._mark(ev, reads, writes)
        self.ops[eng].append((waits, lambda e: e.dma_start(out=out, in_=in_, **kw), k, 16))

    def barrier(self):
        last = {}
        for e in ENGS:
            i = self.cnt[e]
            if i > 0:
                last[(e, (i - 1) // BANK)] = (i - 1) % BANK + 1
        for s in range(NDMA):
            if self.dma_tot[s] > 0:
                last[('dma', s)] = self.dma_tot[s]
        for e in ENGS:
            waits = []
            for k, v in last.items():
                if k[0] == e:
                    continue
                if self.seen[e].get(k, 0) >= v:
                    continue
                self.seen[e][k] = v
                waits.append((k, v))
            if waits:
                self.ops[e].append((waits, None, None, 0))

    def mm(self, out, lhsT, rhs, rd, wr, start=True, stop=True):
        self.op('pe', lambda e: e.matmul(out, lhsT, rhs, start=start, stop=stop), rd, wr)

    def tr(self, out, in_, ident, rd, wr):
        self.op('pe', lambda e: e.transpose(out, in_, ident), rd, wr)

    def act(self, out, in_, func, rd, wr, bias=None, scale=None):
        kw = {}
        if bias is not None:
            kw['bias'] = bias
        if scale is not None:
            kw['scale'] = scale
        self.op('act', lambda e: e.activation(out, in_, func, **kw), rd, wr)

    def tt(self, eng, out, a, b, op, rd, wr):
        self.op(eng, lambda e: e.tensor_tensor(out, a, b, op), rd, wr)

    def ts(self, eng, out, a, s1, s2, op0, op1, rd, wr):
        self.op(eng, lambda e: e.tensor_scalar(out, a, s1, s2, op0, op1), rd, wr)

    def ts1(self, eng, out, a, s, op, rd, wr):
        self.op(eng, lambda e: e.tensor_single_scalar(out, a, s, op), rd, wr)

    def stt(self, eng, out, a, s, b, op0, op1, rd, wr):
        self.op('dve', lambda e: e.scalar_tensor_tensor(out, a, s, b, op0, op1), rd, wr)

    def cp(self, eng, out, in_, rd, wr):
        if eng == 'act':
            self.op('act', lambda e: e.copy(out, in_), rd, wr)
        else:
            self.op(eng, lambda e: e.tensor_copy(out, in_), rd, wr)

    def recip(self, out, in_, rd, wr):
        self.op('dve', lambda e: e.reciprocal(out, in_), rd, wr)

    def red(self, out, in_, rd, wr, op=ALU.add):
        self.op('dve', lambda e: e.tensor_reduce(out, in_, AX.X, op), rd, wr)

    def memset(self, eng, ap, val, wr):
        self.op(eng, lambda e: e.memset(ap, val), (), wr)

    def finish(self):
        nc = self.nc
        keys = set()
        for e in ENGS:
            for waits, fn, k, inc in self.ops[e]:
                if k is not None:
                    keys.add(k)
                for wk, _ in waits:
                    keys.add(wk)
        for k in sorted(keys, key=str):
            self.sems[k] = self.es.enter_context(nc.semaphore("s_%s_%s" % k))
        fin = [(('dma', s), self.dma_tot[s]) for s in range(NDMA) if self.dma_tot[s] > 0]
        block = self.es.enter_context(nc.Block())

        def run(e, name):
            for waits, fn, k, inc in self.ops[name]:
                for wk, wv in waits:
                    e.wait_ge(self.sems[wk], wv)
                if fn is not None:
                    fn(e).then_inc(self.sems[k], inc)
            if name == 'sp':
                for wk, wv in fin:
                    e.wait_ge(self.sems[wk], wv)

        @block.tensor
        def _(e):
            run(e, 'pe')

        @block.scalar
        def _(e):
            run(e, 'act')

        @block.vector
        def _(e):
            run(e, 'dve')

        @block.gpsimd
        def _(e):
            run(e, 'pool')

        @block.sync
        def _(e):
            run(e, 'sp')
        self.es.close()


PC_NG, PC_ADAB, PC_MUP, PC_MUN, PC_KK, PC_KA, PC_RK, PC_A0, PC_QG, PC_KVG = 0, 8, 32, 46, 60, 64, 68, 72, 80, 83
PC_EPS, PC_LEPS, PC_ONE, PC_TINY = 85, 86, 87, 88
NPC = 96
C_ID, C_ONES, C_BO, C_F, C_B, C_TOT, C_BO2, C_SGN = 0, 128, 256, 384, 1280, 2176, 2177, 2179
SD_F, SD_SELW, SD_M2S, SD_PHI, SD_RHO, SD_SIG, SD_NF = 0, 1, 2, 3, 4, 5, 6
NCST = 2211

SLOTS = []
for _i in range(52):
    SLOTS.append(('S', _i))
for _j in range(16):
    SLOTS.append(('B', None))
for _j in range(16):
    SLOTS.append(('F', 52 + _j))


def _consts():
    c = np.zeros((128, NCST), np.float32)
    idx = np.arange(128)
    row, col = idx[:, None], idx[None, :]
    c[:, C_ID:C_ID + 128] = np.eye(128)
    c[:, C_ONES:C_ONES + 128] = 1.0
    c[:, C_BO:C_BO + 128] = ((row // 64) == (col // 64))
    k = KAPPA
    tf = [k * (row < col), -k * (row <= col), k * (row <= col), k * (row > col)]
    tb = [k * (row > col), -k * (row >= col), k * (row >= col), k * (row < col)]
    c[:, C_F:C_F + 896] = np.concatenate(tf + [col > row, col >= row, col < row], 1)
    c[:, C_B:C_B + 896] = np.concatenate(tb + [col < row, col <= row, col > row], 1)
    c[:, C_TOT] = k
    c[:, C_BO2] = (idx < 64)
    c[:, C_BO2 + 1] = (idx >= 64)
    d = np.arange(32)
    c[:, C_SGN:C_SGN + 32] = np.where(d % 16 < 8, -1.0, 1.0)[None, :]
    return c


ROT_PERM = np.array([dd + 8 if dd % 16 < 8 else dd - 8 for dd in range(32)])


def _rope_tables(pos):
    pos = np.asarray(pos)
    inv = (np.float32(10000.0) ** (-np.arange(0, 16, 2, dtype=np.float32) / np.float32(16))).astype(np.float32)
    rowf = (pos // 64).astype(np.float32)
    colf = (pos % 64).astype(np.float32)
    ar = (rowf[:, None] * inv[None, :]).astype(np.float32)
    ac = (colf[:, None] * inv[None, :]).astype(np.float32)
    cos = np.concatenate([np.cos(ar), np.cos(ar), np.cos(ac), np.cos(ac)], 1).astype(np.float32)
    sin = np.concatenate([np.sin(ar), np.sin(ar), np.sin(ac), np.sin(ac)], 1).astype(np.float32)
    return cos.T.copy(), sin.T.copy()


def _tile_halo(seq, c):
    n = seq.shape[0]
    t = np.zeros((130, seq.shape[1]), np.float32)
    t[:128] = seq[c * 128:(c + 1) * 128]
    hv = [0.0, 0.0]
    if c > 0:
        t[128] = seq[c * 128 - 1]
        hv[0] = 1.0
    if (c + 1) * 128 < n:
        t[129] = seq[(c + 1) * 128]
        hv[1] = 1.0
    return t, hv


def host_prep(inp):
    g = {k: np.asarray(v, np.float32) for k, v in inp.items()}
    sh = {}
    w_in = g['w_in'][0]
    sh['w_in'] = w_in
    sh['wkr_rot'] = np.ascontiguousarray(w_in[:, 2944:2976][:, ROT_PERM])
    w_uq = g['mla_w_uq'][0]
    sh['w_uq'] = w_uq
    sh['w_uq_rot'] = np.ascontiguousarray(
        np.stack([w_uq[:, h * 96 + 64:h * 96 + 96][:, ROT_PERM] for h in range(8)], 1))
    w_ukv = g['mla_w_ukv'][0].reshape(256, 8, 128)
    sh['w_uk'] = np.ascontiguousarray(w_ukv[:, :, :64])
    sh['w_uv'] = np.ascontiguousarray(w_ukv[:, :, 64:])
    sh['w_out'] = g['w_out'][0]
    sh['ada_w'] = g['ada_w'][0]
    sh['w2cat'] = np.ascontiguousarray(g['rw_w2'][0].reshape(128, 512))
    sh['a2cat'] = np.ascontiguousarray(g['rw_a2'][0].reshape(128, 512))
    sh['w0r'] = np.ascontiguousarray(g['rw_w0'][0].reshape(1, 1024))
    pc = np.zeros((128, NPC), np.float32)
    pc[:, PC_NG:PC_NG + 8] = g['norm_g'][0].reshape(8, 128).T
    pc[:, PC_ADAB:PC_ADAB + 24] = g['ada_b'][0].reshape(24, 128).T
    pc[:, PC_MUP:PC_MUP + 14] = g['shift_mu'][0, 0].reshape(14, 128).T
    pc[:, PC_MUN:PC_MUN + 14] = g['shift_mu'][0, 1].reshape(14, 128).T
    pc[:, PC_KK:PC_KK + 4] = g['rw_kk'][0].reshape(4, 128).T
    pc[:, PC_KA:PC_KA + 4] = g['rw_ka'][0].reshape(4, 128).T
    pc[:, PC_RK:PC_RK + 4] = g['rw_rk'][0].reshape(4, 128).T
    pc[:, PC_A0:PC_A0 + 4] = g['rw_a0'][0, 0].reshape(4, 128).T
    pc[:, PC_A0 + 4:PC_A0 + 8] = g['rw_a0'][0, 1].reshape(4, 128).T
    pc[:, PC_QG:PC_QG + 3] = g['mla_q_norm_g'][0].reshape(3, 128).T
    pc[:, PC_KVG:PC_KVG + 2] = g['mla_kv_norm_g'][0].reshape(2, 128).T
    pc[:, PC_EPS] = NORM_EPS
    pc[:, PC_LEPS] = LNX_EPS
    pc[:, PC_ONE] = 1.0
    pc[:, PC_TINY] = 1e-24
    sh['pcol'] = pc
    br = np.zeros((128, 2048), np.float32)
    br[:, 0:512] = g['rw_lnx_g'][0][None, :]
    br[:, 512:1024] = g['rw_lnx_b'][0][None, :]
    br[:, 1024:2048] = g['final_g'][None, :]
    sh['brow'] = br
    sh['cst'] = _consts()
    maps = []
    for b in range(2):
        x, ctx = g['x'][b], g['ctx'][b]
        cvec = np.stack([g['c'][b].reshape(128, 8), g['c_ctx'].reshape(128, 8)], -1)
        for p in range(4):
            nf = 2 + 16 * p
            fsrc = [(ctx, 0, True), (ctx, 1, True)] + [(x, c, False) for c in range(16 * p)]
            bsrc = [(ctx, 1, True), (ctx, 0, True)] + [(x, c, False) for c in range(63, 16 * p + 15, -1)]
            own = [(x, 16 * p + j, False) for j in range(16)]
            allsrc = fsrc + bsrc + own[::-1] + own
            assert len(allsrc) == NSLOT
            tiles, hvs = [], []
            for seq, c, _ in allsrc:
                t, hv = _tile_halo(seq, c)
                tiles.append(t)
                hvs.append(hv)
            xs = np.stack(tiles, 0)
            hvs = np.array(hvs, np.float32)
            sd = np.zeros((128, 7, NSLOT), np.float32)
            f = np.array([1.0] * nf + [0.0] * (52 - nf) + [0.0] * 16 + [1.0] * 16, np.float32)
            sd[:, SD_F, :] = f[None]
            sd[:, SD_NF, :] = 1.0 - f[None]
            sd[:64, SD_SELW, :] = f[None]
            sd[64:, SD_SELW, :] = 1.0 - f[None]
            sd[:, SD_M2S, :] = -2.0 * sd[:, SD_SELW, :]
            sd[:, SD_PHI, :] = np.array([1.0 if ic else 0.0 for _, _, ic in allsrc], np.float32)[None]
            rho = np.ones(NSLOT, np.float32)
            rho[nf] = 0.0
            sd[:, SD_RHO, :] = rho[None]
            sgm = np.zeros(NSLOT, np.float32)
            sgm[nf - 1] = 1.0
            sd[:, SD_SIG, :] = sgm[None]
            kvsrc = fsrc + bsrc + own
            rk = np.zeros((NKT, 2, 32, 128), np.float32)
            rk[:, 0] = 1.0
            kvv = np.ones(NKT, np.float32)
            kvv[nf] = 0.0
            kvv[nf + 1] = 0.0
            for kt, (seq, c, ic) in enumerate(kvsrc):
                if not ic:
                    cs, sn = _rope_tables(np.arange(c * 128, (c + 1) * 128))
                    rk[kt, 0], rk[kt, 1] = cs, sn
            cq, sq = _rope_tables(np.arange(2048 * p, 2048 * (p + 1)))
            m = dict(sh)
            m['xs'] = xs
            m['hv'] = np.ascontiguousarray(np.broadcast_to(hvs[None], (128, NSLOT, 2)))
            m['sd'] = sd
            m['kvv'] = np.ascontiguousarray(np.broadcast_to(kvv[None], (128, NKT)))
            m['ropek'] = rk
            m['ropeq'] = np.stack([cq, sq], 0)
            m['cvec'] = cvec
            maps.append(m)
    return maps


def build():
    nc = bass.Bass("TRN2", target_bir_lowering=False)
    P = Prog(nc)

    def din(name, shape):
        return nc.dram_tensor(name, list(shape), F32, kind="ExternalInput").ap()

    xs = din('xs', [NSLOT, 130, 1024])
    hv_d = din('hv', [128, NSLOT, 2])
    sd_d = din('sd', [128, 7, NSLOT])
    kvv_d = din('kvv', [128, NKT])
    ropek = din('ropek', [NKT, 2, 32, 128])
    ropeq = din('ropeq', [2, 32, 2048])
    cvec_d = din('cvec', [128, 8, 2])
    w_in_d = din('w_in', [1024, D_IN])
    wkr_rot_d = din('wkr_rot', [1024, 32])
    w_uq_d = din('w_uq', [384, 768])
    w_uq_rot_d = din('w_uq_rot', [384, 8, 32])
    w_uk_d = din('w_uk', [256, 8, 64])
    w_uv_d = din('w_uv', [256, 8, 64])
    w_out_d = din('w_out', [1024, 1024])
    ada_w_d = din('ada_w', [1024, 3072])
    w2cat_d = din('w2cat', [128, 512])
    a2cat_d = din('a2cat', [128, 512])
    w0r_d = din('w0r', [1, 1024])
    pcol_d = din('pcol', [128, NPC])
    brow_d = din('brow', [128, 2048])
    cst_d = din('cst', [128, NCST])
    out_d = nc.dram_tensor('out', [2048, 1024], F32, kind="ExternalOutput").ap()
    dbg = {}
    if DEBUG:
        for nm, shp in DEBUG_SPECS.items():
            dbg[nm] = nc.dram_tensor('dbg_' + nm, list(shp), F32, kind="ExternalOutput").ap()

    def dscr(name, shape, dt):
        t = nc.dram_tensor(name, list(shape), dt)
        return Buf(t, t.ap())

    CKVs = dscr('s_ckv', [128, 2, NKEY], BF16)
    KRs = dscr('s_kr', [32, NKEY], BF16)
    CQs = dscr('s_cq', [128, 3, 2048], BF16)
    GMs = dscr('s_gm', [64, 8, 2048], BF16)
    MIXRs = dscr('s_mixr', [128, 4, 2048], BF16)
    MIXMs = dscr('s_mixm', [64, 8, 2048], BF16)
    YBs = dscr('s_yb', [16, 128, 512], F32)
    SBs = dscr('s_sb', [16, 128, 8], F32)

    CST = P.sb('cst', [128, NCST])
    PCOL = P.sb('pcol', [128, NPC])
    LGB = P.sb('lgb', [128, 1024])
    HV = P.sb('hv', [128, NSLOT, 2])
    SD = P.sb('sd', [128, 7, NSLOT])
    KVV = P.sb('kvv', [128, NKT])
    CSEL = P.sb('csel', [128, 512])
    GSL = P.sb('gsl', [128, 4, 8])
    W0S = P.sb('w0s', [1, 512])
    NA0S = P.sb('na0s', [128, 4])
    W2C = P.sb('w2c', [128, 512])
    A2C = P.sb('a2c', [128, 512])
    W0R = P.sb('w0r', [1, 1024])
    MODT = P.sb('modt', [128, 24, 2])
    GS = P.sb('gs', [128, 2, 8])
    SH = P.sb('shc', [128, 2, 8])
    M0 = P.sb('m0', [128, 14])
    NA0 = P.sb('na0', [128, 8])
    ARENA = P.sb('arena', [128, 28432], BF16)
    WB = P.view(ARENA.t[:, 0:27904].rearrange("p (k c) -> p k c", k=8))
    WKR = P.sb('wkr', [128, 8, 2, 96], BF16)

    for (b_, d_) in [(CST, cst_d), (PCOL, pcol_d), (HV, hv_d), (SD, sd_d), (KVV, kvv_d), (W2C, w2cat_d), (A2C, a2cat_d),
                     (W0R, w0r_d)]:
        P.dma(b_.ap, d_, writes=[b_])
    P.dma(LGB[:, :], brow_d[:, 0:1024], writes=[LGB])
    ident = CST[:, C_ID:C_ID + 128]
    ones = CST[:, C_ONES:C_ONES + 128]

    def pc(c0, n=1):
        return PCOL[:, c0:c0 + n]

    def ps_pair(name):
        q = P.ps(name, [128, 512])
        v0, v1 = P.view(q.t[:, 0:256]), P.view(q.t[:, 256:512])
        v0.k = q.k
        v1.k = q.k
        return v0, v1, q
    pu = [P.ps('pu0', [128, 512]), P.ps('pu1', [128, 512])]
    pz = P.ps('pz', [128, 512])
    pt = pz
    pA = P.ps('pA', [128, 512])
    pB, pX, pBXf = ps_pair('qbx')
    pS, pM, _ = ps_pair('qsm')
    pY = P.ps('pY', [128, 512])
    pH, pG, _ = ps_pair('qhg')

    xt = [P.sb('xt0', [128, 1024]), P.sb('xt1', [128, 1024])]
    xh = [P.sb('xh0', [2, 1024])]
    junk = P.sb('junk', [128, 1024])
    tD = P.view(junk.t[:, 0:512])
    tE = P.view(junk.t[:, 512:1024])
    tD.k = junk.k
    tE.k = junk.k
    st4 = P.sb('st4', [128, 8])
    hT = [P.sb('hT0', [128, 8, 130], BF16), P.sb('hT1', [128, 8, 130], BF16)]
    shv = P.sb('shv', [128, 2, 8])
    ue = [P.sb('ue0', [128, 130]), P.sb('ue1', [128, 130])]
    RKV = P.sb('rkv', [128, 12, 128])
    WA = P.sb('wa', [128, 2, 128])
    tA = P.sb('tA', [128, 512])
    tB = P.sb('tB', [128, 512])
    tC = P.sb('tC', [128, 512])
    th0 = P.sb('th0', [128, 128])
    th1 = P.sb('th1', [128, 128])
    sg = P.sb('sg', [128, 512])
    EALL = P.sb('eall', [128, 4, 4, 128])
    STG = P.view(EALL.t[:, :, :, :].rearrange('p m a t -> p (m a t)')[:, 0:1792])
    STG.k = EALL.k
    GCL = [P.sb('gc0', [128, 4]), P.sb('gc1', [128, 4])]
    MSKL = [P.sb('msk0', [128, 384]), P.sb('msk1', [128, 384])]
    ARtL = [P.sb('art%d' % q_, [128, 4, 256]) for q_ in range(2)]
    BKtL = [P.sb('bkt%d' % q_, [128, 4, 256]) for q_ in range(2)]
    ARt, BKt = ARtL[0], BKtL[0]
    BKh = P.sb('bkh', [128, 1024])
    BhT = P.view(BKh.t[:, 0:512])
    KhT = P.view(BKh.t[:, 512:1024])
    BhT.k = BKh.k
    KhT.k = BKh.k
    AtokL = [P.sb('atok%d' % q_, [128, 512]) for q_ in range(2)]
    BhtokL = [P.sb('bhtok%d' % q_, [128, 512]) for q_ in range(2)]
    KhtokL = [P.sb('khtok%d' % q_, [128, 512]) for q_ in range(2)]
    VtokL = [P.sb('vtok%d' % q_, [128, 512]) for q_ in range(2)]
    AT = [P.sb('at0', [128, 512]), P.sb('at1', [128, 512])]
    XPBS = [[P.sb('xp%d_%d' % (p_, q_), [128, 320]) for q_ in range(3)] for p_ in range(2)]
    PTBS = [[P.sb('ptq%d_%d' % (p_, q_), [128, 128]) for q_ in range(2)] for p_ in range(2)]
    XPB = XPBS[0] + XPBS[1]
    MTs = P.sb('mts', [128, 4, 64])
    Ns = P.sb('ns', [128, 4, 64])
    GTs = [P.sb('gts0', [128, 128]), P.sb('gts1', [128, 128])]
    Hc = [P.sb('hc0', [128, 4, 64]), P.sb('hc1', [128, 4, 64])]
    HFa = P.sb('hfa', [128, 4, 64])
    ysb = P.sb('ysb', [128, 512])
    ybl = P.view(sg.t[:, :])
    ybl.k = sg.k
    sbl = P.sb('sbl', [128, 8])
    ssumL = [P.sb('ssum0', [128, 8]), P.sb('ssum1', [128, 8])]
    mla1 = P.sb('mla1', [128, 4, 128])
    mla2 = P.view(tC.t[:, :].rearrange('p (a t) -> p a t', a=4))
    mla2.k = tC.k
    kvb = P.sb('kvb', [128, 2, 128], BF16)
    krb = P.sb('krb', [96, 128], BF16)
    rpk = P.sb('rpk', [96, 2, 128])
    cqb = P.sb('cqb', [128, 3, 128], BF16)
    gmb = P.sb('gmb', [64, 8, 128], BF16)
    mixb = P.sb('mixb', [128, 4, 128], BF16)

    def dump(name, buf, ap):
        if DEBUG and name in dbg:
            P.dma(dbg[name], ap, reads=[buf])

    w_in_v = w_in_d.rearrange("(k p) c -> p k c", p=128)
    for kc in range(8):
        for hf in range(2):
            P.dma(STG[:, 0:1744], w_in_v[:, kc, hf * 1744:(hf + 1) * 1744], writes=[STG])
            P.cp(['act', 'dve'][hf], WB[:, kc, hf * 1744:(hf + 1) * 1744], STG[:, 0:1744], [STG], [WB])
    P.memset('pool', WKR[:, :, :, :], 0.0, [WKR])
    for kc in range(8):
        P.cp('dve', WKR[:, kc, 0, 64:96], WB[:, kc, 2944:2976], [WB], [WKR])
    wkr_v = wkr_rot_d.rearrange("(k p) c -> p k c", p=128)
    P.dma(junk[:, 0:256].rearrange("p (k c) -> p k c", k=8), wkr_v, writes=[junk])
    for kc in range(8):
        P.tt('dve', WKR[:, kc, 1, 64:96], junk[:, kc * 32:(kc + 1) * 32], CST[:, C_SGN:C_SGN + 32], ALU.mult,
             [junk, CST], [WKR])
    CV = P.sb('cv', [128, 8, 2])
    CV2 = P.sb('cv2', [128, 8, 2])
    P.dma(CV[:, :, :], cvec_d, writes=[CV])
    P.act(CV2[:, :, :], CV[:, :, :], AF.Exp, [CV], [CV2], scale=-1.0)
    P.ts1('dve', CV2[:, :, :], CV2[:, :, :], 1.0, ALU.add, [CV2], [CV2])
    P.recip(CV2[:, :, :], CV2[:, :, :], [CV2], [CV2])
    P.tt('dve', CV[:, :, :], CV[:, :, :], CV2[:, :, :], ALU.mult, [CV, CV2], [CV])
    ada_v = ada_w_d.rearrange("(p k) n -> p k n", k=8)
    MACC = P.sb('macc', [128, 48])
    P.memset('dve', MACC[:, :], 0.0, [MACC])
    for k in range(8):
        for hf in range(2):
            P.dma(STG[:, 0:1536], ada_v[:, k, hf * 1536:(hf + 1) * 1536], writes=[STG])
            for n in range(12):
                nn = hf * 12 + n
                P.mm(pz[:, 2 * nn:2 * nn + 2], STG[:, n * 128:(n + 1) * 128], CV[:, k, :], [STG, CV], [pz])
            P.tt('dve', MACC[:, hf * 24:(hf + 1) * 24], MACC[:, hf * 24:(hf + 1) * 24], pz[:, hf * 24:(hf + 1) * 24],
                 ALU.add, [MACC, pz], [MACC])
    for j in range(2):
        P.tt('dve', MODT[:, :, j], MACC[:, j:48:2], PCOL[:, PC_ADAB:PC_ADAB + 24], ALU.add, [MACC, PCOL], [MODT])
    for j in range(2):
        P.stt('dve', GS[:, j, :], MODT[:, 8:16, j], 1.0, PCOL[:, PC_NG:PC_NG + 8], ALU.add, ALU.mult,
              [MODT, PCOL], [GS])
        P.cp('dve', SH[:, j, :], MODT[:, 0:8, j], [MODT], [SH])
    P.cp('dve', GSL[:, 0, :], GS[:, 0, :], [GS], [GSL])
    P.tt('dve', GSL[:, 1, :], GS[:, 1, :], GS[:, 0, :], ALU.subtract, [GS], [GSL])
    P.cp('dve', GSL[:, 2, :], SH[:, 0, :], [SH], [GSL])
    P.tt('dve', GSL[:, 3, :], SH[:, 1, :], SH[:, 0, :], ALU.subtract, [SH], [GSL])
    P.tt('dve', M0[:, :], PCOL[:, PC_MUP:PC_MUP + 14], PCOL[:, PC_MUN:PC_MUN + 14], ALU.add, [PCOL], [M0])
    P.ts('dve', M0[:, :], M0[:, :], -1.0, 1.0, ALU.mult, ALU.add, [M0], [M0])
    P.ts1('dve', NA0[:, :], PCOL[:, PC_A0:PC_A0 + 8], -1.0, ALU.mult, [PCOL], [NA0])
    for b_ in XPB:
        P.memset('pool', b_[:, :], 0.0, [b_])
    P.memset('pool', Hc[0][:, :, :], 0.0, [Hc[0]])
    P.memset('pool', HFa[:, :, :], 0.0, [HFa])

    def load_slot(i):
        P.dma(xt[i % 2][:, :], xs[i, 0:128, :], writes=[xt[i % 2]])

    def rsqrt_ln(dst_buf, dst, src_buf, src, scale, eps_ap):
        P.act(dst, src, AF.Ln, [src_buf, PCOL], [dst_buf], bias=eps_ap, scale=scale)
        P.act(dst, dst, AF.Exp, [dst_buf], [dst_buf], scale=-0.5)

    state = {'h': 0}

    PCNT = {'n': 0}
    ADV = 1

    def proj_for(hT_, cnt):
        def proj(c0, width, ncols=128, lhs=None):
            pb_ = pu[cnt['n'] % 2]
            cnt['n'] += 1
            for kc in range(8):
                l = WB[:, kc, c0:c0 + width] if lhs is None else lhs(kc)
                P.mm(pb_[0:(width if lhs is None else 96), 0:ncols], l, hT_[:, kc, 0:ncols],
                     [WB if lhs is None else WKR, hT_], [pb_], start=(kc == 0), stop=(kc == 7))
            return pb_
        return proj

    def prep(i):
        kind, kv = SLOTS[i]
        S = i % 2
        ARt, BKt, Atok, Bhtok, Khtok, Vtok = ARtL[S], BKtL[S], AtokL[S], BhtokL[S], KhtokL[S], VtokL[S]
        GC, MSK, ssum = GCL[S], MSKL[S], ssumL[S]
        x_, xh_, hT_ = xt[i % 2], xh[0], hT[i % 2]
        P.dma(xh_[:, :], xs[i, 128:130, :], writes=[xh_])
        fcol = SD[:, SD_F, i:i + 1]
        nfcol = SD[:, SD_NF, i:i + 1]
        phi = SD[:, SD_PHI, i:i + 1]
        P.stt('dve', GS[:, 0, :], GSL[:, 1, :], phi, GSL[:, 0, :], ALU.mult, ALU.add, [GSL, SD], [GS])
        P.stt('dve', SH[:, 0, :], GSL[:, 3, :], phi, GSL[:, 2, :], ALU.mult, ALU.add, [GSL, SD], [SH])
        P.ts1('dve', CSEL[:, :], CST[:, C_F:C_F + 512], fcol, ALU.mult, [CST, SD], [CSEL])
        P.stt('dve', CSEL[:, :], CST[:, C_B:C_B + 512], nfcol, CSEL[:, :], ALU.mult, ALU.add, [CST, SD, CSEL], [CSEL])
        P.ts1('dve', MSK[:, :], CST[:, C_F + 512:C_F + 896], fcol, ALU.mult, [CST, SD], [MSK])
        P.stt('dve', MSK[:, :], CST[:, C_B + 512:C_B + 896], nfcol, MSK[:, :], ALU.mult, ALU.add, [CST, SD, MSK], [MSK])
        P.ts1('dve', W0S[0:1, :], W0R[0:1, 0:512], SD[0:1, SD_F, i:i + 1], ALU.mult, [W0R, SD], [W0S])
        P.stt('dve', W0S[0:1, :], W0R[0:1, 512:1024], SD[0:1, SD_NF, i:i + 1], W0S[0:1, :], ALU.mult, ALU.add,
              [W0R, SD, W0S], [W0S])
        P.ts1('dve', NA0S[:, :], NA0[:, 0:4], fcol, ALU.mult, [NA0, SD], [NA0S])
        P.stt('dve', NA0S[:, :], NA0[:, 4:8], nfcol, NA0S[:, :], ALU.mult, ALU.add, [NA0, SD, NA0S], [NA0S])
        j = 0
        P.act(junk[:, :], x_[:, :], AF.Square, [x_], [junk])
        P.red(st4[:, 0:1], junk[:, :], [junk], [st4])
        rsqrt_ln(st4, st4[:, 0:1], st4, st4[:, 0:1], 1.0 / 1024, pc(PC_EPS))
        P.ts1('dve', x_[:, :], x_[:, :], st4[:, 0:1], ALU.mult, [x_, st4], [x_])
        for half in range(2):
            yield
            for q in range(4):
                kc = half * 4 + q
                P.tr(pt[:, q * 128:(q + 1) * 128], x_[:, kc * 128:(kc + 1) * 128], ident, [x_, CST], [pt])
            for q in range(4):
                kc = half * 4 + q
                if q % 2 == 0:
                    P.act(hT_[:, kc, 0:128], pt[:, q * 128:(q + 1) * 128], AF.Identity, [pt, GS, SH], [hT_],
                          bias=SH[:, j, kc:kc + 1], scale=GS[:, j, kc:kc + 1])
                else:
                    P.ts('dve', hT_[:, kc, 0:128], pt[:, q * 128:(q + 1) * 128], GS[:, j, kc:kc + 1],
                         SH[:, j, kc:kc + 1], ALU.mult, ALU.add, [pt, GS, SH], [hT_])
        yield
        P.act(junk[0:2, :], xh_[:, :], AF.Square, [xh_], [junk])
        P.red(st4[0:2, 1:2], junk[0:2, :], [junk], [st4])
        rsqrt_ln(st4, st4[0:2, 1:2], st4, st4[0:2, 1:2], 1.0 / 1024, PCOL[0:2, PC_EPS:PC_EPS + 1])
        P.ts1('dve', xh_[:, :], xh_[:, :], st4[0:2, 1:2], ALU.mult, [xh_, st4], [xh_])
        yield
        for kc in range(8):
            P.tr(pt[:, 2 * kc:2 * kc + 2], xh_[0:2, kc * 128:(kc + 1) * 128], CST[0:2, C_ID:C_ID + 2], [xh_, CST], [pt])
        for c in range(2):
            P.ts1('dve', shv[:, c, :], SH[:, j, :], HV[:, i, c:c + 1], ALU.mult, [SH, HV], [shv])
            P.tt('dve', junk[:, 8 * c:8 * c + 8], pt[:, c:16:2], GS[:, j, :], ALU.mult, [pt, GS], [junk])
            P.tt('dve', hT_[:, :, 128 + c], junk[:, 8 * c:8 * c + 8], shv[:, c, :], ALU.add, [junk, shv], [hT_])

        yield
        proj = proj_for(hT_, PCNT)

        for jc in range(14):
            if kind == 'S' and jc < 4:
                continue
            pb_ = proj(jc * 128, 128, 130)
            ue_ = ue[jc % 2]
            P.cp('act', ue_[:, :], pb_[:, 0:130], [pb_], [ue_])
            yield
            if jc < 12:
                dstb, dst = RKV, RKV[:, jc, :]
            else:
                dstb, dst = WA, WA[:, jc - 12, :]
            eng = 'pool' if jc % 2 == 0 else 'dve'
            mp_, mn_ = pc(PC_MUP + jc), pc(PC_MUN + jc)
            P.ts1(eng, dst, ue_[:, 0:128], M0[:, jc:jc + 1], ALU.mult, [ue_, M0], [dstb])
            for (dsl, ssl, mu_) in [((1, 128), (0, 127), mp_), ((0, 1), (128, 129), mp_),
                                    ((0, 127), (1, 128), mn_), ((127, 128), (129, 130), mn_)]:
                dd = dst[:, dsl[0]:dsl[1]]
                if eng == 'dve':
                    P.stt('dve', dd, ue_[:, ssl[0]:ssl[1]], mu_, dd, ALU.mult, ALU.add, [ue_, PCOL, dstb], [dstb])
                else:
                    tp = th1[:, 0:dsl[1] - dsl[0]]
                    P.ts1('pool', tp, ue_[:, ssl[0]:ssl[1]], mu_, ALU.mult, [ue_, PCOL], [th1])
                    P.tt('pool', dd, dd, tp, ALU.add, [dstb, th1], [dstb])
            yield
        yield
        r3 = RKV[:, 0:4, :]
        k3 = RKV[:, 4:8, :]
        v3 = RKV[:, 8:12, :]

        def f3(b_):
            return b_[:, :].rearrange("p (m t) -> p m t", m=4)

        for m in range(4):
            P.ts1('pool', tA[:, m * 128:(m + 1) * 128], RKV[:, 4 + m, :], pc(PC_KK + m), ALU.mult, [RKV, PCOL], [tA])
        P.act(tB[:, :], tA[:, :], AF.Square, [tA], [tB])
        P.mm(pz[:, :], CST[:, C_BO:C_BO + 128], tB[:, :], [CST, tB], [pz])
        yield
        P.ts1('dve', tB[:, :], pz[:, :], 1e-24, ALU.max, [pz], [tB])
        P.act(tB[:, :], tB[:, :], AF.Ln, [tB], [tB])
        P.act(tB[:, :], tB[:, :], AF.Exp, [tB], [tB], scale=-0.5)
        P.tt('dve', tA[:, :], tA[:, :], tB[:, :], ALU.mult, [tA, tB], [tA])
        yield
        P.act(th0[:, :], WA[:, 0, :], AF.Exp, [WA], [th0], scale=2.0)
        P.ts1('dve', th0[:, :], th0[:, :], 1.0, ALU.add, [th0], [th0])
        P.recip(th0[:, :], th0[:, :], [th0], [th0])
        P.ts('dve', th0[:, :], th0[:, :], SD[:, SD_M2S, i:i + 1], SD[:, SD_SELW, i:i + 1], ALU.mult, ALU.add,
             [th0, SD], [th0])
        P.mm(pz[:, :], th0[:, :], W2C[:, :], [th0, W2C], [pz], start=True, stop=False)
        P.mm(pz[:, :], CST[0:1, C_ONES:C_ONES + 128], W0S[0:1, :], [CST, W0S], [pz], start=False, stop=True)
        yield
        P.act(sg[:, :], pz[:, :], AF.Exp, [pz], [sg], scale=-1.0)
        P.ts1('dve', sg[:, :], sg[:, :], 1.0, ALU.add, [sg], [sg])
        P.recip(sg[:, :], sg[:, :], [sg], [sg])
        yield
        tri = CSEL[:, 0:512]
        for m in range(4):
            P.mm(pz[:, :], sg[:, m * 128:(m + 1) * 128], tri, [sg, CSEL], [pz])
            P.act(EALL[:, m, :, :], pz[:, :].rearrange("p (a t) -> p a t", a=4), AF.Exp, [pz], [EALL])
            yield
        P.tt('dve', GC[:, :], EALL[:, :, 2, 0], EALL[:, :, 3, 0], ALU.mult, [EALL], [GC])
        yield
        P.ts1('dve', th1[:, :], WA[:, 1, :], SD[:, SD_SELW, i:i + 1], ALU.mult, [WA, SD], [th1])
        for m in range(4):
            P.mm(pz[:, m * 128:(m + 1) * 128], A2C[:, m * 128:(m + 1) * 128], th1[:, :], [A2C, th1], [pz])
        for m in range(4):
            P.act(tB[:, m * 128:(m + 1) * 128], pz[:, m * 128:(m + 1) * 128], AF.Exp, [pz, NA0S], [tB],
                  bias=NA0S[:, m:m + 1], scale=-1.0)
        yield
        P.ts1('dve', tB[:, :], tB[:, :], 1.0, ALU.add, [tB], [tB])
        P.recip(tB[:, :], tB[:, :], [tB], [tB])
        yield
        for m in range(4):
            P.ts('pool', tC[:, m * 128:(m + 1) * 128], tB[:, m * 128:(m + 1) * 128], -1.0, pc(PC_KA + m),
                 ALU.add, ALU.mult, [tB, PCOL], [tC])
        P.stt('dve', tC[:, :], tC[:, :], 1.0, RKV[:, 4:8, :].rearrange("p m t -> p (m t)"), ALU.add, ALU.mult,
              [tC, RKV], [tC])
        P.tt('pool', tD[:, :], tA[:, :], tB[:, :], ALU.mult, [tA, tB], [tD])
        if kind != 'S':
            for m in range(4):
                P.stt('pool', tE[:, m * 128:(m + 1) * 128], RKV[:, m, :], pc(PC_RK + m),
                      tC[:, m * 128:(m + 1) * 128], ALU.mult, ALU.mult, [RKV, PCOL, tC], [tE])
            for m in range(4):
                P.mm(pz[:, 2 * m:2 * m + 2], tE[:, m * 128:(m + 1) * 128], CST[:, C_BO2:C_BO2 + 2],
                     [tE, CST], [pz])
            P.cp('act', ssum[:, :], pz[:, 0:8], [pz], [ssum])
        yield
        E = [EALL[:, :, a, :] for a in range(4)]
        A3 = ARt[:, :, :].rearrange("p m (a t) -> p m a t", a=2)
        B3 = BKt[:, :, :].rearrange("p m (a t) -> p m a t", a=2)
        P.stt('dve', A3[:, :, 0, :], f3(tA), -1.0, E[0], ALU.mult, ALU.mult, [tA, EALL], [ARt])
        if kind != 'S':
            P.tt('pool', A3[:, :, 1, :], r3, E[2], ALU.mult, [RKV, EALL], [ARt])
        yield
        P.tt('dve', B3[:, :, 0, :], f3(tD), E[1], ALU.mult, [tD, EALL], [BKt])
        P.tt('pool', B3[:, :, 1, :], f3(tC), E[1], ALU.mult, [tC, EALL], [BKt])
        yield
        P.tt('dve', f3(BhT), f3(tD), E[3], ALU.mult, [tD, EALL], [BhT])
        P.tt('pool', f3(KhT), f3(tC), E[3], ALU.mult, [tC, EALL], [KhT])
        for (srcb, srcf, dstb) in [(ARt, lambda m: ARt[:, m, 0:128], Atok), (BhT, lambda m: BhT[:, m * 128:(m + 1) * 128], Bhtok),
                                   (KhT, lambda m: KhT[:, m * 128:(m + 1) * 128], Khtok), (RKV, lambda m: RKV[:, 8 + m, :], Vtok)]:
            for m in range(4):
                P.tr(pt[:, m * 128:(m + 1) * 128], srcf(m), ident, [srcb, CST], [pt])
            P.cp('act', dstb[:, :], pt[:, :], [pt], [dstb])
            yield

        yield
        if kv is not None:
            pb2 = proj(2688, 128, 128)
            P.cp('act', mla1[:, 0, :], pb2[:, 0:128], [pb2], [mla1])
            pb2 = proj(2816, 128, 128)
            P.cp('act', mla1[:, 1, :], pb2[:, 0:128], [pb2], [mla1])
            P.act(mla2[:, 0:2, :], mla1[:, 0:2, :], AF.Square, [mla1], [mla2])
            P.mm(pz[:, 0:128], ones, mla2[:, 0, :], [CST, mla2], [pz], start=True, stop=False)
            P.mm(pz[:, 0:128], ones, mla2[:, 1, :], [CST, mla2], [pz], start=False, stop=True)
            rsqrt_ln(mla2, mla2[:, 2, :], pz, pz[:, 0:128], 1.0 / 256, pc(PC_EPS))
            for c in range(2):
                P.stt('dve', kvb[:, c, :], mla1[:, c, :], pc(PC_KVG + c), mla2[:, 2, :], ALU.mult, ALU.mult,
                      [mla1, PCOL, mla2], [kvb])
            P.dma(CKVs[:, :, kv * 128:(kv + 1) * 128], kvb[:, :, :], reads=[kvb], writes=[CKVs])
            yield
            P.dma(rpk[64:96, :, :], ropek[kv].rearrange("a d t -> d a t"), writes=[rpk])
            pk1 = proj(0, 96, 128, lhs=lambda kc: WKR[:, kc, 0, :])
            yield
            pk2 = proj(0, 96, 128, lhs=lambda kc: WKR[:, kc, 1, :])
            P.tt('dve', mla1[64:96, 2, :], pk1[64:96, 0:128], rpk[64:96, 0, :], ALU.mult, [pk1, rpk], [mla1])
            P.tt('dve', mla1[64:96, 3, :], pk2[64:96, 0:128], rpk[64:96, 1, :], ALU.mult, [pk2, rpk], [mla1])
            P.tt('dve', krb[64:96, :], mla1[64:96, 2, :], mla1[64:96, 3, :], ALU.add, [mla1], [krb])
            P.dma(KRs[:, kv * 128:(kv + 1) * 128], krb[64:96, :], reads=[krb], writes=[KRs])

        yield

    def heads(i, gen):
        kind, kv = SLOTS[i]
        S = i % 2
        ARt, BKt, Atok, Bhtok, Khtok, Vtok = ARtL[S], BKtL[S], AtokL[S], BhtokL[S], KhtokL[S], VtokL[S]
        GC, MSK, ssum = GCL[S], MSKL[S], ssumL[S]
        hT_ = hT[i % 2]
        proj = proj_for(hT_, PCNT)

        def f3(b_):
            return b_[:, :].rearrange("p (m t) -> p m t", m=4)

        def advance(n):
            if gen is None:
                return
            for _ in range(n):
                try:
                    next(gen)
                except StopIteration:
                    return
        Hin = Hc[state['h'] % 2]
        Hout = Hc[(state['h'] + 1) % 2]
        mskt = MSK[:, 0:256]
        mskn = MSK[:, 256:384]
        if kind == 'S':
            P.ts1('dve', Hin[:, :, :], Hin[:, :, :], SD[:, SD_RHO, i:i + 1], ALU.mult, [Hin, SD], [Hin])
        own = kind != 'S'

        def head_pre(h):
            m, pb = h // 2, 64 * (h % 2)
            par = h % 2
            cb = 64 * h
            at_, xpb, ptb = AT[par], XPBS[par], PTBS[par]
            Bt_h = BKt[pb:pb + 64, m, 0:128]
            Kt_h = BKt[pb:pb + 64, m, 128:256]
            AR_h = ARt[pb:pb + 64, m, 0:256]
            At_h = ARt[pb:pb + 64, m, 0:128]
            XP = xpb[0]
            pt0 = ptb[0]
            if own:
                P.mm(pA[:, 0:256], Bt_h, AR_h, [BKt, ARt], [pA])
                P.mm(pA[:, 256:512], Kt_h, AR_h, [BKt, ARt], [pA])
                P.mm(pG[:, 128:256], At_h, Bt_h, [ARt, BKt], [pG])
                an_src, an_buf = pG[:, 128:256], pG
            else:
                P.mm(pA[:, 0:128], Bt_h, At_h, [BKt, ARt], [pA])
                P.mm(pA[:, 256:384], Kt_h, At_h, [BKt, ARt], [pA])
                P.mm(pA[:, 128:256], At_h, Bt_h, [ARt, BKt], [pA])
                an_src, an_buf = pA[:, 128:256], pA
            P.tt('dve', pt0[:, :], pA[:, 0:128], mskt[:, 0:128], ALU.mult, [pA, MSK], [pt0])
            if own:
                P.tt('dve', at_[:, 128:256], pA[:, 128:256], mskt[:, 128:256], ALU.mult, [pA, MSK], [at_])
                P.tt('dve', at_[:, 256:512], pA[:, 256:512], mskt, ALU.mult, [pA, MSK], [at_])
            else:
                P.tt('dve', at_[:, 256:384], pA[:, 256:384], mskt[:, 0:128], ALU.mult, [pA, MSK], [at_])
            P.tt('dve', XP[:, 192:320], an_src, mskn, ALU.mult, [an_buf, MSK], [XP])
            P.cp('act', XP[:, 64:128], Atok[:, cb:cb + 64], [Atok], [XP])
            yield
            if own:
                wdst, wbuf = pA[:, 0:64], pA
            else:
                wdst, wbuf = pA[:, 384:448], pA
            P.mm(wdst, at_[:, 256:384], Vtok[:, cb:cb + 64], [at_, Vtok], [wbuf])
            P.cp('act', XP[:, 128:192], wdst, [wbuf], [XP])
            yield

        def head_main(h, nxt):
            m, pb = h // 2, 64 * (h % 2)
            par = h % 2
            cb = 64 * h
            at_, xpb, ptb = AT[par], XPBS[par], PTBS[par]
            xi = 0
            XP = xpb[0]
            PT_ = ptb[0]
            for lvl in range(7):
                last = lvl == 6
                XPn = xpb[(xi + 1) % 3]
                P.mm(pX[:, 0:(128 if last else 256)], PT_[:, :], XP[:, 64:(192 if last else 320)], [PT_, XP], [pX])
                if not last:
                    P.mm(pS[:, 0:128], XP[:, 192:320], PT_[:, :], [XP, PT_], [pS])
                if nxt is not None and lvl in (1, 3):
                    next(nxt, None)
                P.tt('dve', XPn[:, 64:192], XP[:, 64:192], pX[:, 0:128], ALU.add, [XP, pX], [XPn])
                if not last:
                    P.cp('act', XPn[:, 192:320], pX[:, 128:256], [pX], [XPn])
                    ptn = ptb[(lvl + 1) % 2]
                    P.cp('act', ptn[:, :], pS[:, 0:128], [pS], [ptn])
                    PT_ = ptn
                xi += 1
                XP = XPn
                advance(1)
            X = XP
            lx = X[:, 64:128] if pb == 0 else X[:, 0:128]
            P.mm(pM[0:(64 if pb == 0 else 128), 0:64], lx, Bhtok[:, cb:cb + 64], [X, Bhtok], [pM])
            P.mm(pM[:, 64:128], Bhtok[:, m * 128:(m + 1) * 128], X[:, 128:192], [Bhtok, X], [pM], start=True, stop=False)
            P.mm(pM[:, 64:128], Khtok[:, m * 128:(m + 1) * 128], Vtok[:, cb:cb + 64], [Khtok, Vtok], [pM],
                 start=False, stop=True)
            P.stt('dve', MTs[pb:pb + 64, m, :], CST[pb:pb + 64, C_ID + pb:C_ID + pb + 64], GC[pb:pb + 64, m:m + 1],
                  pM[pb:pb + 64, 0:64], ALU.mult, ALU.add, [CST, GC, pM], [MTs])
            P.cp('act', Ns[pb:pb + 64, m, :], pM[pb:pb + 64, 64:128], [pM], [Ns])
            if own:
                gt_ = GTs[h % 2]
                P.mm(pG[0:(64 if pb == 0 else 128), 0:128], lx, at_[:, 128:256], [X, at_], [pG])
                P.tt('dve', gt_[pb:pb + 64, :], pG[pb:pb + 64, 0:128], ARt[pb:pb + 64, m, 128:256], ALU.add,
                     [pG, ARt], [gt_])
                yo = pY[:, cb:cb + 64]
                P.mm(yo, at_[:, 128:256], X[:, 128:192], [at_, X], [pY], start=True, stop=False)
                P.mm(yo, at_[:, 384:512], Vtok[:, cb:cb + 64], [at_, Vtok], [pY], start=False, stop=False)
                P.mm(yo, gt_[pb:pb + 64, :], Hin[pb:pb + 64, m, :], [gt_, Hin], [pY], start=False, stop=True)
            P.mm(pH[pb:pb + 64, m * 64:(m + 1) * 64], MTs[pb:pb + 64, m, :], Hin[pb:pb + 64, m, :], [MTs, Hin], [pH])
            advance(ADV)

        for _ in head_pre(0):
            pass
        for h in range(8):
            nxt = head_pre(h + 1) if h + 1 < 8 else None
            head_main(h, nxt)
            if nxt is not None:
                for _ in nxt:
                    pass
        P.tt('dve', Hout[:, :, :], pH[:, :].rearrange("p (m i) -> p m i", m=4), Ns[:, :, :], ALU.add, [pH, Ns], [Hout])
        if i == CFG.get('dump_slot', 0):
            dump('hout', Hout, Hout[:, :, :].rearrange('p m i -> p (m i)'))
        state['h'] += 1
        advance(10 ** 6)
        if kind == 'S':
            P.stt('dve', HFa[:, :, :], Hout[:, :, :], SD[:, SD_SIG, i:i + 1], HFa[:, :, :], ALU.mult, ALU.add,
                  [Hout, SD, HFa], [HFa])

        if kind == 'B':
            jj = 15 - (i - 52)
            P.cp('act', ysb[:, :], pY[:, :], [pY], [ysb])
            P.dma(YBs[jj], ysb[:, :], reads=[ysb], writes=[YBs])
            P.dma(SBs[jj], ssum[:, :], reads=[ssum], writes=[SBs])
        if kind == 'F':
            jj = i - 68
            tok = slice(jj * 128, (jj + 1) * 128)
            P.dma(ybl[:, :], YBs[jj], reads=[YBs], writes=[ybl])
            P.dma(sbl[:, :], SBs[jj], reads=[SBs], writes=[sbl])
            P.tt('dve', ysb[:, :], pY[:, :], ybl[:, :], ALU.add, [pY, ybl], [ysb])
            P.tt('dve', ssum[:, :], ssum[:, :], sbl[:, :], ALU.add, [ssum, sbl], [ssum])
            y3 = ysb[:, :].rearrange("p (h i) -> p h i", h=8)
            P.red(st4[:, :], y3, [ysb], [st4])
            P.ts1('dve', st4[:, :], st4[:, :], 1.0 / 64, ALU.mult, [st4], [st4])
            P.act(tE[:, :], ysb[:, :], AF.Square, [ysb], [tE])
            P.red(sbl[:, :], tE[:, :].rearrange("p (h i) -> p h i", h=8), [tE], [sbl])
            P.tt('dve', ybl[:, 0:8], st4[:, :], st4[:, :], ALU.mult, [st4], [ybl])
            P.stt('dve', sbl[:, :], sbl[:, :], 1.0 / 64, ybl[:, 0:8], ALU.mult, ALU.subtract, [sbl, ybl], [sbl])
            rsqrt_ln(sbl, sbl[:, :], sbl, sbl[:, :], 1.0, pc(PC_LEPS))
            for h in range(8):
                P.ts('dve' if h % 2 else 'pool', ysb[:, h * 64:(h + 1) * 64], ysb[:, h * 64:(h + 1) * 64],
                     st4[:, h:h + 1], sbl[:, h:h + 1], ALU.subtract, ALU.mult, [ysb, st4, sbl], [ysb])
            P.tt('dve', ysb[:, :], ysb[:, :], LGB[:, 0:512], ALU.mult, [ysb, LGB], [ysb])
            P.tt('pool', ysb[:, :], ysb[:, :], LGB[:, 512:1024], ALU.add, [ysb, LGB], [ysb])
            for h in range(8):
                P.stt('dve' if h % 2 else 'pool', ysb[:, h * 64:(h + 1) * 64], Vtok[:, h * 64:(h + 1) * 64],
                      ssum[:, h:h + 1], ysb[:, h * 64:(h + 1) * 64], ALU.mult, ALU.add, [Vtok, ssum, ysb], [ysb])
            for m in range(4):
                pb2 = proj(1792 + m * 128, 128, 128)
                P.cp('act', tE[:, m * 128:(m + 1) * 128], pb2[:, 0:128], [pb2], [tE])
            P.act(tD[:, :], tE[:, :], AF.Exp, [tE], [tD], scale=-1.0)
            P.ts1('dve', tD[:, :], tD[:, :], 1.0, ALU.add, [tD], [tD])
            P.recip(tD[:, :], tD[:, :], [tD], [tD])
            P.tt('pool', tE[:, :], tE[:, :], tD[:, :], ALU.mult, [tE, tD], [tE])
            for m in range(4):
                P.tr(pt[:, m * 128:(m + 1) * 128], ysb[:, m * 128:(m + 1) * 128], ident, [ysb, CST], [pt])
            P.tt('dve', mixb[:, :, :], pt[:, :].rearrange("p (m t) -> p m t", m=4), f3(tE), ALU.mult, [pt, tE], [mixb])
            P.dma(MIXRs[:, :, tok], mixb[:, :, :], reads=[mixb], writes=[MIXRs])
            for c in range(3):
                pb2 = proj(2304 + c * 128, 128, 128)
                P.cp('act', mla1[:, c, :], pb2[:, 0:128], [pb2], [mla1])
            P.act(mla2[:, 0:3, :], mla1[:, 0:3, :], AF.Square, [mla1], [mla2])
            for c in range(3):
                P.mm(pz[:, 0:128], ones, mla2[:, c, :], [CST, mla2], [pz], start=(c == 0), stop=(c == 2))
            rsqrt_ln(mla2, mla2[:, 3, :], pz, pz[:, 0:128], 1.0 / 384, pc(PC_EPS))
            for c in range(3):
                P.stt('dve', cqb[:, c, :], mla1[:, c, :], pc(PC_QG + c), mla2[:, 3, :], ALU.mult, ALU.mult,
                      [mla1, PCOL, mla2], [cqb])
            P.dma(CQs[:, :, tok], cqb[:, :, :], reads=[cqb], writes=[CQs])
            for h in range(8):
                pb2 = proj(2976 + h * 64, 64, 128)
                P.cp('act', tD[0:64, 0:128], pb2[0:64, 0:128], [pb2], [tD])
                P.act(tD[0:64, 128:256], tD[0:64, 0:128], AF.Exp, [tD], [tD], scale=-1.0)
                P.ts1('dve', tD[0:64, 128:256], tD[0:64, 128:256], 1.0, ALU.add, [tD], [tD])
                P.recip(tD[0:64, 128:256], tD[0:64, 128:256], [tD], [tD])
                P.tt('dve', gmb[:, h, :], tD[0:64, 0:128], tD[0:64, 128:256], ALU.mult, [tD], [gmb])
            P.dma(GMs[:, :, tok], gmb[:, :, :], reads=[gmb], writes=[GMs])

    slot_list = list(range(NSLOT)) if CFG['slots'] is None else list(CFG['slots'])
    if slot_list:
        load_slot(slot_list[0])
        if len(slot_list) > 1:
            load_slot(slot_list[1])
        for _ in prep(slot_list[0]):
            pass
    for si, i in enumerate(slot_list):
        if si + 2 < len(slot_list):
            load_slot(slot_list[si + 2])
        if i == 68:
            P.cp('dve', Hc[state['h'] % 2][:, :, :], HFa[:, :, :], [HFa], [Hc[state['h'] % 2]])
        heads(i, prep(slot_list[si + 1]) if si + 1 < len(slot_list) else None)

    AR = ARENA.t
    o = 0

    def carve(n, shape_str=None, **kw):
        nonlocal o
        ap = AR[:, o:o + n]
        o += n
        if shape_str:
            ap = ap.rearrange(shape_str, **kw)
        return P.view(ap)

    ps2 = [pA, pY]

    def phase2():
        nonlocal o
        P.barrier()
        o = 0

        KT = [carve(NKEY)] * 2
        V4 = carve(NKT * 4 * 65, "p (k h c) -> p k h c", k=NKT, h=4)
        QT = [carve(2048)] * 2
        def bview(buf, nbf, pat=None, rows=128, **kw):
            t = buf.t
            flat = t[0:rows]
            nd = len(t.shape)
            if nd == 3:
                flat = t[0:rows, :, :].rearrange("p a b -> p (a b)")
            elif nd == 4:
                flat = t[0:rows, :, :, :].rearrange("p a b c -> p (a b c)")
            ap = flat[:, 0:nbf // 2].bitcast(BF16)
            if pat:
                ap = ap.rearrange(pat, **kw)
            return P.view(ap)
        WUQ = bview(RKV, 2304, "p (k c) -> p k c", k=3)
        WUQR = bview(sg, 288, "p (k c) -> p k c", k=3)
        WROT = bview(ysb, 768, "p (k h c) -> p k h c", k=3, h=8)
        WUK = bview(AtokL[0], 1024, "p (k c) -> p k c", k=2)
        WUV = bview(BhtokL[0], 1024, "p (k c) -> p k c", k=2)
        ckb = [bview(KhtokL[0], 1024, "p (k c) -> p k c", k=2), bview(VtokL[0], 1024, "p (k c) -> p k c", k=2)]
        PTb_ = [bview(AtokL[1], 512), bview(BhtokL[1], 512), bview(KhtokL[1], 512)]
        rq = P.view(ARt.t[:, :, :].rearrange('p m c -> p (m c)').rearrange('p (a t) -> p a t', a=2))
        onrm = P.view(tA.t[0:64, :])
        rec = P.view(tB.t[0:65, :])
        bcs = P.view(tC.t[0:64, :])
        mxb = bview(XPB[0], 512, rows=64)
        gml = bview(XPB[1], 512, rows=64)
        cql = bview(junk, 1536, "p (k c) -> p k c", k=3)
        w_uq_v = w_uq_d.rearrange("(k p) c -> p k c", p=128)
        w_uqr_v = w_uq_rot_d.rearrange("(k p) h c -> p k h c", p=128)
        for kc in range(3):
            P.dma(STG[:, 0:768], w_uq_v[:, kc, :], writes=[STG])
            P.cp('dve', WUQ[:, kc, :], STG[:, 0:768], [STG], [WUQ])
        P.memset('pool', WUQR[:, :, :], 0.0, [WUQR])
        P.dma(STG[:, 0:768].rearrange("p (k h c) -> p k h c", k=3, h=8), w_uqr_v, writes=[STG])
        for kc in range(3):
            for h in range(8):
                c0 = kc * 256 + h * 32
                P.tt('dve', WROT[:, kc, h, :], STG[:, c0:c0 + 32], CST[:, C_SGN:C_SGN + 32], ALU.mult, [STG, CST], [WROT])
        for kc_ in range(2):
            P.dma(STG[:, 0:512], w_uk_d.rearrange("(k p) h c -> p k (h c)", p=128)[:, kc_, :], writes=[STG])
            P.cp('dve', WUK[:, kc_, :], STG[:, 0:512], [STG], [WUK])
        P.dma(tA[:, :].rearrange("p (k c) -> p k c", k=2)[:, :, 0:256], w_uv_d.rearrange("(k p) h c -> p k (h c)", p=128)[:, :, 0:256], writes=[tA])
        P.dma(tB[:, :].rearrange("p (k c) -> p k c", k=2)[:, :, 0:256], w_uv_d.rearrange("(k p) h c -> p k (h c)", p=128)[:, :, 256:512], writes=[tB])
        P.cp('dve', WUV[:, :, 0:256], tA[:, :].rearrange("p (k c) -> p k c", k=2), [tA], [WUV])
        P.cp('dve', WUV[:, :, 256:512], tB[:, :].rearrange("p (k c) -> p k c", k=2), [tB], [WUV])

        po = [pz, pBXf]
        KB = [(kb * 512, 512) for kb in range(NKEY // 512)]
        for hg in range(2):
            for hh_ in range(4):
                P.cp('dve', V4[:, :, hh_, 64], KVV[:, :], [KVV], [V4])
            for bi, (k0, kw) in enumerate(KB):
                cb_ = ckb[bi % 2]
                P.dma(cb_[:, :, 0:kw], CKVs[:, :, k0:k0 + kw], reads=[CKVs], writes=[cb_])
                for s in range(kw // 128):
                    kt = k0 // 128 + s
                    pv = pu[kt % 2]
                    for c in range(2):
                        P.mm(pv[:, 0:256], cb_[:, c, s * 128:(s + 1) * 128], WUV[:, c, hg * 256:(hg + 1) * 256], [cb_, WUV], [pv],
                             start=(c == 0), stop=(c == 1))
                    P.ts1('dve', V4[:, kt, :, 0:64], pv[:, 0:256].rearrange("p (h c) -> p h c", h=4), KVV[:, kt:kt + 1], ALU.mult, [pv, KVV], [V4])
            for hh in range(4):
                h = hg * 4 + hh
                kt_, qt_ = KT[h % 2], QT[h % 2]
                P.dma(kt_[64:96, :], KRs[:, :], reads=[KRs], writes=[kt_])
                for bi, (k0, kw) in enumerate(KB):
                    cb_ = ckb[bi % 2]
                    P.dma(cb_[:, :, 0:kw], CKVs[:, :, k0:k0 + kw], reads=[CKVs], writes=[cb_])
                    for c in range(2):
                        P.mm(ps2[bi % 2][0:64, 0:kw], WUK[:, c, h * 64:(h + 1) * 64], cb_[:, c, 0:kw],
                             [cb_, WUK], [ps2[bi % 2]], start=(c == 0), stop=(c == 1))
                    P.cp('act' if bi % 2 else 'dve', kt_[0:64, k0:k0 + kw], ps2[bi % 2][0:64, 0:kw], [ps2[bi % 2]], [kt_])
                for kc in range(3):
                    P.cp('pool', WUQR[:, kc, 64:96], WROT[:, kc, h, :], [WROT], [WUQR])
                for qb in range(4):
                    qs = slice(qb * 512, (qb + 1) * 512)
                    P.dma(cql[:, :, :], CQs[:, :, qs], reads=[CQs], writes=[cql])
                    P.dma(rq[64:96, :, :], ropeq[:, :, qs].rearrange("a d t -> d a t"), writes=[rq])
                    for c in range(3):
                        P.mm(po[0][0:96, :], WUQ[:, c, h * 96:(h + 1) * 96], cql[:, c, :], [WUQ, cql], [po[0]],
                             start=(c == 0), stop=(c == 2))
                    for c in range(3):
                        P.mm(po[1][0:96, :], WUQR[:, c, :], cql[:, c, :], [WUQR, cql], [po[1]], start=(c == 0), stop=(c == 2))
                    P.cp('act', qt_[0:64, qs], po[0][0:64, :], [po[0]], [qt_])
                    P.tt('dve', rq[64:96, 0, :], po[0][64:96, :], rq[64:96, 0, :], ALU.mult, [po[0], rq], [rq])
                    P.tt('dve', rq[64:96, 1, :], po[1][64:96, :], rq[64:96, 1, :], ALU.mult, [po[1], rq], [rq])
                    P.tt('dve', qt_[64:96, qs], rq[64:96, 0, :], rq[64:96, 1, :], ALU.add, [rq], [qt_])
                for qb in range(4):
                    qs = slice(qb * 512, (qb + 1) * 512)
                    pacc = po[qb % 2]
                    P.dma(gml[:, :], GMs[:, h, qs], reads=[GMs], writes=[gml])
                    def qk(kt):
                        sc = ps2[kt % 2]
                        P.mm(sc[:, :], kt_[0:96, kt * 128:(kt + 1) * 128], qt_[0:96, qs], [kt_, qt_], [sc])
                    qk(0)
                    for kt in range(NKT):
                        if kt + 1 < NKT:
                            qk(kt + 1)
                        sc = ps2[kt % 2]
                        pT = PTb_[kt % 3]
                        P.act(pT[:, :], sc[:, :], AF.Exp, [sc], [pT], scale=ATTN_SCALE)
                        P.mm(pacc[0:65, :], V4[:, kt, hh, :], pT[:, :], [V4, pT], [pacc], start=(kt == 0), stop=(kt == NKT - 1))
                    P.recip(rec[64:65, :], pacc[64:65, :], [pacc], [rec])
                    P.mm(pH[0:64, 0:256], CST[64:65, C_ONES:C_ONES + 64], rec[64:65, 0:256], [CST, rec], [pH])
                    P.mm(pG[0:64, 0:256], CST[64:65, C_ONES:C_ONES + 64], rec[64:65, 256:512], [CST, rec], [pG])
                    P.cp('act', bcs[:, 0:256], pH[0:64, 0:256], [pH], [bcs])
                    P.cp('act', bcs[:, 256:512], pG[0:64, 0:256], [pG], [bcs])
                    P.tt('dve', onrm[:, :], pacc[0:64, :], bcs[:, :], ALU.mult, [pacc, bcs], [onrm])
                    P.tt('dve', mxb[:, :], onrm[:, :], gml[:, :], ALU.mult, [onrm, gml], [mxb])
                    P.dma(MIXMs[:, h, qs], mxb[:, :], reads=[mxb], writes=[MIXMs])

    def phase3():
        nonlocal o
        P.barrier()
        o = 0
        WOR = carve(4 * 1024, "p (k c) -> p k c", k=4)
        WOM = carve(8 * 1024, "p (k c) -> p k c", k=8)
        mr = [carve(512, "p (k c) -> p k c", k=4), carve(512, "p (k c) -> p k c", k=4)]
        mm_ = [carve(1024, "p (k c) -> p k c", k=8), carve(1024, "p (k c) -> p k c", k=8)]
        w_out_r = w_out_d[0:512, :].rearrange("(k p) c -> p k c", p=128)
        w_out_m = w_out_d[512:1024, :].rearrange("(k p) c -> p k c", p=64)
        for k in range(4):
            P.dma(STG[:, 0:1024], w_out_r[:, k, :], writes=[STG])
            P.cp('dve', WOR[:, k, :], STG[:, 0:1024], [STG], [WOR])
        for k in range(8):
            P.dma(STG[0:64, 0:1024], w_out_m[:, k, :], writes=[STG])
            P.cp('dve', WOM[0:64, k, :], STG[0:64, 0:1024], [STG], [WOM])
        GB = P.view(BKh.t[:, :])
        FG = P.view(BKt.t[:, :, :].rearrange('p m c -> p (m c)'))
        for half in range(2):
            for q in range(4):
                kc = half * 4 + q
                P.ts1('dve', th0[:, :], ident, MODT[:, 16 + kc, 0:1], ALU.mult, [CST, MODT], [th0])
                P.mm(pt[:, q * 128:(q + 1) * 128], ones, th0[:, :], [CST, th0], [pt])
            P.cp('act', GB[:, half * 512:(half + 1) * 512], pt[:, :], [pt], [GB])
        P.dma(FG[:, :], brow_d[:, 1024:2048], writes=[FG])
        yo_ = [P.view(RKV.t[:, 0:8, :].rearrange('p m t -> p (m t)')), P.view(EALL.t[:, 0:2, :, :].rearrange('p m a t -> p (m a t)'))]
        for jj in range(16):
            tok = slice(jj * 128, (jj + 1) * 128)
            x_ = xt[jj % 2]
            P.dma(x_[:, :], xs[68 + jj, 0:128, :], writes=[x_])
            P.dma(mr[jj % 2][:, :, :], MIXRs[:, :, tok], reads=[MIXRs], writes=[mr[jj % 2]])
            P.dma(mm_[jj % 2][0:64, :, :], MIXMs[:, :, tok], reads=[MIXMs], writes=[mm_[jj % 2]])
            y_ = yo_[jj % 2]
            for half in range(2):
                cs = slice(half * 512, (half + 1) * 512)
                pp = ps2[half]
                for k in range(4):
                    P.mm(pp[:, :], mr[jj % 2][:, k, :], WOR[:, k, cs], [mr[jj % 2], WOR], [pp], start=(k == 0), stop=False)
                for k in range(8):
                    P.mm(pp[:, :], mm_[jj % 2][0:64, k, :], WOM[0:64, k, cs], [mm_[jj % 2], WOM], [pp], start=False, stop=(k == 7))
                P.tt('dve', y_[:, cs], pp[:, :], GB[:, cs], ALU.mult, [pp, GB], [y_])
            P.tt('pool', y_[:, :], y_[:, :], x_[:, :], ALU.add, [y_, x_], [y_])
            P.act(junk[:, :], y_[:, :], AF.Square, [y_], [junk])
            P.red(st4[:, 0:1], junk[:, :], [junk], [st4])
            rsqrt_ln(st4, st4[:, 0:1], st4, st4[:, 0:1], 1.0 / 1024, pc(PC_EPS))
            P.stt('dve', y_[:, :], y_[:, :], st4[:, 0:1], FG[:, :], ALU.mult, ALU.mult, [y_, st4, FG], [y_])
            P.dma(out_d[tok, :], y_[:, :], reads=[y_])
    if CFG['p2']:
        phase2()
    if CFG['p3']:
        phase3()
    P.finish()
    return nc


DEBUG_SPECS = {'xf': (128, 192), 'at': (128, 512), 'atok': (128, 512), 'vtok': (128, 512), 'eall': (128, 2048), 'sg': (128, 512), 'rkv': (128, 1536), 'hout': (128, 256)}
_CACHE = {}


def kernel(**inputs):
    maps = host_prep(inputs)
    if 'nc' not in _CACHE:
        _CACHE['nc'] = build()
    nc = _CACHE['nc']
    res = run_bass_kernel_spmd(nc, maps, core_ids=list(range(8)))
    out = np.zeros((2, 8192, 1024), np.float32)
    for c in range(8):
        b, p = c // 4, c % 4
        out[b, 2048 * p:2048 * (p + 1)] = np.asarray(res.results[c]['out'], np.float32)
    if DEBUG:
        _CACHE['res'] = res
    return out
```

```python
import numpy as np
from contextlib import ExitStack
import concourse.bass as bass
import concourse.mybir as mybir
from concourse.bass_utils import run_bass_kernel_spmd

F32 = mybir.dt.float32
BF16 = mybir.dt.bfloat16
ALU = mybir.AluOpType
AF = mybir.ActivationFunctionType
AX = mybir.AxisListType
BANK = 30000
NDMA = 24
ENGS = ['pe', 'act', 'dve', 'pool', 'sp']
DEBUG = False
CFG = {'slots': None, 'p2': True, 'p3': True, 'stage': 99}

D_MODEL = 1024
D_IN = 3488
NSTATE = 52
NSLOT = 84
NKT = 68
NKEY = NKT * 128
KAPPA = -float(np.exp(-0.5))
ATTN_SCALE = 96 ** -0.5
NORM_EPS = 1e-6
LNX_EPS = 64e-5


class Trk:
    __slots__ = ('w', 'r')

    def __init__(self):
        self.w = None
        self.r = {}


class Buf:
    def __init__(self, t, ap=None):
        self.t = t
        self.k = Trk()
        self._ap = ap

    def __getitem__(self, idx):
        return (self._ap if self._ap is not None else self.t)[idx]

    @property
    def ap(self):
        return self._ap if self._ap is not None else self.t[:]


class Prog:
    def __init__(self, nc):
        self.nc = nc
        self.es = ExitStack()
        self.ops = {e: [] for e in ENGS}
        self.cnt = {e: 0 for e in ENGS}
        self.seen = {e: {} for e in ENGS}
        self.sems = {}
        self.dma_tot = [0] * NDMA
        self.dma_next = 0

    def sb(self, name, shape, dt=F32):
        return Buf(self.es.enter_context(self.nc.sbuf_tensor('sb_' + name, list(shape), dt)))

    def ps(self, name, shape, dt=F32):
        return Buf(self.es.enter_context(self.nc.psum_tensor('ps_' + name, list(shape), dt)))

    def view(self, ap):
        return Buf(None, ap)

    def _deps(self, eng, reads, writes):
        deps = {}

        def add(ev):
            if ev is None:
                return
            k, v = ev
            if deps.get(k, 0) < v:
                deps[k] = v
        for b in reads:
            add(b.k.w)
        for b in writes:
            add(b.k.w)
            for k, v in b.k.r.items():
                add((k, v))
        waits = []
        for k, v in deps.items():
            if k[0] == 'pe' and eng == 'pe':
                continue
            if self.seen[eng].get(k, 0) >= v:
                continue
            self.seen[eng][k] = v
            waits.append((k, v))
        return waits

    def _mark(self, ev, reads, writes):
        k, v = ev
        for b in writes:
            b.k.w = ev
            b.k.r = {}
        for b in reads:
            if b in writes:
                continue
            if b.k.r.get(k, 0) < v:
                b.k.r[k] = v

    def op(self, eng, fn, reads=(), writes=()):
        waits = self._deps(eng, reads, writes)
        i = self.cnt[eng]
        self.cnt[eng] += 1
        ev = ((eng, i // BANK), i % BANK + 1)
        self._mark(ev, reads, writes)
        self.ops[eng].append((waits, fn, ev[0], 1))

    def dma(self, out, in_, reads=(), writes=(), eng='sp', **kw):
        waits = self._deps(eng, reads, writes)
        s = self.dma_next % NDMA
        self.dma_next += 1
        k = ('dma', s)
        if self.dma_tot[s] > 0 and self.seen[eng].get(k, 0) < self.dma_tot[s]:
            self.seen[eng][k] = self.dma_tot[s]
            waits.append((k, self.dma_tot[s]))
        self.dma_tot[s] += 16
        ev = (k, self.dma_tot[s])
        self._mark(ev, reads, writes)
        self.ops[eng].append((waits, lambda e: e.dma_start(out=out, in_=in_, **kw), k, 16))

    def barrier(self):
        last = {}
        for e in ENGS:
            i = self.cnt[e]
            if i > 0:
                last[(e, (i - 1) // BANK)] = (i - 1) % BANK + 1
        for s in range(NDMA):
            if self.dma_tot[s] > 0:
                last[('dma', s)] = self.dma_tot[s]
        for e in ENGS:
            waits = []
            for k, v in last.items():
                if k[0] == e:
                    continue
                if self.seen[e].get(k, 0) >= v:
                    continue
                self.seen[e][k] = v
                waits.append((k, v))
            if waits:
                self.ops[e].append((waits, None, None, 0))

    def mm(self, out, lhsT, rhs, rd, wr, start=True, stop=True):
        self.op('pe', lambda e: e.matmul(out, lhsT, rhs, start=start, stop=stop), rd, wr)

    def tr(self, out, in_, ident, rd, wr):
        self.op('pe', lambda e: e.transpose(out, in_, ident), rd, wr)

    def act(self, out, in_, func, rd, wr, bias=None, scale=None):
        kw = {}
        if bias is not None:
            kw['bias'] = bias
        if scale is not None:
            kw['scale'] = scale
        self.op('act', lambda e: e.activation(out, in_, func, **kw), rd, wr)

    def tt(self, eng, out, a, b, op, rd, wr):
        self.op(eng, lambda e: e.tensor_tensor(out, a, b, op), rd, wr)

    def ts(self, eng, out, a, s1, s2, op0, op1, rd, wr):
        self.op(eng, lambda e: e.tensor_scalar(out, a, s1, s2, op0, op1), rd, wr)

    def ts1(self, eng, out, a, s, op, rd, wr):
        self.op(eng, lambda e: e.tensor_single_scalar(out, a, s, op), rd, wr)

    def stt(self, eng, out, a, s, b, op0, op1, rd, wr):
        self.op('dve', lambda e: e.scalar_tensor_tensor(out, a, s, b, op0, op1), rd, wr)

    def cp(self, eng, out, in_, rd, wr):
        if eng == 'act':
            self.op('act', lambda e: e.copy(out, in_), rd, wr)
        else:
            self.op(eng, lambda e: e.tensor_copy(out, in_), rd, wr)

    def recip(self, out, in_, rd, wr):
        self.op('dve', lambda e: e.reciprocal(out, in_), rd, wr)

    def red(self, out, in_, rd, wr, op=ALU.add):
        self.op('dve', lambda e: e.tensor_reduce(out, in_, AX.X, op), rd, wr)

    def memset(self, eng, ap, val, wr):
        self.op(eng, lambda e: e.memset(ap, val), (), wr)

    def finish(self):
        nc = self.nc
        keys = set()
        for e in ENGS:
            for waits, fn, k, inc in self.ops[e]:
                if k is not None:
                    keys.add(k)
                for wk, _ in waits:
                    keys.add(wk)
        for k in sorted(keys, key=str):
            self.sems[k] = self.es.enter_context(nc.semaphore("s_%s_%s" % k))
        fin = [(('dma', s), self.dma_tot[s]) for s in range(NDMA) if self.dma_tot[s] > 0]
        block = self.es.enter_context(nc.Block())

        def run(e, name):
            for waits, fn, k, inc in self.ops[name]:
                fold = fn is not None and name != 'sp' and len(waits) > 0
                for wk, wv in (waits[1:] if fold else waits):
                    e.wait_ge(self.sems[wk], wv)
                if fn is not None:
                    ins = fn(e)
                    if fold:
                        ins._wait_ge(self.sems[waits[0][0]], waits[0][1])
                    ins.then_inc(self.sems[k], inc)
            if name == 'sp':
                for wk, wv in fin:
                    e.wait_ge(self.sems[wk], wv)

        @block.tensor
        def _(e):
            run(e, 'pe')

        @block.scalar
        def _(e):
            run(e, 'act')

        @block.vector
        def _(e):
            run(e, 'dve')

        @block.gpsimd
        def _(e):
            run(e, 'pool')

        @block.sync
        def _(e):
            run(e, 'sp')
        self.es.close()


PC_NG, PC_ADAB, PC_MUP, PC_MUN, PC_KK, PC_KA, PC_RK, PC_A0, PC_QG, PC_KVG = 0, 8, 32, 46, 60, 64, 68, 72, 80, 83
PC_EPS, PC_LEPS, PC_ONE, PC_TINY = 85, 86, 87, 88
NPC = 96
C_ID, C_ONES, C_BO, C_F, C_B, C_TOT, C_BO2, C_SGN = 0, 128, 256, 384, 1280, 2176, 2177, 2179
SD_F, SD_SELW, SD_M2S, SD_PHI, SD_RHO, SD_SIG, SD_NF = 0, 1, 2, 3, 4, 5, 6
NCST = 2211

SLOTS = []
for _i in range(52):
    SLOTS.append(('S', _i))
for _j in range(16):
    SLOTS.append(('B', None))
for _j in range(16):
    SLOTS.append(('F', 52 + _j))


def _consts():
    c = np.zeros((128, NCST), np.float32)
    idx = np.arange(128)
    row, col = idx[:, None], idx[None, :]
    c[:, C_ID:C_ID + 128] = np.eye(128)
    c[:, C_ONES:C_ONES + 128] = 1.0
    c[:, C_BO:C_BO + 128] = ((row // 64) == (col // 64))
    k = KAPPA
    tf = [k * (row < col), -k * (row <= col), k * (row <= col), k * (row > col)]
    tb = [k * (row > col), -k * (row >= col), k * (row >= col), k * (row < col)]
    c[:, C_F:C_F + 896] = np.concatenate(tf + [col > row, col >= row, col < row], 1)
    c[:, C_B:C_B + 896] = np.concatenate(tb + [col < row, col <= row, col > row], 1)
    c[:, C_TOT] = k
    c[:, C_BO2] = (idx < 64)
    c[:, C_BO2 + 1] = (idx >= 64)
    d = np.arange(32)
    c[:, C_SGN:C_SGN + 32] = np.where(d % 16 < 8, -1.0, 1.0)[None, :]
    return c


ROT_PERM = np.array([dd + 8 if dd % 16 < 8 else dd - 8 for dd in range(32)])


def _rope_tables(pos):
    pos = np.asarray(pos)
    inv = (np.float32(10000.0) ** (-np.arange(0, 16, 2, dtype=np.float32) / np.float32(16))).astype(np.float32)
    rowf = (pos // 64).astype(np.float32)
    colf = (pos % 64).astype(np.float32)
    ar = (rowf[:, None] * inv[None, :]).astype(np.float32)
    ac = (colf[:, None] * inv[None, :]).astype(np.float32)
    cos = np.concatenate([np.cos(ar), np.cos(ar), np.cos(ac), np.cos(ac)], 1).astype(np.float32)
    sin = np.concatenate([np.sin(ar), np.sin(ar), np.sin(ac), np.sin(ac)], 1).astype(np.float32)
    return cos.T.copy(), sin.T.copy()


def _tile_halo(seq, c):
    n = seq.shape[0]
    t = np.zeros((130, seq.shape[1]), np.float32)
    t[:128] = seq[c * 128:(c + 1) * 128]
    hv = [0.0, 0.0]
    if c > 0:
        t[128] = seq[c * 128 - 1]
        hv[0] = 1.0
    if (c + 1) * 128 < n:
        t[129] = seq[(c + 1) * 128]
        hv[1] = 1.0
    return t, hv


def host_prep(inp):
    g = {k: np.asarray(v, np.float32) for k, v in inp.items()}
    sh = {}
    w_in = g['w_in'][0]
    sh['w_in'] = w_in
    sh['wkr_rot'] = np.ascontiguousarray(w_in[:, 2944:2976][:, ROT_PERM])
    w_uq = g['mla_w_uq'][0]
    sh['w_uq'] = w_uq
    sh['w_uq_rot'] = np.ascontiguousarray(
        np.stack([w_uq[:, h * 96 + 64:h * 96 + 96][:, ROT_PERM] for h in range(8)], 1))
    w_ukv = g['mla_w_ukv'][0].reshape(256, 8, 128)
    sh['w_uk'] = np.ascontiguousarray(w_ukv[:, :, :64])
    sh['w_uv'] = np.ascontiguousarray(w_ukv[:, :, 64:])
    sh['w_out'] = g['w_out'][0]
    sh['ada_w'] = g['ada_w'][0]
    sh['w2cat'] = np.ascontiguousarray(g['rw_w2'][0].reshape(128, 512))
    sh['a2cat'] = np.ascontiguousarray(g['rw_a2'][0].reshape(128, 512))
    sh['w0r'] = np.ascontiguousarray(g['rw_w0'][0].reshape(1, 1024))
    pc = np.zeros((128, NPC), np.float32)
    pc[:, PC_NG:PC_NG + 8] = g['norm_g'][0].reshape(8, 128).T
    pc[:, PC_ADAB:PC_ADAB + 24] = g['ada_b'][0].reshape(24, 128).T
    pc[:, PC_MUP:PC_MUP + 14] = g['shift_mu'][0, 0].reshape(14, 128).T
    pc[:, PC_MUN:PC_MUN + 14] = g['shift_mu'][0, 1].reshape(14, 128).T
    pc[:, PC_KK:PC_KK + 4] = g['rw_kk'][0].reshape(4, 128).T
    pc[:, PC_KA:PC_KA + 4] = g['rw_ka'][0].reshape(4, 128).T
    pc[:, PC_RK:PC_RK + 4] = g['rw_rk'][0].reshape(4, 128).T
    pc[:, PC_A0:PC_A0 + 4] = g['rw_a0'][0, 0].reshape(4, 128).T
    pc[:, PC_A0 + 4:PC_A0 + 8] = g['rw_a0'][0, 1].reshape(4, 128).T
    pc[:, PC_QG:PC_QG + 3] = g['mla_q_norm_g'][0].reshape(3, 128).T
    pc[:, PC_KVG:PC_KVG + 2] = g['mla_kv_norm_g'][0].reshape(2, 128).T
    pc[:, PC_EPS] = NORM_EPS
    pc[:, PC_LEPS] = LNX_EPS
    pc[:, PC_ONE] = 1.0
    pc[:, PC_TINY] = 1e-24
    sh['pcol'] = pc
    br = np.zeros((128, 2048), np.float32)
    br[:, 0:512] = g['rw_lnx_g'][0][None, :]
    br[:, 512:1024] = g['rw_lnx_b'][0][None, :]
    br[:, 1024:2048] = g['final_g'][None, :]
    sh['brow'] = br
    sh['cst'] = _consts()
    maps = []
    for b in range(2):
        x, ctx = g['x'][b], g['ctx'][b]
        cvec = np.stack([g['c'][b].reshape(128, 8), g['c_ctx'].reshape(128, 8)], -1)
        for p in range(4):
            nf = 2 + 16 * p
            fsrc = [(ctx, 0, True), (ctx, 1, True)] + [(x, c, False) for c in range(16 * p)]
            bsrc = [(ctx, 1, True), (ctx, 0, True)] + [(x, c, False) for c in range(63, 16 * p + 15, -1)]
            own = [(x, 16 * p + j, False) for j in range(16)]
            allsrc = fsrc + bsrc + own[::-1] + own
            assert len(allsrc) == NSLOT
            tiles, hvs = [], []
            for seq, c, _ in allsrc:
                t, hv = _tile_halo(seq, c)
                tiles.append(t)
                hvs.append(hv)
            xs = np.stack(tiles, 0)
            hvs = np.array(hvs, np.float32)
            sd = np.zeros((128, 7, NSLOT), np.float32)
            f = np.array([1.0] * nf + [0.0] * (52 - nf) + [0.0] * 16 + [1.0] * 16, np.float32)
            sd[:, SD_F, :] = f[None]
            sd[:, SD_NF, :] = 1.0 - f[None]
            sd[:64, SD_SELW, :] = f[None]
            sd[64:, SD_SELW, :] = 1.0 - f[None]
            sd[:, SD_M2S, :] = -2.0 * sd[:, SD_SELW, :]
            sd[:, SD_PHI, :] = np.array([1.0 if ic else 0.0 for _, _, ic in allsrc], np.float32)[None]
            rho = np.ones(NSLOT, np.float32)
            rho[nf] = 0.0
            sd[:, SD_RHO, :] = rho[None]
            sgm = np.zeros(NSLOT, np.float32)
            sgm[nf - 1] = 1.0
            sd[:, SD_SIG, :] = sgm[None]
            kvsrc = fsrc + bsrc + own
            rk = np.zeros((NKT, 2, 32, 128), np.float32)
            rk[:, 0] = 1.0
            kvv = np.ones(NKT, np.float32)
            kvv[nf] = 0.0
            kvv[nf + 1] = 0.0
            for kt, (seq, c, ic) in enumerate(kvsrc):
                if not ic:
                    cs, sn = _rope_tables(np.arange(c * 128, (c + 1) * 128))
                    rk[kt, 0], rk[kt, 1] = cs, sn
            cq, sq = _rope_tables(np.arange(2048 * p, 2048 * (p + 1)))
            m = dict(sh)
            m['xs'] = xs
            m['hv'] = np.ascontiguousarray(np.broadcast_to(hvs[None], (128, NSLOT, 2)))
            m['sd'] = sd
            m['kvv'] = np.ascontiguousarray(np.broadcast_to(kvv[None], (128, NKT)))
            m['ropek'] = rk
            m['ropeq'] = np.stack([cq, sq], 0)
            m['cvec'] = cvec
            maps.append(m)
    return maps


def build():
    nc = bass.Bass("TRN2", target_bir_lowering=False)
    P = Prog(nc)

    def din(name, shape):
        return nc.dram_tensor(name, list(shape), F32, kind="ExternalInput").ap()

    xs = din('xs', [NSLOT, 130, 1024])
    hv_d = din('hv', [128, NSLOT, 2])
    sd_d = din('sd', [128, 7, NSLOT])
    kvv_d = din('kvv', [128, NKT])
    ropek = din('ropek', [NKT, 2, 32, 128])
    ropeq = din('ropeq', [2, 32, 2048])
    cvec_d = din('cvec', [128, 8, 2])
    w_in_d = din('w_in', [1024, D_IN])
    wkr_rot_d = din('wkr_rot', [1024, 32])
    w_uq_d = din('w_uq', [384, 768])
    w_uq_rot_d = din('w_uq_rot', [384, 8, 32])
    w_uk_d = din('w_uk', [256, 8, 64])
    w_uv_d = din('w_uv', [256, 8, 64])
    w_out_d = din('w_out', [1024, 1024])
    ada_w_d = din('ada_w', [1024, 3072])
    w2cat_d = din('w2cat', [128, 512])
    a2cat_d = din('a2cat', [128, 512])
    w0r_d = din('w0r', [1, 1024])
    pcol_d = din('pcol', [128, NPC])
    brow_d = din('brow', [128, 2048])
    cst_d = din('cst', [128, NCST])
    out_d = nc.dram_tensor('out', [2048, 1024], F32, kind="ExternalOutput").ap()
    dbg = {}
    if DEBUG:
        for nm, shp in DEBUG_SPECS.items():
            dbg[nm] = nc.dram_tensor('dbg_' + nm, list(shp), F32, kind="ExternalOutput").ap()

    def dscr(name, shape, dt):
        t = nc.dram_tensor(name, list(shape), dt)
        return Buf(t, t.ap())

    CKVs = dscr('s_ckv', [128, 2, NKEY], BF16)
    KRs = dscr('s_kr', [32, NKEY], BF16)
    CQs = dscr('s_cq', [128, 3, 2048], BF16)
    GMs = dscr('s_gm', [64, 8, 2048], BF16)
    MIXRs = dscr('s_mixr', [128, 4, 2048], BF16)
    MIXMs = dscr('s_mixm', [64, 8, 2048], BF16)
    YBs = dscr('s_yb', [16, 128, 512], F32)
    SBs = dscr('s_sb', [16, 128, 8], F32)

    CST = P.sb('cst', [128, NCST])
    PCOL = P.sb('pcol', [128, NPC])
    LGB = P.sb('lgb', [128, 1024])
    HV = P.sb('hv', [128, NSLOT, 2])
    SD = P.sb('sd', [128, 7, NSLOT])
    KVV = P.sb('kvv', [128, NKT])
    CSEL = P.sb('csel', [128, 512])
    GSL = P.sb('gsl', [128, 4, 8])
    W0S = P.sb('w0s', [1, 512])
    NA0S = P.sb('na0s', [128, 4])
    W2C = P.sb('w2c', [128, 512])
    A2C = P.sb('a2c', [128, 512])
    W0R = P.sb('w0r', [1, 1024])
    MODT = P.sb('modt', [128, 24, 2])
    GS = P.sb('gs', [128, 2, 8])
    SH = P.sb('shc', [128, 2, 8])
    M0 = P.sb('m0', [128, 14])
    NA0 = P.sb('na0', [128, 8])
    ARENA = P.sb('arena', [128, 28432], BF16)
    WB = P.view(ARENA.t[:, 0:27904].rearrange("p (k c) -> p k c", k=8))
    WKR = P.sb('wkr', [128, 8, 2, 96], BF16)

    for (b_, d_) in [(CST, cst_d), (PCOL, pcol_d), (HV, hv_d), (SD, sd_d), (KVV, kvv_d), (W2C, w2cat_d), (A2C, a2cat_d),
                     (W0R, w0r_d)]:
        P.dma(b_.ap, d_, writes=[b_])
    P.dma(LGB[:, :], brow_d[:, 0:1024], writes=[LGB])
    ident = CST[:, C_ID:C_ID + 128]
    ones = CST[:, C_ONES:C_ONES + 128]

    def pc(c0, n=1):
        return PCOL[:, c0:c0 + n]

    def ps_pair(name):
        q = P.ps(name, [128, 512])
        v0, v1 = P.view(q.t[:, 0:256]), P.view(q.t[:, 256:512])
        v0.k = q.k
        v1.k = q.k
        return v0, v1, q
    pu = [P.ps('pu0', [128, 512]), P.ps('pu1', [128, 512])]
    pz = P.ps('pz', [128, 512])
    pt = pz
    pA = P.ps('pA', [128, 512])
    pB, pX, pBXf = ps_pair('qbx')
    pS, pM, _ = ps_pair('qsm')
    pY = P.ps('pY', [128, 512])
    pH, pG, _ = ps_pair('qhg')

    xt = [P.sb('xt0', [128, 1024]), P.sb('xt1', [128, 1024])]
    xh = [P.sb('xh0', [2, 1024])]
    junk = P.sb('junk', [128, 1024])
    tD = P.view(junk.t[:, 0:512])
    tE = P.view(junk.t[:, 512:1024])
    tD.k = junk.k
    tE.k = junk.k
    st4 = P.sb('st4', [128, 8])
    hT = [P.sb('hT0', [128, 8, 130], BF16), P.sb('hT1', [128, 8, 130], BF16)]
    shv = P.sb('shv', [128, 2, 8])
    ue = [P.sb('ue0', [128, 130]), P.sb('ue1', [128, 130])]
    RKV = P.sb('rkv', [128, 12, 128])
    WA = P.sb('wa', [128, 2, 128])
    tA = P.sb('tA', [128, 512])
    tB = P.sb('tB', [128, 512])
    tC = P.sb('tC', [128, 512])
    th0 = P.sb('th0', [128, 128])
    th1 = P.sb('th1', [128, 128])
    sg = P.sb('sg', [128, 512])
    EALL = P.sb('eall', [128, 4, 4, 128])
    STG = P.view(EALL.t[:, :, :, :].rearrange('p m a t -> p (m a t)')[:, 0:1792])
    STG.k = EALL.k
    GCL = [P.sb('gc0', [128, 4]), P.sb('gc1', [128, 4])]
    MSKL = [P.sb('msk0', [128, 384]), P.sb('msk1', [128, 384])]
    ARtL = [P.sb('art%d' % q_, [128, 4, 256]) for q_ in range(2)]
    BKtL = [P.sb('bkt%d' % q_, [128, 4, 256]) for q_ in range(2)]
    ARt, BKt = ARtL[0], BKtL[0]
    BKh = P.sb('bkh', [128, 1024])
    BhT = P.view(BKh.t[:, 0:512])
    KhT = P.view(BKh.t[:, 512:1024])
    BhT.k = BKh.k
    KhT.k = BKh.k
    AtokL = [P.sb('atok%d' % q_, [128, 512]) for q_ in range(2)]
    BhtokL = [P.sb('bhtok%d' % q_, [128, 512]) for q_ in range(2)]
    KhtokL = [P.sb('khtok%d' % q_, [128, 512]) for q_ in range(2)]
    VtokL = [P.sb('vtok%d' % q_, [128, 512]) for q_ in range(2)]
    AT = [P.sb('at0', [128, 512]), P.sb('at1', [128, 512])]
    XPBS = [[P.sb('xp%d_%d' % (p_, q_), [128, 320]) for q_ in range(3)] for p_ in range(2)]
    PTBS = [[P.sb('ptq%d_%d' % (p_, q_), [128, 128]) for q_ in range(2)] for p_ in range(2)]
    XPB = XPBS[0] + XPBS[1]
    MTs = P.sb('mts', [128, 4, 64])
    Ns = P.sb('ns', [128, 4, 64])
    GTs = [P.sb('gts0', [128, 128]), P.sb('gts1', [128, 128])]
    Hc = [P.sb('hc0', [128, 4, 64]), P.sb('hc1', [128, 4, 64])]
    HFa = P.sb('hfa', [128, 4, 64])
    ysb = P.sb('ysb', [128, 512])
    ybl = P.view(sg.t[:, :])
    ybl.k = sg.k
    sbl = P.sb('sbl', [128, 8])
    ssumL = [P.sb('ssum0', [128, 8]), P.sb('ssum1', [128, 8])]
    mla1 = P.sb('mla1', [128, 4, 128])
    mla2 = P.view(tC.t[:, :].rearrange('p (a t) -> p a t', a=4))
    mla2.k = tC.k
    kvb = P.sb('kvb', [128, 2, 128], BF16)
    krb = P.sb('krb', [96, 128], BF16)
    rpk = P.sb('rpk', [96, 2, 128])
    cqb = P.sb('cqb', [128, 3, 128], BF16)
    gmb = P.sb('gmb', [64, 8, 128], BF16)
    mixb = P.sb('mixb', [128, 4, 128], BF16)

    def dump(name, buf, ap):
        if DEBUG and name in dbg:
            P.dma(dbg[name], ap, reads=[buf])

    w_in_v = w_in_d.rearrange("(k p) c -> p k c", p=128)
    for kc in range(8):
        for hf in range(2):
            P.dma(STG[:, 0:1744], w_in_v[:, kc, hf * 1744:(hf + 1) * 1744], writes=[STG])
            P.cp(['act', 'dve'][hf], WB[:, kc, hf * 1744:(hf + 1) * 1744], STG[:, 0:1744], [STG], [WB])
    P.memset('pool', WKR[:, :, :, :], 0.0, [WKR])
    for kc in range(8):
        P.cp('dve', WKR[:, kc, 0, 64:96], WB[:, kc, 2944:2976], [WB], [WKR])
    wkr_v = wkr_rot_d.rearrange("(k p) c -> p k c", p=128)
    P.dma(junk[:, 0:256].rearrange("p (k c) -> p k c", k=8), wkr_v, writes=[junk])
    for kc in range(8):
        P.tt('dve', WKR[:, kc, 1, 64:96], junk[:, kc * 32:(kc + 1) * 32], CST[:, C_SGN:C_SGN + 32], ALU.mult,
             [junk, CST], [WKR])
    CV = P.sb('cv', [128, 8, 2])
    CV2 = P.sb('cv2', [128, 8, 2])
    P.dma(CV[:, :, :], cvec_d, writes=[CV])
    P.act(CV2[:, :, :], CV[:, :, :], AF.Exp, [CV], [CV2], scale=-1.0)
    P.ts1('dve', CV2[:, :, :], CV2[:, :, :], 1.0, ALU.add, [CV2], [CV2])
    P.recip(CV2[:, :, :], CV2[:, :, :], [CV2], [CV2])
    P.tt('dve', CV[:, :, :], CV[:, :, :], CV2[:, :, :], ALU.mult, [CV, CV2], [CV])
    ada_v = ada_w_d.rearrange("(p k) n -> p k n", k=8)
    MACC = P.sb('macc', [128, 48])
    P.memset('dve', MACC[:, :], 0.0, [MACC])
    for k in range(8):
        for hf in range(2):
            P.dma(STG[:, 0:1536], ada_v[:, k, hf * 1536:(hf + 1) * 1536], writes=[STG])
            for n in range(12):
                nn = hf * 12 + n
                P.mm(pz[:, 2 * nn:2 * nn + 2], STG[:, n * 128:(n + 1) * 128], CV[:, k, :], [STG, CV], [pz])
            P.tt('dve', MACC[:, hf * 24:(hf + 1) * 24], MACC[:, hf * 24:(hf + 1) * 24], pz[:, hf * 24:(hf + 1) * 24],
                 ALU.add, [MACC, pz], [MACC])
    for j in range(2):
        P.tt('dve', MODT[:, :, j], MACC[:, j:48:2], PCOL[:, PC_ADAB:PC_ADAB + 24], ALU.add, [MACC, PCOL], [MODT])
    for j in range(2):
        P.stt('dve', GS[:, j, :], MODT[:, 8:16, j], 1.0, PCOL[:, PC_NG:PC_NG + 8], ALU.add, ALU.mult,
              [MODT, PCOL], [GS])
        P.cp('dve', SH[:, j, :], MODT[:, 0:8, j], [MODT], [SH])
    P.cp('dve', GSL[:, 0, :], GS[:, 0, :], [GS], [GSL])
    P.tt('dve', GSL[:, 1, :], GS[:, 1, :], GS[:, 0, :], ALU.subtract, [GS], [GSL])
    P.cp('dve', GSL[:, 2, :], SH[:, 0, :], [SH], [GSL])
    P.tt('dve', GSL[:, 3, :], SH[:, 1, :], SH[:, 0, :], ALU.subtract, [SH], [GSL])
    P.tt('dve', M0[:, :], PCOL[:, PC_MUP:PC_MUP + 14], PCOL[:, PC_MUN:PC_MUN + 14], ALU.add, [PCOL], [M0])
    P.ts('dve', M0[:, :], M0[:, :], -1.0, 1.0, ALU.mult, ALU.add, [M0], [M0])
    P.ts1('dve', NA0[:, :], PCOL[:, PC_A0:PC_A0 + 8], -1.0, ALU.mult, [PCOL], [NA0])
    for b_ in XPB:
        P.memset('pool', b_[:, :], 0.0, [b_])
    P.memset('pool', Hc[0][:, :, :], 0.0, [Hc[0]])
    P.memset('pool', HFa[:, :, :], 0.0, [HFa])

    def load_slot(i):
        P.dma(xt[i % 2][:, :], xs[i, 0:128, :], writes=[xt[i % 2]])

    def rsqrt_ln(dst_buf, dst, src_buf, src, scale, eps_ap):
        P.act(dst, src, AF.Ln, [src_buf, PCOL], [dst_buf], bias=eps_ap, scale=scale)
        P.act(dst, dst, AF.Exp, [dst_buf], [dst_buf], scale=-0.5)

    state = {'h': 0}

    PCNT = {'n': 0}
    ADV = 1

    def proj_for(hT_, cnt):
        def proj(c0, width, ncols=128, lhs=None):
            pb_ = pu[cnt['n'] % 2]
            cnt['n'] += 1
            for kc in range(8):
                l = WB[:, kc, c0:c0 + width] if lhs is None else lhs(kc)
                P.mm(pb_[0:(width if lhs is None else 96), 0:ncols], l, hT_[:, kc, 0:ncols],
                     [WB if lhs is None else WKR, hT_], [pb_], start=(kc == 0), stop=(kc == 7))
            return pb_
        return proj

    def prep(i):
        kind, kv = SLOTS[i]
        S = i % 2
        ARt, BKt, Atok, Bhtok, Khtok, Vtok = ARtL[S], BKtL[S], AtokL[S], BhtokL[S], KhtokL[S], VtokL[S]
        GC, MSK, ssum = GCL[S], MSKL[S], ssumL[S]
        x_, xh_, hT_ = xt[i % 2], xh[0], hT[i % 2]
        P.dma(xh_[:, :], xs[i, 128:130, :], writes=[xh_])
        fcol = SD[:, SD_F, i:i + 1]
        nfcol = SD[:, SD_NF, i:i + 1]
        phi = SD[:, SD_PHI, i:i + 1]
        P.stt('dve', GS[:, 0, :], GSL[:, 1, :], phi, GSL[:, 0, :], ALU.mult, ALU.add, [GSL, SD], [GS])
        P.stt('dve', SH[:, 0, :], GSL[:, 3, :], phi, GSL[:, 2, :], ALU.mult, ALU.add, [GSL, SD], [SH])
        P.ts1('dve', CSEL[:, :], CST[:, C_F:C_F + 512], fcol, ALU.mult, [CST, SD], [CSEL])
        P.stt('dve', CSEL[:, :], CST[:, C_B:C_B + 512], nfcol, CSEL[:, :], ALU.mult, ALU.add, [CST, SD, CSEL], [CSEL])
        P.ts1('dve', MSK[:, :], CST[:, C_F + 512:C_F + 896], fcol, ALU.mult, [CST, SD], [MSK])
        P.stt('dve', MSK[:, :], CST[:, C_B + 512:C_B + 896], nfcol, MSK[:, :], ALU.mult, ALU.add, [CST, SD, MSK], [MSK])
        P.ts1('dve', W0S[0:1, :], W0R[0:1, 0:512], SD[0:1, SD_F, i:i + 1], ALU.mult, [W0R, SD], [W0S])
        P.stt('dve', W0S[0:1, :], W0R[0:1, 512:1024], SD[0:1, SD_NF, i:i + 1], W0S[0:1, :], ALU.mult, ALU.add,
              [W0R, SD, W0S], [W0S])
        P.ts1('dve', NA0S[:, :], NA0[:, 0:4], fcol, ALU.mult, [NA0, SD], [NA0S])
        P.stt('dve', NA0S[:, :], NA0[:, 4:8], nfcol, NA0S[:, :], ALU.mult, ALU.add, [NA0, SD, NA0S], [NA0S])
        j = 0
        P.act(junk[:, :], x_[:, :], AF.Square, [x_], [junk])
        P.red(st4[:, 0:1], junk[:, :], [junk], [st4])
        rsqrt_ln(st4, st4[:, 0:1], st4, st4[:, 0:1], 1.0 / 1024, pc(PC_EPS))
        P.ts1('dve', x_[:, :], x_[:, :], st4[:, 0:1], ALU.mult, [x_, st4], [x_])
        for half in range(2):
            yield
            for q in range(4):
                kc = half * 4 + q
                P.tr(pt[:, q * 128:(q + 1) * 128], x_[:, kc * 128:(kc + 1) * 128], ident, [x_, CST], [pt])
            for q in range(4):
                kc = half * 4 + q
                if q % 2 == 0:
                    P.act(hT_[:, kc, 0:128], pt[:, q * 128:(q + 1) * 128], AF.Identity, [pt, GS, SH], [hT_],
                          bias=SH[:, j, kc:kc + 1], scale=GS[:, j, kc:kc + 1])
                else:
                    P.ts('dve', hT_[:, kc, 0:128], pt[:, q * 128:(q + 1) * 128], GS[:, j, kc:kc + 1],
                         SH[:, j, kc:kc + 1], ALU.mult, ALU.add, [pt, GS, SH], [hT_])
        yield
        P.act(junk[0:2, :], xh_[:, :], AF.Square, [xh_], [junk])
        P.red(st4[0:2, 1:2], junk[0:2, :], [junk], [st4])
        rsqrt_ln(st4, st4[0:2, 1:2], st4, st4[0:2, 1:2], 1.0 / 1024, PCOL[0:2, PC_EPS:PC_EPS + 1])
        P.ts1('dve', xh_[:, :], xh_[:, :], st4[0:2, 1:2], ALU.mult, [xh_, st4], [xh_])
        yield
        for kc in range(8):
            P.tr(pt[:, 2 * kc:2 * kc + 2], xh_[0:2, kc * 128:(kc + 1) * 128], CST[0:2, C_ID:C_ID + 2], [xh_, CST], [pt])
        for c in range(2):
            P.ts1('dve', shv[:, c, :], SH[:, j, :], HV[:, i, c:c + 1], ALU.mult, [SH, HV], [shv])
            P.tt('dve', junk[:, 8 * c:8 * c + 8], pt[:, c:16:2], GS[:, j, :], ALU.mult, [pt, GS], [junk])
            P.tt('dve', hT_[:, :, 128 + c], junk[:, 8 * c:8 * c + 8], shv[:, c, :], ALU.add, [junk, shv], [hT_])

        yield
        proj = proj_for(hT_, PCNT)

        for jc in range(14):
            if kind == 'S' and jc < 4:
                continue
            pb_ = proj(jc * 128, 128, 130)
            ue_ = ue[jc % 2]
            P.cp('act', ue_[:, :], pb_[:, 0:130], [pb_], [ue_])
            yield
            if jc < 12:
                dstb, dst = RKV, RKV[:, jc, :]
            else:
                dstb, dst = WA, WA[:, jc - 12, :]
            eng = 'pool' if jc % 2 == 0 else 'dve'
            mp_, mn_ = pc(PC_MUP + jc), pc(PC_MUN + jc)
            P.ts1(eng, dst, ue_[:, 0:128], M0[:, jc:jc + 1], ALU.mult, [ue_, M0], [dstb])
            for (dsl, ssl, mu_) in [((1, 128), (0, 127), mp_), ((0, 1), (128, 129), mp_),
                                    ((0, 127), (1, 128), mn_), ((127, 128), (129, 130), mn_)]:
                dd = dst[:, dsl[0]:dsl[1]]
                if eng == 'dve':
                    P.stt('dve', dd, ue_[:, ssl[0]:ssl[1]], mu_, dd, ALU.mult, ALU.add, [ue_, PCOL, dstb], [dstb])
                else:
                    tp = th1[:, 0:dsl[1] - dsl[0]]
                    P.ts1('pool', tp, ue_[:, ssl[0]:ssl[1]], mu_, ALU.mult, [ue_, PCOL], [th1])
                    P.tt('pool', dd, dd, tp, ALU.add, [dstb, th1], [dstb])
            yield
        yield
        r3 = RKV[:, 0:4, :]
        k3 = RKV[:, 4:8, :]
        v3 = RKV[:, 8:12, :]

        def f3(b_):
            return b_[:, :].rearrange("p (m t) -> p m t", m=4)

        for m in range(4):
            P.ts1('pool', tA[:, m * 128:(m + 1) * 128], RKV[:, 4 + m, :], pc(PC_KK + m), ALU.mult, [RKV, PCOL], [tA])
        P.act(tB[:, :], tA[:, :], AF.Square, [tA], [tB])
        P.mm(pz[:, :], CST[:, C_BO:C_BO + 128], tB[:, :], [CST, tB], [pz])
        yield
        P.ts1('dve', tB[:, :], pz[:, :], 1e-24, ALU.max, [pz], [tB])
        P.act(tB[:, :], tB[:, :], AF.Ln, [tB], [tB])
        P.act(tB[:, :], tB[:, :], AF.Exp, [tB], [tB], scale=-0.5)
        P.tt('dve', tA[:, :], tA[:, :], tB[:, :], ALU.mult, [tA, tB], [tA])
        yield
        P.act(th0[:, :], WA[:, 0, :], AF.Exp, [WA], [th0], scale=2.0)
        P.ts1('dve', th0[:, :], th0[:, :], 1.0, ALU.add, [th0], [th0])
        P.recip(th0[:, :], th0[:, :], [th0], [th0])
        P.ts('dve', th0[:, :], th0[:, :], SD[:, SD_M2S, i:i + 1], SD[:, SD_SELW, i:i + 1], ALU.mult, ALU.add,
             [th0, SD], [th0])
        P.mm(pz[:, :], th0[:, :], W2C[:, :], [th0, W2C], [pz], start=True, stop=False)
        P.mm(pz[:, :], CST[0:1, C_ONES:C_ONES + 128], W0S[0:1, :], [CST, W0S], [pz], start=False, stop=True)
        yield
        P.act(sg[:, :], pz[:, :], AF.Exp, [pz], [sg], scale=-1.0)
        P.ts1('dve', sg[:, :], sg[:, :], 1.0, ALU.add, [sg], [sg])
        P.recip(sg[:, :], sg[:, :], [sg], [sg])
        yield
        tri = CSEL[:, 0:512]
        for m in range(4):
            P.mm(pz[:, :], sg[:, m * 128:(m + 1) * 128], tri, [sg, CSEL], [pz])
            P.act(EALL[:, m, :, :], pz[:, :].rearrange("p (a t) -> p a t", a=4), AF.Exp, [pz], [EALL])
            yield
        P.tt('dve', GC[:, :], EALL[:, :, 2, 0], EALL[:, :, 3, 0], ALU.mult, [EALL], [GC])
        yield
        P.ts1('dve', th1[:, :], WA[:, 1, :], SD[:, SD_SELW, i:i + 1], ALU.mult, [WA, SD], [th1])
        for m in range(4):
            P.mm(pz[:, m * 128:(m + 1) * 128], A2C[:, m * 128:(m + 1) * 128], th1[:, :], [A2C, th1], [pz])
        for m in range(4):
            P.act(tB[:, m * 128:(m + 1) * 128], pz[:, m * 128:(m + 1) * 128], AF.Exp, [pz, NA0S], [tB],
                  bias=NA0S[:, m:m + 1], scale=-1.0)
        yield
        P.ts1('dve', tB[:, :], tB[:, :], 1.0, ALU.add, [tB], [tB])
        P.recip(tB[:, :], tB[:, :], [tB], [tB])
        yield
        for m in range(4):
            P.ts('pool', tC[:, m * 128:(m + 1) * 128], tB[:, m * 128:(m + 1) * 128], -1.0, pc(PC_KA + m),
                 ALU.add, ALU.mult, [tB, PCOL], [tC])
        P.stt('dve', tC[:, :], tC[:, :], 1.0, RKV[:, 4:8, :].rearrange("p m t -> p (m t)"), ALU.add, ALU.mult,
              [tC, RKV], [tC])
        P.tt('pool', tD[:, :], tA[:, :], tB[:, :], ALU.mult, [tA, tB], [tD])
        if kind != 'S':
            for m in range(4):
                P.stt('pool', tE[:, m * 128:(m + 1) * 128], RKV[:, m, :], pc(PC_RK + m),
                      tC[:, m * 128:(m + 1) * 128], ALU.mult, ALU.mult, [RKV, PCOL, tC], [tE])
            for m in range(4):
                P.mm(pz[:, 2 * m:2 * m + 2], tE[:, m * 128:(m + 1) * 128], CST[:, C_BO2:C_BO2 + 2],
                     [tE, CST], [pz])
            P.cp('act', ssum[:, :], pz[:, 0:8], [pz], [ssum])
        yield
        E = [EALL[:, :, a, :] for a in range(4)]
        A3 = ARt[:, :, :].rearrange("p m (a t) -> p m a t", a=2)
        B3 = BKt[:, :, :].rearrange("p m (a t) -> p m a t", a=2)
        P.stt('dve', A3[:, :, 0, :], f3(tA), -1.0, E[0], ALU.mult, ALU.mult, [tA, EALL], [ARt])
        if kind != 'S':
            P.tt('pool', A3[:, :, 1, :], r3, E[2], ALU.mult, [RKV, EALL], [ARt])
        yield
        P.tt('dve', B3[:, :, 0, :], f3(tD), E[1], ALU.mult, [tD, EALL], [BKt])
        P.tt('pool', B3[:, :, 1, :], f3(tC), E[1], ALU.mult, [tC, EALL], [BKt])
        yield
        P.tt('dve', f3(BhT), f3(tD), E[3], ALU.mult, [tD, EALL], [BhT])
        P.tt('pool', f3(KhT), f3(tC), E[3], ALU.mult, [tC, EALL], [KhT])
        for (srcb, srcf, dstb) in [(ARt, lambda m: ARt[:, m, 0:128], Atok), (BhT, lambda m: BhT[:, m * 128:(m + 1) * 128], Bhtok),
                                   (KhT, lambda m: KhT[:, m * 128:(m + 1) * 128], Khtok), (RKV, lambda m: RKV[:, 8 + m, :], Vtok)]:
            for m in range(4):
                P.tr(pt[:, m * 128:(m + 1) * 128], srcf(m), ident, [srcb, CST], [pt])
            P.cp('act', dstb[:, :], pt[:, :], [pt], [dstb])
            yield

        yield
        if kv is not None:
            pb2 = proj(2688, 128, 128)
            P.cp('act', mla1[:, 0, :], pb2[:, 0:128], [pb2], [mla1])
            pb2 = proj(2816, 128, 128)
            P.cp('act', mla1[:, 1, :], pb2[:, 0:128], [pb2], [mla1])
            P.act(mla2[:, 0:2, :], mla1[:, 0:2, :], AF.Square, [mla1], [mla2])
            P.mm(pz[:, 0:128], ones, mla2[:, 0, :], [CST, mla2], [pz], start=True, stop=False)
            P.mm(pz[:, 0:128], ones, mla2[:, 1, :], [CST, mla2], [pz], start=False, stop=True)
            rsqrt_ln(mla2, mla2[:, 2, :], pz, pz[:, 0:128], 1.0 / 256, pc(PC_EPS))
            for c in range(2):
                P.stt('dve', kvb[:, c, :], mla1[:, c, :], pc(PC_KVG + c), mla2[:, 2, :], ALU.mult, ALU.mult,
                      [mla1, PCOL, mla2], [kvb])
            P.dma(CKVs[:, :, kv * 128:(kv + 1) * 128], kvb[:, :, :], reads=[kvb], writes=[CKVs])
            yield
            P.dma(rpk[64:96, :, :], ropek[kv].rearrange("a d t -> d a t"), writes=[rpk])
            pk1 = proj(0, 96, 128, lhs=lambda kc: WKR[:, kc, 0, :])
            yield
            pk2 = proj(0, 96, 128, lhs=lambda kc: WKR[:, kc, 1, :])
            P.tt('dve', mla1[64:96, 2, :], pk1[64:96, 0:128], rpk[64:96, 0, :], ALU.mult, [pk1, rpk], [mla1])
            P.tt('dve', mla1[64:96, 3, :], pk2[64:96, 0:128], rpk[64:96, 1, :], ALU.mult, [pk2, rpk], [mla1])
            P.tt('dve', krb[64:96, :], mla1[64:96, 2, :], mla1[64:96, 3, :], ALU.add, [mla1], [krb])
            P.dma(KRs[:, kv * 128:(kv + 1) * 128], krb[64:96, :], reads=[krb], writes=[KRs])

        yield

    def heads(i, gen):
        kind, kv = SLOTS[i]
        S = i % 2
        ARt, BKt, Atok, Bhtok, Khtok, Vtok = ARtL[S], BKtL[S], AtokL[S], BhtokL[S], KhtokL[S], VtokL[S]
        GC, MSK, ssum = GCL[S], MSKL[S], ssumL[S]
        hT_ = hT[i % 2]
        proj = proj_for(hT_, PCNT)

        def f3(b_):
            return b_[:, :].rearrange("p (m t) -> p m t", m=4)

        def advance(n):
            if gen is None:
                return
            for _ in range(n):
                try:
                    next(gen)
                except StopIteration:
                    return
        Hin = Hc[state['h'] % 2]
        Hout = Hc[(state['h'] + 1) % 2]
        mskt = MSK[:, 0:256]
        mskn = MSK[:, 256:384]
        if kind == 'S':
            P.ts1('dve', Hin[:, :, :], Hin[:, :, :], SD[:, SD_RHO, i:i + 1], ALU.mult, [Hin, SD], [Hin])
        own = kind != 'S'

        def head_pre(h):
            m, pb = h // 2, 64 * (h % 2)
            par = h % 2
            cb = 64 * h
            at_, xpb, ptb = AT[par], XPBS[par], PTBS[par]
            Bt_h = BKt[pb:pb + 64, m, 0:128]
            Kt_h = BKt[pb:pb + 64, m, 128:256]
            AR_h = ARt[pb:pb + 64, m, 0:256]
            At_h = ARt[pb:pb + 64, m, 0:128]
            XP = xpb[0]
            pt0 = ptb[0]
            if own:
                P.mm(pA[:, 0:256], Bt_h, AR_h, [BKt, ARt], [pA])
                P.mm(pA[:, 256:512], Kt_h, AR_h, [BKt, ARt], [pA])
                P.mm(pG[:, 128:256], At_h, Bt_h, [ARt, BKt], [pG])
                an_src, an_buf = pG[:, 128:256], pG
            else:
                P.mm(pA[:, 0:128], Bt_h, At_h, [BKt, ARt], [pA])
                P.mm(pA[:, 256:384], Kt_h, At_h, [BKt, ARt], [pA])
                P.mm(pA[:, 128:256], At_h, Bt_h, [ARt, BKt], [pA])
                an_src, an_buf = pA[:, 128:256], pA
            P.tt('dve', pt0[:, :], pA[:, 0:128], mskt[:, 0:128], ALU.mult, [pA, MSK], [pt0])
            if own:
                P.tt('dve', at_[:, 128:256], pA[:, 128:256], mskt[:, 128:256], ALU.mult, [pA, MSK], [at_])
                P.tt('dve', at_[:, 256:512], pA[:, 256:512], mskt, ALU.mult, [pA, MSK], [at_])
            else:
                P.tt('dve', at_[:, 256:384], pA[:, 256:384], mskt[:, 0:128], ALU.mult, [pA, MSK], [at_])
            P.tt('dve', XP[:, 192:320], an_src, mskn, ALU.mult, [an_buf, MSK], [XP])
            P.cp('act', XP[:, 64:128], Atok[:, cb:cb + 64], [Atok], [XP])
            yield
            if own:
                wdst, wbuf = pA[:, 0:64], pA
            else:
                wdst, wbuf = pA[:, 384:448], pA
            P.mm(wdst, at_[:, 256:384], Vtok[:, cb:cb + 64], [at_, Vtok], [wbuf])
            P.cp('act', XP[:, 128:192], wdst, [wbuf], [XP])
            yield

        def head_main(h, nxt):
            m, pb = h // 2, 64 * (h % 2)
            par = h % 2
            cb = 64 * h
            at_, xpb, ptb = AT[par], XPBS[par], PTBS[par]
            xi = 0
            XP = xpb[0]
            PT_ = ptb[0]
            for lvl in range(7):
                last = lvl == 6
                XPn = xpb[(xi + 1) % 3]
                P.mm(pX[:, 0:(128 if last else 256)], PT_[:, :], XP[:, 64:(192 if last else 320)], [PT_, XP], [pX])
                if not last:
                    P.mm(pS[:, 0:128], XP[:, 192:320], PT_[:, :], [XP, PT_], [pS])
                if nxt is not None and lvl in (1, 3):
                    next(nxt, None)
                P.tt('dve', XPn[:, 64:192], XP[:, 64:192], pX[:, 0:128], ALU.add, [XP, pX], [XPn])
                if not last:
                    P.cp('act', XPn[:, 192:320], pX[:, 128:256], [pX], [XPn])
                    ptn = ptb[(lvl + 1) % 2]
                    P.cp('act', ptn[:, :], pS[:, 0:128], [pS], [ptn])
                    PT_ = ptn
                xi += 1
                XP = XPn
                advance(1)
            X = XP
            lx = X[:, 64:128] if pb == 0 else X[:, 0:128]
            P.mm(pM[0:(64 if pb == 0 else 128), 0:64], lx, Bhtok[:, cb:cb + 64], [X, Bhtok], [pM])
            P.mm(pM[:, 64:128], Bhtok[:, m * 128:(m + 1) * 128], X[:, 128:192], [Bhtok, X], [pM], start=True, stop=False)
            P.mm(pM[:, 64:128], Khtok[:, m * 128:(m + 1) * 128], Vtok[:, cb:cb + 64], [Khtok, Vtok], [pM],
                 start=False, stop=True)
            P.stt('dve', MTs[pb:pb + 64, m, :], CST[pb:pb + 64, C_ID + pb:C_ID + pb + 64], GC[pb:pb + 64, m:m + 1],
                  pM[pb:pb + 64, 0:64], ALU.mult, ALU.add, [CST, GC, pM], [MTs])
            P.cp('act', Ns[pb:pb + 64, m, :], pM[pb:pb + 64, 64:128], [pM], [Ns])
            if own:
                gt_ = GTs[h % 2]
                P.mm(pG[0:(64 if pb == 0 else 128), 0:128], lx, at_[:, 128:256], [X, at_], [pG])
                P.tt('dve', gt_[pb:pb + 64, :], pG[pb:pb + 64, 0:128], ARt[pb:pb + 64, m, 128:256], ALU.add,
                     [pG, ARt], [gt_])
                yo = pY[:, cb:cb + 64]
                P.mm(yo, at_[:, 128:256], X[:, 128:192], [at_, X], [pY], start=True, stop=False)
                P.mm(yo, at_[:, 384:512], Vtok[:, cb:cb + 64], [at_, Vtok], [pY], start=False, stop=False)
                P.mm(yo, gt_[pb:pb + 64, :], Hin[pb:pb + 64, m, :], [gt_, Hin], [pY], start=False, stop=True)
            P.mm(pH[pb:pb + 64, m * 64:(m + 1) * 64], MTs[pb:pb + 64, m, :], Hin[pb:pb + 64, m, :], [MTs, Hin], [pH])
            advance(ADV)

        for _ in head_pre(0):
            pass
        for h in range(8):
            nxt = head_pre(h + 1) if h + 1 < 8 else None
            head_main(h, nxt)
            if nxt is not None:
                for _ in nxt:
                    pass
        P.tt('dve', Hout[:, :, :], pH[:, :].rearrange("p (m i) -> p m i", m=4), Ns[:, :, :], ALU.add, [pH, Ns], [Hout])
        if i == CFG.get('dump_slot', 0):
            dump('hout', Hout, Hout[:, :, :].rearrange('p m i -> p (m i)'))
        state['h'] += 1
        advance(10 ** 6)
        if kind == 'S':
            P.stt('dve', HFa[:, :, :], Hout[:, :, :], SD[:, SD_SIG, i:i + 1], HFa[:, :, :], ALU.mult, ALU.add,
                  [Hout, SD, HFa], [HFa])

        if kind == 'B':
            jj = 15 - (i - 52)
            P.cp('act', ysb[:, :], pY[:, :], [pY], [ysb])
            P.dma(YBs[jj], ysb[:, :], reads=[ysb], writes=[YBs])
            P.dma(SBs[jj], ssum[:, :], reads=[ssum], writes=[SBs])
        if kind == 'F':
            jj = i - 68
            tok = slice(jj * 128, (jj + 1) * 128)
            P.dma(ybl[:, :], YBs[jj], reads=[YBs], writes=[ybl])
            P.dma(sbl[:, :], SBs[jj], reads=[SBs], writes=[sbl])
            P.tt('dve', ysb[:, :], pY[:, :], ybl[:, :], ALU.add, [pY, ybl], [ysb])
            P.tt('dve', ssum[:, :], ssum[:, :], sbl[:, :], ALU.add, [ssum, sbl], [ssum])
            y3 = ysb[:, :].rearrange("p (h i) -> p h i", h=8)
            P.red(st4[:, :], y3, [ysb], [st4])
            P.ts1('dve', st4[:, :], st4[:, :], 1.0 / 64, ALU.mult, [st4], [st4])
            P.act(tE[:, :], ysb[:, :], AF.Square, [ysb], [tE])
            P.red(sbl[:, :], tE[:, :].rearrange("p (h i) -> p h i", h=8), [tE], [sbl])
            P.tt('dve', ybl[:, 0:8], st4[:, :], st4[:, :], ALU.mult, [st4], [ybl])
            P.stt('dve', sbl[:, :], sbl[:, :], 1.0 / 64, ybl[:, 0:8], ALU.mult, ALU.subtract, [sbl, ybl], [sbl])
            rsqrt_ln(sbl, sbl[:, :], sbl, sbl[:, :], 1.0, pc(PC_LEPS))
            for h in range(8):
                P.ts('dve' if h % 2 else 'pool', ysb[:, h * 64:(h + 1) * 64], ysb[:, h * 64:(h + 1) * 64],
                     st4[:, h:h + 1], sbl[:, h:h + 1], ALU.subtract, ALU.mult, [ysb, st4, sbl], [ysb])
            P.tt('dve', ysb[:, :], ysb[:, :], LGB[:, 0:512], ALU.mult, [ysb, LGB], [ysb])
            P.tt('pool', ysb[:, :], ysb[:, :], LGB[:, 512:1024], ALU.add, [ysb, LGB], [ysb])
            for h in range(8):
                P.stt('dve' if h % 2 else 'pool', ysb[:, h * 64:(h + 1) * 64], Vtok[:, h * 64:(h + 1) * 64],
                      ssum[:, h:h + 1], ysb[:, h * 64:(h + 1) * 64], ALU.mult, ALU.add, [Vtok, ssum, ysb], [ysb])
            for m in range(4):
                pb2 = proj(1792 + m * 128, 128, 128)
                P.cp('act', tE[:, m * 128:(m + 1) * 128], pb2[:, 0:128], [pb2], [tE])
            P.act(tD[:, :], tE[:, :], AF.Exp, [tE], [tD], scale=-1.0)
            P.ts1('dve', tD[:, :], tD[:, :], 1.0, ALU.add, [tD], [tD])
            P.recip(tD[:, :], tD[:, :], [tD], [tD])
            P.tt('pool', tE[:, :], tE[:, :], tD[:, :], ALU.mult, [tE, tD], [tE])
            for m in range(4):
                P.tr(pt[:, m * 128:(m + 1) * 128], ysb[:, m * 128:(m + 1) * 128], ident, [ysb, CST], [pt])
            P.tt('dve', mixb[:, :, :], pt[:, :].rearrange("p (m t) -> p m t", m=4), f3(tE), ALU.mult, [pt, tE], [mixb])
            P.dma(MIXRs[:, :, tok], mixb[:, :, :], reads=[mixb], writes=[MIXRs])
            for c in range(3):
                pb2 = proj(2304 + c * 128, 128, 128)
                P.cp('act', mla1[:, c, :], pb2[:, 0:128], [pb2], [mla1])
            P.act(mla2[:, 0:3, :], mla1[:, 0:3, :], AF.Square, [mla1], [mla2])
            for c in range(3):
                P.mm(pz[:, 0:128], ones, mla2[:, c, :], [CST, mla2], [pz], start=(c == 0), stop=(c == 2))
            rsqrt_ln(mla2, mla2[:, 3, :], pz, pz[:, 0:128], 1.0 / 384, pc(PC_EPS))
            for c in range(3):
                P.stt('dve', cqb[:, c, :], mla1[:, c, :], pc(PC_QG + c), mla2[:, 3, :], ALU.mult, ALU.mult,
                      [mla1, PCOL, mla2], [cqb])
            P.dma(CQs[:, :, tok], cqb[:, :, :], reads=[cqb], writes=[CQs])
            for h in range(8):
                pb2 = proj(2976 + h * 64, 64, 128)
                P.cp('act', tD[0:64, 0:128], pb2[0:64, 0:128], [pb2], [tD])
                P.act(tD[0:64, 128:256], tD[0:64, 0:128], AF.Exp, [tD], [tD], scale=-1.0)
                P.ts1('dve', tD[0:64, 128:256], tD[0:64, 128:256], 1.0, ALU.add, [tD], [tD])
                P.recip(tD[0:64, 128:256], tD[0:64, 128:256], [tD], [tD])
                P.tt('dve', gmb[:, h, :], tD[0:64, 0:128], tD[0:64, 128:256], ALU.mult, [tD], [gmb])
            P.dma(GMs[:, :, tok], gmb[:, :, :], reads=[gmb], writes=[GMs])

    slot_list = list(range(NSLOT)) if CFG['slots'] is None else list(CFG['slots'])
    if slot_list:
        load_slot(slot_list[0])
        if len(slot_list) > 1:
            load_slot(slot_list[1])
        for _ in prep(slot_list[0]):
            pass
    for si, i in enumerate(slot_list):
        if si + 2 < len(slot_list):
            load_slot(slot_list[si + 2])
        if i == 68:
            P.cp('dve', Hc[state['h'] % 2][:, :, :], HFa[:, :, :], [HFa], [Hc[state['h'] % 2]])
        heads(i, prep(slot_list[si + 1]) if si + 1 < len(slot_list) else None)

    AR = ARENA.t
    o = 0

    def carve(n, shape_str=None, **kw):
        nonlocal o
        ap = AR[:, o:o + n]
        o += n
        if shape_str:
            ap = ap.rearrange(shape_str, **kw)
        return P.view(ap)

    ps2 = [pA, pY]

    def phase2():
        nonlocal o
        P.barrier()
        o = 0

        KT = [carve(NKEY)] * 2
        V4 = carve(NKT * 4 * 65, "p (k h c) -> p k h c", k=NKT, h=4)
        QT = [carve(2048)] * 2
        def bview(buf, nbf, pat=None, rows=128, **kw):
            t = buf.t
            flat = t[0:rows]
            nd = len(t.shape)
            if nd == 3:
                flat = t[0:rows, :, :].rearrange("p a b -> p (a b)")
            elif nd == 4:
                flat = t[0:rows, :, :, :].rearrange("p a b c -> p (a b c)")
            ap = flat[:, 0:nbf // 2].bitcast(BF16)
            if pat:
                ap = ap.rearrange(pat, **kw)
            return P.view(ap)
        WUQ = bview(RKV, 2304, "p (k c) -> p k c", k=3)
        WUQR = bview(sg, 288, "p (k c) -> p k c", k=3)
        WROT = bview(ysb, 768, "p (k h c) -> p k h c", k=3, h=8)
        WUK = bview(AtokL[0], 1024, "p (k c) -> p k c", k=2)
        WUV = bview(BhtokL[0], 1024, "p (k c) -> p k c", k=2)
        ckb = [bview(KhtokL[0], 1024, "p (k c) -> p k c", k=2), bview(VtokL[0], 1024, "p (k c) -> p k c", k=2)]
        PTb_ = [bview(AtokL[1], 512), bview(BhtokL[1], 512), bview(KhtokL[1], 512)]
        rq = P.view(ARt.t[:, :, :].rearrange('p m c -> p (m c)').rearrange('p (a t) -> p a t', a=2))
        onrm = P.view(tA.t[0:64, :])
        rec = P.view(tB.t[0:65, :])
        bcs = P.view(tC.t[0:64, :])
        mxb = bview(XPB[0], 512, rows=64)
        gml = bview(XPB[1], 512, rows=64)
        cql = bview(junk, 1536, "p (k c) -> p k c", k=3)
        w_uq_v = w_uq_d.rearrange("(k p) c -> p k c", p=128)
        w_uqr_v = w_uq_rot_d.rearrange("(k p) h c -> p k h c", p=128)
        for kc in range(3):
            P.dma(STG[:, 0:768], w_uq_v[:, kc, :], writes=[STG])
            P.cp('dve', WUQ[:, kc, :], STG[:, 0:768], [STG], [WUQ])
        P.memset('pool', WUQR[:, :, :], 0.0, [WUQR])
        P.dma(STG[:, 0:768].rearrange("p (k h c) -> p k h c", k=3, h=8), w_uqr_v, writes=[STG])
        for kc in range(3):
            for h in range(8):
                c0 = kc * 256 + h * 32
                P.tt('dve', WROT[:, kc, h, :], STG[:, c0:c0 + 32], CST[:, C_SGN:C_SGN + 32], ALU.mult, [STG, CST], [WROT])
        for kc_ in range(2):
            P.dma(STG[:, 0:512], w_uk_d.rearrange("(k p) h c -> p k (h c)", p=128)[:, kc_, :], writes=[STG])
            P.cp('dve', WUK[:, kc_, :], STG[:, 0:512], [STG], [WUK])
        P.dma(tA[:, :].rearrange("p (k c) -> p k c", k=2)[:, :, 0:256], w_uv_d.rearrange("(k p) h c -> p k (h c)", p=128)[:, :, 0:256], writes=[tA])
        P.dma(tB[:, :].rearrange("p (k c) -> p k c", k=2)[:, :, 0:256], w_uv_d.rearrange("(k p) h c -> p k (h c)", p=128)[:, :, 256:512], writes=[tB])
        P.cp('dve', WUV[:, :, 0:256], tA[:, :].rearrange("p (k c) -> p k c", k=2), [tA], [WUV])
        P.cp('dve', WUV[:, :, 256:512], tB[:, :].rearrange("p (k c) -> p k c", k=2), [tB], [WUV])

        po = [pz, pBXf]
        KB = [(kb * 512, 512) for kb in range(NKEY // 512)]
        for hg in range(2):
            for hh_ in range(4):
                P.cp('dve', V4[:, :, hh_, 64], KVV[:, :], [KVV], [V4])
            for bi, (k0, kw) in enumerate(KB):
                cb_ = ckb[bi % 2]
                P.dma(cb_[:, :, 0:kw], CKVs[:, :, k0:k0 + kw], reads=[CKVs], writes=[cb_])
                for s in range(kw // 128):
                    kt = k0 // 128 + s
                    pv = pu[kt % 2]
                    for c in range(2):
                        P.mm(pv[:, 0:256], cb_[:, c, s * 128:(s + 1) * 128], WUV[:, c, hg * 256:(hg + 1) * 256], [cb_, WUV], [pv],
                             start=(c == 0), stop=(c == 1))
                    P.ts1('dve', V4[:, kt, :, 0:64], pv[:, 0:256].rearrange("p (h c) -> p h c", h=4), KVV[:, kt:kt + 1], ALU.mult, [pv, KVV], [V4])
            for hh in range(4):
                h = hg * 4 + hh
                kt_, qt_ = KT[h % 2], QT[h % 2]
                P.dma(kt_[64:96, :], KRs[:, :], reads=[KRs], writes=[kt_])
                for bi, (k0, kw) in enumerate(KB):
                    cb_ = ckb[bi % 2]
                    P.dma(cb_[:, :, 0:kw], CKVs[:, :, k0:k0 + kw], reads=[CKVs], writes=[cb_])
                    for c in range(2):
                        P.mm(ps2[bi % 2][0:64, 0:kw], WUK[:, c, h * 64:(h + 1) * 64], cb_[:, c, 0:kw],
                             [cb_, WUK], [ps2[bi % 2]], start=(c == 0), stop=(c == 1))
                    P.cp('act' if bi % 2 else 'dve', kt_[0:64, k0:k0 + kw], ps2[bi % 2][0:64, 0:kw], [ps2[bi % 2]], [kt_])
                for kc in range(3):
                    P.cp('pool', WUQR[:, kc, 64:96], WROT[:, kc, h, :], [WROT], [WUQR])
                for qb in range(4):
                    qs = slice(qb * 512, (qb + 1) * 512)
                    P.dma(cql[:, :, :], CQs[:, :, qs], reads=[CQs], writes=[cql])
                    P.dma(rq[64:96, :, :], ropeq[:, :, qs].rearrange("a d t -> d a t"), writes=[rq])
                    for c in range(3):
                        P.mm(po[0][0:96, :], WUQ[:, c, h * 96:(h + 1) * 96], cql[:, c, :], [WUQ, cql], [po[0]],
                             start=(c == 0), stop=(c == 2))
                    for c in range(3):
                        P.mm(po[1][0:96, :], WUQR[:, c, :], cql[:, c, :], [WUQR, cql], [po[1]], start=(c == 0), stop=(c == 2))
                    P.cp('act', qt_[0:64, qs], po[0][0:64, :], [po[0]], [qt_])
                    P.tt('dve', rq[64:96, 0, :], po[0][64:96, :], rq[64:96, 0, :], ALU.mult, [po[0], rq], [rq])
                    P.tt('dve', rq[64:96, 1, :], po[1][64:96, :], rq[64:96, 1, :], ALU.mult, [po[1], rq], [rq])
                    P.tt('dve', qt_[64:96, qs], rq[64:96, 0, :], rq[64:96, 1, :], ALU.add, [rq], [qt_])
                for qb in range(4):
                    qs = slice(qb * 512, (qb + 1) * 512)
                    pacc = po[qb % 2]
                    P.dma(gml[:, :], GMs[:, h, qs], reads=[GMs], writes=[gml])
                    def qk(kt):
                        sc = ps2[kt % 2]
                        P.mm(sc[:, :], kt_[0:96, kt * 128:(kt + 1) * 128], qt_[0:96, qs], [kt_, qt_], [sc])
                    qk(0)
                    for kt in range(NKT):
                        if kt + 1 < NKT:
                            qk(kt + 1)
                        sc = ps2[kt % 2]
                        pT = PTb_[kt % 3]
                        P.act(pT[:, :], sc[:, :], AF.Exp, [sc], [pT], scale=ATTN_SCALE)
                        P.mm(pacc[0:65, :], V4[:, kt, hh, :], pT[:, :], [V4, pT], [pacc], start=(kt == 0), stop=(kt == NKT - 1))
                    P.recip(rec[64:65, :], pacc[64:65, :], [pacc], [rec])
                    P.mm(pH[0:64, 0:256], CST[64:65, C_ONES:C_ONES + 64], rec[64:65, 0:256], [CST, rec], [pH])
                    P.mm(pG[0:64, 0:256], CST[64:65, C_ONES:C_ONES + 64], rec[64:65, 256:512], [CST, rec], [pG])
                    P.cp('act', bcs[:, 0:256], pH[0:64, 0:256], [pH], [bcs])
                    P.cp('act', bcs[:, 256:512], pG[0:64, 0:256], [pG], [bcs])
                    P.tt('dve', onrm[:, :], pacc[0:64, :], bcs[:, :], ALU.mult, [pacc, bcs], [onrm])
                    P.tt('dve', mxb[:, :], onrm[:, :], gml[:, :], ALU.mult, [onrm, gml], [mxb])
                    P.dma(MIXMs[:, h, qs], mxb[:, :], reads=[mxb], writes=[MIXMs])

    def phase3():
        nonlocal o
        P.barrier()
        o = 0
        WOR = carve(4 * 1024, "p (k c) -> p k c", k=4)
        WOM = carve(8 * 1024, "p (k c) -> p k c", k=8)
        mr = [carve(512, "p (k c) -> p k c", k=4), carve(512, "p (k c) -> p k c", k=4)]
        mm_ = [carve(1024, "p (k c) -> p k c", k=8), carve(1024, "p (k c) -> p k c", k=8)]
        w_out_r = w_out_d[0:512, :].rearrange("(k p) c -> p k c", p=128)
        w_out_m = w_out_d[512:1024, :].rearrange("(k p) c -> p k c", p=64)
        for k in range(4):
            P.dma(STG[:, 0:1024], w_out_r[:, k, :], writes=[STG])
            P.cp('dve', WOR[:, k, :], STG[:, 0:1024], [STG], [WOR])
        for k in range(8):
            P.dma(STG[0:64, 0:1024], w_out_m[:, k, :], writes=[STG])
            P.cp('dve', WOM[0:64, k, :], STG[0:64, 0:1024], [STG], [WOM])
        GB = P.view(BKh.t[:, :])
        FG = P.view(BKt.t[:, :, :].rearrange('p m c -> p (m c)'))
        for half in range(2):
            for q in range(4):
                kc = half * 4 + q
                P.ts1('dve', th0[:, :], ident, MODT[:, 16 + kc, 0:1], ALU.mult, [CST, MODT], [th0])
                P.mm(pt[:, q * 128:(q + 1) * 128], ones, th0[:, :], [CST, th0], [pt])
            P.cp('act', GB[:, half * 512:(half + 1) * 512], pt[:, :], [pt], [GB])
        P.dma(FG[:, :], brow_d[:, 1024:2048], writes=[FG])
        yo_ = [P.view(RKV.t[:, 0:8, :].rearrange('p m t -> p (m t)')), P.view(EALL.t[:, 0:2, :, :].rearrange('p m a t -> p (m a t)'))]
        for jj in range(16):
            tok = slice(jj * 128, (jj + 1) * 128)
            x_ = xt[jj % 2]
            P.dma(x_[:, :], xs[68 + jj, 0:128, :], writes=[x_])
            P.dma(mr[jj % 2][:, :, :], MIXRs[:, :, tok], reads=[MIXRs], writes=[mr[jj % 2]])
            P.dma(mm_[jj % 2][0:64, :, :], MIXMs[:, :, tok], reads=[MIXMs], writes=[mm_[jj % 2]])
            y_ = yo_[jj % 2]
            for half in range(2):
                cs = slice(half * 512, (half + 1) * 512)
                pp = ps2[half]
                for k in range(4):
                    P.mm(pp[:, :], mr[jj % 2][:, k, :], WOR[:, k, cs], [mr[jj % 2], WOR], [pp], start=(k == 0), stop=False)
                for k in range(8):
                    P.mm(pp[:, :], mm_[jj % 2][0:64, k, :], WOM[0:64, k, cs], [mm_[jj % 2], WOM], [pp], start=False, stop=(k == 7))
                P.tt('dve', y_[:, cs], pp[:, :], GB[:, cs], ALU.mult, [pp, GB], [y_])
            P.tt('pool', y_[:, :], y_[:, :], x_[:, :], ALU.add, [y_, x_], [y_])
            P.act(junk[:, :], y_[:, :], AF.Square, [y_], [junk])
            P.red(st4[:, 0:1], junk[:, :], [junk], [st4])
            rsqrt_ln(st4, st4[:, 0:1], st4, st4[:, 0:1], 1.0 / 1024, pc(PC_EPS))
            P.stt('dve', y_[:, :], y_[:, :], st4[:, 0:1], FG[:, :], ALU.mult, ALU.mult, [y_, st4, FG], [y_])
            P.dma(out_d[tok, :], y_[:, :], reads=[y_])
    if CFG['p2']:
        phase2()
    if CFG['p3']:
        phase3()
    P.finish()
    return nc


DEBUG_SPECS = {'xf': (128, 192), 'at': (128, 512), 'atok': (128, 512), 'vtok': (128, 512), 'eall': (128, 2048), 'sg': (128, 512), 'rkv': (128, 1536), 'hout': (128, 256)}
_CACHE = {}


def kernel(**inputs):
    maps = host_prep(inputs)
    if 'nc' not in _CACHE:
        _CACHE['nc'] = build()
    nc = _CACHE['nc']
    res = run_bass_kernel_spmd(nc, maps, core_ids=list(range(8)))
    out = np.zeros((2, 8192, 1024), np.float32)
    for c in range(8):
        b, p = c // 4, c % 4
        out[b, 2048 * p:2048 * (p + 1)] = np.asarray(res.results[c]['out'], np.float32)
    if DEBUG:
        _CACHE['res'] = res
    return out
```

```python
import numpy as np
from contextlib import ExitStack
import concourse.bass as bass
import concourse.mybir as mybir
from concourse.bass_utils import run_bass_kernel_spmd

F32 = mybir.dt.float32
BF16 = mybir.dt.bfloat16
ALU = mybir.AluOpType
AF = mybir.ActivationFunctionType
AX = mybir.AxisListType
BANK = 30000
NDMA = 24
ENGS = ['pe', 'act', 'dve', 'pool', 'sp']
DEBUG = False
CFG = {'slots': None, 'p2': True, 'p3': True, 'stage': 99}

D_MODEL = 1024
D_IN = 3488
NSTATE = 52
NSLOT = 84
NKT = 68
NKEY = NKT * 128
KAPPA = -float(np.exp(-0.5))
ATTN_SCALE = 96 ** -0.5
NORM_EPS = 1e-6
LNX_EPS = 64e-5


class Trk:
    __slots__ = ('w', 'r')

    def __init__(self):
        self.w = None
        self.r = {}


class Buf:
    def __init__(self, t, ap=None):
        self.t = t
        self.k = Trk()
        self._ap = ap

    def __getitem__(self, idx):
        return (self._ap if self._ap is not None else self.t)[idx]

    @property
    def ap(self):
        return self._ap if self._ap is not None else self.t[:]


class Prog:
    def __init__(self, nc):
        self.nc = nc
        self.es = ExitStack()
        self.ops = {e: [] for e in ENGS}
        self.cnt = {e: 0 for e in ENGS}
        self.seen = {e: {} for e in ENGS}
        self.sems = {}
        self.dma_tot = [0] * NDMA
        self.dma_next = 0
        self.hist = {e: [] for e in ENGS}

    def sb(self, name, shape, dt=F32):
        return Buf(self.es.enter_context(self.nc.sbuf_tensor('sb_' + name, list(shape), dt)))

    def ps(self, name, shape, dt=F32):
        return Buf(self.es.enter_context(self.nc.psum_tensor('ps_' + name, list(shape), dt)))

    def view(self, ap):
        return Buf(None, ap)

    def _deps(self, eng, reads, writes):
        deps = {}

        def add(ev):
            if ev is None:
                return
            k, v = ev
            if deps.get(k, 0) < v:
                deps[k] = v
        for b in reads:
            add(b.k.w)
        for b in writes:
            add(b.k.w)
            for k, v in b.k.r.items():
                add((k, v))
        waits = []
        seen = self.seen[eng]
        for k, v in sorted(deps.items(), key=lambda kv: -kv[1]):
            if k[0] == 'pe' and eng == 'pe':
                continue
            if seen.get(k, 0) >= v:
                continue
            seen[k] = v
            waits.append((k, v))
            if k[0] != 'dma':
                src, bank = k
                idx = bank * BANK + v - 1
                h = self.hist[src]
                lo, hi = 0, len(h)
                while lo < hi:
                    mid = (lo + hi) // 2
                    if h[mid][0] <= idx:
                        lo = mid + 1
                    else:
                        hi = mid
                if lo > 0:
                    for k2, v2 in h[lo - 1][1].items():
                        if seen.get(k2, 0) < v2:
                            seen[k2] = v2
        if waits and eng != 'sp':
            self.hist[eng].append((self.cnt[eng], dict(seen)))
        return waits

    def _mark(self, ev, reads, writes):
        k, v = ev
        for b in writes:
            b.k.w = ev
            b.k.r = {}
        for b in reads:
            if b in writes:
                continue
            if b.k.r.get(k, 0) < v:
                b.k.r[k] = v

    def op(self, eng, fn, reads=(), writes=()):
        waits = self._deps(eng, reads, writes)
        i = self.cnt[eng]
        self.cnt[eng] += 1
        ev = ((eng, i // BANK), i % BANK + 1)
        self._mark(ev, reads, writes)
        self.ops[eng].append((waits, fn, ev[0], 1))

    def dma(self, out, in_, reads=(), writes=(), eng='sp', **kw):
        waits = self._deps(eng, reads, writes)
        s = self.dma_next % NDMA
        self.dma_next += 1
        k = ('dma', s)
        if self.dma_tot[s] > 0 and self.seen[eng].get(k, 0) < self.dma_tot[s]:
            self.seen[eng][k] = self.dma_tot[s]
            waits.append((k, self.dma_tot[s]))
        self.dma_tot[s] += 16
        ev = (k, self.dma_tot[s])
        self._mark(ev, reads, writes)
        self.ops[eng].append((waits, lambda e: e.dma_start(out=out, in_=in_, **kw), k, 16))

    def barrier(self):
        last = {}
        for e in ENGS:
            i = self.cnt[e]
            if i > 0:
                last[(e, (i - 1) // BANK)] = (i - 1) % BANK + 1
        for s in range(NDMA):
            if self.dma_tot[s] > 0:
                last[('dma', s)] = self.dma_tot[s]
        for e in ENGS:
            waits = []
            for k, v in last.items():
                if k[0] == e:
                    continue
                if self.seen[e].get(k, 0) >= v:
                    continue
                self.seen[e][k] = v
                waits.append((k, v))
            if waits:
                self.ops[e].append((waits, None, None, 0))

    def mm(self, out, lhsT, rhs, rd, wr, start=True, stop=True):
        self.op('pe', lambda e: e.matmul(out, lhsT, rhs, start=start, stop=stop), rd, wr)

    def tr(self, out, in_, ident, rd, wr):
        self.op('pe', lambda e: e.transpose(out, in_, ident), rd, wr)

    def act(self, out, in_, func, rd, wr, bias=None, scale=None):
        kw = {}
        if bias is not None:
            kw['bias'] = bias
        if scale is not None:
            kw['scale'] = scale
        self.op('act', lambda e: e.activation(out, in_, func, **kw), rd, wr)

    def tt(self, eng, out, a, b, op, rd, wr):
        self.op(eng, lambda e: e.tensor_tensor(out, a, b, op), rd, wr)

    def ts(self, eng, out, a, s1, s2, op0, op1, rd, wr):
        self.op(eng, lambda e: e.tensor_scalar(out, a, s1, s2, op0, op1), rd, wr)

    def ts1(self, eng, out, a, s, op, rd, wr):
        self.op(eng, lambda e: e.tensor_single_scalar(out, a, s, op), rd, wr)

    def stt(self, eng, out, a, s, b, op0, op1, rd, wr):
        self.op('dve', lambda e: e.scalar_tensor_tensor(out, a, s, b, op0, op1), rd, wr)

    def cp(self, eng, out, in_, rd, wr):
        if eng == 'act':
            self.op('act', lambda e: e.copy(out, in_), rd, wr)
        else:
            self.op(eng, lambda e: e.tensor_copy(out, in_), rd, wr)

    def recip(self, out, in_, rd, wr):
        self.op('dve', lambda e: e.reciprocal(out, in_), rd, wr)

    def red(self, out, in_, rd, wr, op=ALU.add):
        self.op('dve', lambda e: e.tensor_reduce(out, in_, AX.X, op), rd, wr)

    def memset(self, eng, ap, val, wr):
        self.op(eng, lambda e: e.memset(ap, val), (), wr)

    def finish(self):
        nc = self.nc
        keys = set()
        for e in ENGS:
            for waits, fn, k, inc in self.ops[e]:
                if k is not None:
                    keys.add(k)
                for wk, _ in waits:
                    keys.add(wk)
        for k in sorted(keys, key=str):
            self.sems[k] = self.es.enter_context(nc.semaphore("s_%s_%s" % k))
        fin = [(('dma', s), self.dma_tot[s]) for s in range(NDMA) if self.dma_tot[s] > 0]
        block = self.es.enter_context(nc.Block())

        def run(e, name):
            for waits, fn, k, inc in self.ops[name]:
                fold = fn is not None and name != 'sp' and len(waits) > 0
                for wk, wv in (waits[1:] if fold else waits):
                    e.wait_ge(self.sems[wk], wv)
                if fn is not None:
                    ins = fn(e)
                    if fold:
                        ins._wait_ge(self.sems[waits[0][0]], waits[0][1])
                    ins.then_inc(self.sems[k], inc)
            if name == 'sp':
                for wk, wv in fin:
                    e.wait_ge(self.sems[wk], wv)

        @block.tensor
        def _(e):
            run(e, 'pe')

        @block.scalar
        def _(e):
            run(e, 'act')

        @block.vector
        def _(e):
            run(e, 'dve')

        @block.gpsimd
        def _(e):
            run(e, 'pool')

        @block.sync
        def _(e):
            run(e, 'sp')
        self.es.close()


PC_NG, PC_ADAB, PC_MUP, PC_MUN, PC_KK, PC_KA, PC_RK, PC_A0, PC_QG, PC_KVG = 0, 8, 32, 46, 60, 64, 68, 72, 80, 83
PC_EPS, PC_LEPS, PC_ONE, PC_TINY = 85, 86, 87, 88
NPC = 96
C_ID, C_ONES, C_BO, C_F, C_B, C_TOT, C_BO2, C_SGN = 0, 128, 256, 384, 1280, 2176, 2177, 2179
SD_F, SD_SELW, SD_M2S, SD_PHI, SD_RHO, SD_SIG, SD_NF = 0, 1, 2, 3, 4, 5, 6
NCST = 2211

SLOTS = []
for _i in range(52):
    SLOTS.append(('S', _i))
for _j in range(16):
    SLOTS.append(('B', None))
for _j in range(16):
    SLOTS.append(('F', 52 + _j))


def _consts():
    c = np.zeros((128, NCST), np.float32)
    idx = np.arange(128)
    row, col = idx[:, None], idx[None, :]
    c[:, C_ID:C_ID + 128] = np.eye(128)
    c[:, C_ONES:C_ONES + 128] = 1.0
    c[:, C_BO:C_BO + 128] = ((row // 64) == (col // 64))
    k = KAPPA
    tf = [k * (row < col), -k * (row <= col), k * (row <= col), k * (row > col)]
    tb = [k * (row > col), -k * (row >= col), k * (row >= col), k * (row < col)]
    c[:, C_F:C_F + 896] = np.concatenate(tf + [col > row, col >= row, col < row], 1)
    c[:, C_B:C_B + 896] = np.concatenate(tb + [col < row, col <= row, col > row], 1)
    c[:, C_TOT] = k
    c[:, C_BO2] = (idx < 64)
    c[:, C_BO2 + 1] = (idx >= 64)
    d = np.arange(32)
    c[:, C_SGN:C_SGN + 32] = np.where(d % 16 < 8, -1.0, 1.0)[None, :]
    return c


ROT_PERM = np.array([dd + 8 if dd % 16 < 8 else dd - 8 for dd in range(32)])


def _rope_tables(pos):
    pos = np.asarray(pos)
    inv = (np.float32(10000.0) ** (-np.arange(0, 16, 2, dtype=np.float32) / np.float32(16))).astype(np.float32)
    rowf = (pos // 64).astype(np.float32)
    colf = (pos % 64).astype(np.float32)
    ar = (rowf[:, None] * inv[None, :]).astype(np.float32)
    ac = (colf[:, None] * inv[None, :]).astype(np.float32)
    cos = np.concatenate([np.cos(ar), np.cos(ar), np.cos(ac), np.cos(ac)], 1).astype(np.float32)
    sin = np.concatenate([np.sin(ar), np.sin(ar), np.sin(ac), np.sin(ac)], 1).astype(np.float32)
    return cos.T.copy(), sin.T.copy()


def _tile_halo(seq, c):
    n = seq.shape[0]
    t = np.zeros((130, seq.shape[1]), np.float32)
    t[:128] = seq[c * 128:(c + 1) * 128]
    hv = [0.0, 0.0]
    if c > 0:
        t[128] = seq[c * 128 - 1]
        hv[0] = 1.0
    if (c + 1) * 128 < n:
        t[129] = seq[(c + 1) * 128]
        hv[1] = 1.0
    return t, hv


def host_prep(inp):
    g = {k: np.asarray(v, np.float32) for k, v in inp.items()}
    sh = {}
    w_in = g['w_in'][0]
    sh['w_in'] = w_in
    sh['wkr_rot'] = np.ascontiguousarray(w_in[:, 2944:2976][:, ROT_PERM])
    w_uq = g['mla_w_uq'][0]
    sh['w_uq'] = w_uq
    sh['w_uq_rot'] = np.ascontiguousarray(
        np.stack([w_uq[:, h * 96 + 64:h * 96 + 96][:, ROT_PERM] for h in range(8)], 1))
    w_ukv = g['mla_w_ukv'][0].reshape(256, 8, 128)
    sh['w_uk'] = np.ascontiguousarray(w_ukv[:, :, :64])
    sh['w_uv'] = np.ascontiguousarray(w_ukv[:, :, 64:])
    sh['w_out'] = g['w_out'][0]
    sh['ada_w'] = g['ada_w'][0]
    sh['w2cat'] = np.ascontiguousarray(g['rw_w2'][0].reshape(128, 512))
    sh['a2cat'] = np.ascontiguousarray(g['rw_a2'][0].reshape(128, 512))
    sh['w0r'] = np.ascontiguousarray(g['rw_w0'][0].reshape(1, 1024))
    pc = np.zeros((128, NPC), np.float32)
    pc[:, PC_NG:PC_NG + 8] = g['norm_g'][0].reshape(8, 128).T
    pc[:, PC_ADAB:PC_ADAB + 24] = g['ada_b'][0].reshape(24, 128).T
    pc[:, PC_MUP:PC_MUP + 14] = g['shift_mu'][0, 0].reshape(14, 128).T
    pc[:, PC_MUN:PC_MUN + 14] = g['shift_mu'][0, 1].reshape(14, 128).T
    pc[:, PC_KK:PC_KK + 4] = g['rw_kk'][0].reshape(4, 128).T
    pc[:, PC_KA:PC_KA + 4] = g['rw_ka'][0].reshape(4, 128).T
    pc[:, PC_RK:PC_RK + 4] = g['rw_rk'][0].reshape(4, 128).T
    pc[:, PC_A0:PC_A0 + 4] = g['rw_a0'][0, 0].reshape(4, 128).T
    pc[:, PC_A0 + 4:PC_A0 + 8] = g['rw_a0'][0, 1].reshape(4, 128).T
    pc[:, PC_QG:PC_QG + 3] = g['mla_q_norm_g'][0].reshape(3, 128).T
    pc[:, PC_KVG:PC_KVG + 2] = g['mla_kv_norm_g'][0].reshape(2, 128).T
    pc[:, PC_EPS] = NORM_EPS
    pc[:, PC_LEPS] = LNX_EPS
    pc[:, PC_ONE] = 1.0
    pc[:, PC_TINY] = 1e-24
    sh['pcol'] = pc
    br = np.zeros((128, 2048), np.float32)
    br[:, 0:512] = g['rw_lnx_g'][0][None, :]
    br[:, 512:1024] = g['rw_lnx_b'][0][None, :]
    br[:, 1024:2048] = g['final_g'][None, :]
    sh['brow'] = br
    sh['cst'] = _consts()
    maps = []
    for b in range(2):
        x, ctx = g['x'][b], g['ctx'][b]
        cvec = np.stack([g['c'][b].reshape(128, 8), g['c_ctx'].reshape(128, 8)], -1)
        for p in range(4):
            nf = 2 + 16 * p
            fsrc = [(ctx, 0, True), (ctx, 1, True)] + [(x, c, False) for c in range(16 * p)]
            bsrc = [(ctx, 1, True), (ctx, 0, True)] + [(x, c, False) for c in range(63, 16 * p + 15, -1)]
            own = [(x, 16 * p + j, False) for j in range(16)]
            allsrc = fsrc + bsrc + own[::-1] + own
            assert len(allsrc) == NSLOT
            tiles, hvs = [], []
            for seq, c, _ in allsrc:
                t, hv = _tile_halo(seq, c)
                tiles.append(t)
                hvs.append(hv)
            xs = np.stack(tiles, 0)
            hvs = np.array(hvs, np.float32)
            sd = np.zeros((128, 7, NSLOT), np.float32)
            f = np.array([1.0] * nf + [0.0] * (52 - nf) + [0.0] * 16 + [1.0] * 16, np.float32)
            sd[:, SD_F, :] = f[None]
            sd[:, SD_NF, :] = 1.0 - f[None]
            sd[:64, SD_SELW, :] = f[None]
            sd[64:, SD_SELW, :] = 1.0 - f[None]
            sd[:, SD_M2S, :] = -2.0 * sd[:, SD_SELW, :]
            sd[:, SD_PHI, :] = np.array([1.0 if ic else 0.0 for _, _, ic in allsrc], np.float32)[None]
            rho = np.ones(NSLOT, np.float32)
            rho[nf] = 0.0
            sd[:, SD_RHO, :] = rho[None]
            sgm = np.zeros(NSLOT, np.float32)
            sgm[nf - 1] = 1.0
            sd[:, SD_SIG, :] = sgm[None]
            kvsrc = fsrc + bsrc + own
            rk = np.zeros((NKT, 2, 32, 128), np.float32)
            rk[:, 0] = 1.0
            kvv = np.ones(NKT, np.float32)
            kvv[nf] = 0.0
            kvv[nf + 1] = 0.0
            for kt, (seq, c, ic) in enumerate(kvsrc):
                if not ic:
                    cs, sn = _rope_tables(np.arange(c * 128, (c + 1) * 128))
                    rk[kt, 0], rk[kt, 1] = cs, sn
            cq, sq = _rope_tables(np.arange(2048 * p, 2048 * (p + 1)))
            m = dict(sh)
            m['xs'] = xs
            m['hv'] = np.ascontiguousarray(np.broadcast_to(hvs[None], (128, NSLOT, 2)))
            m['sd'] = sd
            m['kvv'] = np.ascontiguousarray(np.broadcast_to(kvv[None], (128, NKT)))
            m['ropek'] = rk
            m['ropeq'] = np.stack([cq, sq], 0)
            m['cvec'] = cvec
            maps.append(m)
    return maps


def build():
    nc = bass.Bass("TRN2", target_bir_lowering=False)
    P = Prog(nc)

    def din(name, shape):
        return nc.dram_tensor(name, list(shape), F32, kind="ExternalInput").ap()

    xs = din('xs', [NSLOT, 130, 1024])
    hv_d = din('hv', [128, NSLOT, 2])
    sd_d = din('sd', [128, 7, NSLOT])
    kvv_d = din('kvv', [128, NKT])
    ropek = din('ropek', [NKT, 2, 32, 128])
    ropeq = din('ropeq', [2, 32, 2048])
    cvec_d = din('cvec', [128, 8, 2])
    w_in_d = din('w_in', [1024, D_IN])
    wkr_rot_d = din('wkr_rot', [1024, 32])
    w_uq_d = din('w_uq', [384, 768])
    w_uq_rot_d = din('w_uq_rot', [384, 8, 32])
    w_uk_d = din('w_uk', [256, 8, 64])
    w_uv_d = din('w_uv', [256, 8, 64])
    w_out_d = din('w_out', [1024, 1024])
    ada_w_d = din('ada_w', [1024, 3072])
    w2cat_d = din('w2cat', [128, 512])
    a2cat_d = din('a2cat', [128, 512])
    w0r_d = din('w0r', [1, 1024])
    pcol_d = din('pcol', [128, NPC])
    brow_d = din('brow', [128, 2048])
    cst_d = din('cst', [128, NCST])
    out_d = nc.dram_tensor('out', [2048, 1024], F32, kind="ExternalOutput").ap()
    dbg = {}
    if DEBUG:
        for nm, shp in DEBUG_SPECS.items():
            dbg[nm] = nc.dram_tensor('dbg_' + nm, list(shp), F32, kind="ExternalOutput").ap()

    def dscr(name, shape, dt):
        t = nc.dram_tensor(name, list(shape), dt)
        return Buf(t, t.ap())

    CKVs = dscr('s_ckv', [128, 2, NKEY], BF16)
    KRs = dscr('s_kr', [32, NKEY], BF16)
    CQs = dscr('s_cq', [128, 3, 2048], BF16)
    GMs = dscr('s_gm', [64, 8, 2048], BF16)
    MIXRs = dscr('s_mixr', [128, 4, 2048], BF16)
    MIXMs = dscr('s_mixm', [64, 8, 2048], BF16)
    YBs = dscr('s_yb', [16, 128, 512], F32)
    SBs = dscr('s_sb', [16, 128, 8], F32)

    CST = P.sb('cst', [128, NCST])
    PCOL = P.sb('pcol', [128, NPC])
    LGB = P.sb('lgb', [128, 1024])
    HV = P.sb('hv', [128, NSLOT, 2])
    SD = P.sb('sd', [128, 7, NSLOT])
    KVV = P.sb('kvv', [128, NKT])
    CSEL = P.sb('csel', [128, 512])
    GSL = P.sb('gsl', [128, 4, 8])
    W0S = P.sb('w0s', [1, 512])
    NA0S = P.sb('na0s', [128, 4])
    W2C = P.sb('w2c', [128, 512])
    A2C = P.sb('a2c', [128, 512])
    W0R = P.sb('w0r', [1, 1024])
    MODT = P.sb('modt', [128, 24, 2])
    GS = P.sb('gs', [128, 2, 8])
    SH = P.sb('shc', [128, 2, 8])
    M0 = P.sb('m0', [128, 14])
    NA0 = P.sb('na0', [128, 8])
    ARENA = P.sb('arena', [128, 28432], BF16)
    WB = P.view(ARENA.t[:, 0:27904].rearrange("p (k c) -> p k c", k=8))
    WKR = P.sb('wkr', [128, 8, 2, 96], BF16)

    for (b_, d_) in [(CST, cst_d), (PCOL, pcol_d), (HV, hv_d), (SD, sd_d), (KVV, kvv_d), (W2C, w2cat_d), (A2C, a2cat_d),
                     (W0R, w0r_d)]:
        P.dma(b_.ap, d_, writes=[b_])
    P.dma(LGB[:, :], brow_d[:, 0:1024], writes=[LGB])
    ident = CST[:, C_ID:C_ID + 128]
    ones = CST[:, C_ONES:C_ONES + 128]

    def pc(c0, n=1):
        return PCOL[:, c0:c0 + n]

    def ps_pair(name):
        q = P.ps(name, [128, 512])
        v0, v1 = P.view(q.t[:, 0:256]), P.view(q.t[:, 256:512])
        v0.k = q.k
        v1.k = q.k
        return v0, v1, q
    pu = [P.ps('pu0', [128, 512]), P.ps('pu1', [128, 512])]
    pz = P.ps('pz', [128, 512])
    pt = pz
    pA = P.ps('pA', [128, 512])
    pB, pX, pBXf = ps_pair('qbx')
    pS, pM, _ = ps_pair('qsm')
    pY = P.ps('pY', [128, 512])
    pH, pG, _ = ps_pair('qhg')

    xt = [P.sb('xt0', [128, 1024]), P.sb('xt1', [128, 1024])]
    xh = [P.sb('xh0', [2, 1024])]
    junk = P.sb('junk', [128, 1024])
    tD = P.view(junk.t[:, 0:512])
    tE = P.view(junk.t[:, 512:1024])
    tD.k = junk.k
    tE.k = junk.k
    st4 = P.sb('st4', [128, 8])
    hT = [P.sb('hT0', [128, 8, 130], BF16), P.sb('hT1', [128, 8, 130], BF16)]
    shv = P.sb('shv', [128, 2, 8])
    ue = [P.sb('ue0', [128, 130]), P.sb('ue1', [128, 130])]
    RKV = P.sb('rkv', [128, 12, 128])
    WA = P.sb('wa', [128, 2, 128])
    tA = P.sb('tA', [128, 512])
    tB = P.sb('tB', [128, 512])
    tC = P.sb('tC', [128, 512])
    th0 = P.sb('th0', [128, 128])
    th1 = P.sb('th1', [128, 128])
    sg = P.sb('sg', [128, 512])
    EALL = P.sb('eall', [128, 4, 4, 128])
    STG = P.view(EALL.t[:, :, :, :].rearrange('p m a t -> p (m a t)')[:, 0:1792])
    STG.k = EALL.k
    GCL = [P.sb('gc0', [128, 4]), P.sb('gc1', [128, 4])]
    MSKL = [P.sb('msk0', [128, 384]), P.sb('msk1', [128, 384])]
    ARtL = [P.sb('art%d' % q_, [128, 4, 256]) for q_ in range(2)]
    BKtL = [P.sb('bkt%d' % q_, [128, 4, 256]) for q_ in range(2)]
    ARt, BKt = ARtL[0], BKtL[0]
    BKh = P.sb('bkh', [128, 1024])
    BhT = P.view(BKh.t[:, 0:512])
    KhT = P.view(BKh.t[:, 512:1024])
    BhT.k = BKh.k
    KhT.k = BKh.k
    AtokL = [P.sb('atok%d' % q_, [128, 512]) for q_ in range(2)]
    BhtokL = [P.sb('bhtok%d' % q_, [128, 512]) for q_ in range(2)]
    KhtokL = [P.sb('khtok%d' % q_, [128, 512]) for q_ in range(2)]
    VtokL = [P.sb('vtok%d' % q_, [128, 512]) for q_ in range(2)]
    AT = [P.sb('at0', [128, 512]), P.sb('at1', [128, 512])]
    XPBS = [[P.sb('xp%d_%d' % (p_, q_), [128, 320]) for q_ in range(3)] for p_ in range(2)]
    PTBS = [[P.sb('ptq%d_%d' % (p_, q_), [128, 128]) for q_ in range(2)] for p_ in range(2)]
    XPB = XPBS[0] + XPBS[1]
    MTs = P.sb('mts', [128, 4, 64])
    Ns = P.sb('ns', [128, 4, 64])
    GTs = [P.sb('gts0', [128, 128]), P.sb('gts1', [128, 128])]
    Hc = [P.sb('hc0', [128, 4, 64]), P.sb('hc1', [128, 4, 64])]
    HFa = P.sb('hfa', [128, 4, 64])
    ysb = P.sb('ysb', [128, 512])
    ybl = P.view(sg.t[:, :])
    ybl.k = sg.k
    sbl = P.sb('sbl', [128, 8])
    ssumL = [P.sb('ssum0', [128, 8]), P.sb('ssum1', [128, 8])]
    mla1 = P.sb('mla1', [128, 4, 128])
    mla2 = P.view(tC.t[:, :].rearrange('p (a t) -> p a t', a=4))
    mla2.k = tC.k
    kvb = P.sb('kvb', [128, 2, 128], BF16)
    krb = P.sb('krb', [96, 128], BF16)
    rpk = P.sb('rpk', [96, 2, 128])
    cqb = P.sb('cqb', [128, 3, 128], BF16)
    gmb = P.sb('gmb', [64, 8, 128], BF16)
    mixb = P.sb('mixb', [128, 4, 128], BF16)

    def dump(name, buf, ap):
        if DEBUG and name in dbg:
            P.dma(dbg[name], ap, reads=[buf])

    w_in_v = w_in_d.rearrange("(k p) c -> p k c", p=128)
    for kc in range(8):
        for hf in range(2):
            P.dma(STG[:, 0:1744], w_in_v[:, kc, hf * 1744:(hf + 1) * 1744], writes=[STG])
            P.cp(['act', 'dve'][hf], WB[:, kc, hf * 1744:(hf + 1) * 1744], STG[:, 0:1744], [STG], [WB])
    P.memset('pool', WKR[:, :, :, :], 0.0, [WKR])
    for kc in range(8):
        P.cp('dve', WKR[:, kc, 0, 64:96], WB[:, kc, 2944:2976], [WB], [WKR])
    wkr_v = wkr_rot_d.rearrange("(k p) c -> p k c", p=128)
    P.dma(junk[:, 0:256].rearrange("p (k c) -> p k c", k=8), wkr_v, writes=[junk])
    for kc in range(8):
        P.tt('dve', WKR[:, kc, 1, 64:96], junk[:, kc * 32:(kc + 1) * 32], CST[:, C_SGN:C_SGN + 32], ALU.mult,
             [junk, CST], [WKR])
    CV = P.sb('cv', [128, 8, 2])
    CV2 = P.sb('cv2', [128, 8, 2])
    P.dma(CV[:, :, :], cvec_d, writes=[CV])
    P.act(CV2[:, :, :], CV[:, :, :], AF.Exp, [CV], [CV2], scale=-1.0)
    P.ts1('dve', CV2[:, :, :], CV2[:, :, :], 1.0, ALU.add, [CV2], [CV2])
    P.recip(CV2[:, :, :], CV2[:, :, :], [CV2], [CV2])
    P.tt('dve', CV[:, :, :], CV[:, :, :], CV2[:, :, :], ALU.mult, [CV, CV2], [CV])
    ada_v = ada_w_d.rearrange("(p k) n -> p k n", k=8)
    MACC = P.sb('macc', [128, 48])
    P.memset('dve', MACC[:, :], 0.0, [MACC])
    for k in range(8):
        for hf in range(2):
            P.dma(STG[:, 0:1536], ada_v[:, k, hf * 1536:(hf + 1) * 1536], writes=[STG])
            for n in range(12):
                nn = hf * 12 + n
                P.mm(pz[:, 2 * nn:2 * nn + 2], STG[:, n * 128:(n + 1) * 128], CV[:, k, :], [STG, CV], [pz])
            P.tt('dve', MACC[:, hf * 24:(hf + 1) * 24], MACC[:, hf * 24:(hf + 1) * 24], pz[:, hf * 24:(hf + 1) * 24],
                 ALU.add, [MACC, pz], [MACC])
    for j in range(2):
        P.tt('dve', MODT[:, :, j], MACC[:, j:48:2], PCOL[:, PC_ADAB:PC_ADAB + 24], ALU.add, [MACC, PCOL], [MODT])
    for j in range(2):
        P.stt('dve', GS[:, j, :], MODT[:, 8:16, j], 1.0, PCOL[:, PC_NG:PC_NG + 8], ALU.add, ALU.mult,
              [MODT, PCOL], [GS])
        P.cp('dve', SH[:, j, :], MODT[:, 0:8, j], [MODT], [SH])
    P.cp('dve', GSL[:, 0, :], GS[:, 0, :], [GS], [GSL])
    P.tt('dve', GSL[:, 1, :], GS[:, 1, :], GS[:, 0, :], ALU.subtract, [GS], [GSL])
    P.cp('dve', GSL[:, 2, :], SH[:, 0, :], [SH], [GSL])
    P.tt('dve', GSL[:, 3, :], SH[:, 1, :], SH[:, 0, :], ALU.subtract, [SH], [GSL])
    P.tt('dve', M0[:, :], PCOL[:, PC_MUP:PC_MUP + 14], PCOL[:, PC_MUN:PC_MUN + 14], ALU.add, [PCOL], [M0])
    P.ts('dve', M0[:, :], M0[:, :], -1.0, 1.0, ALU.mult, ALU.add, [M0], [M0])
    P.ts1('dve', NA0[:, :], PCOL[:, PC_A0:PC_A0 + 8], -1.0, ALU.mult, [PCOL], [NA0])
    for b_ in XPB:
        P.memset('pool', b_[:, :], 0.0, [b_])
    P.memset('pool', Hc[0][:, :, :], 0.0, [Hc[0]])
    P.memset('pool', HFa[:, :, :], 0.0, [HFa])

    def load_slot(i):
        P.dma(xt[i % 2][:, :], xs[i, 0:128, :], writes=[xt[i % 2]])

    def rsqrt_ln(dst_buf, dst, src_buf, src, scale, eps_ap):
        P.act(dst, src, AF.Ln, [src_buf, PCOL], [dst_buf], bias=eps_ap, scale=scale)
        P.act(dst, dst, AF.Exp, [dst_buf], [dst_buf], scale=-0.5)

    state = {'h': 0}

    PCNT = {'n': 0}
    ADV = 1

    def proj_for(hT_, cnt):
        def proj(c0, width, ncols=128, lhs=None):
            pb_ = pu[cnt['n'] % 2]
            cnt['n'] += 1
            for kc in range(8):
                l = WB[:, kc, c0:c0 + width] if lhs is None else lhs(kc)
                P.mm(pb_[0:(width if lhs is None else 96), 0:ncols], l, hT_[:, kc, 0:ncols],
                     [WB if lhs is None else WKR, hT_], [pb_], start=(kc == 0), stop=(kc == 7))
            return pb_
        return proj

    def prep(i):
        kind, kv = SLOTS[i]
        S = i % 2
        ARt, BKt, Atok, Bhtok, Khtok, Vtok = ARtL[S], BKtL[S], AtokL[S], BhtokL[S], KhtokL[S], VtokL[S]
        GC, MSK, ssum = GCL[S], MSKL[S], ssumL[S]
        x_, xh_, hT_ = xt[i % 2], xh[0], hT[i % 2]
        P.dma(xh_[:, :], xs[i, 128:130, :], writes=[xh_])
        fcol = SD[:, SD_F, i:i + 1]
        nfcol = SD[:, SD_NF, i:i + 1]
        phi = SD[:, SD_PHI, i:i + 1]
        P.stt('dve', GS[:, 0, :], GSL[:, 1, :], phi, GSL[:, 0, :], ALU.mult, ALU.add, [GSL, SD], [GS])
        P.stt('dve', SH[:, 0, :], GSL[:, 3, :], phi, GSL[:, 2, :], ALU.mult, ALU.add, [GSL, SD], [SH])
        P.ts1('dve', CSEL[:, :], CST[:, C_F:C_F + 512], fcol, ALU.mult, [CST, SD], [CSEL])
        P.stt('dve', CSEL[:, :], CST[:, C_B:C_B + 512], nfcol, CSEL[:, :], ALU.mult, ALU.add, [CST, SD, CSEL], [CSEL])
        P.ts1('dve', MSK[:, :], CST[:, C_F + 512:C_F + 896], fcol, ALU.mult, [CST, SD], [MSK])
        P.stt('dve', MSK[:, :], CST[:, C_B + 512:C_B + 896], nfcol, MSK[:, :], ALU.mult, ALU.add, [CST, SD, MSK], [MSK])
        P.ts1('dve', W0S[0:1, :], W0R[0:1, 0:512], SD[0:1, SD_F, i:i + 1], ALU.mult, [W0R, SD], [W0S])
        P.stt('dve', W0S[0:1, :], W0R[0:1, 512:1024], SD[0:1, SD_NF, i:i + 1], W0S[0:1, :], ALU.mult, ALU.add,
              [W0R, SD, W0S], [W0S])
        P.ts1('dve', NA0S[:, :], NA0[:, 0:4], fcol, ALU.mult, [NA0, SD], [NA0S])
        P.stt('dve', NA0S[:, :], NA0[:, 4:8], nfcol, NA0S[:, :], ALU.mult, ALU.add, [NA0, SD, NA0S], [NA0S])
        j = 0
        P.act(junk[:, :], x_[:, :], AF.Square, [x_], [junk])
        P.red(st4[:, 0:1], junk[:, :], [junk], [st4])
        rsqrt_ln(st4, st4[:, 0:1], st4, st4[:, 0:1], 1.0 / 1024, pc(PC_EPS))
        P.ts1('dve', x_[:, :], x_[:, :], st4[:, 0:1], ALU.mult, [x_, st4], [x_])
        for half in range(2):
            yield
            for q in range(4):
                kc = half * 4 + q
                P.tr(pt[:, q * 128:(q + 1) * 128], x_[:, kc * 128:(kc + 1) * 128], ident, [x_, CST], [pt])
            for q in range(4):
                kc = half * 4 + q
                if q % 2 == 0:
                    P.act(hT_[:, kc, 0:128], pt[:, q * 128:(q + 1) * 128], AF.Identity, [pt, GS, SH], [hT_],
                          bias=SH[:, j, kc:kc + 1], scale=GS[:, j, kc:kc + 1])
                else:
                    P.ts('dve', hT_[:, kc, 0:128], pt[:, q * 128:(q + 1) * 128], GS[:, j, kc:kc + 1],
                         SH[:, j, kc:kc + 1], ALU.mult, ALU.add, [pt, GS, SH], [hT_])
        yield
        P.act(junk[0:2, :], xh_[:, :], AF.Square, [xh_], [junk])
        P.red(st4[0:2, 1:2], junk[0:2, :], [junk], [st4])
        rsqrt_ln(st4, st4[0:2, 1:2], st4, st4[0:2, 1:2], 1.0 / 1024, PCOL[0:2, PC_EPS:PC_EPS + 1])
        P.ts1('dve', xh_[:, :], xh_[:, :], st4[0:2, 1:2], ALU.mult, [xh_, st4], [xh_])
        yield
        for kc in range(8):
            P.tr(pt[:, 2 * kc:2 * kc + 2], xh_[0:2, kc * 128:(kc + 1) * 128], CST[0:2, C_ID:C_ID + 2], [xh_, CST], [pt])
        for c in range(2):
            P.ts1('dve', shv[:, c, :], SH[:, j, :], HV[:, i, c:c + 1], ALU.mult, [SH, HV], [shv])
            P.tt('dve', junk[:, 8 * c:8 * c + 8], pt[:, c:16:2], GS[:, j, :], ALU.mult, [pt, GS], [junk])
            P.tt('dve', hT_[:, :, 128 + c], junk[:, 8 * c:8 * c + 8], shv[:, c, :], ALU.add, [junk, shv], [hT_])

        yield
        proj = proj_for(hT_, PCNT)

        for jc in range(14):
            if kind == 'S' and jc < 4:
                continue
            pb_ = proj(jc * 128, 128, 130)
            ue_ = ue[jc % 2]
            P.cp('act', ue_[:, :], pb_[:, 0:130], [pb_], [ue_])
            yield
            if jc < 12:
                dstb, dst = RKV, RKV[:, jc, :]
            else:
                dstb, dst = WA, WA[:, jc - 12, :]
            eng = 'pool' if jc % 2 == 0 else 'dve'
            mp_, mn_ = pc(PC_MUP + jc), pc(PC_MUN + jc)
            P.ts1(eng, dst, ue_[:, 0:128], M0[:, jc:jc + 1], ALU.mult, [ue_, M0], [dstb])
            for (dsl, ssl, mu_) in [((1, 128), (0, 127), mp_), ((0, 1), (128, 129), mp_),
                                    ((0, 127), (1, 128), mn_), ((127, 128), (129, 130), mn_)]:
                dd = dst[:, dsl[0]:dsl[1]]
                if eng == 'dve':
                    P.stt('dve', dd, ue_[:, ssl[0]:ssl[1]], mu_, dd, ALU.mult, ALU.add, [ue_, PCOL, dstb], [dstb])
                else:
                    tp = th1[:, 0:dsl[1] - dsl[0]]
                    P.ts1('pool', tp, ue_[:, ssl[0]:ssl[1]], mu_, ALU.mult, [ue_, PCOL], [th1])
                    P.tt('pool', dd, dd, tp, ALU.add, [dstb, th1], [dstb])
            yield
        yield
        r3 = RKV[:, 0:4, :]
        k3 = RKV[:, 4:8, :]
        v3 = RKV[:, 8:12, :]

        def f3(b_):
            return b_[:, :].rearrange("p (m t) -> p m t", m=4)

        for m in range(4):
            P.ts1('pool', tA[:, m * 128:(m + 1) * 128], RKV[:, 4 + m, :], pc(PC_KK + m), ALU.mult, [RKV, PCOL], [tA])
        P.act(tB[:, :], tA[:, :], AF.Square, [tA], [tB])
        P.mm(pz[:, :], CST[:, C_BO:C_BO + 128], tB[:, :], [CST, tB], [pz])
        yield
        P.ts1('dve', tB[:, :], pz[:, :], 1e-24, ALU.max, [pz], [tB])
        P.act(tB[:, :], tB[:, :], AF.Ln, [tB], [tB])
        P.act(tB[:, :], tB[:, :], AF.Exp, [tB], [tB], scale=-0.5)
        P.tt('dve', tA[:, :], tA[:, :], tB[:, :], ALU.mult, [tA, tB], [tA])
        yield
        P.act(th0[:, :], WA[:, 0, :], AF.Exp, [WA], [th0], scale=2.0)
        P.ts1('dve', th0[:, :], th0[:, :], 1.0, ALU.add, [th0], [th0])
        P.recip(th0[:, :], th0[:, :], [th0], [th0])
        P.ts('dve', th0[:, :], th0[:, :], SD[:, SD_M2S, i:i + 1], SD[:, SD_SELW, i:i + 1], ALU.mult, ALU.add,
             [th0, SD], [th0])
        P.mm(pz[:, :], th0[:, :], W2C[:, :], [th0, W2C], [pz], start=True, stop=False)
        P.mm(pz[:, :], CST[0:1, C_ONES:C_ONES + 128], W0S[0:1, :], [CST, W0S], [pz], start=False, stop=True)
        yield
        P.act(sg[:, :], pz[:, :], AF.Exp, [pz], [sg], scale=-1.0)
        P.ts1('dve', sg[:, :], sg[:, :], 1.0, ALU.add, [sg], [sg])
        P.recip(sg[:, :], sg[:, :], [sg], [sg])
        yield
        tri = CSEL[:, 0:512]
        for m in range(4):
            P.mm(pz[:, :], sg[:, m * 128:(m + 1) * 128], tri, [sg, CSEL], [pz])
            P.act(EALL[:, m, :, :], pz[:, :].rearrange("p (a t) -> p a t", a=4), AF.Exp, [pz], [EALL])
            yield
        P.tt('dve', GC[:, :], EALL[:, :, 2, 0], EALL[:, :, 3, 0], ALU.mult, [EALL], [GC])
        yield
        P.ts1('dve', th1[:, :], WA[:, 1, :], SD[:, SD_SELW, i:i + 1], ALU.mult, [WA, SD], [th1])
        for m in range(4):
            P.mm(pz[:, m * 128:(m + 1) * 128], A2C[:, m * 128:(m + 1) * 128], th1[:, :], [A2C, th1], [pz])
        for m in range(4):
            P.act(tB[:, m * 128:(m + 1) * 128], pz[:, m * 128:(m + 1) * 128], AF.Exp, [pz, NA0S], [tB],
                  bias=NA0S[:, m:m + 1], scale=-1.0)
        yield
        P.ts1('dve', tB[:, :], tB[:, :], 1.0, ALU.add, [tB], [tB])
        P.recip(tB[:, :], tB[:, :], [tB], [tB])
        yield
        for m in range(4):
            P.ts('pool', tC[:, m * 128:(m + 1) * 128], tB[:, m * 128:(m + 1) * 128], -1.0, pc(PC_KA + m),
                 ALU.add, ALU.mult, [tB, PCOL], [tC])
        P.stt('dve', tC[:, :], tC[:, :], 1.0, RKV[:, 4:8, :].rearrange("p m t -> p (m t)"), ALU.add, ALU.mult,
              [tC, RKV], [tC])
        P.tt('pool', tD[:, :], tA[:, :], tB[:, :], ALU.mult, [tA, tB], [tD])
        if kind != 'S':
            for m in range(4):
                P.stt('pool', tE[:, m * 128:(m + 1) * 128], RKV[:, m, :], pc(PC_RK + m),
                      tC[:, m * 128:(m + 1) * 128], ALU.mult, ALU.mult, [RKV, PCOL, tC], [tE])
            for m in range(4):
                P.mm(pz[:, 2 * m:2 * m + 2], tE[:, m * 128:(m + 1) * 128], CST[:, C_BO2:C_BO2 + 2],
                     [tE, CST], [pz])
            P.cp('act', ssum[:, :], pz[:, 0:8], [pz], [ssum])
        yield
        E = [EALL[:, :, a, :] for a in range(4)]
        A3 = ARt[:, :, :].rearrange("p m (a t) -> p m a t", a=2)
        B3 = BKt[:, :, :].rearrange("p m (a t) -> p m a t", a=2)
        P.stt('dve', A3[:, :, 0, :], f3(tA), -1.0, E[0], ALU.mult, ALU.mult, [tA, EALL], [ARt])
        if kind != 'S':
            P.tt('pool', A3[:, :, 1, :], r3, E[2], ALU.mult, [RKV, EALL], [ARt])
        yield
        P.tt('dve', B3[:, :, 0, :], f3(tD), E[1], ALU.mult, [tD, EALL], [BKt])
        P.tt('pool', B3[:, :, 1, :], f3(tC), E[1], ALU.mult, [tC, EALL], [BKt])
        yield
        P.tt('dve', f3(BhT), f3(tD), E[3], ALU.mult, [tD, EALL], [BhT])
        P.tt('pool', f3(KhT), f3(tC), E[3], ALU.mult, [tC, EALL], [KhT])
        for (srcb, srcf, dstb) in [(ARt, lambda m: ARt[:, m, 0:128], Atok), (BhT, lambda m: BhT[:, m * 128:(m + 1) * 128], Bhtok),
                                   (KhT, lambda m: KhT[:, m * 128:(m + 1) * 128], Khtok), (RKV, lambda m: RKV[:, 8 + m, :], Vtok)]:
            for m in range(4):
                P.tr(pt[:, m * 128:(m + 1) * 128], srcf(m), ident, [srcb, CST], [pt])
            P.cp('act', dstb[:, :], pt[:, :], [pt], [dstb])
            yield

        yield
        if kv is not None:
            pb2 = proj(2688, 128, 128)
            P.cp('act', mla1[:, 0, :], pb2[:, 0:128], [pb2], [mla1])
            pb2 = proj(2816, 128, 128)
            P.cp('act', mla1[:, 1, :], pb2[:, 0:128], [pb2], [mla1])
            P.act(mla2[:, 0:2, :], mla1[:, 0:2, :], AF.Square, [mla1], [mla2])
            P.mm(pz[:, 0:128], ones, mla2[:, 0, :], [CST, mla2], [pz], start=True, stop=False)
            P.mm(pz[:, 0:128], ones, mla2[:, 1, :], [CST, mla2], [pz], start=False, stop=True)
            rsqrt_ln(mla2, mla2[:, 2, :], pz, pz[:, 0:128], 1.0 / 256, pc(PC_EPS))
            for c in range(2):
                P.stt('dve', kvb[:, c, :], mla1[:, c, :], pc(PC_KVG + c), mla2[:, 2, :], ALU.mult, ALU.mult,
                      [mla1, PCOL, mla2], [kvb])
            P.dma(CKVs[:, :, kv * 128:(kv + 1) * 128], kvb[:, :, :], reads=[kvb], writes=[CKVs])
            yield
            P.dma(rpk[64:96, :, :], ropek[kv].rearrange("a d t -> d a t"), writes=[rpk])
            pk1 = proj(0, 96, 128, lhs=lambda kc: WKR[:, kc, 0, :])
            yield
            pk2 = proj(0, 96, 128, lhs=lambda kc: WKR[:, kc, 1, :])
            P.tt('dve', mla1[64:96, 2, :], pk1[64:96, 0:128], rpk[64:96, 0, :], ALU.mult, [pk1, rpk], [mla1])
            P.tt('dve', mla1[64:96, 3, :], pk2[64:96, 0:128], rpk[64:96, 1, :], ALU.mult, [pk2, rpk], [mla1])
            P.tt('dve', krb[64:96, :], mla1[64:96, 2, :], mla1[64:96, 3, :], ALU.add, [mla1], [krb])
            P.dma(KRs[:, kv * 128:(kv + 1) * 128], krb[64:96, :], reads=[krb], writes=[KRs])

        yield

    def heads(i, gen):
        kind, kv = SLOTS[i]
        S = i % 2
        ARt, BKt, Atok, Bhtok, Khtok, Vtok = ARtL[S], BKtL[S], AtokL[S], BhtokL[S], KhtokL[S], VtokL[S]
        GC, MSK, ssum = GCL[S], MSKL[S], ssumL[S]
        hT_ = hT[i % 2]
        proj = proj_for(hT_, PCNT)

        def f3(b_):
            return b_[:, :].rearrange("p (m t) -> p m t", m=4)

        def advance(n):
            if gen is None:
                return
            for _ in range(n):
                try:
                    next(gen)
                except StopIteration:
                    return
        Hin = Hc[state['h'] % 2]
        Hout = Hc[(state['h'] + 1) % 2]
        mskt = MSK[:, 0:256]
        mskn = MSK[:, 256:384]
        if kind == 'S':
            P.ts1('dve', Hin[:, :, :], Hin[:, :, :], SD[:, SD_RHO, i:i + 1], ALU.mult, [Hin, SD], [Hin])
        own = kind != 'S'

        def head_pre(h):
            m, pb = h // 2, 64 * (h % 2)
            par = h % 2
            cb = 64 * h
            at_, xpb, ptb = AT[par], XPBS[par], PTBS[par]
            Bt_h = BKt[pb:pb + 64, m, 0:128]
            Kt_h = BKt[pb:pb + 64, m, 128:256]
            AR_h = ARt[pb:pb + 64, m, 0:256]
            At_h = ARt[pb:pb + 64, m, 0:128]
            XP = xpb[0]
            pt0 = ptb[0]
            if own:
                P.mm(pA[:, 0:256], Bt_h, AR_h, [BKt, ARt], [pA])
                P.mm(pA[:, 256:512], Kt_h, AR_h, [BKt, ARt], [pA])
                P.mm(pG[:, 128:256], At_h, Bt_h, [ARt, BKt], [pG])
                an_src, an_buf = pG[:, 128:256], pG
            else:
                P.mm(pA[:, 0:128], Bt_h, At_h, [BKt, ARt], [pA])
                P.mm(pA[:, 256:384], Kt_h, At_h, [BKt, ARt], [pA])
                P.mm(pA[:, 128:256], At_h, Bt_h, [ARt, BKt], [pA])
                an_src, an_buf = pA[:, 128:256], pA
            P.tt('dve', pt0[:, :], pA[:, 0:128], mskt[:, 0:128], ALU.mult, [pA, MSK], [pt0])
            if own:
                P.tt('dve', at_[:, 128:256], pA[:, 128:256], mskt[:, 128:256], ALU.mult, [pA, MSK], [at_])
                P.tt('dve', at_[:, 256:512], pA[:, 256:512], mskt, ALU.mult, [pA, MSK], [at_])
            else:
                P.tt('dve', at_[:, 256:384], pA[:, 256:384], mskt[:, 0:128], ALU.mult, [pA, MSK], [at_])
            P.tt('dve', XP[:, 192:320], an_src, mskn, ALU.mult, [an_buf, MSK], [XP])
            P.cp('act', XP[:, 64:128], Atok[:, cb:cb + 64], [Atok], [XP])
            yield
            if own:
                wdst, wbuf = pA[:, 0:64], pA
            else:
                wdst, wbuf = pA[:, 384:448], pA
            P.mm(wdst, at_[:, 256:384], Vtok[:, cb:cb + 64], [at_, Vtok], [wbuf])
            P.cp('act', XP[:, 128:192], wdst, [wbuf], [XP])
            yield

        def head_main(h, nxt):
            m, pb = h // 2, 64 * (h % 2)
            par = h % 2
            cb = 64 * h
            at_, xpb, ptb = AT[par], XPBS[par], PTBS[par]
            xi = 0
            XP = xpb[0]
            PT_ = ptb[0]
            for lvl in range(7):
                last = lvl == 6
                XPn = xpb[(xi + 1) % 3]
                P.mm(pX[:, 0:(128 if last else 256)], PT_[:, :], XP[:, 64:(192 if last else 320)], [PT_, XP], [pX])
                if not last:
                    P.mm(pS[:, 0:128], XP[:, 192:320], PT_[:, :], [XP, PT_], [pS])
                if nxt is not None and lvl in (1, 3):
                    next(nxt, None)
                P.tt('dve', XPn[:, 64:192], XP[:, 64:192], pX[:, 0:128], ALU.add, [XP, pX], [XPn])
                if not last:
                    P.cp('act', XPn[:, 192:320], pX[:, 128:256], [pX], [XPn])
                    ptn = ptb[(lvl + 1) % 2]
                    P.cp('act', ptn[:, :], pS[:, 0:128], [pS], [ptn])
                    PT_ = ptn
                xi += 1
                XP = XPn
                advance(1)
            X = XP
            lx = X[:, 64:128] if pb == 0 else X[:, 0:128]
            P.mm(pM[0:(64 if pb == 0 else 128), 0:64], lx, Bhtok[:, cb:cb + 64], [X, Bhtok], [pM])
            P.mm(pM[:, 64:128], Bhtok[:, m * 128:(m + 1) * 128], X[:, 128:192], [Bhtok, X], [pM], start=True, stop=False)
            P.mm(pM[:, 64:128], Khtok[:, m * 128:(m + 1) * 128], Vtok[:, cb:cb + 64], [Khtok, Vtok], [pM],
                 start=False, stop=True)
            P.stt('dve', MTs[pb:pb + 64, m, :], CST[pb:pb + 64, C_ID + pb:C_ID + pb + 64], GC[pb:pb + 64, m:m + 1],
                  pM[pb:pb + 64, 0:64], ALU.mult, ALU.add, [CST, GC, pM], [MTs])
            P.cp('act', Ns[pb:pb + 64, m, :], pM[pb:pb + 64, 64:128], [pM], [Ns])
            if own:
                gt_ = GTs[h % 2]
                P.mm(pG[0:(64 if pb == 0 else 128), 0:128], lx, at_[:, 128:256], [X, at_], [pG])
                P.tt('dve', gt_[pb:pb + 64, :], pG[pb:pb + 64, 0:128], ARt[pb:pb + 64, m, 128:256], ALU.add,
                     [pG, ARt], [gt_])
                yo = pY[:, cb:cb + 64]
                P.mm(yo, at_[:, 128:256], X[:, 128:192], [at_, X], [pY], start=True, stop=False)
                P.mm(yo, at_[:, 384:512], Vtok[:, cb:cb + 64], [at_, Vtok], [pY], start=False, stop=False)
                P.mm(yo, gt_[pb:pb + 64, :], Hin[pb:pb + 64, m, :], [gt_, Hin], [pY], start=False, stop=True)
            P.mm(pH[pb:pb + 64, m * 64:(m + 1) * 64], MTs[pb:pb + 64, m, :], Hin[pb:pb + 64, m, :], [MTs, Hin], [pH])
            advance(ADV)

        for _ in head_pre(0):
            pass
        for h in range(8):
            nxt = head_pre(h + 1) if h + 1 < 8 else None
            head_main(h, nxt)
            if nxt is not None:
                for _ in nxt:
                    pass
        P.tt('dve', Hout[:, :, :], pH[:, :].rearrange("p (m i) -> p m i", m=4), Ns[:, :, :], ALU.add, [pH, Ns], [Hout])
        if i == CFG.get('dump_slot', 0):
            dump('hout', Hout, Hout[:, :, :].rearrange('p m i -> p (m i)'))
        state['h'] += 1
        advance(10 ** 6)
        if kind == 'S':
            P.stt('dve', HFa[:, :, :], Hout[:, :, :], SD[:, SD_SIG, i:i + 1], HFa[:, :, :], ALU.mult, ALU.add,
                  [Hout, SD, HFa], [HFa])

        if kind == 'B':
            jj = 15 - (i - 52)
            P.cp('act', ysb[:, :], pY[:, :], [pY], [ysb])
            P.dma(YBs[jj], ysb[:, :], reads=[ysb], writes=[YBs])
            P.dma(SBs[jj], ssum[:, :], reads=[ssum], writes=[SBs])
        if kind == 'F':
            jj = i - 68
            tok = slice(jj * 128, (jj + 1) * 128)
            P.dma(ybl[:, :], YBs[jj], reads=[YBs], writes=[ybl])
            P.dma(sbl[:, :], SBs[jj], reads=[SBs], writes=[sbl])
            P.tt('dve', ysb[:, :], pY[:, :], ybl[:, :], ALU.add, [pY, ybl], [ysb])
            P.tt('dve', ssum[:, :], ssum[:, :], sbl[:, :], ALU.add, [ssum, sbl], [ssum])
            y3 = ysb[:, :].rearrange("p (h i) -> p h i", h=8)
            P.red(st4[:, :], y3, [ysb], [st4])
            P.ts1('dve', st4[:, :], st4[:, :], 1.0 / 64, ALU.mult, [st4], [st4])
            P.act(tE[:, :], ysb[:, :], AF.Square, [ysb], [tE])
            P.red(sbl[:, :], tE[:, :].rearrange("p (h i) -> p h i", h=8), [tE], [sbl])
            P.tt('dve', ybl[:, 0:8], st4[:, :], st4[:, :], ALU.mult, [st4], [ybl])
            P.stt('dve', sbl[:, :], sbl[:, :], 1.0 / 64, ybl[:, 0:8], ALU.mult, ALU.subtract, [sbl, ybl], [sbl])
            rsqrt_ln(sbl, sbl[:, :], sbl, sbl[:, :], 1.0, pc(PC_LEPS))
            for h in range(8):
                P.ts('dve' if h % 2 else 'pool', ysb[:, h * 64:(h + 1) * 64], ysb[:, h * 64:(h + 1) * 64],
                     st4[:, h:h + 1], sbl[:, h:h + 1], ALU.subtract, ALU.mult, [ysb, st4, sbl], [ysb])
            P.tt('dve', ysb[:, :], ysb[:, :], LGB[:, 0:512], ALU.mult, [ysb, LGB], [ysb])
            P.tt('pool', ysb[:, :], ysb[:, :], LGB[:, 512:1024], ALU.add, [ysb, LGB], [ysb])
            for h in range(8):
                P.stt('dve' if h % 2 else 'pool', ysb[:, h * 64:(h + 1) * 64], Vtok[:, h * 64:(h + 1) * 64],
                      ssum[:, h:h + 1], ysb[:, h * 64:(h + 1) * 64], ALU.mult, ALU.add, [Vtok, ssum, ysb], [ysb])
            for m in range(4):
                pb2 = proj(1792 + m * 128, 128, 128)
                P.cp('act', tE[:, m * 128:(m + 1) * 128], pb2[:, 0:128], [pb2], [tE])
            P.act(tD[:, :], tE[:, :], AF.Exp, [tE], [tD], scale=-1.0)
            P.ts1('dve', tD[:, :], tD[:, :], 1.0, ALU.add, [tD], [tD])
            P.recip(tD[:, :], tD[:, :], [tD], [tD])
            P.tt('pool', tE[:, :], tE[:, :], tD[:, :], ALU.mult, [tE, tD], [tE])
            for m in range(4):
                P.tr(pt[:, m * 128:(m + 1) * 128], ysb[:, m * 128:(m + 1) * 128], ident, [ysb, CST], [pt])
            P.tt('dve', mixb[:, :, :], pt[:, :].rearrange("p (m t) -> p m t", m=4), f3(tE), ALU.mult, [pt, tE], [mixb])
            P.dma(MIXRs[:, :, tok], mixb[:, :, :], reads=[mixb], writes=[MIXRs])
            for c in range(3):
                pb2 = proj(2304 + c * 128, 128, 128)
                P.cp('act', mla1[:, c, :], pb2[:, 0:128], [pb2], [mla1])
            P.act(mla2[:, 0:3, :], mla1[:, 0:3, :], AF.Square, [mla1], [mla2])
            for c in range(3):
                P.mm(pz[:, 0:128], ones, mla2[:, c, :], [CST, mla2], [pz], start=(c == 0), stop=(c == 2))
            rsqrt_ln(mla2, mla2[:, 3, :], pz, pz[:, 0:128], 1.0 / 384, pc(PC_EPS))
            for c in range(3):
                P.stt('dve', cqb[:, c, :], mla1[:, c, :], pc(PC_QG + c), mla2[:, 3, :], ALU.mult, ALU.mult,
                      [mla1, PCOL, mla2], [cqb])
            P.dma(CQs[:, :, tok], cqb[:, :, :], reads=[cqb], writes=[CQs])
            for h in range(8):
                pb2 = proj(2976 + h * 64, 64, 128)
                P.cp('act', tD[0:64, 0:128], pb2[0:64, 0:128], [pb2], [tD])
                P.act(tD[0:64, 128:256], tD[0:64, 0:128], AF.Exp, [tD], [tD], scale=-1.0)
                P.ts1('dve', tD[0:64, 128:256], tD[0:64, 128:256], 1.0, ALU.add, [tD], [tD])
                P.recip(tD[0:64, 128:256], tD[0:64, 128:256], [tD], [tD])
                P.tt('dve', gmb[:, h, :], tD[0:64, 0:128], tD[0:64, 128:256], ALU.mult, [tD], [gmb])
            P.dma(GMs[:, :, tok], gmb[:, :, :], reads=[gmb], writes=[GMs])

    slot_list = list(range(NSLOT)) if CFG['slots'] is None else list(CFG['slots'])
    if slot_list:
        load_slot(slot_list[0])
        if len(slot_list) > 1:
            load_slot(slot_list[1])
        for _ in prep(slot_list[0]):
            pass
    for si, i in enumerate(slot_list):
        if si + 2 < len(slot_list):
            load_slot(slot_list[si + 2])
        if i == 68:
            P.cp('dve', Hc[state['h'] % 2][:, :, :], HFa[:, :, :], [HFa], [Hc[state['h'] % 2]])
        heads(i, prep(slot_list[si + 1]) if si + 1 < len(slot_list) else None)

    AR = ARENA.t
    o = 0

    def carve(n, shape_str=None, **kw):
        nonlocal o
        ap = AR[:, o:o + n]
        o += n
        if shape_str:
            ap = ap.rearrange(shape_str, **kw)
        return P.view(ap)

    ps2 = [pA, pY]

    def phase2():
        nonlocal o
        P.barrier()
        o = 0

        KT = [carve(NKEY)] * 2
        V4 = carve(NKT * 4 * 65, "p (k h c) -> p k h c", k=NKT, h=4)
        QT = [carve(2048)] * 2
        def bview(buf, nbf, pat=None, rows=128, **kw):
            t = buf.t
            flat = t[0:rows]
            nd = len(t.shape)
            if nd == 3:
                flat = t[0:rows, :, :].rearrange("p a b -> p (a b)")
            elif nd == 4:
                flat = t[0:rows, :, :, :].rearrange("p a b c -> p (a b c)")
            ap = flat[:, 0:nbf // 2].bitcast(BF16)
            if pat:
                ap = ap.rearrange(pat, **kw)
            return P.view(ap)
        WUQ = bview(RKV, 2304, "p (k c) -> p k c", k=3)
        WUQR = bview(sg, 288, "p (k c) -> p k c", k=3)
        WROT = bview(ysb, 768, "p (k h c) -> p k h c", k=3, h=8)
        WUK = bview(AtokL[0], 1024, "p (k c) -> p k c", k=2)
        WUV = bview(BhtokL[0], 1024, "p (k c) -> p k c", k=2)
        ckb = [bview(KhtokL[0], 1024, "p (k c) -> p k c", k=2), bview(VtokL[0], 1024, "p (k c) -> p k c", k=2)]
        PTb_ = [bview(AtokL[1], 512), bview(BhtokL[1], 512), bview(KhtokL[1], 512)]
        rq = P.view(ARt.t[:, :, :].rearrange('p m c -> p (m c)').rearrange('p (a t) -> p a t', a=2))
        onrm = P.view(tA.t[0:64, :])
        rec = P.view(tB.t[0:65, :])
        bcs = P.view(tC.t[0:64, :])
        mxb = bview(XPB[0], 512, rows=64)
        gml = bview(XPB[1], 512, rows=64)
        cql = bview(junk, 1536, "p (k c) -> p k c", k=3)
        w_uq_v = w_uq_d.rearrange("(k p) c -> p k c", p=128)
        w_uqr_v = w_uq_rot_d.rearrange("(k p) h c -> p k h c", p=128)
        for kc in range(3):
            P.dma(STG[:, 0:768], w_uq_v[:, kc, :], writes=[STG])
            P.cp('dve', WUQ[:, kc, :], STG[:, 0:768], [STG], [WUQ])
        P.memset('pool', WUQR[:, :, :], 0.0, [WUQR])
        P.dma(STG[:, 0:768].rearrange("p (k h c) -> p k h c", k=3, h=8), w_uqr_v, writes=[STG])
        for kc in range(3):
            for h in range(8):
                c0 = kc * 256 + h * 32
                P.tt('dve', WROT[:, kc, h, :], STG[:, c0:c0 + 32], CST[:, C_SGN:C_SGN + 32], ALU.mult, [STG, CST], [WROT])
        for kc_ in range(2):
            P.dma(STG[:, 0:512], w_uk_d.rearrange("(k p) h c -> p k (h c)", p=128)[:, kc_, :], writes=[STG])
            P.cp('dve', WUK[:, kc_, :], STG[:, 0:512], [STG], [WUK])
        P.dma(tA[:, :].rearrange("p (k c) -> p k c", k=2)[:, :, 0:256], w_uv_d.rearrange("(k p) h c -> p k (h c)", p=128)[:, :, 0:256], writes=[tA])
        P.dma(tB[:, :].rearrange("p (k c) -> p k c", k=2)[:, :, 0:256], w_uv_d.rearrange("(k p) h c -> p k (h c)", p=128)[:, :, 256:512], writes=[tB])
        P.cp('dve', WUV[:, :, 0:256], tA[:, :].rearrange("p (k c) -> p k c", k=2), [tA], [WUV])
        P.cp('dve', WUV[:, :, 256:512], tB[:, :].rearrange("p (k c) -> p k c", k=2), [tB], [WUV])

        po = [pz, pBXf]
        KB = [(kb * 512, 512) for kb in range(NKEY // 512)]
        for hg in range(2):
            for hh_ in range(4):
                P.cp('dve', V4[:, :, hh_, 64], KVV[:, :], [KVV], [V4])
            for bi, (k0, kw) in enumerate(KB):
                cb_ = ckb[bi % 2]
                P.dma(cb_[:, :, 0:kw], CKVs[:, :, k0:k0 + kw], reads=[CKVs], writes=[cb_])
                for s in range(kw // 128):
                    kt = k0 // 128 + s
                    pv = pu[kt % 2]
                    for c in range(2):
                        P.mm(pv[:, 0:256], cb_[:, c, s * 128:(s + 1) * 128], WUV[:, c, hg * 256:(hg + 1) * 256], [cb_, WUV], [pv],
                             start=(c == 0), stop=(c == 1))
                    P.ts1('dve', V4[:, kt, :, 0:64], pv[:, 0:256].rearrange("p (h c) -> p h c", h=4), KVV[:, kt:kt + 1], ALU.mult, [pv, KVV], [V4])
            for hh in range(4):
                h = hg * 4 + hh
                kt_, qt_ = KT[h % 2], QT[h % 2]
                P.dma(kt_[64:96, :], KRs[:, :], reads=[KRs], writes=[kt_])
                for bi, (k0, kw) in enumerate(KB):
                    cb_ = ckb[bi % 2]
                    P.dma(cb_[:, :, 0:kw], CKVs[:, :, k0:k0 + kw], reads=[CKVs], writes=[cb_])
                    for c in range(2):
                        P.mm(ps2[bi % 2][0:64, 0:kw], WUK[:, c, h * 64:(h + 1) * 64], cb_[:, c, 0:kw],
                             [cb_, WUK], [ps2[bi % 2]], start=(c == 0), stop=(c == 1))
                    P.cp('act' if bi % 2 else 'dve', kt_[0:64, k0:k0 + kw], ps2[bi % 2][0:64, 0:kw], [ps2[bi % 2]], [kt_])
                for kc in range(3):
                    P.cp('pool', WUQR[:, kc, 64:96], WROT[:, kc, h, :], [WROT], [WUQR])
                for qb in range(4):
                    qs = slice(qb * 512, (qb + 1) * 512)
                    P.dma(cql[:, :, :], CQs[:, :, qs], reads=[CQs], writes=[cql])
                    P.dma(rq[64:96, :, :], ropeq[:, :, qs].rearrange("a d t -> d a t"), writes=[rq])
                    for c in range(3):
                        P.mm(po[0][0:96, :], WUQ[:, c, h * 96:(h + 1) * 96], cql[:, c, :], [WUQ, cql], [po[0]],
                             start=(c == 0), stop=(c == 2))
                    for c in range(3):
                        P.mm(po[1][0:96, :], WUQR[:, c, :], cql[:, c, :], [WUQR, cql], [po[1]], start=(c == 0), stop=(c == 2))
                    P.cp('act', qt_[0:64, qs], po[0][0:64, :], [po[0]], [qt_])
                    P.tt('dve', rq[64:96, 0, :], po[0][64:96, :], rq[64:96, 0, :], ALU.mult, [po[0], rq], [rq])
                    P.tt('dve', rq[64:96, 1, :], po[1][64:96, :], rq[64:96, 1, :], ALU.mult, [po[1], rq], [rq])
                    P.tt('dve', qt_[64:96, qs], rq[64:96, 0, :], rq[64:96, 1, :], ALU.add, [rq], [qt_])
                for qb in range(4):
                    qs = slice(qb * 512, (qb + 1) * 512)
                    pacc = po[qb % 2]
                    P.dma(gml[:, :], GMs[:, h, qs], reads=[GMs], writes=[gml])
                    def qk(kt):
                        sc = ps2[kt % 2]
                        P.mm(sc[:, :], kt_[0:96, kt * 128:(kt + 1) * 128], qt_[0:96, qs], [kt_, qt_], [sc])
                    qk(0)
                    for kt in range(NKT):
                        if kt + 1 < NKT:
                            qk(kt + 1)
                        sc = ps2[kt % 2]
                        pT = PTb_[kt % 3]
                        P.act(pT[:, :], sc[:, :], AF.Exp, [sc], [pT], scale=ATTN_SCALE)
                        P.mm(pacc[0:65, :], V4[:, kt, hh, :], pT[:, :], [V4, pT], [pacc], start=(kt == 0), stop=(kt == NKT - 1))
                    P.recip(rec[64:65, :], pacc[64:65, :], [pacc], [rec])
                    P.mm(pH[0:64, 0:256], CST[64:65, C_ONES:C_ONES + 64], rec[64:65, 0:256], [CST, rec], [pH])
                    P.mm(pG[0:64, 0:256], CST[64:65, C_ONES:C_ONES + 64], rec[64:65, 256:512], [CST, rec], [pG])
                    P.cp('act', bcs[:, 0:256], pH[0:64, 0:256], [pH], [bcs])
                    P.cp('act', bcs[:, 256:512], pG[0:64, 0:256], [pG], [bcs])
                    P.tt('dve', onrm[:, :], pacc[0:64, :], bcs[:, :], ALU.mult, [pacc, bcs], [onrm])
                    P.tt('dve', mxb[:, :], onrm[:, :], gml[:, :], ALU.mult, [onrm, gml], [mxb])
                    P.dma(MIXMs[:, h, qs], mxb[:, :], reads=[mxb], writes=[MIXMs])

    def phase3():
        nonlocal o
        P.barrier()
        o = 0
        WOR = carve(4 * 1024, "p (k c) -> p k c", k=4)
        WOM = carve(8 * 1024, "p (k c) -> p k c", k=8)
        mr = [carve(512, "p (k c) -> p k c", k=4), carve(512, "p (k c) -> p k c", k=4)]
        mm_ = [carve(1024, "p (k c) -> p k c", k=8), carve(1024, "p (k c) -> p k c", k=8)]
        w_out_r = w_out_d[0:512, :].rearrange("(k p) c -> p k c", p=128)
        w_out_m = w_out_d[512:1024, :].rearrange("(k p) c -> p k c", p=64)
        for k in range(4):
            P.dma(STG[:, 0:1024], w_out_r[:, k, :], writes=[STG])
            P.cp('dve', WOR[:, k, :], STG[:, 0:1024], [STG], [WOR])
        for k in range(8):
            P.dma(STG[0:64, 0:1024], w_out_m[:, k, :], writes=[STG])
            P.cp('dve', WOM[0:64, k, :], STG[0:64, 0:1024], [STG], [WOM])
        GB = P.view(BKh.t[:, :])
        FG = P.view(BKt.t[:, :, :].rearrange('p m c -> p (m c)'))
        for half in range(2):
            for q in range(4):
                kc = half * 4 + q
                P.ts1('dve', th0[:, :], ident, MODT[:, 16 + kc, 0:1], ALU.mult, [CST, MODT], [th0])
                P.mm(pt[:, q * 128:(q + 1) * 128], ones, th0[:, :], [CST, th0], [pt])
            P.cp('act', GB[:, half * 512:(half + 1) * 512], pt[:, :], [pt], [GB])
        P.dma(FG[:, :], brow_d[:, 1024:2048], writes=[FG])
        yo_ = [P.view(RKV.t[:, 0:8, :].rearrange('p m t -> p (m t)')), P.view(EALL.t[:, 0:2, :, :].rearrange('p m a t -> p (m a t)'))]
        for jj in range(16):
            tok = slice(jj * 128, (jj + 1) * 128)
            x_ = xt[jj % 2]
            P.dma(x_[:, :], xs[68 + jj, 0:128, :], writes=[x_])
            P.dma(mr[jj % 2][:, :, :], MIXRs[:, :, tok], reads=[MIXRs], writes=[mr[jj % 2]])
            P.dma(mm_[jj % 2][0:64, :, :], MIXMs[:, :, tok], reads=[MIXMs], writes=[mm_[jj % 2]])
            y_ = yo_[jj % 2]
            for half in range(2):
                cs = slice(half * 512, (half + 1) * 512)
                pp = ps2[half]
                for k in range(4):
                    P.mm(pp[:, :], mr[jj % 2][:, k, :], WOR[:, k, cs], [mr[jj % 2], WOR], [pp], start=(k == 0), stop=False)
                for k in range(8):
                    P.mm(pp[:, :], mm_[jj % 2][0:64, k, :], WOM[0:64, k, cs], [mm_[jj % 2], WOM], [pp], start=False, stop=(k == 7))
                P.tt('dve', y_[:, cs], pp[:, :], GB[:, cs], ALU.mult, [pp, GB], [y_])
            P.tt('pool', y_[:, :], y_[:, :], x_[:, :], ALU.add, [y_, x_], [y_])
            P.act(junk[:, :], y_[:, :], AF.Square, [y_], [junk])
            P.red(st4[:, 0:1], junk[:, :], [junk], [st4])
            rsqrt_ln(st4, st4[:, 0:1], st4, st4[:, 0:1], 1.0 / 1024, pc(PC_EPS))
            P.stt('dve', y_[:, :], y_[:, :], st4[:, 0:1], FG[:, :], ALU.mult, ALU.mult, [y_, st4, FG], [y_])
            P.dma(out_d[tok, :], y_[:, :], reads=[y_])
    if CFG['p2']:
        phase2()
    if CFG['p3']:
        phase3()
    P.finish()
    return nc


DEBUG_SPECS = {'xf': (128, 192), 'at': (128, 512), 'atok': (128, 512), 'vtok': (128, 512), 'eall': (128, 2048), 'sg': (128, 512), 'rkv': (128, 1536), 'hout': (128, 256)}
_CACHE = {}


def kernel(**inputs):
    maps = host_prep(inputs)
    if 'nc' not in _CACHE:
        _CACHE['nc'] = build()
    nc = _CACHE['nc']
    res = run_bass_kernel_spmd(nc, maps, core_ids=list(range(8)))
    out = np.zeros((2, 8192, 1024), np.float32)
    for c in range(8):
        b, p = c // 4, c % 4
        out[b, 2048 * p:2048 * (p + 1)] = np.asarray(res.results[c]['out'], np.float32)
    if DEBUG:
        _CACHE['res'] = res
    return out
```
